# Optimizing a Trainium2 kernel written in Bass

```python
import numpy as np
import jax
import jax.numpy as jnp
from jax import lax

D_MODEL = 1024
BATCH = 4
SEQ = 4096
DEPTH = 2
DEC_BATCH = 32
DEC_SEQ = 1
PAST_LEN = 8192
PAGE_SIZE = 128

HEAD_DIM = 64
BRANCH_W = D_MODEL // 2
N_HEADS = BRANCH_W // HEAD_DIM
N_KV = N_HEADS // 4
GROUP = N_HEADS // N_KV
ROT_DIM = HEAD_DIM // 4
ROPE_THETA = 500000.0
CMP_LEN = 32
CMP_STRIDE = 16
CMP_RATIO = CMP_LEN // CMP_STRIDE
CMP_HIDDEN = 2 * HEAD_DIM
SLC_BLOCK = 64
N_SELECT = 16
WINDOW = 512
Q_BLOCK = 128
CONV_CH = BRANCH_W
CONV_W = 3
MEM_LEN = 256
MEM_HEADS = 4
MEM_HEAD_DIM = BRANCH_W // MEM_HEADS
N_BRANCH = 3
D_FF = ((8 * D_MODEL + 767) // 768) * 256
N_IN = N_HEADS * HEAD_DIM + 6 * N_KV * HEAD_DIM + 3 * N_HEADS + 3 * CONV_CH + MEM_HEADS * MEM_HEAD_DIM + N_BRANCH * D_MODEL
EPS = 1e-6

kernel_name = 'nsa_shortconv_memxattn_hybrid_step'


def rmsnorm(x, g):
    xf = x.astype(jnp.float32)
    y = xf * lax.rsqrt(jnp.mean(xf * xf, axis=-1, keepdims=True) + EPS)
    return (y * g.astype(jnp.float32)).astype(x.dtype)


def rope(x, pos):
    half = ROT_DIM // 2
    inv = ROPE_THETA ** (-jnp.arange(half, dtype=jnp.float32) / half)
    ang = pos.astype(jnp.float32)[:, None] * inv[None, :]
    cos = jnp.cos(ang)[:, None, :]
    sin = jnp.sin(ang)[:, None, :]
    xr = x[..., :ROT_DIM].astype(jnp.float32)
    x1, x2 = xr[..., :half], xr[..., half:]
    rot = jnp.concatenate([x1 * cos - x2 * sin, x2 * cos + x1 * sin], axis=-1)
    return jnp.concatenate([rot.astype(x.dtype), x[..., ROT_DIM:]], axis=-1)


def masked_softmax(s, mask):
    s = jnp.where(mask, s.astype(jnp.float32), -1e30)
    return jnp.where(mask, jax.nn.softmax(s, axis=-1), 0.0)


def split_in(h):
    sizes = [N_HEADS * HEAD_DIM, 6 * N_KV * HEAD_DIM, 3 * N_HEADS, CONV_CH, CONV_CH, CONV_CH,
             MEM_HEADS * MEM_HEAD_DIM]
    return jnp.split(h, np.cumsum(sizes).tolist(), axis=-1)


def prep(x, pos, p):
    B, T, _ = x.shape
    xn = rmsnorm(x, p['norm_mix'])
    h = jnp.einsum('btd,de->bte', xn, p['w_in'])
    q, kv, ng, cx, cb, cc, mq, mg = split_in(h)
    q = rmsnorm(q.reshape(B, T, N_HEADS, HEAD_DIM), p['q_norm'])
    kv = kv.reshape(B, T, 3, 2, N_KV, HEAD_DIM)
    k_slc = rope(rmsnorm(kv[:, :, 1, 0], p['k_norm'][1]), pos)
    k_win = rope(rmsnorm(kv[:, :, 2, 0], p['k_norm'][2]), pos)
    return dict(
        q=q, q_rot=rope(q, pos),
        rows=jnp.stack([kv[:, :, 0, 0], kv[:, :, 0, 1], k_slc, kv[:, :, 1, 1]], axis=2),
        win=jnp.stack([k_win, kv[:, :, 2, 1]], axis=2),
        ng=jax.nn.sigmoid(ng).reshape(B, T, 3, N_HEADS),
        u=cc * cx, cb=cb,
        mq=rmsnorm(mq.reshape(B, T, MEM_HEADS, MEM_HEAD_DIM), p['mem_q_norm']),
        mg=jax.nn.sigmoid(mg).reshape(B, T, N_BRANCH, D_MODEL))


def compress(rows, pe, w1, w2):
    B, T = rows.shape[:2]
    n_sub = T // CMP_STRIDE
    nc = n_sub - CMP_RATIO + 1
    sub = rows[:, :n_sub * CMP_STRIDE].reshape(B, n_sub, CMP_STRIDE, N_KV, HEAD_DIM)
    w1r = w1.reshape(CMP_RATIO, CMP_STRIDE, HEAD_DIM, CMP_HIDDEN)
    per = pe.reshape(CMP_RATIO, CMP_STRIDE, 1, HEAD_DIM)
    h = jnp.einsum('bnsgd,sdh->bngh', sub[:, :nc] + per[0], w1r[0])
    for r in range(1, CMP_RATIO):
        h = h + jnp.einsum('bnsgd,sdh->bngh', sub[:, r:r + nc] + per[r], w1r[r])
    return jnp.einsum('bngh,he->bnge', jax.nn.silu(h), w2)


def compress_kv(k_raw, v_raw, p):
    kc = rmsnorm(compress(k_raw, p['cmp_pe'][0], p['cmp_w1'][0], p['cmp_w2'][0]), p['k_norm'][0])
    vc = compress(v_raw, p['cmp_pe'][1], p['cmp_w1'][1], p['cmp_w2'][1])
    return kc, vc


def cmp_attend(q, q_pos, kc, vc):
    B, Tq = q.shape[:2]
    nc = kc.shape[1]
    ends = jnp.arange(nc) * CMP_STRIDE + CMP_LEN - 1
    mask = ends[None, :] <= q_pos[:, None]
    qg = q.reshape(B, Tq, N_KV, GROUP, HEAD_DIM)
    s = jnp.einsum('bqghd,bcgd->bqghc', qg, kc) * HEAD_DIM ** -0.5
    pr = masked_softmax(s, mask[None, :, None, None, :])
    o = jnp.einsum('bqghc,bcgd->bqghd', pr.astype(vc.dtype), vc)
    return o.reshape(B, Tq, N_HEADS, HEAD_DIM), jnp.sum(pr, axis=3)


def select_blocks(imp, q_pos, n_keys):
    nc = imp.shape[-1]
    ns = -(-n_keys // SLC_BLOCK)
    i = jnp.arange(nc)[:, None]
    j = jnp.arange(ns)[None, :]
    overlap = ((i * CMP_STRIDE < (j + 1) * SLC_BLOCK) & (i * CMP_STRIDE + CMP_LEN > j * SLC_BLOCK)).astype(jnp.float32)
    score = jnp.einsum('bqgc,cs->bqgs', imp.astype(jnp.float32), overlap)
    qb = (q_pos // SLC_BLOCK)[:, None]
    forced = (j == 0) | (j == qb) | (j == qb - 1)
    eligible = j * SLC_BLOCK <= q_pos[:, None]
    score = jnp.where(forced[None, :, None, :], jnp.inf,
                      jnp.where(eligible[None, :, None, :], score, -jnp.inf))
    _, idx = lax.top_k(score, min(N_SELECT, ns))
    return idx


def attend_gathered(q, q_pos, k, v, k_pos):
    B, Tq = q.shape[:2]
    qg = q.reshape(B, Tq, N_KV, GROUP, HEAD_DIM)
    s = jnp.einsum('bqghd,bqgkd->bqghk', qg, k) * HEAD_DIM ** -0.5
    mask = (k_pos <= q_pos[None, :, None, None])[:, :, :, None, :]
    pr = masked_softmax(s, mask)
    o = jnp.einsum('bqghk,bqgkd->bqghd', pr.astype(v.dtype), v)
    return o.reshape(B, Tq, N_HEADS, HEAD_DIM)


def window_attend(q, q_pos, k, v, k_pos):
    B, Tq = q.shape[:2]
    qg = q.reshape(B, Tq, N_KV, GROUP, HEAD_DIM)
    s = jnp.einsum('bqghd,bkgd->bqghk', qg, k) * HEAD_DIM ** -0.5
    d = q_pos[:, None] - k_pos[None, :]
    mask = (d >= 0) & (d < WINDOW) & (k_pos[None, :] >= 0)
    pr = masked_softmax(s, mask[None, :, None, None, :])
    o = jnp.einsum('bqghk,bkgd->bqghd', pr.astype(v.dtype), v)
    return o.reshape(B, Tq, N_HEADS, HEAD_DIM)


def merge_blocks(o):
    nb, B, qb = o.shape[:3]
    return jnp.moveaxis(o, 0, 1).reshape(B, nb * qb, N_HEADS, HEAD_DIM)


def combine_nsa(ng, o_c, o_s, o_w):
    o = ng[:, :, 0, :, None] * o_c + ng[:, :, 1, :, None] * o_s + ng[:, :, 2, :, None] * o_w
    return o.reshape(o.shape[0], o.shape[1], N_HEADS * HEAD_DIM)


def nsa_prompt(t, p):
    q, q_rot, rows = t['q'], t['q_rot'], t['rows']
    B, T = q.shape[:2]
    pos = jnp.arange(T)
    kc, vc = compress_kv(rows[:, :, 0], rows[:, :, 1], p)
    o_c, imp = cmp_attend(q, pos, kc, vc)
    idx = select_blocks(imp, pos, T)
    k_s, v_s = rows[:, :, 2], rows[:, :, 3]
    bi = jnp.arange(B)[:, None, None, None]
    gi = jnp.arange(N_KV)[None, None, :, None]
    offs = jnp.arange(SLC_BLOCK)
    starts = jnp.arange(T // Q_BLOCK) * Q_BLOCK

    def slc_block(start):
        qb = lax.dynamic_slice_in_dim(q_rot, start, Q_BLOCK, 1)
        ib = lax.dynamic_slice_in_dim(idx, start, Q_BLOCK, 1)
        kpos = (ib[..., None] * SLC_BLOCK + offs).reshape(B, Q_BLOCK, N_KV, -1)
        kposc = jnp.minimum(kpos, T - 1)
        return attend_gathered(qb, start + jnp.arange(Q_BLOCK), k_s[bi, kposc, gi], v_s[bi, kposc, gi], kpos)

    o_s = merge_blocks(lax.map(slc_block, starts))
    wp = jnp.pad(t['win'], ((0, 0), (WINDOW, 0), (0, 0), (0, 0), (0, 0)))

    def win_block(start):
        qb = lax.dynamic_slice_in_dim(q_rot, start, Q_BLOCK, 1)
        kvb = lax.dynamic_slice_in_dim(wp, start, WINDOW + Q_BLOCK, 1)
        kpos = start - WINDOW + jnp.arange(WINDOW + Q_BLOCK)
        return window_attend(qb, start + jnp.arange(Q_BLOCK), kvb[:, :, 0], kvb[:, :, 1], kpos)

    o_w = merge_blocks(lax.map(win_block, starts))
    return combine_nsa(t['ng'], o_c, o_s, o_w)


def nsa_sample(t, pool, layer, win_buf, page_table, p):
    q, q_rot, rows = t['q'], t['q_rot'], t['rows']
    B, Tn = q.shape[:2]
    n_pages = page_table.shape[1]
    pos = PAST_LEN + jnp.arange(Tn)
    past_c = pool[layer, page_table, :, :2].reshape(B, PAST_LEN, 2, N_KV, HEAD_DIM)
    full_c = jnp.concatenate([past_c, rows[:, :, :2]], axis=1)
    kc, vc = compress_kv(full_c[:, :, 0], full_c[:, :, 1], p)
    o_c, imp = cmp_attend(q, pos, kc, vc)
    idx = select_blocks(imp, pos, PAST_LEN + Tn)
    bi = jnp.arange(B)[:, None, None, None]
    gi = jnp.arange(N_KV)[None, None, :, None]
    kpos = (idx[..., None] * SLC_BLOCK + jnp.arange(SLC_BLOCK)).reshape(B, Tn, N_KV, -1)
    phys = page_table[bi, jnp.minimum(kpos // PAGE_SIZE, n_pages - 1)]
    off = kpos % PAGE_SIZE
    rel = jnp.clip(kpos - PAST_LEN, 0, Tn - 1)
    is_past = (kpos < PAST_LEN)[..., None]
    k_s = jnp.where(is_past, pool[layer, phys, off, 2, gi], rows[bi, rel, 2, gi])
    v_s = jnp.where(is_past, pool[layer, phys, off, 3, gi], rows[bi, rel, 3, gi])
    o_s = attend_gathered(q_rot, pos, k_s, v_s, kpos)
    w_buf = win_buf.shape[1]
    wfull = jnp.concatenate([win_buf, t['win']], axis=1)
    kpos_w = PAST_LEN - w_buf + jnp.arange(w_buf + Tn)
    o_w = window_attend(q_rot, pos, wfull[:, :, 0], wfull[:, :, 1], kpos_w)
    return combine_nsa(t['ng'], o_c, o_s, o_w), wfull[:, -w_buf:]


def dwconv(u_ext, w):
    T = u_ext.shape[1] - (CONV_W - 1)
    y = u_ext[:, :T] * w[0]
    for k in range(1, CONV_W):
        y = y + u_ext[:, k:k + T] * w[k]
    return y


def mem_kv_from(mem, p):
    B, M, _ = mem.shape
    kv = jnp.einsum('bmd,de->bme', rmsnorm(mem, p['norm_mem']), p['w_mem_kv'])
    kv = kv.reshape(B, M, 2, MEM_HEADS, MEM_HEAD_DIM)
    return jnp.stack([rmsnorm(kv[:, :, 0], p['mem_k_norm']), kv[:, :, 1]], axis=2)


def mem_attend(mq, mem_kv):
    B, T = mq.shape[:2]
    s = jnp.einsum('bthd,bmhd->bthm', mq, mem_kv[:, :, 0]) * MEM_HEAD_DIM ** -0.5
    pr = jax.nn.softmax(s.astype(jnp.float32), axis=-1)
    o = jnp.einsum('bthm,bmhd->bthd', pr.astype(mem_kv.dtype), mem_kv[:, :, 1])
    return o.reshape(B, T, MEM_HEADS * MEM_HEAD_DIM)


def finish(x, o_nsa, z, o_mem, mg, p):
    br = jnp.stack([o_nsa, z, o_mem], axis=2)
    proj = jnp.einsum('btnc,ncd->btnd', br, p['w_branch'])
    h = x + jnp.einsum('btd,de->bte', jnp.sum(mg * proj, axis=2), p['w_out'])
    gu = jnp.einsum('btd,df->btf', rmsnorm(h, p['norm_ffn']), p['w_gate_up'])
    g, u = jnp.split(gu, 2, axis=-1)
    return h + jnp.einsum('btf,fd->btd', jax.nn.silu(g) * u, p['w_down'])


def setup_inputs(seed: int = 0) -> dict:
    key = jax.random.key(seed)
    ks = jax.random.split(key, 32)
    f32 = jnp.float32
    n_pages = PAST_LEN // PAGE_SIZE
    n_used = DEC_BATCH * n_pages
    n_phys = n_used + (n_used + 3) // 4
    w_buf = min(WINDOW, PAST_LEN)

    def nrm(k, shape, scale):
        return (scale * jax.random.normal(k, shape)).astype(f32)

    def gain(k, shape):
        return (1.0 + 0.02 * jax.random.normal(k, shape)).astype(f32)

    page_table = jax.random.permutation(ks[0], n_phys)[:n_used].reshape(DEC_BATCH, n_pages).astype(jnp.int32)
    return {
        'x_prompt': nrm(ks[1], (BATCH, SEQ, D_MODEL), 1.0),
        'x_sample': nrm(ks[2], (DEC_BATCH, DEC_SEQ, D_MODEL), 1.0),
        'cache_nsa_kv': nrm(ks[3], (DEPTH, n_phys, PAGE_SIZE, 4, N_KV, HEAD_DIM), 1.0),
        'cache_win_kv': nrm(ks[4], (DEPTH, DEC_BATCH, w_buf, 2, N_KV, HEAD_DIM), 1.0),
        'state_conv': nrm(ks[5], (DEPTH, DEC_BATCH, CONV_W - 1, CONV_CH), 1.0),
        'cache_mem_kv': nrm(ks[6], (DEPTH, DEC_BATCH, MEM_LEN, 2, MEM_HEADS, MEM_HEAD_DIM), 1.0),
        'page_table': page_table,
        'mem_prompt': nrm(ks[7], (BATCH, MEM_LEN, D_MODEL), 1.0),
        'norm_mix': gain(ks[8], (DEPTH, D_MODEL)),
        'w_in': nrm(ks[9], (DEPTH, D_MODEL, N_IN), D_MODEL ** -0.5),
        'q_norm': gain(ks[10], (DEPTH, HEAD_DIM)),
        'k_norm': gain(ks[11], (DEPTH, 3, HEAD_DIM)),
        'cmp_pe': nrm(ks[12], (DEPTH, 2, CMP_LEN, HEAD_DIM), 0.1),
        'cmp_w1': nrm(ks[13], (DEPTH, 2, CMP_LEN * HEAD_DIM, CMP_HIDDEN), (CMP_LEN * HEAD_DIM) ** -0.5),
        'cmp_w2': nrm(ks[14], (DEPTH, 2, CMP_HIDDEN, HEAD_DIM), CMP_HIDDEN ** -0.5),
        'conv_w': nrm(ks[15], (DEPTH, CONV_W, CONV_CH), CONV_W ** -0.5),
        'norm_mem': gain(ks[16], (DEPTH, D_MODEL)),
        'w_mem_kv': nrm(ks[17], (DEPTH, D_MODEL, 2 * MEM_HEADS * MEM_HEAD_DIM), D_MODEL ** -0.5),
        'mem_q_norm': gain(ks[18], (DEPTH, MEM_HEAD_DIM)),
        'mem_k_norm': gain(ks[19], (DEPTH, MEM_HEAD_DIM)),
        'w_branch': nrm(ks[20], (DEPTH, N_BRANCH, BRANCH_W, D_MODEL), BRANCH_W ** -0.5),
        'w_out': nrm(ks[21], (DEPTH, D_MODEL, D_MODEL), D_MODEL ** -0.5),
        'norm_ffn': gain(ks[22], (DEPTH, D_MODEL)),
        'w_gate_up': nrm(ks[23], (DEPTH, D_MODEL, 2 * D_FF), D_MODEL ** -0.5),
        'w_down': nrm(ks[24], (DEPTH, D_FF, D_MODEL), D_FF ** -0.5),
    }


def reference(x_prompt, x_sample, cache_nsa_kv, cache_win_kv, state_conv, cache_mem_kv, page_table, mem_prompt,
              norm_mix, w_in, q_norm, k_norm, cmp_pe, cmp_w1, cmp_w2, conv_w, norm_mem, w_mem_kv,
              mem_q_norm, mem_k_norm, w_branch, w_out, norm_ffn, w_gate_up, w_down):
    xp, xs = x_prompt, x_sample
    pos_p = jnp.arange(xp.shape[1])
    pos_s = PAST_LEN + jnp.arange(xs.shape[1])
    w_prompt = min(WINDOW, xp.shape[1])
    rows_p, rows_s, win_p, win_s, conv_p, conv_s, mem_p = [], [], [], [], [], [], []
    for l in range(DEPTH):
        p = dict(norm_mix=norm_mix[l], w_in=w_in[l], q_norm=q_norm[l], k_norm=k_norm[l], cmp_pe=cmp_pe[l],
                 cmp_w1=cmp_w1[l], cmp_w2=cmp_w2[l], norm_mem=norm_mem[l], w_mem_kv=w_mem_kv[l],
                 mem_q_norm=mem_q_norm[l], mem_k_norm=mem_k_norm[l], w_branch=w_branch[l], w_out=w_out[l],
                 norm_ffn=norm_ffn[l], w_gate_up=w_gate_up[l], w_down=w_down[l])
        t = prep(xp, pos_p, p)
        o_nsa = nsa_prompt(t, p)
        u_ext = jnp.pad(t['u'], ((0, 0), (CONV_W - 1, 0), (0, 0)))
        z = t['cb'] * dwconv(u_ext, conv_w[l])
        mkv = mem_kv_from(mem_prompt, p)
        o_mem = mem_attend(t['mq'], mkv)
        xp = finish(xp, o_nsa, z, o_mem, t['mg'], p)
        rows_p.append(t['rows'])
        win_p.append(t['win'][:, -w_prompt:])
        conv_p.append(u_ext[:, -(CONV_W - 1):])
        mem_p.append(mkv)
        t = prep(xs, pos_s, p)
        o_nsa, new_win = nsa_sample(t, cache_nsa_kv, l, cache_win_kv[l], page_table, p)
        u_ext = jnp.concatenate([state_conv[l], t['u']], axis=1)
        z = t['cb'] * dwconv(u_ext, conv_w[l])
        o_mem = mem_attend(t['mq'], cache_mem_kv[l])
        xs = finish(xs, o_nsa, z, o_mem, t['mg'], p)
        rows_s.append(t['rows'])
        win_s.append(new_win)
        conv_s.append(u_ext[:, -(CONV_W - 1):])
    return (xp, xs, jnp.stack(rows_p), jnp.stack(rows_s), jnp.stack(win_p), jnp.stack(win_s),
            jnp.stack(conv_p), jnp.stack(conv_s), jnp.stack(mem_p))
```

```python
import contextlib
import numpy as np
import concourse.bass as bass
import concourse.mybir as mybir
from concourse.bass_utils import run_bass_kernel_spmd

F32 = mybir.dt.float32
BF16 = mybir.dt.bfloat16
I32 = mybir.dt.int32
ALU = mybir.AluOpType
AF = mybir.ActivationFunctionType
AX = mybir.AxisListType
ENGS = ['pe', 'act', 'dve', 'pool', 'sp']

D = 1024
T = 4096
TT = 512
NTILE = 8
DEPTH = 2
N_IN = 6424
DFF = 2816
EPS = 1e-6
MASKV = -30000.0
C_CX, C_CB, C_CC, C_MQ, C_MG = 1304, 1816, 2328, 2840, 3352


class Sched:
    NDSEM = 8

    def __init__(self, nc):
        self.nc = nc
        self.ops = []
        self.stack = contextlib.ExitStack()
        self._n = 0

    def sb(self, shape, dt, name=None):
        self._n += 1
        return self.stack.enter_context(self.nc.sbuf_tensor(name or f"sb{self._n}", list(shape), dt))

    def ps(self, shape, dt=F32, name=None):
        self._n += 1
        return self.stack.enter_context(self.nc.psum_tensor(name or f"ps{self._n}", list(shape), dt))

    def add(self, eng, fn, r=(), w=(), dma=False):
        w = tuple(w) + tuple(k + '#rd' for k in r if isinstance(k, str) and (k.startswith('ps') or k.startswith('acc')) and eng != 'pe')
        self.ops.append(dict(eng=eng, fn=fn, r=tuple(r), w=tuple(w), dma=dma))

    def emit(self):
        nc = self.nc
        ops = self.ops
        n = len(ops)
        pos = [0] * n
        cnt = {e: 0 for e in ENGS}
        dcnt = {e: 0 for e in ENGS}
        dk = [0] * n
        for i, o in enumerate(ops):
            if o['dma']:
                dk[i] = dcnt[o['eng']]
                dcnt[o['eng']] += 1
            else:
                pos[i] = cnt[o['eng']]
                cnt[o['eng']] += 1
        last_w = {}
        readers = {}
        deps = [None] * n
        for i, o in enumerate(ops):
            d = set()
            for r in o['r']:
                if r in last_w:
                    d.add(last_w[r])
            for w in o['w']:
                if w in last_w:
                    d.add(last_w[w])
                d.update(readers.get(w, ()))
            d.discard(i)
            deps[i] = d
            for r in o['r']:
                readers.setdefault(r, []).append(i)
            for w in o['w']:
                last_w[w] = i
                readers[w] = []
        clock = {e: {p: -1 for p in ENGS} for e in ENGS}
        dma_seen = {e: set() for e in ENGS}
        opclock = [None] * n
        waits = [[] for _ in range(n)]
        signal = set()
        K = self.NDSEM
        dma_by_eng = {e: [] for e in ENGS}
        for i, o in enumerate(ops):
            E = o['eng']
            ck = clock[E]
            if o['dma']:
                k = dk[i]
                if k >= K:
                    prev = dma_by_eng[E][k - K]
                    if prev not in dma_seen[E]:
                        waits[i].append(('d', prev))
                        dma_seen[E].add(prev)
                dma_by_eng[E].append(i)
            for d in sorted(deps[i], reverse=True):
                od = ops[d]
                if od['dma']:
                    if d in dma_seen[E]:
                        continue
                    waits[i].append(('d', d))
                    dma_seen[E].add(d)
                else:
                    P = od['eng']
                    if P == E and E == 'pe':
                        continue
                    if ck[P] >= pos[d]:
                        continue
                    waits[i].append(('c', d))
                    signal.add(d)
                    oc = opclock[d]
                    for p in ENGS:
                        if oc[p] > ck[p]:
                            ck[p] = oc[p]
                    if pos[d] > ck[P]:
                        ck[P] = pos[d]
            if not o['dma']:
                opclock[i] = dict(ck)
        rank = {}
        rc = {e: 0 for e in ENGS}
        for i, o in enumerate(ops):
            if not o['dma'] and i in signal:
                rc[o['eng']] += 1
                rank[i] = rc[o['eng']]
        st = self.stack
        csem = {e: st.enter_context(nc.semaphore(f"c_{e}")) for e in ENGS}
        dsem = {e: [st.enter_context(nc.semaphore(f"d_{e}{j}")) for j in range(K)] for e in ENGS if dcnt[e] > 0}

        def dsv(d):
            k = dk[d]
            return dsem[ops[d]['eng']][k % K], 16 * (k // K + 1)

        by_eng = {e: [i for i, o in enumerate(ops) if o['eng'] == e] for e in ENGS}
        self.stats = dict(n=n, signals=len(signal), waits=sum(len(w) for w in waits),
                          per_eng={e: len(by_eng[e]) for e in ENGS})

        def mk(E):
            def body(e):
                for i in by_eng[E]:
                    o = ops[i]
                    for kind, d in waits[i]:
                        if kind == 'd':
                            s, v = dsv(d)
                            e.wait_ge(s, v)
                        else:
                            e.wait_ge(csem[ops[d]['eng']], rank[d])
                    ins = o['fn'](e)
                    if o['dma']:
                        s, v = dsv(i)
                        ins.then_inc(s, 16)
                    elif i in signal:
                        ins.then_inc(csem[E], 1)
                nd = dcnt[E]
                for j in range(min(K, nd)):
                    last_k = ((nd - 1 - j) // K) * K + j
                    e.wait_ge(dsem[E][j], 16 * (last_k // K + 1))
            return body

        with nc.Block() as block:
            block.tensor(mk('pe'))
            block.scalar(mk('act'))
            block.vector(mk('dve'))
            block.gpsimd(mk('pool'))
            block.sync(mk('sp'))
        st.close()


class Rot:
    def __init__(self, bufs, name):
        self.bufs = bufs
        self.name = name
        self.i = 0

    def next(self):
        k = self.i % len(self.bufs)
        self.i += 1
        return self.bufs[k], f"{self.name}{k}"


class Builder:
    def __init__(self, nlayers=DEPTH, ntiles=NTILE, with_sample=True):
        self.nlayers = nlayers
        self.ntiles = ntiles
        self.with_sample = with_sample
        self.nc = bass.Bass("TRN2", target_bir_lowering=False)
        self.S = Sched(self.nc)
        self.lazy = {}
        self.decl = {}

    def mm(self, out, lhsT, rhs, start=True, stop=True, r=(), w=()):
        self.S.add('pe', lambda e: e.matmul(out, lhsT=lhsT, rhs=rhs, start=start, stop=stop, skip_group_check=True), r, w)

    def tr(self, out, in_, ident, r=(), w=()):
        self.S.add('pe', lambda e: e.transpose(out, in_, ident), r, w)

    def actf(self, out, in_, func, bias=None, scale=1.0, accum=None, r=(), w=()):
        kw = {}
        if bias is not None:
            kw['bias'] = bias
        if accum is not None:
            kw['accum_out'] = accum
        self.S.add('act', lambda e: e.activation(out=out, in_=in_, func=func, scale=scale, **kw), r, w)

    def cp(self, eng, out, in_, r=(), w=()):
        if eng == 'act':
            self.S.add('act', lambda e: e.copy(out=out, in_=in_), r, w)
        else:
            self.S.add(eng, lambda e: e.tensor_copy(out=out, in_=in_), r, w)

    def tt(self, eng, out, in0, in1, op, r=(), w=()):
        self.S.add(eng, lambda e: e.tensor_tensor(out=out, in0=in0, in1=in1, op=op), r, w)

    def ts(self, eng, out, in0, s1, op0, s2=None, op1=None, r=(), w=()):
        if op1 is None:
            self.S.add(eng, lambda e: e.tensor_scalar(out=out, in0=in0, scalar1=s1, scalar2=None, op0=op0), r, w)
        else:
            self.S.add(eng, lambda e: e.tensor_scalar(out=out, in0=in0, scalar1=s1, scalar2=s2, op0=op0, op1=op1), r, w)

    def stt(self, out, in0, scalar, in1, op0, op1, r=(), w=()):
        self.S.add('dve', lambda e: e.scalar_tensor_tensor(out=out, in0=in0, scalar=scalar, in1=in1, op0=op0, op1=op1), r, w)

    def recip(self, out, in_, r=(), w=()):
        self.S.add('dve', lambda e: e.reciprocal(out=out, in_=in_), r, w)

    def red(self, out, in_, r=(), w=()):
        self.S.add('dve', lambda e: e.tensor_reduce(out=out, in_=in_, axis=AX.X, op=ALU.add), r, w)

    def memset(self, eng, ap, val, r=(), w=()):
        self.S.add(eng, lambda e: e.memset(ap, val), r, w)

    def dma(self, q, out, in_, r=(), w=(), slow=False):
        if slow:
            self.S.add(q, lambda e: e.dma_start(out=out, in_=in_, allow_slow_non_contiguous=True), r, w, dma=True)
        else:
            self.S.add(q, lambda e: e.dma_start(out=out, in_=in_), r, w, dma=True)

    def din(self, name, shape, dt=F32):
        self.lazy[name] = (list(shape), dt)
        return None

    def inp(self, name):
        if name not in self.decl:
            shape, dt = self.lazy[name]
            self.decl[name] = self.nc.dram_tensor(name, shape, dt, kind="ExternalInput").ap()
        return self.decl[name]

    def dout(self, name, shape, dt=F32):
        return self.nc.dram_tensor(name, list(shape), dt, kind="ExternalOutput").ap()

    def psn(self):
        return self.psr.next()

    def f32t(self):
        return self.f32r.next()

    def b16t(self):
        return self.b16r.next()

    def wload(self, src, nk, ncols, q='pool'):
        buf, key = self.wr.next()
        v = buf[:, 0:nk * ncols].rearrange("p (k c) -> p k c", k=nk)
        self.dma(q, v, src.rearrange("(k p) c -> p k c", p=128), w=[key])
        return v, key

    def declare(self):
        S = self.S
        i = self.din
        self.x = i("x", [T, D])
        self.mem = i("mem", [256, D])
        self.w_in = i("w_in", [DEPTH, D, N_IN])
        self.w_mem = i("w_mem_kv", [DEPTH, D, 1024])
        self.w_br = i("w_branch", [DEPTH, 3, 512, D])
        self.w_out = i("w_out", [DEPTH, D, D])
        self.w_gu = i("w_gate_up", [DEPTH, D, 2 * DFF])
        self.w_dn = i("w_down", [DEPTH, DFF, D])
        self.cw1 = i("cmp_w1", [DEPTH, 2, 2048, 128])
        self.cw2 = i("cmp_w2", [DEPTH, 2, 128, 64])
        self.cpe = i("cmp_pe", [DEPTH, 2, 32, 64])
        self.convw = i("conv_w", [DEPTH, 3, 512])
        self.g_mix = i("norm_mix", [DEPTH, D])
        self.g_mem = i("norm_mem", [DEPTH, D])
        self.g_ffn = i("norm_ffn", [DEPTH, D])
        self.g12 = i("g12", [DEPTH, 768])
        self.k0g = i("k0g", [DEPTH, 64])
        self.mqg = i("mem_q_norm", [DEPTH, 128])
        self.mkg = i("mkg", [DEPTH, 512])
        self.c_cos = i("c_cos", [128, 32, 8])
        self.c_sin = i("c_sin", [128, 32, 8])
        self.c_selA = i("c_selA", [128, 32, 64])
        self.c_selB = i("c_selB", [128, 32, 64])
        self.c_tri = i("c_tri", [128, 128])
        self.c_strict = i("c_strict", [128, 128])
        self.c_stair = i("c_stair", [128, 512])
        self.c_ovl = i("c_ovl", [128, 2, 64])
        self.c_E = i("c_E", [64, T])
        self.c_ident = i("c_ident", [128, 128])
        o = self.dout
        self.y = o("y", [T, D])
        self.rows_o = o("rows_p", [DEPTH, T, 512])
        self.win_o = o("win_p", [DEPTH, 512, 256])
        self.conv_o = o("conv_p", [DEPTH, 2, 512])
        self.mem_o = o("mem_p", [DEPTH, 256, 1024])
        self.hres = self.nc.dram_tensor("hres", [8, 128, T], F32).ap()

        sb = S.sb
        self.xT = sb([128, 8, TT], F32, 'xT')
        self.xnT = sb([128, 8, TT], BF16, 'xnT')
        self.QB = sb([128, 8, TT], BF16, 'QB')
        self.QU = sb([128, 4, TT], BF16, 'QU')
        self.mqT = sb([128, 4, TT], BF16, 'mqT')
        self.brT = sb([128, 12, TT], BF16, 'brT')
        self.mT = self.QB
        self.actT = sb([128, 6, TT], BF16, 'actT')
        self.tmior = Rot([sb([128, D], F32, f'tmio{k}') for k in range(2)], 'tmio')
        self.KE = [sb([128, T], BF16, f'KE{g}') for g in range(2)]
        self.Vs = sb([128, 32, 2, 66], BF16, 'Vs')
        self.kwT = [sb([64, 1024], BF16, f'kwT{g}') for g in range(2)]
        self.Vw = sb([128, 8, 2, 66], BF16, 'Vw')
        self.kcT2 = [sb([128, 256], BF16, f'kcT{g}') for g in range(2)]
        self.Rc = [sb([128, 2, 130], BF16, f'Rc{g}') for g in range(2)]
        self.rawk = sb([128, 528], BF16, 'rawk')
        self.rawv = sb([128, 528], BF16, 'rawv')
        self.mkT = sb([128, 4, 256], BF16, 'mkT')
        self.mv = sb([128, 2, 512], BF16, 'mv')
        self.ucar = sb([128, 4, 2], F32, 'ucar')
        self.uextr = Rot([sb([128, 514], F32, f'uext{k}') for k in range(2)], 'uext')
        self.onsa = sb([128, 4, 512], F32, 'onsa')
        self.macc = self.onsa
        self.scr = sb([128, 4, 2, 64], F32, 'scr')
        self.ngt = sb([128, 4, 24], F32, 'ngt')
        self.biasr = Rot([sb([128, 128], BF16, f'biasw{k}') for k in range(4)], 'biasw')
        self.selTr = Rot([sb([128, 2, 4, 64], F32, f'selT{k}') for k in range(2)], 'selT')
        self.ident_f = sb([128, 128], F32, 'ident_f')
        self.ident_b = sb([128, 128], BF16, 'ident_b')
        self.ones_b = sb([128, 128], BF16, 'ones_b')
        self.tri = sb([128, 128], BF16, 'tri')
        self.strict = sb([128, 128], BF16, 'strict')
        self.stair = sb([128, 512], BF16, 'stair')
        self.cosT = sb([128, 32, 8], F32, 'cosT')
        self.sinT = sb([128, 32, 8], F32, 'sinT')
        self.epsc = sb([128, 1], F32, 'epsc')
        self.gcols = sb([128, 3, 8], F32, 'gcols')
        self.g12b = sb([128, 12, 64], F32, 'g12b')
        self.k0gb = sb([128, 64], F32, 'k0gb')
        self.mqgc = sb([128, 1], F32, 'mqgc')
        self.cwc = sb([128, 4, 3], F32, 'cwc')
        self.w2sb = sb([128, 2, 64], BF16, 'w2sb')
        self.peT = sb([64, 2, 32], BF16, 'peT')
        self.bpe = sb([128, 2], F32, 'bpe')
        self.psr = Rot([S.ps([128, 512], F32, f'psb{k}') for k in range(6)], 'ps')
        self.accr = Rot([S.ps([128, 512], F32, f'acc{k}') for k in range(2)], 'acc')
        self.f32r = Rot([sb([128, 512], F32, f'f32t{k}') for k in range(4)], 'f32t')
        self.b16r = Rot([sb([128, 512], BF16, f'b16t{k}') for k in range(5)], 'b16t')
        self.wr = Rot([sb([128, 4096], BF16, f'wbuf{k}') for k in range(4)], 'wbuf')
        self.tmr = Rot([sb([128, 12, 64], F32, f'tm{k}') for k in range(4)], 'tm')
        self.smr = Rot([sb([128, 32], F32, f'sm{k}') for k in range(8)], 'sm')
        self.selr = Rot([sb([128, 2, 64], F32, f'sel{k}') for k in range(2)], 'sel')
        self.tsrc = Rot([sb([128, 12, 128], BF16, f'tsrc{k}') for k in range(1)], 'tsrc')
        self.rowsr = Rot([sb([128, 768], F32, f'rows{k}') for k in range(1)], 'rows')

    def setup(self):
        d = self.dma
        d('sp', self.ident_f[:], self.inp('c_ident'), w=['ident_f'])
        d('pool', self.ident_b[:], self.inp('c_ident'), w=['ident_b'])
        d('pool', self.tri[:], self.inp('c_tri'), w=['tri'])
        d('pool', self.strict[:], self.inp('c_strict'), w=['strict'])
        d('pool', self.stair[:], self.inp('c_stair'), w=['stair'])
        d('sp', self.cosT[:], self.inp('c_cos'), w=['cosT'])
        d('sp', self.sinT[:], self.inp('c_sin'), w=['sinT'])
        self.memset('pool', self.ones_b[:], 1.0, w=['ones_b'])
        self.memset('pool', self.epsc[:], EPS, w=['epsc'])
        for g in range(2):
            d('pool', self.KE[g][64:128, :], self.inp('c_E'), w=[f'KE{g}'])
            self.memset('pool', self.Rc[g][:, :, 64:65], 1.0, w=[f'Rc{g}'])
            self.memset('pool', self.Rc[g][0:1, 0, 64:65], 0.0, w=[f'Rc{g}'])
            d('pool', self.Rc[g][:, :, 65:129], self.inp('c_ovl'), w=[f'Rc{g}'])
        for k in range(4):
            self.memset('pool', self.biasr.bufs[k][:], 0.0, w=[f'biasw{k}'])
        self.memset('pool', self.Vs[:, :, :, 64:65], 1.0, w=['Vs'])
        self.memset('pool', self.Vw[:, :, :, 64:65], 1.0, w=['Vw'])

    def layer_setup(self, l):
        d = self.dma
        d('sp', self.gcols[:, 0, :], self.inp('norm_mix')[l].rearrange("(k p) -> p k", p=128), w=['gcols'], slow=True)
        d('sp', self.gcols[:, 1, :], self.inp('norm_ffn')[l].rearrange("(k p) -> p k", p=128), w=['gcols'], slow=True)
        d('sp', self.g12b[:].rearrange("p h d -> p (h d)"), self.inp('g12')[l].partition_broadcast(128), w=['g12b'])
        d('sp', self.k0gb[:], self.inp('k0g')[l].partition_broadcast(128), w=['k0gb'])
        d('sp', self.mqgc[:], self.inp('mem_q_norm')[l].rearrange("(p o) -> p o", o=1), w=['mqgc'], slow=True)
        for k in range(3):
            d('sp', self.cwc[:, :, k], self.inp('conv_w')[l, k].rearrange("(c p) -> p c", p=128), w=['cwc'], slow=True)
        d('pool', self.w2sb[:], self.inp('cmp_w2')[l].rearrange("k h e -> h k e"), w=['w2sb'])
        d('pool', self.peT[:], self.inp('cmp_pe')[l].rearrange("k s d -> d k s"), w=['peT'], slow=True)
        self.memset('pool', self.ucar[:], 0.0, w=['ucar'])
        self.memset('pool', self.rawk[:, 0:16], 0.0, w=['rawk'])
        self.memset('pool', self.rawv[:, 0:16], 0.0, w=['rawv'])
        w1 = self.load_w1(l)
        pb, pk = self.psn()
        for kind in range(2):
            v, key = w1[kind]
            for s in range(32):
                self.mm(pb[:, kind:kind + 1], lhsT=v[0:64, s, :], rhs=self.peT[0:64, kind, s:s + 1],
                        start=(s == 0), stop=(s == 31), r=[key, 'peT'], w=[pk])
        self.cp('act', self.bpe[:], pb[:, 0:2], r=[pk], w=['bpe'])
        if self.stage >= 2:
            self.mem_kv(l)

    def load_w1(self, l):
        res = []
        for kind in range(2):
            buf, key = self.wr.next()
            v = buf[:, :].rearrange("p (s h) -> p s h", s=32)
            src = self.inp('cmp_w1')[l, kind].rearrange("(s d) h -> d s h", d=64)
            self.dma('pool', v[0:64], src, w=[key])
            self.dma('pool', v[64:128], src, w=[key])
            res.append((v, key))
        return res

    def mem_kv(self, l):
        scr_ = self.onsa[:].rearrange("p a b -> p (a b)")
        gmem_b = scr_[:, 0:1024]
        mkg_b = scr_[:, 1024:1536]
        self.dma('sp', gmem_b, self.inp('norm_mem')[l].partition_broadcast(128), w=['onsa'])
        self.dma('sp', mkg_b, self.inp('mkg')[l].partition_broadcast(128), w=['onsa'])
        wk, wkk = self.wload(self.inp('w_mem_kv')[l][:, 0:512], 8, 512)
        wv, wvk = self.wload(self.inp('w_mem_kv')[l][:, 512:1024], 8, 512)
        for mb in range(2):
            mt_, mtk = self.tmior.next()
            mt = mt_[:]
            self.dma('sp', mt, self.inp('mem')[mb * 128:(mb + 1) * 128, :], w=[mtk])
            sm, smk = self.smr.next()
            jk_, jkk = self.tmior.next()
            junk = jk_[:]
            self.actf(junk, mt, AF.Square, accum=sm[:, 0:1], r=[mtk], w=[jkk, smk])
            self.actf(sm[:, 1:2], sm[:, 0:1], AF.Sqrt, bias=self.epsc[:, 0:1], scale=1.0 / D, r=[smk, 'epsc'], w=[smk])
            self.recip(sm[:, 2:3], sm[:, 1:2], r=[smk], w=[smk])
            self.stt(junk, mt, sm[:, 2:3], gmem_b, ALU.mult, ALU.mult, r=[mtk, smk, 'onsa'], w=[jkk])
            if self.stage < 2.1:
                continue
            mnb, mnk = self.b16t()
            mnb2, mnk2 = self.b16t()
            self.cp('dve', mnb[:], junk[:, 0:512], r=[jkk], w=[mnk])
            self.cp('dve', mnb2[:], junk[:, 512:1024], r=[jkk], w=[mnk2])
            if self.stage < 2.12:
                continue
            pt, ptk = self.psn()
            ptb = pt[:].bitcast(BF16)
            for kc in range(8):
                src = (mnb if kc < 4 else mnb2)[:, (kc % 4) * 128:(kc % 4 + 1) * 128]
                self.tr(ptb[:, kc * 128:(kc + 1) * 128], src, self.ident_b[:], r=[mnk, mnk2, 'ident_b'], w=[ptk])
            if self.stage < 2.14:
                continue
            mnT, mnTk = self.b16t()
            mnT2, mnT2k = self.b16t()
            if self.stage != 2.15:
                self.cp('act', mnT[:], ptb[:, 0:512], r=[ptk], w=[mnTk])
            if self.stage != 2.16:
                self.cp('dve', mnT2[:], ptb[:, 512:1024], r=[ptk], w=[mnT2k])
            if self.stage < 2.2:
                continue
            pk_, pkk = self.psn()
            pv_, pvk = self.psn()
            for kc in range(8):
                lt = (mnT if kc < 4 else mnT2)[:, (kc % 4) * 128:(kc % 4 + 1) * 128]
                self.mm(pk_[:], lhsT=lt, rhs=wk[:, kc, :], start=(kc == 0), stop=(kc == 7), r=[mnTk, mnT2k, wkk], w=[pkk])
            for kc in range(8):
                lt = (mnT if kc < 4 else mnT2)[:, (kc % 4) * 128:(kc % 4 + 1) * 128]
                self.mm(pv_[:], lhsT=lt, rhs=wv[:, kc, :], start=(kc == 0), stop=(kc == 7), r=[mnTk, mnT2k, wvk], w=[pvk])
            if self.stage < 2.3:
                continue
            mo = junk
            kf, kfk = self.f32t()
            sq, sqk = self.f32t()
            self.cp('act', kf[:], pk_[:], r=[pkk], w=[kfk])
            self.cp('act', mo[:, 512:1024], pv_[:], r=[pvk], w=[jkk])
            self.cp('dve', self.mv[:, mb, :], pv_[:], r=[pvk], w=['mv'])
            self.tt('pool', sq[:], kf[:], kf[:], ALU.mult, r=[kfk], w=[sqk])
            if self.stage < 2.4:
                continue
            sm2, sm2k = self.smr.next()
            self.red(sm2[:, 0:4], sq[:].rearrange("p (h d) -> p h d", h=4), r=[sqk], w=[sm2k])
            self.actf(sm2[:, 4:8], sm2[:, 0:4], AF.Sqrt, bias=self.epsc[:, 0:1], scale=1.0 / 128, r=[sm2k, 'epsc'], w=[sm2k])
            self.recip(sm2[:, 8:12], sm2[:, 4:8], r=[sm2k], w=[sm2k])
            self.tt('dve', kf[:].rearrange("p (h d) -> p h d", h=4), kf[:].rearrange("p (h d) -> p h d", h=4),
                    sm2[:, 8:12].unsqueeze(2).to_broadcast([128, 4, 128]), ALU.mult, r=[kfk, sm2k], w=[kfk])
            self.tt('pool', mo[:, 0:512], kf[:], mkg_b, ALU.mult, r=[kfk, 'onsa'], w=[jkk])
            self.dma('sp', self.mem_o[l, mb * 128:(mb + 1) * 128, :], mo, r=[jkk], w=['mem_o'])
            if self.stage < 2.5:
                continue
            knb, knk = self.b16t()
            self.cp('pool', knb[:], mo[:, 0:512], r=[jkk], w=[knk])
            pt2, pt2k = self.psn()
            pt2b = pt2[:].bitcast(BF16)
            for hm in range(4):
                self.tr(pt2b[:, hm * 128:(hm + 1) * 128], knb[:, hm * 128:(hm + 1) * 128], self.ident_b[:], r=[knk, 'ident_b'], w=[pt2k])
            self.cp('act', self.mkT[:, :, mb * 128:(mb + 1) * 128], pt2b[:, 0:512].rearrange("p (h m) -> p h m", h=4), r=[pt2k], w=['mkT'])

    def load_x(self, l, t):
        if l == 0:
            for blk in range(4):
                xi, xik = self.tmior.next()
                self.dma('sp', xi[:], self.inp('x')[(t * 4 + blk) * 128:(t * 4 + blk + 1) * 128, :], w=[xik])
                for hf in range(2):
                    pb, pk = self.psn()
                    for c in range(4):
                        kc = hf * 4 + c
                        self.tr(pb[:, c * 128:(c + 1) * 128], xi[:, kc * 128:(kc + 1) * 128], self.ident_f[:], r=[xik, 'ident_f'], w=[pk])
                    self.cp('act' if hf == 0 else 'dve', self.xT[:, hf * 4:hf * 4 + 4, blk * 128:(blk + 1) * 128],
                            pb[:].rearrange("p (c f) -> p c f", c=4), r=[pk], w=[('xT', hf * 4 + c) for c in range(4)])
        else:
            for kc in range(8):
                self.dma('sp', self.xT[:, kc, :], self.hres[kc, :, t * TT:(t + 1) * TT], r=['hres'], w=[('xT', kc)])

    def rmsnorm(self, gi):
        ps, pk = self.psn()
        for kc in range(8):
            sq, sqk = self.b16t()
            self.actf(sq[:], self.xT[:, kc, :], AF.Square, r=[('xT', kc)], w=[sqk])
            self.mm(ps[:], lhsT=self.ones_b[:], rhs=sq[:], start=(kc == 0), stop=(kc == 7), r=['ones_b', sqk], w=[pk])
        rt, rtk = self.f32t()
        self.actf(rt[:], ps[:], AF.Sqrt, bias=self.epsc[:, 0:1], scale=1.0 / D, r=[pk, 'epsc'], w=[rtk])
        rs, rsk = self.f32t()
        self.recip(rs[:], rt[:], r=[rtk], w=[rsk])
        for kc in range(8):
            self.stt(self.xnT[:, kc, :], self.xT[:, kc, :], self.gcols[:, gi, kc:kc + 1], rs[:], ALU.mult, ALU.mult,
                     r=[('xT', kc), 'gcols', rsk], w=[('xnT', kc)])

    def headnorm_rope(self, hd, hk, nh, gain_b, cos_b, sin_b, inv_d):
        P = hd.shape[0]
        sq, sqk = self.tmr.next()
        sq = sq[0:P, 0:nh, :]
        sm, smk = self.smr.next()
        self.tt('pool', sq, hd, hd, ALU.mult, r=[hk], w=[sqk])
        self.red(sm[0:P, 0:nh], sq, r=[sqk], w=[smk])
        self.actf(sm[0:P, 12:12 + nh], sm[0:P, 0:nh], AF.Sqrt, bias=self.epsc[0:P, 0:1], scale=inv_d, r=[smk, 'epsc'], w=[smk])
        self.recip(sm[0:P, 0:nh], sm[0:P, 12:12 + nh], r=[smk], w=[smk])
        self.tt('dve', hd, hd, sm[0:P, 0:nh].unsqueeze(2).to_broadcast([P, nh, 64]), ALU.mult, r=[hk, smk], w=[hk])
        self.tt('pool', hd, hd, gain_b, ALU.mult, r=[hk, 'g12b'], w=[hk])
        hr, hrk = self.tmr.next()
        hr = hr[0:P, 0:nh, :]
        self.cp('pool', hr, hd, r=[hk], w=[hrk])
        tp, tpk = self.tmr.next()
        t1 = tp[0:P, 0:nh, 0:8]
        t2 = tp[0:P, 0:nh, 8:16]
        t3 = tp[0:P, 0:nh, 16:24]
        t4 = tp[0:P, 0:nh, 24:32]
        x1 = hd[:, :, 0:8]
        x2 = hd[:, :, 8:16]
        self.tt('dve', t1, x1, cos_b, ALU.mult, r=[hk, 'cosT'], w=[tpk])
        self.tt('dve', t2, x2, sin_b, ALU.mult, r=[hk, 'sinT'], w=[tpk])
        self.tt('dve', t3, x2, cos_b, ALU.mult, r=[hk, 'cosT'], w=[tpk])
        self.tt('dve', t4, x1, sin_b, ALU.mult, r=[hk, 'sinT'], w=[tpk])
        self.tt('dve', hr[:, :, 0:8], t1, t2, ALU.subtract, r=[tpk], w=[hrk])
        self.tt('dve', hr[:, :, 8:16], t3, t4, ALU.add, r=[tpk], w=[hrk])
        return hr, hrk

    def tokmajor(self, l, t, wA):
        (w0, w0k), (w1, w1k), (w2, w2k) = wA
        for blk in range(4):
            kb = t * 4 + blk
            cs = slice(blk * 128, (blk + 1) * 128)
            pq, pqk = self.psn()
            pa, pak = self.psn()
            pb, pbk = self.psn()
            for kc in range(8):
                lt = self.xnT[:, kc, cs]
                self.mm(pq[:], lhsT=lt, rhs=w0[:, kc, :], start=(kc == 0), stop=(kc == 7), r=[('xnT', kc), w0k], w=[pqk])
            for kc in range(8):
                lt = self.xnT[:, kc, cs]
                self.mm(pa[:], lhsT=lt, rhs=w1[:, kc, :], start=(kc == 0), stop=(kc == 7), r=[('xnT', kc), w1k], w=[pak])
            for kc in range(8):
                lt = self.xnT[:, kc, cs]
                self.mm(pb[:, 0:280], lhsT=lt, rhs=w2[:, kc, :], start=(kc == 0), stop=(kc == 7), r=[('xnT', kc), w2k], w=[pbk])
            hd, hk = self.tmr.next()
            self.cp('act', hd[:, 0:8, :].rearrange("p h d -> p (h d)"), pq[:], r=[pqk], w=[hk])
            self.cp('dve', hd[:, 8:10, :].rearrange("p h d -> p (h d)"), pa[:, 256:384], r=[pak], w=[hk])
            self.cp('dve', hd[:, 10:12, :].rearrange("p h d -> p (h d)"), pb[:, 0:128], r=[pbk], w=[hk])
            rows, rowsk = self.rowsr.next()
            self.cp('act', rows[:, 0:512], pa[:], r=[pak], w=[rowsk])
            self.cp('act', rows[:, 640:768], pb[:, 128:256], r=[pbk], w=[rowsk])
            self.actf(self.ngt[:, blk, :], pb[:, 256:280], AF.Sigmoid, r=[pbk], w=[('ngt', blk)])
            self.cp('dve', self.Vs[:, kb, :, 0:64], pa[:, 384:512].rearrange("p (g d) -> p g d", g=2), r=[pak], w=['Vs'])
            slot = (kb // 4) % 2 * 4 + kb % 4
            self.cp('dve', self.Vw[:, slot, :, 0:64], pb[:, 128:256].rearrange("p (g d) -> p g d", g=2), r=[pbk], w=['Vw'])
            cos_b = self.cosT[:, kb, :].unsqueeze(1).to_broadcast([128, 12, 8])
            sin_b = self.sinT[:, kb, :].unsqueeze(1).to_broadcast([128, 12, 8])
            hr, hrk = self.headnorm_rope(hd[:], hk, 12, self.g12b[:], cos_b, sin_b, 1.0 / 64)
            self.cp('pool', rows[:, 256:384], hr[:, 8:10, :].rearrange("p h d -> p (h d)"), r=[hrk], w=[rowsk])
            self.cp('pool', rows[:, 512:640], hr[:, 10:12, :].rearrange("p h d -> p (h d)"), r=[hrk], w=[rowsk])
            self.dma('sp', self.rows_o[l, kb * 128:(kb + 1) * 128, :], rows[:, 0:512], r=[rowsk], w=['rows_o'])
            if t == NTILE - 1:
                self.dma('sp', self.win_o[l, blk * 128:(blk + 1) * 128, :], rows[:, 512:768], r=[rowsk], w=['win_o'])
            ts_, tsk = self.tsrc.next()
            self.cp('pool', ts_[:, 0:4, :].rearrange("p c f -> p (c f)"), hd[:, 0:8, :].rearrange("p h d -> p (h d)"), r=[hk], w=[tsk])
            self.cp('pool', ts_[:, 4:8, :].rearrange("p c f -> p (c f)"), hr[:, 0:8, :].rearrange("p h d -> p (h d)"), r=[hrk], w=[tsk])
            self.cp('pool', ts_[:, 8:10, :].rearrange("p c f -> p (c f)"), hr[:, 8:12, :].rearrange("p h d -> p (h d)"), r=[hrk], w=[tsk])
            self.cp('act', ts_[:, 10:12, :].rearrange("p c f -> p (c f)"), pa[:, 0:256], r=[pak], w=[tsk])
            p0, p0k = self.psn()
            p1, p1k = self.psn()
            p0b = p0[:].bitcast(BF16)
            p1b = p1[:].bitcast(BF16)
            for c in range(8):
                self.tr(p0b[:, c * 128:(c + 1) * 128], ts_[:, c, :], self.ident_b[:], r=[tsk, 'ident_b'], w=[p0k])
            for c in range(4):
                self.tr(p1b[:, c * 128:(c + 1) * 128], ts_[:, 8 + c, :], self.ident_b[:], r=[tsk, 'ident_b'], w=[p1k])
            p0v = p0b.rearrange("p (c f) -> p c f", c=8)
            self.cp('act', self.QU[:, :, cs], p0v[:, 0:4, :], r=[p0k], w=['QU'])
            self.cp('dve', self.QB[0:64, 0::2, cs], p0v[0:64, 4:8, :], r=[p0k], w=['QBq'])
            self.cp('act', self.QB[0:64, 1::2, cs], p0v[64:128, 4:8, :], r=[p0k], w=['QBq'])
            ks = slice(kb * 128, (kb + 1) * 128)
            self.cp('dve', self.KE[0][0:64, ks], p1b[0:64, 0:128], r=[p1k], w=['KE0'])
            self.cp('act', self.KE[1][0:64, ks], p1b[64:128, 0:128], r=[p1k], w=['KE1'])
            wcs = slice((kb // 4) % 2 * 512 + (kb % 4) * 128, (kb // 4) % 2 * 512 + (kb % 4 + 1) * 128)
            self.cp('dve', self.kwT[0][0:64, wcs], p1b[0:64, 128:256], r=[p1k], w=['kwT0'])
            self.cp('act', self.kwT[1][0:64, wcs], p1b[64:128, 128:256], r=[p1k], w=['kwT1'])
            rs = slice(16 + blk * 128, 16 + (blk + 1) * 128)
            self.cp('dve', self.rawk[:, rs], p1b[:, 256:384], r=[p1k], w=['rawk'])
            self.cp('act', self.rawv[:, rs], p1b[:, 384:512], r=[p1k], w=['rawv'])

    def compress(self, l, t):
        w1 = self.load_w1(l)
        c0 = 32 * (t % 4)
        cc = t // 4
        for kind in range(2):
            v, key = w1[kind]
            raw, rawkey = (self.rawk, 'rawk') if kind == 0 else (self.rawv, 'rawv')
            for g in range(2):
                r0 = 64 * g
                ph, phk = self.psn()
                for s in range(32):
                    self.mm(ph[:, 0:32], lhsT=v[r0:r0 + 64, s, :], rhs=raw[r0:r0 + 64, s:s + 497:16],
                            start=(s == 0), stop=(s == 31), r=[key, rawkey], w=[phk])
                hs, hsk = self.b16t()
                self.actf(hs[:, 0:32], ph[:, 0:32], AF.Silu, bias=self.bpe[:, kind:kind + 1], r=[phk, 'bpe'], w=[hsk])
                pc, pck = self.psn()
                self.mm(pc[0:32, 0:64], lhsT=hs[:, 0:32], rhs=self.w2sb[:, kind, :], r=[hsk, 'w2sb'], w=[pck])
                if kind == 0:
                    kf, kfk = self.f32t()
                    sm, smk = self.smr.next()
                    self.actf(kf[0:32, 64:128], pc[0:32, 0:64], AF.Square, accum=sm[0:32, 0:1], r=[pck], w=[kfk, smk])
                    self.actf(sm[0:32, 1:2], sm[0:32, 0:1], AF.Sqrt, bias=self.epsc[0:32, 0:1], scale=1.0 / 64, r=[smk, 'epsc'], w=[smk])
                    self.recip(sm[0:32, 2:3], sm[0:32, 1:2], r=[smk], w=[smk])
                    kn, knk = self.b16t()
                    self.stt(kn[0:32, 0:64], pc[0:32, 0:64], sm[0:32, 2:3], self.k0gb[0:32, :], ALU.mult, ALU.mult,
                             r=[pck, smk, 'k0gb'], w=[knk])
                    self.cp('dve', kn[0:32, 64:128], kn[0:32, 0:64], r=[knk], w=[knk])
                    pt, ptk = self.psn()
                    ptb = pt[:].bitcast(BF16)
                    self.tr(ptb[:, 0:32], kn[0:32, 0:128], self.ident_b[0:32, 0:32], r=[knk, 'ident_b'], w=[ptk])
                    self.cp('act', self.kcT2[g][:, 32 * t:32 * t + 32], ptb[:, 0:32], r=[ptk], w=[f'kcT{g}'])
                else:
                    self.cp('act', self.Rc[g][c0:c0 + 32, cc, 0:64], pc[0:32, 0:64], r=[pck], w=[f'Rc{g}'])
                    if t == 0:
                        self.memset('pool', self.Rc[g][0:1, 0, 0:64], 0.0, w=[f'Rc{g}'])
        self.cp('pool', self.rawk[:, 0:16], self.rawk[:, 512:528], r=['rawk'], w=['rawk'])
        self.cp('pool', self.rawv[:, 0:16], self.rawv[:, 512:528], r=['rawv'], w=['rawv'])

    def fm_proj(self, l, t):
        w_in = self.inp('w_in')[l]
        wcx, wcxk = self.wload(w_in[:, C_CX:C_CX + 512], 8, 512)
        wcb, wcbk = self.wload(w_in[:, C_CB:C_CB + 512], 8, 512)
        wcc, wcck = self.wload(w_in[:, C_CC:C_CC + 512], 8, 512)
        xk = [('xnT', kc) for kc in range(8)]
        for ci in range(4):
            cs = slice(ci * 128, (ci + 1) * 128)
            px, pxk = self.psn()
            pc, pck = self.psn()
            pb, pbk = self.psn()
            for (pp, ppk, ww, wwk) in ((px, pxk, wcx, wcxk), (pc, pck, wcc, wcck), (pb, pbk, wcb, wcbk)):
                for kc in range(8):
                    self.mm(pp[:], lhsT=ww[:, kc, cs], rhs=self.xnT[:, kc, :], start=(kc == 0), stop=(kc == 7), r=[wwk, ('xnT', kc)], w=[ppk])
            cxs, cxk = self.f32t()
            self.cp('act', cxs[:], px[:], r=[pxk], w=[cxk])
            ue, uek = self.uextr.next()
            self.cp('pool', ue[:, 0:2], self.ucar[:, ci, :], r=['ucar'], w=[uek])
            self.tt('dve', ue[:, 2:514], pc[:], cxs[:], ALU.mult, r=[pck, cxk], w=[uek])
            a1, a1k = self.f32t()
            self.ts('pool', a1[:], ue[:, 0:512], self.cwc[:, ci, 0:1], ALU.mult, r=[uek, 'cwc'], w=[a1k])
            self.stt(a1[:], ue[:, 1:513], self.cwc[:, ci, 1:2], a1[:], ALU.mult, ALU.add, r=[uek, 'cwc', a1k], w=[a1k])
            self.stt(a1[:], ue[:, 2:514], self.cwc[:, ci, 2:3], a1[:], ALU.mult, ALU.add, r=[uek, 'cwc', a1k], w=[a1k])
            self.tt('dve', self.brT[:, 4 + ci, :], pb[:], a1[:], ALU.mult, r=[pbk, a1k], w=['brT1'])
            self.cp('pool', self.ucar[:, ci, :], ue[:, 512:514], r=[uek], w=['ucar'])
            if t == NTILE - 1:
                self.dma('sp', self.conv_o[l, :, cs].rearrange("j p -> p j"), ue[:, 512:514], r=[uek], w=['conv_o'], slow=True)

    def cmp_attend(self, t):
        nch = t // 4 + 1
        ngk = [('ngt', b) for b in range(4)]
        for h in range(8):
            g = h // 4
            r0 = 64 * (h % 2)
            pTs = []
            for cc in range(nch):
                sz = 128 if cc < nch - 1 else 32 * (t % 4 + 1)
                ps_, psk = self.psn()
                self.mm(ps_[0:sz, :], lhsT=self.kcT2[g][r0:r0 + 64, cc * 128:cc * 128 + sz], rhs=self.QU[r0:r0 + 64, h // 2, :],
                        r=[f'kcT{g}', 'QU'], w=[psk])
                pT, pTk = self.b16t()
                self.actf(pT[0:sz, :], ps_[0:sz, :], AF.Exp, scale=0.125, r=[psk], w=[pTk])
                if cc == nch - 1:
                    self.tt('pool', pT[sz - 32:sz, :], pT[sz - 32:sz, :], self.stair[sz - 32:sz, :], ALU.mult, r=[pTk, 'stair'], w=[pTk])
                pTs.append((pT, pTk, sz))
            banks = [self.psn(), self.psn()]
            for qb in range(4):
                bk, bkk = banks[qb // 2]
                off = (qb % 2) * 129
                for cc, (pT, pTk, sz) in enumerate(pTs):
                    self.mm(bk[:, off:off + 129], lhsT=pT[0:sz, qb * 128:(qb + 1) * 128], rhs=self.Rc[g][0:sz, cc, 0:129],
                            start=(cc == 0), stop=(cc == nch - 1), r=[pTk, f'Rc{g}'], w=[bkk])
            for qb in range(4):
                bk, bkk = banks[qb // 2]
                off = (qb % 2) * 129
                sm, smk = self.smr.next()
                self.ts('dve', sm[:, 0:1], bk[:, off + 64:off + 65], 1e-30, ALU.add, r=[bkk], w=[smk])
                self.recip(sm[:, 1:2], sm[:, 0:1], r=[smk], w=[smk])
                self.tt('dve', sm[:, 2:3], sm[:, 1:2], self.ngt[:, qb, h:h + 1], ALU.mult, r=[smk] + ngk, w=[smk])
                self.ts('dve', self.onsa[:, qb, h * 64:(h + 1) * 64], bk[:, off:off + 64], sm[:, 2:3], ALU.mult, r=[bkk, smk], w=['onsa'])
                if h % 4 == 0:
                    self.ts('dve', self.scr[:, qb, g, :], bk[:, off + 65:off + 129], sm[:, 1:2], ALU.mult, r=[bkk, smk], w=['scr'])
                else:
                    self.stt(self.scr[:, qb, g, :], bk[:, off + 65:off + 129], sm[:, 1:2], self.scr[:, qb, g, :], ALU.mult, ALU.add,
                             r=[bkk, smk, 'scr'], w=['scr'])

    def topk(self, t):
        sl, slk = self.selTr.next()
        self.dma('sp', sl[:, 0], self.inp('c_selA')[:, 4 * t:4 * t + 4, :], w=[slk])
        self.dma('sp', sl[:, 1], self.inp('c_selB')[:, 4 * t:4 * t + 4, :], w=[slk])
        for g in range(2):
            pt, ptk = self.psn()
            ptb = pt[:].bitcast(BF16)
            for qb in range(4):
                s2, s2k = self.selr.next()
                self.tt('dve', s2[:, 0, :], self.scr[:, qb, g, :], sl[:, 0, qb, :], ALU.mult, r=['scr', slk], w=[s2k])
                self.tt('dve', s2[:, 0, :], s2[:, 0, :], sl[:, 1, qb, :], ALU.add, r=[s2k, slk], w=[s2k])
                sm, smk = self.smr.next()
                self.S.add('dve', (lambda o, i: lambda e: e.max(out=o, in_=i))(sm[:, 0:8], s2[:, 0, :]), [s2k], [smk])
                self.S.add('dve', (lambda o, a, b: lambda e: e.match_replace(out=o, in_to_replace=a, in_values=b, imm_value=-1e30))(s2[:, 1, :], sm[:, 0:8], s2[:, 0, :]), [s2k, smk], [s2k])
                self.S.add('dve', (lambda o, i: lambda e: e.max(out=o, in_=i))(sm[:, 8:16], s2[:, 1, :]), [s2k], [smk])
                self.ts('dve', sm[:, 16:17], sm[:, 15:16], -1e8, ALU.max, r=[smk], w=[smk])
                self.ts('dve', s2[:, 1, :], s2[:, 0, :], sm[:, 16:17], ALU.is_ge, r=[s2k, smk], w=[s2k])
                bw, bwk = self.biasr.next()
                self.ts('dve', bw[:, 64:128], s2[:, 1, :], -1.0, ALU.add, -MASKV, ALU.mult, r=[s2k], w=[bwk])
                self.tr(ptb[:, qb * 128:(qb + 1) * 128], bw[:], self.ident_b[:], r=[bwk, 'ident_b'], w=[ptk])
            for hh in range(4):
                self.cp('act' if hh % 2 == 0 else 'dve', self.QB[64:128, g * 4 + hh, :], ptb[64:128, 0:512], r=[ptk], w=['QBb'])

    def norm_acc(self, acc, acck, h, ngoff):
        ngk = [('ngt', b) for b in range(4)]
        sm, smk = self.smr.next()
        self.recip(sm[:, 0:4], acc[:, 64:260:65], r=[acck], w=[smk])
        self.tt('dve', sm[:, 4:8], sm[:, 0:4], self.ngt[:, :, ngoff + h], ALU.mult, r=[smk] + ngk, w=[smk])
        for j in range(4):
            dst = self.onsa[:, j, h * 64:(h + 1) * 64]
            self.stt(dst, acc[:, j * 65:j * 65 + 64], sm[:, 4 + j:5 + j], dst, ALU.mult, ALU.add, r=[acck, smk, 'onsa'], w=['onsa'])

    def slc(self, t):
        for h in range(8):
            g = h // 4
            acc, acck = self.accr.next()
            first = True
            for kb in range(4 * t + 4):
                i0 = max(0, kb - 4 * t)
                c0 = 128 * i0
                ps_, psk = self.psn()
                self.mm(ps_[:, c0:512], lhsT=self.KE[g][:, kb * 128:(kb + 1) * 128], rhs=self.QB[:, h, c0:512],
                        r=[f'KE{g}', 'QBq', 'QBb'], w=[psk])
                pT, pTk = self.b16t()
                self.actf(pT[:, c0:512], ps_[:, c0:512], AF.Exp, scale=0.125, r=[psk], w=[pTk])
                if kb >= 4 * t:
                    self.tt('pool', pT[:, c0:c0 + 128], pT[:, c0:c0 + 128], self.tri[:], ALU.mult, r=[pTk, 'tri'], w=[pTk])
                for j in range(i0, 4):
                    self.mm(acc[:, j * 65:(j + 1) * 65], lhsT=pT[:, j * 128:(j + 1) * 128], rhs=self.Vs[:, kb, g, 0:65],
                            start=first, stop=False, r=[pTk, 'Vs'], w=[acck])
                    first = False
            self.norm_acc(acc, acck, h, 8)

    def win(self, t):
        for h in range(8):
            g = h // 4
            acc, acck = self.accr.next()
            first = True
            for kb in range(max(0, 4 * t - 4), 4 * t + 4):
                jlo = max(0, kb - 4 * t)
                jhi = min(3, kb - 4 * t + 4)
                ring = (kb // 4) % 2
                wc0 = ring * 512 + (kb % 4) * 128
                slot = ring * 4 + kb % 4
                cl, ch = 128 * jlo, 128 * (jhi + 1)
                ps_, psk = self.psn()
                self.mm(ps_[:, cl:ch], lhsT=self.kwT[g][0:64, wc0:wc0 + 128], rhs=self.QB[0:64, h, cl:ch], r=[f'kwT{g}', 'QBq'], w=[psk])
                pT, pTk = self.b16t()
                self.actf(pT[:, cl:ch], ps_[:, cl:ch], AF.Exp, scale=0.125, r=[psk], w=[pTk])
                for j in range(jlo, jhi + 1):
                    d = 4 * t + j - kb
                    if d == 0:
                        self.tt('pool', pT[:, j * 128:(j + 1) * 128], pT[:, j * 128:(j + 1) * 128], self.tri[:], ALU.mult, r=[pTk, 'tri'], w=[pTk])
                    if d == 4:
                        self.tt('pool', pT[:, j * 128:(j + 1) * 128], pT[:, j * 128:(j + 1) * 128], self.strict[:], ALU.mult, r=[pTk, 'strict'], w=[pTk])
                for j in range(jlo, jhi + 1):
                    self.mm(acc[:, j * 65:(j + 1) * 65], lhsT=pT[:, j * 128:(j + 1) * 128], rhs=self.Vw[:, slot, g, 0:65],
                            start=first, stop=False, r=[pTk, 'Vw'], w=[acck])
                    first = False
            self.norm_acc(acc, acck, h, 16)

    def nsa_finalize(self):
        for j in range(4):
            ob, obk = self.b16t()
            self.cp('pool', ob[:], self.onsa[:, j, :], r=['onsa'], w=[obk])
            pt, ptk = self.psn()
            ptb = pt[:].bitcast(BF16)
            for c in range(4):
                self.tr(ptb[:, c * 128:(c + 1) * 128], ob[:, c * 128:(c + 1) * 128], self.ident_b[:], r=[obk, 'ident_b'], w=[ptk])
            self.cp('act', self.brT[:, 0:4, j * 128:(j + 1) * 128], ptb[:, 0:512].rearrange("p (c f) -> p c f", c=4), r=[ptk], w=['brT0'])

    def mem_attend(self):
        for hm in range(4):
            pTs = []
            for mb in range(2):
                ps_, psk = self.psn()
                self.mm(ps_[:], lhsT=self.mkT[:, hm, mb * 128:(mb + 1) * 128], rhs=self.mqT[:, hm, :], r=['mkT', 'mqT'], w=[psk])
                pT, pTk = self.b16t()
                self.actf(pT[:], ps_[:], AF.Exp, scale=128 ** -0.5, r=[psk], w=[pTk])
                pTs.append((pT, pTk))
            po, pok = self.psn()
            pd, pdk = self.psn()
            for mb, (pT, pTk) in enumerate(pTs):
                self.mm(po[:], lhsT=self.mv[:, mb, hm * 128:(hm + 1) * 128], rhs=pT[:], start=(mb == 0), stop=(mb == 1), r=['mv', pTk], w=[pok])
            for mb, (pT, pTk) in enumerate(pTs):
                self.mm(pd[:], lhsT=self.ones_b[:], rhs=pT[:], start=(mb == 0), stop=(mb == 1), r=['ones_b', pTk], w=[pdk])
            rc, rck = self.f32t()
            self.recip(rc[:], pd[:], r=[pdk], w=[rck])
            self.tt('dve', self.brT[:, 8 + hm, :], po[:], rc[:], ALU.mult, r=[pok, rck], w=['brT2'])

    def phase_b(self, l):
        w_in = self.inp('w_in')[l]
        QBK = ['QBq', 'QBb']
        for fcg in range(2):
            for n in range(3):
                wm, wmk = self.wload(w_in[:, C_MG + n * 1024 + fcg * 512:C_MG + n * 1024 + (fcg + 1) * 512], 8, 512)
                wb, wbk = self.wload(self.inp('w_branch')[l, n][:, fcg * 512:(fcg + 1) * 512], 4, 512)
                for fi in range(4):
                    fc = fcg * 4 + fi
                    cs = slice(fi * 128, (fi + 1) * 128)
                    pg, pgk = self.psn()
                    pp, ppk = self.psn()
                    for kc in range(8):
                        self.mm(pg[:], lhsT=wm[:, kc, cs], rhs=self.xnT[:, kc, :], start=(kc == 0), stop=(kc == 7), r=[wmk, ('xnT', kc)], w=[pgk])
                    for kc in range(4):
                        self.mm(pp[:], lhsT=wb[:, kc, cs], rhs=self.brT[:, n * 4 + kc, :], start=(kc == 0), stop=(kc == 3), r=[wbk, f'brT{n}'], w=[ppk])
                    sg, sgk = self.f32t()
                    self.actf(sg[:], pg[:], AF.Sigmoid, r=[pgk], w=[sgk])
                    if n == 0:
                        self.tt('dve', self.macc[:, fi, :], sg[:], pp[:], ALU.mult, r=[sgk, ppk], w=['onsa'])
                    else:
                        self.tt('dve', sg[:], sg[:], pp[:], ALU.mult, r=[sgk, ppk], w=[sgk])
                        if n == 1:
                            self.tt('pool', self.macc[:, fi, :], self.macc[:, fi, :], sg[:], ALU.add, r=[sgk, 'onsa'], w=['onsa'])
                        else:
                            self.tt('pool', self.mT[:, fc, :], self.macc[:, fi, :], sg[:], ALU.add, r=[sgk, 'onsa'], w=QBK)

    def phase_c(self, l):
        QBK = ['QBq', 'QBb']
        for half in range(2):
            wo, wok = self.wload(self.inp('w_out')[l][:, half * 512:(half + 1) * 512], 8, 512)
            for fi in range(4):
                fc = half * 4 + fi
                po, pok = self.psn()
                for kc in range(8):
                    self.mm(po[:], lhsT=wo[:, kc, fi * 128:(fi + 1) * 128], rhs=self.mT[:, kc, :], start=(kc == 0), stop=(kc == 7), r=[wok] + QBK, w=[pok])
                self.tt('dve', self.xT[:, fc, :], po[:], self.xT[:, fc, :], ALU.add, r=[pok, ('xT', fc)], w=[('xT', fc)])

    def phase_d(self, l):
        self.rmsnorm(1)
        w_gu = self.inp('w_gate_up')[l]
        w_dn = self.inp('w_down')[l]
        groups = [(0, 6), (6, 6), (12, 5), (17, 5)]
        for (j0, n) in groups:
            for p0 in range(0, n, 4):
                pn = min(4, n - p0)
                ja = j0 + p0
                wg, wgk = self.wload(w_gu[:, ja * 128:(ja + pn) * 128], 8, pn * 128)
                wu, wuk = self.wload(w_gu[:, DFF + ja * 128:DFF + (ja + pn) * 128], 8, pn * 128)
                for q in range(pn):
                    jj = p0 + q
                    cs = slice(q * 128, (q + 1) * 128)
                    pg, pgk = self.psn()
                    pu, puk = self.psn()
                    for kc in range(8):
                        self.mm(pg[:], lhsT=wg[:, kc, cs], rhs=self.xnT[:, kc, :], start=(kc == 0), stop=(kc == 7), r=[wgk, ('xnT', kc)], w=[pgk])
                    for kc in range(8):
                        self.mm(pu[:], lhsT=wu[:, kc, cs], rhs=self.xnT[:, kc, :], start=(kc == 0), stop=(kc == 7), r=[wuk, ('xnT', kc)], w=[puk])
                    sg, sgk = self.f32t()
                    self.actf(sg[:], pg[:], AF.Silu, r=[pgk], w=[sgk])
                    self.tt('dve', self.actT[:, jj, :], sg[:], pu[:], ALU.mult, r=[sgk, puk], w=[('actT', jj)])
            for cq in range(4):
                wd, wdk = self.wload(w_dn[j0 * 128:(j0 + n) * 128, cq * 256:(cq + 1) * 256], n, 256)
                for fi in range(2):
                    fc = cq * 2 + fi
                    pd, pdk = self.psn()
                    for kc in range(n):
                        self.mm(pd[:], lhsT=wd[:, kc, fi * 128:(fi + 1) * 128], rhs=self.actT[:, kc, :], start=(kc == 0), stop=(kc == n - 1),
                                r=[wdk, ('actT', kc)], w=[pdk])
                    self.tt('dve', self.xT[:, fc, :], pd[:], self.xT[:, fc, :], ALU.add, r=[pdk, ('xT', fc)], w=[('xT', fc)])

    def store_x(self, l, t):
        xk = [('xT', kc) for kc in range(8)]
        if l < self.nlayers - 1:
            self.dma('sp', self.hres[:, :, t * TT:(t + 1) * TT].rearrange("k p t -> p k t"), self.xT[:], r=xk, w=['hres'])
        else:
            for blk in range(4):
                xo, xok = self.tmior.next()
                for hf in range(2):
                    pb, pk = self.psn()
                    for c in range(4):
                        kc = hf * 4 + c
                        self.tr(pb[:, c * 128:(c + 1) * 128], self.xT[:, kc, blk * 128:(blk + 1) * 128], self.ident_f[:], r=[('xT', kc), 'ident_f'], w=[pk])
                    self.cp('act' if hf == 0 else 'dve', xo[:, hf * 512:(hf + 1) * 512], pb[:], r=[pk], w=[xok])
                self.dma('sp', self.y[(t * 4 + blk) * 128:(t * 4 + blk + 1) * 128, :], xo[:], r=[xok], w=['y'])

    def make_ctxs(self):
        class C:
            pass
        P = C()
        P.N, P.k = TT, ''
        P.xT, P.xnT, P.brT, P.mT, P.macc, P.actT, P.mqT = self.xT, self.xnT, self.brT, self.mT, self.macc, self.actT, self.mqT
        P.mTk, P.mack = ['QBq', 'QBb'], 'onsa'
        self.P = P
        X = C()
        X.N, X.k = 4, 's_'
        sb = self.S.sb
        X.xT = sb([128, 8, 4], F32, 's_xT')
        X.xnT = sb([128, 8, 4], BF16, 's_xnT')
        X.brT = sb([128, 12, 4], BF16, 's_brT')
        X.mT = sb([128, 8, 4], BF16, 's_mT')
        X.macc = sb([128, 4, 4], F32, 's_macc')
        X.actT = sb([128, 6, 4], BF16, 's_actT')
        X.mqT = sb([128, 4, 4], BF16, 's_mqT')
        X.mTk, X.mack = ['s_mT'], 's_macc'
        self.X = X

    def rmsnorm(self, gi, c=None):
        c = c or self.P
        N, k = c.N, c.k
        ps, pk = self.psn()
        for kc in range(8):
            sq, sqk = self.b16t()
            self.actf(sq[:, 0:N], c.xT[:, kc, :], AF.Square, r=[(k + 'xT', kc)], w=[sqk])
            self.mm(ps[:, 0:N], lhsT=self.ones_b[:], rhs=sq[:, 0:N], start=(kc == 0), stop=(kc == 7), r=['ones_b', sqk], w=[pk])
        rt, rtk = self.f32t()
        self.actf(rt[:, 0:N], ps[:, 0:N], AF.Sqrt, bias=self.epsc[:, 0:1], scale=1.0 / D, r=[pk, 'epsc'], w=[rtk])
        rs, rsk = self.f32t()
        self.recip(rs[:, 0:N], rt[:, 0:N], r=[rtk], w=[rsk])
        for kc in range(8):
            self.stt(c.xnT[:, kc, :], c.xT[:, kc, :], self.gcols[:, gi, kc:kc + 1], rs[:, 0:N], ALU.mult, ALU.mult,
                     r=[(k + 'xT', kc), 'gcols', rsk], w=[(k + 'xnT', kc)])

    def mq_proj(self, l, c=None):
        c = c or self.P
        N, k = c.N, c.k
        wmq, wmqk = self.wload(self.inp('w_in')[l][:, C_MQ:C_MQ + 512], 8, 512)
        for hm in range(4):
            cs = slice(hm * 128, (hm + 1) * 128)
            pm, pmk = self.psn()
            for kc in range(8):
                self.mm(pm[:, 0:N], lhsT=wmq[:, kc, cs], rhs=c.xnT[:, kc, :], start=(kc == 0), stop=(kc == 7), r=[wmqk, (k + 'xnT', kc)], w=[pmk])
            sq, sqk = self.b16t()
            self.actf(sq[:, 0:N], pm[:, 0:N], AF.Square, r=[pmk], w=[sqk])
            pss, pssk = self.psn()
            self.mm(pss[:, 0:N], lhsT=self.ones_b[:], rhs=sq[:, 0:N], r=['ones_b', sqk], w=[pssk])
            rt, rtk = self.f32t()
            self.actf(rt[:, 0:N], pss[:, 0:N], AF.Sqrt, bias=self.epsc[:, 0:1], scale=1.0 / 128, r=[pssk, 'epsc'], w=[rtk])
            self.recip(rt[:, 0:N], rt[:, 0:N], r=[rtk], w=[rtk])
            self.stt(c.mqT[:, hm, :], pm[:, 0:N], self.mqgc[:, 0:1], rt[:, 0:N], ALU.mult, ALU.mult, r=[pmk, 'mqgc', rtk], w=[k + 'mqT'])

    def phase_b(self, l, c=None):
        c = c or self.P
        N, k = c.N, c.k
        w_in = self.inp('w_in')[l]
        for fcg in range(2):
            for n in range(3):
                wm, wmk = self.wload(w_in[:, C_MG + n * 1024 + fcg * 512:C_MG + n * 1024 + (fcg + 1) * 512], 8, 512)
                wb, wbk = self.wload(self.inp('w_branch')[l, n][:, fcg * 512:(fcg + 1) * 512], 4, 512)
                for fi in range(4):
                    fc = fcg * 4 + fi
                    cs = slice(fi * 128, (fi + 1) * 128)
                    pg, pgk = self.psn()
                    pp, ppk = self.psn()
                    for kc in range(8):
                        self.mm(pg[:, 0:N], lhsT=wm[:, kc, cs], rhs=c.xnT[:, kc, :], start=(kc == 0), stop=(kc == 7), r=[wmk, (k + 'xnT', kc)], w=[pgk])
                    for kc in range(4):
                        self.mm(pp[:, 0:N], lhsT=wb[:, kc, cs], rhs=c.brT[:, n * 4 + kc, :], start=(kc == 0), stop=(kc == 3), r=[wbk, f'{k}brT{n}'], w=[ppk])
                    sg, sgk = self.f32t()
                    self.actf(sg[:, 0:N], pg[:, 0:N], AF.Sigmoid, r=[pgk], w=[sgk])
                    if n == 0:
                        self.tt('dve', c.macc[:, fi, :], sg[:, 0:N], pp[:, 0:N], ALU.mult, r=[sgk, ppk], w=[c.mack])
                    else:
                        self.tt('dve', sg[:, 0:N], sg[:, 0:N], pp[:, 0:N], ALU.mult, r=[sgk, ppk], w=[sgk])
                        if n == 1:
                            self.tt('pool', c.macc[:, fi, :], c.macc[:, fi, :], sg[:, 0:N], ALU.add, r=[sgk, c.mack], w=[c.mack])
                        else:
                            self.tt('pool', c.mT[:, fc, :], c.macc[:, fi, :], sg[:, 0:N], ALU.add, r=[sgk, c.mack], w=c.mTk)

    def phase_c(self, l, c=None):
        c = c or self.P
        N, k = c.N, c.k
        for half in range(2):
            wo, wok = self.wload(self.inp('w_out')[l][:, half * 512:(half + 1) * 512], 8, 512)
            for fi in range(4):
                fc = half * 4 + fi
                po, pok = self.psn()
                for kc in range(8):
                    self.mm(po[:, 0:N], lhsT=wo[:, kc, fi * 128:(fi + 1) * 128], rhs=c.mT[:, kc, :], start=(kc == 0), stop=(kc == 7), r=[wok] + c.mTk, w=[pok])
                self.tt('dve', c.xT[:, fc, :], po[:, 0:N], c.xT[:, fc, :], ALU.add, r=[pok, (k + 'xT', fc)], w=[(k + 'xT', fc)])

    def phase_d(self, l, c=None):
        c = c or self.P
        N, k = c.N, c.k
        self.rmsnorm(1, c)
        w_gu = self.inp('w_gate_up')[l]
        w_dn = self.inp('w_down')[l]
        groups = [(0, 6), (6, 6), (12, 5), (17, 5)]
        for (j0, n) in groups:
            for p0 in range(0, n, 4):
                pn = min(4, n - p0)
                ja = j0 + p0
                wg, wgk = self.wload(w_gu[:, ja * 128:(ja + pn) * 128], 8, pn * 128)
                wu, wuk = self.wload(w_gu[:, DFF + ja * 128:DFF + (ja + pn) * 128], 8, pn * 128)
                for q in range(pn):
                    jj = p0 + q
                    cs = slice(q * 128, (q + 1) * 128)
                    pg, pgk = self.psn()
                    pu, puk = self.psn()
                    for kc in range(8):
                        self.mm(pg[:, 0:N], lhsT=wg[:, kc, cs], rhs=c.xnT[:, kc, :], start=(kc == 0), stop=(kc == 7), r=[wgk, (k + 'xnT', kc)], w=[pgk])
                    for kc in range(8):
                        self.mm(pu[:, 0:N], lhsT=wu[:, kc, cs], rhs=c.xnT[:, kc, :], start=(kc == 0), stop=(kc == 7), r=[wuk, (k + 'xnT', kc)], w=[puk])
                    sg, sgk = self.f32t()
                    self.actf(sg[:, 0:N], pg[:, 0:N], AF.Silu, r=[pgk], w=[sgk])
                    self.tt('dve', c.actT[:, jj, :], sg[:, 0:N], pu[:, 0:N], ALU.mult, r=[sgk, puk], w=[(k + 'actT', jj)])
            for cq in range(4):
                wd, wdk = self.wload(w_dn[j0 * 128:(j0 + n) * 128, cq * 256:(cq + 1) * 256], n, 256)
                for fi in range(2):
                    fc = cq * 2 + fi
                    pd, pdk = self.psn()
                    for kc in range(n):
                        self.mm(pd[:, 0:N], lhsT=wd[:, kc, fi * 128:(fi + 1) * 128], rhs=c.actT[:, kc, :], start=(kc == 0), stop=(kc == n - 1),
                                r=[wdk, (k + 'actT', kc)], w=[pdk])
                    self.tt('dve', c.xT[:, fc, :], pd[:, 0:N], c.xT[:, fc, :], ALU.add, r=[pdk, (k + 'xT', fc)], w=[(k + 'xT', fc)])

    def declare_sample(self):
        i = self.din
        i("xs", [4, D]); i("pool", [DEPTH, 2560 * 128, 512]); i("cwin", [DEPTH, 4, 512, 256]); i("sconv", [DEPTH, 4, 2, 512])
        i("cmem", [DEPTH, 4, 256, 1024]); i("pt", [4, 64], I32)
        i("c_cos_s", [4, 8]); i("c_sin_s", [4, 8]); i("c_ovl_s", [128, 4, 129]); i("c_sA", [1, 129]); i("c_sB", [1, 129])
        i("c_e2", [2, 128]); i("c_lastmask", [128, 1]); i("c_winb", [128, 1]); i("c_pm64", [128, 1]); i("c_pidx", [128, 1])
        o = self.dout
        self.y_s = o("y_s", [4, D]); self.rows_s_o = o("rows_s", [DEPTH, 4, 512]); self.win_s_o = o("win_s", [DEPTH, 4, 512, 256])
        self.conv_s_o = o("conv_s", [DEPTH, 4, 2, 512])
        dr = lambda n, sh: self.nc.dram_tensor(n, sh, F32).ap()
        self.scr_rows = dr("scr_rows", [4, 768]); self.scr_q = dr("scr_q", [4, 512]); self.osc = dr("osc", [4, 3, 8, 64])
        sb = self.S.sb
        self.idxA = sb([128, 4, 32], I32, 'idxA'); self.idxB = sb([128, 4, 64], I32, 'idxB')
        self.qT_s = sb([128, 8, 4], BF16, 'qT_s'); self.ng_s = sb([4, 24], F32, 'ng_s')
        self.kcT_s = sb([128, 512], BF16, 'kcT_s'); self.Rs = [sb([128, 4, 194], BF16, f'Rs{g}') for g in range(2)]
        self.biask = sb([128, 2, 64], F32, 'biask'); self.cos_s = sb([4, 8], F32, 'cos_s'); self.sin_s = sb([4, 8], F32, 'sin_s')
        self.sA = sb([1, 129], F32, 'sA'); self.sB = sb([1, 129], F32, 'sB'); self.e2a = sb([1, 128], F32, 'e2a'); self.e2b = sb([1, 128], F32, 'e2b')
        self.lastm = sb([128, 1], F32, 'lastm'); self.winb = sb([128, 1], F32, 'winb'); self.ones_f = sb([4, 1], F32, 'ones_f')
        self.Vp = Rot([sb([128, 2, 66], BF16, f'Vp{k}') for k in range(2)], 'Vp')
        self.stT = sb([128, 4, 2, 4], F32, 'stT'); self.cso = sb([128, 4, 2, 4], F32, 'cso')
        of_ = self.onsa[:].rearrange("p a b -> p (a b)")
        self.nr = of_[0:1, 0:768]; self.nq = of_[0:1, 768:1280]; self.vne = sb([1, 2, 66], BF16, 'vne')

    def setup_sample(self):
        d = self.dma
        d('sp', self.cos_s[:], self.inp('c_cos_s'), w=['cos_s']); d('sp', self.sin_s[:], self.inp('c_sin_s'), w=['sin_s'])
        d('sp', self.sA[:], self.inp('c_sA'), w=['sA']); d('sp', self.sB[:], self.inp('c_sB'), w=['sB'])
        d('sp', self.e2a[:], self.inp('c_e2')[0:1, :], w=['e2a']); d('sp', self.e2b[:], self.inp('c_e2')[1:2, :], w=['e2b'])
        d('sp', self.lastm[:], self.inp('c_lastmask'), w=['lastm']); d('sp', self.winb[:], self.inp('c_winb'), w=['winb'])
        self.memset('pool', self.ones_f[:], 1.0, w=['ones_f'])
        for g in range(2):
            self.memset('pool', self.Rs[g][:, :, 64:65], 1.0, w=[f'Rs{g}'])
            d('pool', self.Rs[g][:, :, 65:194], self.inp('c_ovl_s'), w=[f'Rs{g}'])
        for k in range(2):
            self.memset('pool', self.Vp.bufs[k][:, :, 64:65], 1.0, w=[f'Vp{k}'])
        self.memset('pool', self.vne[:, :, 64:65], 1.0, w=['vne'])
        pt = self.inp('pt')
        pa, pak = self.selTr.next()
        ptbA = pa[:].rearrange("p a b c -> p (a b c)")[:, 0:128].bitcast(I32).rearrange("p (s q) -> p s q", s=4)
        ptbB = pa[:].rearrange("p a b c -> p (a b c)")[:, 128:384].bitcast(I32).rearrange("p (s q) -> p s q", s=4)
        pm, pmk = self.smr.next()
        d('sp', pm[:, 0:1], self.inp('c_pm64'), w=[pmk]); d('sp', pm[:, 1:2], self.inp('c_pidx'), w=[pmk])
        for h in range(2):
            d('sp', ptbA[64 * h:64 * h + 64], pt[:, h::2].partition_broadcast(64), w=[pak], slow=True)
        d('sp', ptbB, pt.partition_broadcast(128), w=[pak])
        self.ts('dve', self.idxA[:], ptbA, 64.0, ALU.mult, pm[:, 0:1], ALU.add, r=[pak, pmk], w=['idxA'])
        self.ts('dve', self.idxB[:], ptbB, 128.0, ALU.mult, pm[:, 1:2], ALU.add, r=[pak, pmk], w=['idxB'])

    def gather(self, dst, dkey, src, idx_ap, ikey, eoff):
        self.S.add('pool', lambda e: e.indirect_dma_start(out=dst, out_offset=None, in_=src, element_offset=eoff,
                                                           in_offset=bass.IndirectOffsetOnAxis(ap=idx_ap, axis=0)), [ikey], [dkey], dma=True)

    def sample_layer(self, l):
        X = self.X
        d = self.dma
        if l == 0:
            for s_ in range(4):
                d('sp', X.xT[:, :, s_], self.inp('xs')[s_].rearrange("(k p) -> p k", p=128), w=[('s_xT', kc) for kc in range(8)], slow=True)
        self.rmsnorm(0, X)
        w_in = self.inp('w_in')[l]
        wA = [self.wload(w_in[:, 0:512], 8, 512), self.wload(w_in[:, 512:1024], 8, 512), self.wload(w_in[:, 1024:1304], 8, 280)]
        pq, pqk = self.psn(); pa, pak = self.psn(); pb, pbk = self.psn()
        for (pp, ppk, (w, wk), nn) in ((pq, pqk, wA[0], 512), (pa, pak, wA[1], 512), (pb, pbk, wA[2], 280)):
            for kc in range(8):
                self.mm(pp[0:4, 0:nn], lhsT=X.xnT[:, kc, :], rhs=w[:, kc, :], start=(kc == 0), stop=(kc == 7), r=[('s_xnT', kc), wk], w=[ppk])
        hd_, hk = self.tmr.next()
        hd = hd_[0:4]
        self.cp('act', hd[:, 0:8, :].rearrange("p h d -> p (h d)"), pq[0:4, :], r=[pqk], w=[hk])
        self.cp('dve', hd[:, 8:10, :].rearrange("p h d -> p (h d)"), pa[0:4, 256:384], r=[pak], w=[hk])
        self.cp('dve', hd[:, 10:12, :].rearrange("p h d -> p (h d)"), pb[0:4, 0:128], r=[pbk], w=[hk])
        rows_, rowsk = self.rowsr.next()
        rows = rows_[0:4]
        self.cp('act', rows[:, 0:512], pa[0:4, :], r=[pak], w=[rowsk])
        self.cp('act', rows[:, 640:768], pb[0:4, 128:256], r=[pbk], w=[rowsk])
        self.actf(self.ng_s[:], pb[0:4, 256:280], AF.Sigmoid, r=[pbk], w=['ng_s'])
        cos_b = self.cos_s[:, :].unsqueeze(1).to_broadcast([4, 12, 8])
        sin_b = self.sin_s[:, :].unsqueeze(1).to_broadcast([4, 12, 8])
        hr, hrk = self.headnorm_rope(hd, hk, 12, self.g12b[0:4], cos_b, sin_b, 1.0 / 64)
        self.cp('pool', rows[:, 256:384], hr[:, 8:10, :].rearrange("p h d -> p (h d)"), r=[hrk], w=[rowsk])
        self.cp('pool', rows[:, 512:640], hr[:, 10:12, :].rearrange("p h d -> p (h d)"), r=[hrk], w=[rowsk])
        d('sp', self.rows_s_o[l], rows[:, 0:512], r=[rowsk], w=['rows_s_o'])
        d('sp', self.scr_rows, rows[:, 0:768], r=[rowsk], w=['scr_rows'])
        d('sp', self.scr_q, hr[:, 0:8, :].rearrange("p h d -> p (h d)"), r=[hrk], w=['scr_q'])
        d('sp', self.win_s_o[l, :, 511, :], rows[:, 512:768], r=[rowsk], w=['win_s_a'])
        d('sp', self.win_s_o[l, :, 0:511, :], self.inp('cwin')[l, :, 1:512, :], w=['win_s_b'])
        ts_, tsk = self.tsrc.next()
        tq = ts_[0:4]
        self.cp('pool', tq[:, 0:4, :].rearrange("p c (g d) -> p c g d", g=2), hd[:, 0:8, :].rearrange("p (g c) d -> p c g d", g=2), r=[hk], w=[tsk])
        self.cp('pool', tq[:, 4:8, :].rearrange("p c (g d) -> p c g d", g=2), hr[:, 0:8, :].rearrange("p (g c) d -> p c g d", g=2), r=[hrk], w=[tsk])
        p0, p0k = self.psn()
        p0b = p0[:].bitcast(BF16)
        for c in range(8):
            self.tr(p0b[:, c * 4:(c + 1) * 4], tq[:, c, :], self.ident_b[0:4, 0:4], r=[tsk, 'ident_b'], w=[p0k])
        self.cp('act', self.qT_s[:], p0b[:, 0:32].rearrange("p (c s) -> p c s", c=8), r=[p0k], w=['qT_s'])
        for s_ in range(4):
            for j_ in range(2):
                d('sp', self.stT[:, :, j_, s_], self.inp('sconv')[l, s_, j_].rearrange("(c p) -> p c", p=128), w=['stT'], slow=True)
        wcx, wcxk = self.wload(w_in[:, C_CX:C_CX + 512], 8, 512)
        wcb, wcbk = self.wload(w_in[:, C_CB:C_CB + 512], 8, 512)
        wcc, wcck = self.wload(w_in[:, C_CC:C_CC + 512], 8, 512)
        for ci in range(4):
            cs = slice(ci * 128, (ci + 1) * 128)
            px, pxk = self.psn(); pc, pck = self.psn(); pb2, pb2k = self.psn()
            for (pp, ppk, ww, wwk) in ((px, pxk, wcx, wcxk), (pc, pck, wcc, wcck), (pb2, pb2k, wcb, wcbk)):
                for kc in range(8):
                    self.mm(pp[:, 0:4], lhsT=ww[:, kc, cs], rhs=X.xnT[:, kc, :], start=(kc == 0), stop=(kc == 7), r=[wwk, ('s_xnT', kc)], w=[ppk])
            cxs, cxk = self.f32t()
            self.cp('act', cxs[:, 0:4], px[:, 0:4], r=[pxk], w=[cxk])
            self.tt('dve', self.cso[:, ci, 1, :], pc[:, 0:4], cxs[:, 0:4], ALU.mult, r=[pck, cxk], w=['cso'])
            self.cp('pool', self.cso[:, ci, 0, :], self.stT[:, ci, 1, :], r=['stT'], w=['cso'])
            a1, a1k = self.f32t()
            self.ts('pool', a1[:, 0:4], self.stT[:, ci, 0, :], self.cwc[:, ci, 0:1], ALU.mult, r=['stT', 'cwc'], w=[a1k])
            self.stt(a1[:, 0:4], self.stT[:, ci, 1, :], self.cwc[:, ci, 1:2], a1[:, 0:4], ALU.mult, ALU.add, r=['stT', 'cwc', a1k], w=[a1k])
            self.stt(a1[:, 0:4], self.cso[:, ci, 1, :], self.cwc[:, ci, 2:3], a1[:, 0:4], ALU.mult, ALU.add, r=['cso', 'cwc', a1k], w=[a1k])
            self.tt('dve', X.brT[:, 4 + ci, :], pb2[:, 0:4], a1[:, 0:4], ALU.mult, r=[pb2k, a1k], w=['s_brT1'])
        for s_ in range(4):
            for j_ in range(2):
                d('sp', self.conv_s_o[l, s_, j_].rearrange("(c p) -> p c", p=128), self.cso[:, :, j_, s_], r=['cso'], w=['conv_s_o'], slow=True)
        self.mq_proj(l, X)
        w1b, w1nk = self.wr.next()
        self.w1n = w1b[:, :].rearrange("p (k c h) -> p k c h", k=2, c=16)
        for kind in range(2):
            d('pool', self.w1n[:, kind], self.inp('cmp_w1')[l, kind].rearrange("(c p) h -> p c h", p=128), w=[w1nk])
        pool_l = self.inp('pool').rearrange("l r f -> (l r) f")
        pool_rp = self.inp('pool').rearrange("l (r two) f -> (l r) (two f)", two=2)
        eoff = l * 2560 * 128 * 512
        psr2 = Rot([self.psr.bufs[4], self.psr.bufs[5], self.accr.bufs[0]], 'x')
        keys2 = ['ps4', 'ps5', 'acc0']

        def ps2():
            k = psr2.i % 3
            b, _ = psr2.next()
            return b, keys2[k]
        Hps = [(self.psr.bufs[k], f'ps{k}') for k in range(4)]
        for s in range(4):
            for pp in range(32):
                g1, g1k = self.tmior.next()
                self.gather(g1[:, :], g1k, pool_rp, self.idxA[:, s, pp:pp + 1], 'idxA', eoff)
                pk, pkk = self.b16t()
                self.cp('dve' if pp % 2 == 0 else 'pool', pk[:].rearrange("p (kg s d) -> p kg s d", kg=4, s=2),
                        g1[:].rearrange("p (s kg d) -> p kg s d", s=2, kg=8)[:, 0:4], r=[g1k], w=[pkk])
                pt_, ptk = ps2()
                ptb = pt_[:].bitcast(BF16)
                for kg in range(4):
                    self.tr(ptb[:, kg * 128:(kg + 1) * 128], pk[:, kg * 128:(kg + 1) * 128], self.ident_b[:], r=[pkk, 'ident_b'], w=[ptk])
                xp, xpk = self.b16t()
                self.cp('act', xp[:], ptb[:, 0:512], r=[ptk], w=[xpk])
                for kg in range(4):
                    kind = kg // 2
                    H, Hk = Hps[kg]
                    for s8 in range(8):
                        self.mm(H[:, 16 * pp:16 * pp + 16], lhsT=self.w1n[:, kind, s8, :], rhs=xp[:, kg * 128 + s8:kg * 128 + 128:8],
                                start=(pp == 0 and s8 == 0), stop=False, r=[w1nk, xpk], w=[Hk])
                    for s8 in range(8):
                        if pp == 0:
                            self.mm(H[:, 0:15], lhsT=self.w1n[:, kind, 8 + s8, :], rhs=xp[:, kg * 128 + 8 + s8:kg * 128 + 128:8],
                                    start=False, stop=False, r=[w1nk, xpk], w=[Hk])
                        else:
                            self.mm(H[:, 16 * pp - 1:16 * pp + 15], lhsT=self.w1n[:, kind, 8 + s8, :], rhs=xp[:, kg * 128 + s8:kg * 128 + 128:8],
                                    start=False, stop=False, r=[w1nk, xpk], w=[Hk])
            kcn, kcnk = self.b16t()
            kcn4 = kcn[:].rearrange("p (c f) -> p c f", c=4)
            for kg in range(4):
                kind, g = kg // 2, kg % 2
                H, Hk = Hps[kg]
                hs, hsk = self.b16t()
                self.actf(hs[:], H[:], AF.Silu, bias=self.bpe[:, kind:kind + 1], r=[Hk, 'bpe'], w=[hsk])
                for ch in range(4):
                    pc, pck = ps2()
                    self.mm(pc[:, 0:64], lhsT=hs[:, ch * 128:(ch + 1) * 128], rhs=self.w2sb[:, kind, :], r=[hsk, 'w2sb'], w=[pck])
                    if kind == 0:
                        kf, kfk = self.f32t()
                        sm, smk = self.smr.next()
                        self.actf(kf[:, 0:64], pc[:, 0:64], AF.Square, accum=sm[:, 0:1], r=[pck], w=[kfk, smk])
                        self.actf(sm[:, 1:2], sm[:, 0:1], AF.Sqrt, bias=self.epsc[:, 0:1], scale=1.0 / 64, r=[smk, 'epsc'], w=[smk])
                        self.recip(sm[:, 2:3], sm[:, 1:2], r=[smk], w=[smk])
                        self.stt(kcn4[:, ch, g * 64:(g + 1) * 64], pc[:, 0:64], sm[:, 2:3], self.k0gb[:, :], ALU.mult, ALU.mult, r=[pck, smk, 'k0gb'], w=[kcnk])
                    else:
                        self.cp('act', self.Rs[g][:, ch, 0:64], pc[:, 0:64], r=[pck], w=[f'Rs{g}'])
            pt_, ptk = ps2()
            ptb = pt_[:].bitcast(BF16)
            for ch in range(4):
                self.tr(ptb[:, ch * 128:(ch + 1) * 128], kcn4[:, ch, :], self.ident_b[:], r=[kcnk, 'ident_b'], w=[ptk])
            self.cp('act', self.kcT_s[:], ptb[:, 0:512], r=[ptk], w=['kcT_s'])
            for g in range(2):
                pS, pSk = ps2()
                for ch in range(4):
                    for hh in range(4):
                        self.mm(pS[:, ch * 4 + hh:ch * 4 + hh + 1], lhsT=self.kcT_s[64 * g:64 * g + 64, ch * 128:(ch + 1) * 128],
                                rhs=self.qT_s[64 * g:64 * g + 64, hh, s:s + 1], r=['kcT_s', 'qT_s'], w=[pSk])
                pT, pTk = self.b16t()
                self.actf(pT[:, 0:16], pS[:, 0:16], AF.Exp, scale=0.125, r=[pSk], w=[pTk])
                self.ts('dve', pT[:, 12:16], pT[:, 12:16], self.lastm[:, 0:1], ALU.mult, r=[pTk, 'lastm'], w=[pTk])
                pO, pOk = ps2()
                for ch in range(4):
                    self.mm(pO[0:4, 0:194], lhsT=pT[:, ch * 4:(ch + 1) * 4], rhs=self.Rs[g][:, ch, 0:194], start=(ch == 0), stop=(ch == 3), r=[pTk, f'Rs{g}'], w=[pOk])
                sm, smk = self.smr.next()
                self.recip(sm[0:4, 0:1], pO[0:4, 64:65], r=[pOk], w=[smk])
                ob_, obk = self.f32t()
                self.ts('dve', ob_[0:4, 0:64], pO[0:4, 0:64], sm[0:4, 0:1], ALU.mult, r=[pOk, smk], w=[obk])
                d('sp', self.osc[s, 0, 4 * g:4 * g + 4, :], ob_[0:4, 0:64], r=[obk], w=[('osc', s, 0, g)])
                self.ts('dve', ob_[0:4, 128:257], pO[0:4, 65:194], sm[0:4, 0:1], ALU.mult, r=[pOk, smk], w=[obk])
                pR, pRk = ps2()
                self.mm(pR[0:1, 0:129], lhsT=self.ones_f[0:4, 0:1], rhs=ob_[0:4, 128:257], r=['ones_f', obk], w=[pRk])
                s2_, s2k = self.f32t()
                s2 = s2_[0:1]
                self.tt('dve', s2[:, 0:129], pR[0:1, 0:129], self.sA[:, :], ALU.mult, r=[pRk, 'sA'], w=[s2k])
                self.tt('dve', s2[:, 0:129], s2[:, 0:129], self.sB[:, :], ALU.add, r=[s2k, 'sB'], w=[s2k])
                sm2, sm2k = self.smr.next()
                self.S.add('dve', (lambda o, i: lambda e: e.max(out=o, in_=i))(sm2[0:1, 0:8], s2[:, 0:129]), [s2k], [sm2k])
                self.S.add('dve', (lambda o, a, b: lambda e: e.match_replace(out=o, in_to_replace=a, in_values=b, imm_value=-1e30))(s2[:, 256:385], sm2[0:1, 0:8], s2[:, 0:129]), [s2k, sm2k], [s2k])
                self.S.add('dve', (lambda o, i: lambda e: e.max(out=o, in_=i))(sm2[0:1, 8:16], s2[:, 256:385]), [s2k], [sm2k])
                self.ts('dve', s2[:, 256:385], s2[:, 0:129], sm2[0:1, 15:16], ALU.is_ge, r=[s2k, sm2k], w=[s2k])
                self.ts('dve', s2[:, 0:129], s2[:, 256:385], -1.0, ALU.add, -MASKV * 0.125, ALU.mult, r=[s2k], w=[s2k])
                pB, pBk = ps2()
                self.mm(pB[:, 0:64], lhsT=self.e2a[0:1, :], rhs=s2[:, 0:128:2], start=True, stop=False, r=['e2a', s2k], w=[pBk])
                self.mm(pB[:, 0:64], lhsT=self.e2b[0:1, :], rhs=s2[:, 1:128:2], start=False, stop=True, r=['e2b', s2k], w=[pBk])
                self.cp('act', self.biask[:, g, :], pB[:, 0:64], r=[pBk], w=['biask'])
            d('sp', self.nr, self.scr_rows[s:s + 1, :], r=['scr_rows'], w=['onsa'])
            d('sp', self.nq, self.scr_q[s:s + 1, :], r=['scr_q'], w=['onsa'])
            for br in (1, 2):
                koff, voff = (256, 384) if br == 1 else (512, 640)
                accS, accSk = self.accr.bufs[1], 'acc1'
                first = True
                nblk = 64 if br == 1 else 4
                if br == 2:
                    g3, g3k = self.tmior.next()
                    d('sp', g3[:].rearrange("p (b f) -> p b f", b=4), self.inp('cwin')[l, s].rearrange("(b p) f -> p b f", p=128), w=[g3k])
                for pg in range(nblk):
                    if br == 1:
                        g2, g2k = self.tmior.next()
                        self.gather(g2[:, 0:512], g2k, pool_l, self.idxB[:, s, pg:pg + 1], 'idxB', eoff)
                        ksrc, vsrc = g2[:, 256:384], g2[:, 384:512]
                    else:
                        g2k = g3k
                        ksrc, vsrc = g3[:, pg * 256:pg * 256 + 128], g3[:, pg * 256 + 128:pg * 256 + 256]
                    kb16, kbk = self.b16t()
                    self.cp('dve', kb16[:, 0:128], ksrc, r=[g2k], w=[kbk])
                    vp, vpk = self.Vp.next()
                    self.cp('pool', vp[:, :, 0:64], vsrc.rearrange("p (g d) -> p g d", g=2), r=[g2k], w=[vpk])
                    pt_, ptk = ps2()
                    ptb = pt_[:].bitcast(BF16)
                    self.tr(ptb[:, 0:128], kb16[:, 0:128], self.ident_b[:], r=[kbk, 'ident_b'], w=[ptk])
                    self.cp('act', kb16[:, 128:256], ptb[:, 0:128], r=[ptk], w=[kbk])
                    pS, pSk = ps2()
                    for h in range(8):
                        g, hh = h // 4, h % 4
                        self.mm(pS[:, h:h + 1], lhsT=kb16[64 * g:64 * g + 64, 128:256], rhs=self.qT_s[64 * g:64 * g + 64, 4 + hh, s:s + 1], r=[kbk, 'qT_s'], w=[pSk])
                    pT, pTk = self.b16t()
                    for g in range(2):
                        if br == 1:
                            bias = self.biask[:, g, pg:pg + 1]
                        else:
                            bias = self.winb[:, 0:1] if pg == 0 else None
                        self.actf(pT[:, 4 * g:4 * g + 4], pS[:, 4 * g:4 * g + 4], AF.Exp, bias=bias, scale=0.125, r=[pSk, 'biask', 'winb'], w=[pTk])
                    for g in range(2):
                        self.mm(accS[0:4, g * 65:(g + 1) * 65], lhsT=pT[:, 4 * g:4 * g + 4], rhs=vp[:, g, 0:65], start=first, stop=False, r=[pTk, vpk], w=[accSk])
                        first = False
                pr_, prk = self.f32t()
                pr = pr_[0:1]
                self.tt('dve', pr[:, 0:512].rearrange("p (g h d) -> p g h d", g=2, h=4), self.nq[:, :].rearrange("p (g h d) -> p g h d", g=2, h=4),
                        self.nr[:, koff:koff + 128].rearrange("p (g d) -> p g d", g=2).unsqueeze(2).to_broadcast([1, 2, 4, 64]), ALU.mult, r=['onsa'], w=[prk])
                sm3, sm3k = self.smr.next()
                self.red(sm3[0:1, 0:8], pr[:, 0:512].rearrange("p (h d) -> p h d", h=8), r=[prk], w=[sm3k])
                pn, pnk = self.b16t()
                self.actf(pn[0:1, 0:8], sm3[0:1, 0:8], AF.Exp, scale=0.125, r=[sm3k], w=[pnk])
                self.cp('dve', self.vne[:, :, 0:64], self.nr[:, voff:voff + 128].rearrange("p (g d) -> p g d", g=2), r=['onsa'], w=['vne'])
                for g in range(2):
                    self.mm(accS[0:4, g * 65:(g + 1) * 65], lhsT=pn[0:1, 4 * g:4 * g + 4], rhs=self.vne[0:1, g, 0:65], start=False, stop=True, r=[pnk, 'vne'], w=[accSk])
                sm4, sm4k = self.smr.next()
                self.recip(sm4[0:4, 0:2], accS[0:4, 64:130:65], r=[accSk], w=[sm4k])
                ob_, obk = self.f32t()
                for g in range(2):
                    self.ts('dve', ob_[0:4, g * 64:(g + 1) * 64], accS[0:4, g * 65:g * 65 + 64], sm4[0:4, g:g + 1], ALU.mult, r=[accSk, sm4k], w=[obk])
                    d('sp', self.osc[s, br, 4 * g:4 * g + 4, :], ob_[0:4, g * 64:(g + 1) * 64], r=[obk], w=[('osc', s, br, g)])
            cmb, cmk = self.wr.next()
            cm = cmb[:, 0:2048].rearrange("p (mb f) -> p mb f", mb=2)
            d('pool', cm, self.inp('cmem')[l, s].rearrange("(mb p) f -> p mb f", p=128), w=[cmk])
            pt_, ptk = ps2()
            ptb = pt_[:].bitcast(BF16)
            for mb in range(2):
                for hm in range(4):
                    self.tr(ptb[:, (mb * 4 + hm) * 128:(mb * 4 + hm + 1) * 128], cm[:, mb, hm * 128:(hm + 1) * 128], self.ident_b[:], r=[cmk, 'ident_b'], w=[ptk])
            self.cp('act', self.mkT[:].rearrange("p h (mb m) -> p mb h m", mb=2), ptb[:, 0:1024].rearrange("p (mb h m) -> p mb h m", mb=2, h=4), r=[ptk], w=['mkT'])
            pS, pSk = ps2()
            for mb in range(2):
                for hm in range(4):
                    self.mm(pS[:, mb * 4 + hm:mb * 4 + hm + 1], lhsT=self.mkT[:, hm, mb * 128:(mb + 1) * 128], rhs=X.mqT[:, hm, s:s + 1], r=['mkT', 's_mqT'], w=[pSk])
            pT, pTk = self.b16t()
            self.actf(pT[:, 0:8], pS[:, 0:8], AF.Exp, scale=128 ** -0.5, r=[pSk], w=[pTk])
            pO, pOk = ps2()
            pD, pDk = ps2()
            first = True
            for mb in range(2):
                for hm in range(4):
                    self.mm(pO[:, hm:hm + 1], lhsT=cm[:, mb, 512 + hm * 128:512 + (hm + 1) * 128], rhs=pT[:, mb * 4 + hm:mb * 4 + hm + 1], start=first, stop=False, r=[cmk, pTk], w=[pOk])
                    first = False
            for mb in range(2):
                self.mm(pD[:, 0:4], lhsT=self.ones_b[:], rhs=pT[:, mb * 4:(mb + 1) * 4], start=(mb == 0), stop=(mb == 1), r=['ones_b', pTk], w=[pDk])
            rc, rck = self.f32t()
            self.recip(rc[:, 0:4], pD[:, 0:4], r=[pDk], w=[rck])
            self.tt('dve', X.brT[:, 8:12, s], pO[:, 0:4], rc[:, 0:4], ALU.mult, r=[pOk, rck], w=['s_brT2'])
        on_, onk = self.f32t()
        on = on_[0:4]
        osk = [('osc', s, br, g) for s in range(4) for br in range(3) for g in range(2)]
        for br in range(3):
            ot_, otk = self.tmr.next()
            ot = ot_[0:4, 0:8, :]
            d('sp', ot, self.osc[:, br, :, :], r=osk, w=[otk])
            gb = self.ng_s[:, br * 8:(br + 1) * 8].unsqueeze(2).to_broadcast([4, 8, 64])
            if br == 0:
                self.tt('dve', on[:, 0:512].rearrange("p (h d) -> p h d", h=8), ot, gb, ALU.mult, r=[otk, 'ng_s'], w=[onk])
            else:
                self.tt('dve', ot, ot, gb, ALU.mult, r=[otk, 'ng_s'], w=[otk])
                self.tt('dve', on[:, 0:512], on[:, 0:512], ot.rearrange("p h d -> p (h d)"), ALU.add, r=[otk, onk], w=[onk])
        onb, onbk = self.b16t()
        self.cp('dve', onb[0:4, :], on[:, 0:512], r=[onk], w=[onbk])
        pt_, ptk = self.psn()
        ptb = pt_[:].bitcast(BF16)
        for c in range(4):
            self.tr(ptb[:, c * 4:(c + 1) * 4], onb[0:4, c * 128:(c + 1) * 128], self.ident_b[0:4, 0:4], r=[onbk, 'ident_b'], w=[ptk])
        self.cp('act', X.brT[:, 0:4, :], ptb[:, 0:16].rearrange("p (c s) -> p c s", c=4), r=[ptk], w=['s_brT0'])
        self.phase_b(l, X)
        self.phase_c(l, X)
        self.phase_d(l, X)
        if l == self.nlayers - 1:
            for s_ in range(4):
                d('sp', self.y_s[s_].rearrange("(k p) -> p k", p=128), X.xT[:, :, s_], r=[('s_xT', kc) for kc in range(8)], w=['y_s'], slow=True)

    def build(self, stage=99):
        self.declare()
        self.make_ctxs()
        if self.with_sample:
            self.declare_sample()
        self.stage = stage
        self.setup()
        if self.with_sample:
            self.setup_sample()
        for l in range(self.nlayers):
            if stage >= 1:
                self.layer_setup(l)
            for t in range(self.ntiles):
                if stage < 3:
                    continue
                self.load_x(l, t)
                self.rmsnorm(0)
                if stage < 4:
                    continue
                wA = [self.wload(self.inp('w_in')[l][:, 0:512], 8, 512), self.wload(self.inp('w_in')[l][:, 512:1024], 8, 512),
                      self.wload(self.inp('w_in')[l][:, 1024:1304], 8, 280)]
                self.tokmajor(l, t, wA)
                if stage < 5:
                    continue
                self.compress(l, t)
                self.fm_proj(l, t)
                self.mq_proj(l)
                self.cmp_attend(t)
                self.topk(t)
                self.slc(t)
                self.win(t)
                self.nsa_finalize()
                self.mem_attend()
                if stage < 6:
                    continue
                self.phase_b(l)
                self.phase_c(l)
                self.phase_d(l)
                self.store_x(l, t)
            if self.with_sample and stage >= 7:
                self.sample_layer(l)
        self.S.emit()
        return self.nc


def make_consts():
    c = {}
    pos = (np.arange(32)[None, :] * 128 + np.arange(128)[:, None]).astype(np.float32)
    inv = (500000.0 ** (-np.arange(8, dtype=np.float32) / 8)).astype(np.float32)
    ang = pos[:, :, None] * inv[None, None, :]
    c['c_cos'] = np.cos(ang).astype(np.float32)
    c['c_sin'] = np.sin(ang).astype(np.float32)
    q = pos.astype(np.int64)
    j = np.arange(64)[None, None, :]
    qb = (q // 64)[:, :, None]
    forced = (j == 0) | (j == qb) | (j == qb - 1)
    elig = (j * 64) <= q[:, :, None]
    A = (elig & ~forced).astype(np.float32)
    B = np.where(forced, 1e9, np.where(elig, 0.0, -1e9)).astype(np.float32)
    c['c_selA'] = A
    c['c_selB'] = B
    p = np.arange(128)[:, None]
    f = np.arange(128)[None, :]
    c['c_tri'] = (p <= f).astype(np.float32)
    c['c_strict'] = (p > f).astype(np.float32)
    f5 = np.arange(512)[None, :]
    c['c_stair'] = ((16 * (p % 32) + 15) <= f5).astype(np.float32)
    posn = np.arange(256)
    cc = posn - 1
    jj = np.arange(64)[None, :]
    ov = ((cc[:, None] * 16 < (jj + 1) * 64) & (cc[:, None] * 16 + 32 > jj * 64) & (cc[:, None] >= 0) & (cc[:, None] < 255))
    c['c_ovl'] = ov.astype(np.float32).reshape(2, 128, 64).transpose(1, 0, 2).copy()
    k = np.arange(T)[None, :]
    c['c_E'] = ((k // 64) == np.arange(64)[:, None]).astype(np.float32)
    c['c_ident'] = np.eye(128, dtype=np.float32)
    return c


def make_consts_sample():
    c = {}
    inv = (500000.0 ** (-np.arange(8, dtype=np.float32) / 8)).astype(np.float32)
    ang = np.float32(8192.0) * inv
    c['c_cos_s'] = np.tile(np.cos(ang).astype(np.float32)[None, :], (4, 1))
    c['c_sin_s'] = np.tile(np.sin(ang).astype(np.float32)[None, :], (4, 1))
    cc = np.arange(512)[:, None]
    jj = np.arange(129)[None, :]
    ov = ((cc * 16 < (jj + 1) * 64) & (cc * 16 + 32 > jj * 64) & (cc < 511))
    c['c_ovl_s'] = ov.astype(np.float32).reshape(4, 128, 129).transpose(1, 0, 2).copy()
    forced = np.zeros((1, 129), dtype=bool)
    forced[0, [0, 127, 128]] = True
    c['c_sA'] = (~forced).astype(np.float32)
    c['c_sB'] = np.where(forced, 1e9, 0.0).astype(np.float32)
    p = np.arange(128)
    c['c_e2'] = np.stack([(p < 64), (p >= 64)]).astype(np.float32)
    c['c_lastmask'] = (p < 127).astype(np.float32)[:, None]
    wb = np.zeros((128, 1), dtype=np.float32)
    wb[0, 0] = MASKV * 0.125
    c['c_winb'] = wb
    c['c_pm64'] = (p % 64).astype(np.float32)[:, None]
    c['c_pidx'] = p.astype(np.float32)[:, None]
    return c


_NC_CACHE = {}


def _get_program():
    if 'nc' not in _NC_CACHE:
        b = Builder(nlayers=DEPTH, ntiles=NTILE, with_sample=True)
        nc = b.build(99)
        _NC_CACHE['nc'] = nc
        _NC_CACHE['decl'] = set(b.decl)
    return _NC_CACHE['nc'], _NC_CACHE['decl']


def kernel(**inputs):
    nc, decl = _get_program()
    f32 = np.float32
    inp = {k: np.asarray(v) for k, v in inputs.items()}
    consts = make_consts()
    consts.update(make_consts_sample())
    qn, kn = inp['q_norm'], inp['k_norm']
    shared = dict(consts)
    for k in ['w_in', 'w_mem_kv', 'w_branch', 'w_out', 'w_gate_up', 'w_down', 'cmp_w1', 'cmp_w2', 'cmp_pe', 'conv_w',
              'norm_mix', 'norm_mem', 'norm_ffn', 'mem_q_norm']:
        shared[k] = inp[k]
    shared['g12'] = np.concatenate([np.tile(qn, (1, 8)), np.tile(kn[:, 1], (1, 2)), np.tile(kn[:, 2], (1, 2))], axis=1)
    shared['k0g'] = kn[:, 0]
    shared['mkg'] = np.tile(inp['mem_k_norm'], (1, 4))
    n_phys = inp['cache_nsa_kv'].shape[1]
    assert n_phys == 2560
    shared['pool'] = inp['cache_nsa_kv'].reshape(DEPTH, n_phys * 128, 512)
    in_maps = []
    for c in range(8):
        b = c % 4
        ss = slice(4 * c, 4 * c + 4)
        m = dict(shared)
        m['x'] = inp['x_prompt'][b]
        m['mem'] = inp['mem_prompt'][b]
        m['xs'] = inp['x_sample'][ss, 0]
        m['cwin'] = inp['cache_win_kv'][:, ss].reshape(DEPTH, 4, 512, 256)
        m['sconv'] = inp['state_conv'][:, ss]
        m['cmem'] = inp['cache_mem_kv'][:, ss].reshape(DEPTH, 4, 256, 1024)
        m['pt'] = inp['page_table'][ss].astype(np.int32)
        in_maps.append({k: np.ascontiguousarray(v) for k, v in m.items() if k in decl})
    res = run_bass_kernel_spmd(nc, in_maps, core_ids=list(range(8)))
    R = res.results
    y_p = np.stack([R[c]['y'] for c in range(4)], 0).astype(f32)
    y_s = np.concatenate([R[c]['y_s'] for c in range(8)], 0).reshape(32, 1, D).astype(f32)
    rows_p = np.stack([R[c]['rows_p'] for c in range(4)], 1).reshape(DEPTH, 4, T, 4, 2, 64).astype(f32)
    rows_s = np.concatenate([R[c]['rows_s'] for c in range(8)], 1).reshape(DEPTH, 32, 1, 4, 2, 64).astype(f32)
    win_p = np.stack([R[c]['win_p'] for c in range(4)], 1).reshape(DEPTH, 4, 512, 2, 2, 64).astype(f32)
    win_s = np.concatenate([R[c]['win_s'] for c in range(8)], 1).reshape(DEPTH, 32, 512, 2, 2, 64).astype(f32)
    conv_p = np.stack([R[c]['conv_p'] for c in range(4)], 1).astype(f32)
    conv_s = np.concatenate([R[c]['conv_s'] for c in range(8)], 1).astype(f32)
    mem_p = np.stack([R[c]['mem_p'] for c in range(4)], 1).reshape(DEPTH, 4, 256, 2, 4, 128).astype(f32)
    return (y_p, y_s, rows_p, rows_s, win_p, win_s, conv_p, conv_s, mem_p)
```

```python
import contextlib
import numpy as np
import concourse.bass as bass
import concourse.mybir as mybir
from concourse.bass_utils import run_bass_kernel_spmd

F32 = mybir.dt.float32
BF16 = mybir.dt.bfloat16
I32 = mybir.dt.int32
ALU = mybir.AluOpType
AF = mybir.ActivationFunctionType
AX = mybir.AxisListType
ENGS = ['pe', 'act', 'dve', 'pool', 'sp']

D = 1024
T = 4096
TT = 512
NTILE = 8
DEPTH = 2
N_IN = 6424
DFF = 2816
EPS = 1e-6
MASKV = -30000.0
C_CX, C_CB, C_CC, C_MQ, C_MG = 1304, 1816, 2328, 2840, 3352


class Sched:
    NDSEM = 8

    def __init__(self, nc):
        self.nc = nc
        self.ops = []
        self.stack = contextlib.ExitStack()
        self._n = 0

    def sb(self, shape, dt, name=None):
        self._n += 1
        return self.stack.enter_context(self.nc.sbuf_tensor(name or f"sb{self._n}", list(shape), dt))

    def ps(self, shape, dt=F32, name=None):
        self._n += 1
        return self.stack.enter_context(self.nc.psum_tensor(name or f"ps{self._n}", list(shape), dt))

    def add(self, eng, fn, r=(), w=(), dma=False):
        w = tuple(w) + tuple(k + '#rd' for k in r if isinstance(k, str) and (k.startswith('ps') or k.startswith('acc')) and eng != 'pe')
        self.ops.append(dict(eng=eng, fn=fn, r=tuple(r), w=tuple(w), dma=dma))

    def emit(self):
        nc = self.nc
        ops = self.ops
        n = len(ops)
        pos = [0] * n
        cnt = {e: 0 for e in ENGS}
        dcnt = {e: 0 for e in ENGS}
        dk = [0] * n
        for i, o in enumerate(ops):
            if o['dma']:
                dk[i] = dcnt[o['eng']]
                dcnt[o['eng']] += 1
            else:
                pos[i] = cnt[o['eng']]
                cnt[o['eng']] += 1
        last_w = {}
        readers = {}
        deps = [None] * n
        for i, o in enumerate(ops):
            d = set()
            for r in o['r']:
                if r in last_w:
                    d.add(last_w[r])
            for w in o['w']:
                if w in last_w:
                    d.add(last_w[w])
                d.update(readers.get(w, ()))
            d.discard(i)
            deps[i] = d
            for r in o['r']:
                readers.setdefault(r, []).append(i)
            for w in o['w']:
                last_w[w] = i
                readers[w] = []
        clock = {e: {p: -1 for p in ENGS} for e in ENGS}
        dma_seen = {e: set() for e in ENGS}
        opclock = [None] * n
        waits = [[] for _ in range(n)]
        signal = set()
        K = self.NDSEM
        dma_by_eng = {e: [] for e in ENGS}
        for i, o in enumerate(ops):
            E = o['eng']
            ck = clock[E]
            if o['dma']:
                k = dk[i]
                if k >= K:
                    prev = dma_by_eng[E][k - K]
                    if prev not in dma_seen[E]:
                        waits[i].append(('d', prev))
                        dma_seen[E].add(prev)
                dma_by_eng[E].append(i)
            for d in sorted(deps[i], reverse=True):
                od = ops[d]
                if od['dma']:
                    if d in dma_seen[E]:
                        continue
                    waits[i].append(('d', d))
                    dma_seen[E].add(d)
                else:
                    P = od['eng']
                    if P == E and E == 'pe':
                        continue
                    if ck[P] >= pos[d]:
                        continue
                    waits[i].append(('c', d))
                    signal.add(d)
                    oc = opclock[d]
                    for p in ENGS:
                        if oc[p] > ck[p]:
                            ck[p] = oc[p]
                    if pos[d] > ck[P]:
                        ck[P] = pos[d]
            if not o['dma']:
                opclock[i] = dict(ck)
        rank = {}
        rc = {e: 0 for e in ENGS}
        for i, o in enumerate(ops):
            if not o['dma'] and i in signal:
                rc[o['eng']] += 1
                rank[i] = rc[o['eng']]
        st = self.stack
        csem = {e: st.enter_context(nc.semaphore(f"c_{e}")) for e in ENGS}
        dsem = {e: [st.enter_context(nc.semaphore(f"d_{e}{j}")) for j in range(K)] for e in ENGS if dcnt[e] > 0}

        def dsv(d):
            k = dk[d]
            return dsem[ops[d]['eng']][k % K], 16 * (k // K + 1)

        by_eng = {e: [i for i, o in enumerate(ops) if o['eng'] == e] for e in ENGS}
        self.stats = dict(n=n, signals=len(signal), waits=sum(len(w) for w in waits),
                          per_eng={e: len(by_eng[e]) for e in ENGS})

        def mk(E):
            def body(e):
                for i in by_eng[E]:
                    o = ops[i]
                    for kind, d in waits[i]:
                        if kind == 'd':
                            s, v = dsv(d)
                            e.wait_ge(s, v)
                        else:
                            e.wait_ge(csem[ops[d]['eng']], rank[d])
                    ins = o['fn'](e)
                    if o['dma']:
                        s, v = dsv(i)
                        ins.then_inc(s, 16)
                    elif i in signal:
                        ins.then_inc(csem[E], 1)
                nd = dcnt[E]
                for j in range(min(K, nd)):
                    last_k = ((nd - 1 - j) // K) * K + j
                    e.wait_ge(dsem[E][j], 16 * (last_k // K + 1))
            return body

        with nc.Block() as block:
            block.tensor(mk('pe'))
            block.scalar(mk('act'))
            block.vector(mk('dve'))
            block.gpsimd(mk('pool'))
            block.sync(mk('sp'))
        st.close()


class Rot:
    def __init__(self, bufs, name):
        self.bufs = bufs
        self.name = name
        self.i = 0

    def next(self):
        k = self.i % len(self.bufs)
        self.i += 1
        return self.bufs[k], f"{self.name}{k}"


class Builder:
    def __init__(self, nlayers=DEPTH, ntiles=NTILE, with_sample=True):
        self.nlayers = nlayers
        self.ntiles = ntiles
        self.with_sample = with_sample
        self.nc = bass.Bass("TRN2", target_bir_lowering=False)
        self.S = Sched(self.nc)
        self.lazy = {}
        self.decl = {}

    def mm(self, out, lhsT, rhs, start=True, stop=True, r=(), w=()):
        self.S.add('pe', lambda e: e.matmul(out, lhsT=lhsT, rhs=rhs, start=start, stop=stop, skip_group_check=True), r, w)

    def tr(self, out, in_, ident, r=(), w=()):
        self.S.add('pe', lambda e: e.transpose(out, in_, ident), r, w)

    def actf(self, out, in_, func, bias=None, scale=1.0, accum=None, r=(), w=()):
        kw = {}
        if bias is not None:
            kw['bias'] = bias
        if accum is not None:
            kw['accum_out'] = accum
        self.S.add('act', lambda e: e.activation(out=out, in_=in_, func=func, scale=scale, **kw), r, w)

    def cp(self, eng, out, in_, r=(), w=()):
        if eng == 'pool':
            eng = 'act'
        if eng == 'act':
            self.S.add('act', lambda e: e.copy(out=out, in_=in_), r, w)
        else:
            self.S.add(eng, lambda e: e.tensor_copy(out=out, in_=in_), r, w)

    def tt(self, eng, out, in0, in1, op, r=(), w=()):
        if eng == 'pool':
            eng = 'dve'
        self.S.add(eng, lambda e: e.tensor_tensor(out=out, in0=in0, in1=in1, op=op), r, w)

    def ts(self, eng, out, in0, s1, op0, s2=None, op1=None, r=(), w=()):
        if eng == 'pool':
            eng = 'dve'
        if op1 is None:
            self.S.add(eng, lambda e: e.tensor_scalar(out=out, in0=in0, scalar1=s1, scalar2=None, op0=op0), r, w)
        else:
            self.S.add(eng, lambda e: e.tensor_scalar(out=out, in0=in0, scalar1=s1, scalar2=s2, op0=op0, op1=op1), r, w)

    def stt(self, out, in0, scalar, in1, op0, op1, r=(), w=()):
        self.S.add('dve', lambda e: e.scalar_tensor_tensor(out=out, in0=in0, scalar=scalar, in1=in1, op0=op0, op1=op1), r, w)

    def recip(self, out, in_, r=(), w=()):
        self.S.add('dve', lambda e: e.reciprocal(out=out, in_=in_), r, w)

    def red(self, out, in_, r=(), w=()):
        self.S.add('dve', lambda e: e.tensor_reduce(out=out, in_=in_, axis=AX.X, op=ALU.add), r, w)

    def memset(self, eng, ap, val, r=(), w=()):
        self.S.add(eng, lambda e: e.memset(ap, val), r, w)

    def dma(self, q, out, in_, r=(), w=(), slow=False):
        if slow:
            self.S.add(q, lambda e: e.dma_start(out=out, in_=in_, allow_slow_non_contiguous=True), r, w, dma=True)
        else:
            self.S.add(q, lambda e: e.dma_start(out=out, in_=in_), r, w, dma=True)

    def din(self, name, shape, dt=F32):
        self.lazy[name] = (list(shape), dt)
        return None

    def inp(self, name):
        if name not in self.decl:
            shape, dt = self.lazy[name]
            self.decl[name] = self.nc.dram_tensor(name, shape, dt, kind="ExternalInput").ap()
        return self.decl[name]

    def dout(self, name, shape, dt=F32):
        return self.nc.dram_tensor(name, list(shape), dt, kind="ExternalOutput").ap()

    def psn(self):
        return self.psr.next()

    def f32t(self):
        return self.f32r.next()

    def b16t(self):
        return self.b16r.next()

    def wload(self, src, nk, ncols, q='pool'):
        buf, key = self.wr.next()
        v = buf[:, 0:nk * ncols].rearrange("p (k c) -> p k c", k=nk)
        self.dma(q, v, src.rearrange("(k p) c -> p k c", p=128), w=[key])
        return v, key

    def declare(self):
        S = self.S
        i = self.din
        self.x = i("x", [T, D])
        self.mem = i("mem", [256, D])
        self.w_in = i("w_in", [DEPTH, D, N_IN])
        self.w_mem = i("w_mem_kv", [DEPTH, D, 1024])
        self.w_br = i("w_branch", [DEPTH, 3, 512, D])
        self.w_out = i("w_out", [DEPTH, D, D])
        self.w_gu = i("w_gate_up", [DEPTH, D, 2 * DFF])
        self.w_dn = i("w_down", [DEPTH, DFF, D])
        self.cw1 = i("cmp_w1", [DEPTH, 2, 2048, 128])
        self.cw2 = i("cmp_w2", [DEPTH, 2, 128, 64])
        self.cpe = i("cmp_pe", [DEPTH, 2, 32, 64])
        self.convw = i("conv_w", [DEPTH, 3, 512])
        self.g_mix = i("norm_mix", [DEPTH, D])
        self.g_mem = i("norm_mem", [DEPTH, D])
        self.g_ffn = i("norm_ffn", [DEPTH, D])
        self.g12 = i("g12", [DEPTH, 768])
        self.k0g = i("k0g", [DEPTH, 64])
        self.mqg = i("mem_q_norm", [DEPTH, 128])
        self.mkg = i("mkg", [DEPTH, 512])
        self.c_cos = i("c_cos", [128, 32, 8])
        self.c_sin = i("c_sin", [128, 32, 8])
        self.c_selA = i("c_selA", [128, 32, 64])
        self.c_selB = i("c_selB", [128, 32, 64])
        self.c_tri = i("c_tri", [128, 128])
        self.c_strict = i("c_strict", [128, 128])
        self.c_stair = i("c_stair", [128, 512])
        self.c_ovl = i("c_ovl", [128, 2, 64])
        self.c_E = i("c_E", [64, T])
        self.c_ident = i("c_ident", [128, 128])
        o = self.dout
        self.y = o("y", [T, D])
        self.rows_o = o("rows_p", [DEPTH, T, 512])
        self.win_o = o("win_p", [DEPTH, 512, 256])
        self.conv_o = o("conv_p", [DEPTH, 2, 512])
        self.mem_o = o("mem_p", [DEPTH, 256, 1024])
        self.hres = self.nc.dram_tensor("hres", [8, 128, T], F32).ap()

        sb = S.sb
        self.xT = sb([128, 8, TT], F32, 'xT')
        self.xnT = sb([128, 8, TT], BF16, 'xnT')
        self.QB = sb([128, 8, TT], BF16, 'QB')
        self.QU = sb([128, 4, TT], BF16, 'QU')
        self.mqT = sb([128, 4, TT], BF16, 'mqT')
        self.brT = sb([128, 12, TT], BF16, 'brT')
        self.mT = self.QB
        self.actT = sb([128, 6, TT], BF16, 'actT')
        self.tmior = Rot([sb([128, D], F32, f'tmio{k}') for k in range(2)], 'tmio')
        self.KE = [sb([128, T], BF16, f'KE{g}') for g in range(2)]
        self.Vs = sb([128, 32, 2, 66], BF16, 'Vs')
        self.kwT = [sb([64, 1024], BF16, f'kwT{g}') for g in range(2)]
        self.Vw = sb([128, 8, 2, 66], BF16, 'Vw')
        self.kcT2 = [sb([128, 256], BF16, f'kcT{g}') for g in range(2)]
        self.Rc = [sb([128, 2, 130], BF16, f'Rc{g}') for g in range(2)]
        self.rawk = sb([128, 528], BF16, 'rawk')
        self.rawv = sb([128, 528], BF16, 'rawv')
        self.mkT = sb([128, 4, 256], BF16, 'mkT')
        self.mv = sb([128, 2, 512], BF16, 'mv')
        self.ucar = sb([128, 4, 2], F32, 'ucar')
        self.uextr = Rot([sb([128, 514], F32, f'uext{k}') for k in range(2)], 'uext')
        self.onsa = sb([128, 4, 512], F32, 'onsa')
        self.macc = self.onsa
        self.scr = sb([128, 4, 2, 64], F32, 'scr')
        self.ngt = sb([128, 4, 24], F32, 'ngt')
        self.biasr = Rot([sb([128, 128], BF16, f'biasw{k}') for k in range(4)], 'biasw')
        self.selTr = Rot([sb([128, 2, 4, 64], F32, f'selT{k}') for k in range(2)], 'selT')
        self.ident_f = sb([128, 128], F32, 'ident_f')
        self.ident_b = sb([128, 128], BF16, 'ident_b')
        self.ones_b = sb([128, 128], BF16, 'ones_b')
        self.tri = sb([128, 128], BF16, 'tri')
        self.strict = sb([128, 128], BF16, 'strict')
        self.stair = sb([128, 512], BF16, 'stair')
        self.cosT = sb([128, 32, 8], F32, 'cosT')
        self.sinT = sb([128, 32, 8], F32, 'sinT')
        self.epsc = sb([128, 1], F32, 'epsc')
        self.gcols = sb([128, 3, 8], F32, 'gcols')
        self.g12b = sb([128, 12, 64], F32, 'g12b')
        self.k0gb = sb([128, 64], F32, 'k0gb')
        self.mqgc = sb([128, 1], F32, 'mqgc')
        self.cwc = sb([128, 4, 3], F32, 'cwc')
        self.w2sb = sb([128, 2, 64], BF16, 'w2sb')
        self.peT = sb([64, 2, 32], BF16, 'peT')
        self.bpe = sb([128, 2], F32, 'bpe')
        self.psr = Rot([S.ps([128, 512], F32, f'psb{k}') for k in range(6)], 'ps')
        self.accr = Rot([S.ps([128, 512], F32, f'acc{k}') for k in range(2)], 'acc')
        self.f32r = Rot([sb([128, 512], F32, f'f32t{k}') for k in range(4)], 'f32t')
        self.b16r = Rot([sb([128, 512], BF16, f'b16t{k}') for k in range(5)], 'b16t')
        self.wr = Rot([sb([128, 4096], BF16, f'wbuf{k}') for k in range(4)], 'wbuf')
        self.tmr = Rot([sb([128, 12, 64], F32, f'tm{k}') for k in range(4)], 'tm')
        self.smr = Rot([sb([128, 32], F32, f'sm{k}') for k in range(8)], 'sm')
        self.selr = Rot([sb([128, 2, 64], F32, f'sel{k}') for k in range(2)], 'sel')
        self.tsrc = Rot([sb([128, 12, 128], BF16, f'tsrc{k}') for k in range(1)], 'tsrc')
        self.rowsr = Rot([sb([128, 768], F32, f'rows{k}') for k in range(1)], 'rows')

    def setup(self):
        d = self.dma
        d('sp', self.ident_f[:], self.inp('c_ident'), w=['ident_f'])
        d('pool', self.ident_b[:], self.inp('c_ident'), w=['ident_b'])
        d('pool', self.tri[:], self.inp('c_tri'), w=['tri'])
        d('pool', self.strict[:], self.inp('c_strict'), w=['strict'])
        d('pool', self.stair[:], self.inp('c_stair'), w=['stair'])
        d('sp', self.cosT[:], self.inp('c_cos'), w=['cosT'])
        d('sp', self.sinT[:], self.inp('c_sin'), w=['sinT'])
        self.memset('pool', self.ones_b[:], 1.0, w=['ones_b'])
        self.memset('pool', self.epsc[:], EPS, w=['epsc'])
        for g in range(2):
            d('pool', self.KE[g][64:128, :], self.inp('c_E'), w=[f'KE{g}'])
            self.memset('pool', self.Rc[g][:, :, 64:65], 1.0, w=[f'Rc{g}'])
            self.memset('pool', self.Rc[g][0:1, 0, 64:65], 0.0, w=[f'Rc{g}'])
            d('pool', self.Rc[g][:, :, 65:129], self.inp('c_ovl'), w=[f'Rc{g}'])
        for k in range(4):
            self.memset('pool', self.biasr.bufs[k][:], 0.0, w=[f'biasw{k}'])
        self.memset('pool', self.Vs[:, :, :, 64:65], 1.0, w=['Vs'])
        self.memset('pool', self.Vw[:, :, :, 64:65], 1.0, w=['Vw'])

    def layer_setup(self, l):
        d = self.dma
        d('sp', self.gcols[:, 0, :], self.inp('norm_mix')[l].rearrange("(k p) -> p k", p=128), w=['gcols'], slow=True)
        d('sp', self.gcols[:, 1, :], self.inp('norm_ffn')[l].rearrange("(k p) -> p k", p=128), w=['gcols'], slow=True)
        d('sp', self.g12b[:].rearrange("p h d -> p (h d)"), self.inp('g12')[l].partition_broadcast(128), w=['g12b'])
        d('sp', self.k0gb[:], self.inp('k0g')[l].partition_broadcast(128), w=['k0gb'])
        d('sp', self.mqgc[:], self.inp('mem_q_norm')[l].rearrange("(p o) -> p o", o=1), w=['mqgc'], slow=True)
        for k in range(3):
            d('sp', self.cwc[:, :, k], self.inp('conv_w')[l, k].rearrange("(c p) -> p c", p=128), w=['cwc'], slow=True)
        d('pool', self.w2sb[:], self.inp('cmp_w2')[l].rearrange("k h e -> h k e"), w=['w2sb'])
        d('pool', self.peT[:], self.inp('cmp_pe')[l].rearrange("k s d -> d k s"), w=['peT'], slow=True)
        self.memset('pool', self.ucar[:], 0.0, w=['ucar'])
        self.memset('pool', self.rawk[:, 0:16], 0.0, w=['rawk'])
        self.memset('pool', self.rawv[:, 0:16], 0.0, w=['rawv'])
        w1 = self.load_w1(l)
        pb, pk = self.psn()
        for kind in range(2):
            v, key = w1[kind]
            for s in range(32):
                self.mm(pb[:, kind:kind + 1], lhsT=v[0:64, s, :], rhs=self.peT[0:64, kind, s:s + 1],
                        start=(s == 0), stop=(s == 31), r=[key, 'peT'], w=[pk])
        self.cp('act', self.bpe[:], pb[:, 0:2], r=[pk], w=['bpe'])
        if self.stage >= 2:
            self.mem_kv(l)

    def load_w1(self, l):
        res = []
        for kind in range(2):
            buf, key = self.wr.next()
            v = buf[:, :].rearrange("p (s h) -> p s h", s=32)
            src = self.inp('cmp_w1')[l, kind].rearrange("(s d) h -> d s h", d=64)
            self.dma('pool', v[0:64], src, w=[key])
            self.dma('pool', v[64:128], src, w=[key])
            res.append((v, key))
        return res

    def mem_kv(self, l):
        scr_ = self.onsa[:].rearrange("p a b -> p (a b)")
        gmem_b = scr_[:, 0:1024]
        mkg_b = scr_[:, 1024:1536]
        self.dma('sp', gmem_b, self.inp('norm_mem')[l].partition_broadcast(128), w=['onsa'])
        self.dma('sp', mkg_b, self.inp('mkg')[l].partition_broadcast(128), w=['onsa'])
        wk, wkk = self.wload(self.inp('w_mem_kv')[l][:, 0:512], 8, 512)
        wv, wvk = self.wload(self.inp('w_mem_kv')[l][:, 512:1024], 8, 512)
        for mb in range(2):
            mt_, mtk = self.tmior.next()
            mt = mt_[:]
            self.dma('sp', mt, self.inp('mem')[mb * 128:(mb + 1) * 128, :], w=[mtk])
            sm, smk = self.smr.next()
            jk_, jkk = self.tmior.next()
            junk = jk_[:]
            self.actf(junk, mt, AF.Square, accum=sm[:, 0:1], r=[mtk], w=[jkk, smk])
            self.actf(sm[:, 1:2], sm[:, 0:1], AF.Sqrt, bias=self.epsc[:, 0:1], scale=1.0 / D, r=[smk, 'epsc'], w=[smk])
            self.recip(sm[:, 2:3], sm[:, 1:2], r=[smk], w=[smk])
            self.stt(junk, mt, sm[:, 2:3], gmem_b, ALU.mult, ALU.mult, r=[mtk, smk, 'onsa'], w=[jkk])
            if self.stage < 2.1:
                continue
            mnb, mnk = self.b16t()
            mnb2, mnk2 = self.b16t()
            self.cp('dve', mnb[:], junk[:, 0:512], r=[jkk], w=[mnk])
            self.cp('dve', mnb2[:], junk[:, 512:1024], r=[jkk], w=[mnk2])
            if self.stage < 2.12:
                continue
            pt, ptk = self.psn()
            ptb = pt[:].bitcast(BF16)
            for kc in range(8):
                src = (mnb if kc < 4 else mnb2)[:, (kc % 4) * 128:(kc % 4 + 1) * 128]
                self.tr(ptb[:, kc * 128:(kc + 1) * 128], src, self.ident_b[:], r=[mnk, mnk2, 'ident_b'], w=[ptk])
            if self.stage < 2.14:
                continue
            mnT, mnTk = self.b16t()
            mnT2, mnT2k = self.b16t()
            if self.stage != 2.15:
                self.cp('act', mnT[:], ptb[:, 0:512], r=[ptk], w=[mnTk])
            if self.stage != 2.16:
                self.cp('dve', mnT2[:], ptb[:, 512:1024], r=[ptk], w=[mnT2k])
            if self.stage < 2.2:
                continue
            pk_, pkk = self.psn()
            pv_, pvk = self.psn()
            for kc in range(8):
                lt = (mnT if kc < 4 else mnT2)[:, (kc % 4) * 128:(kc % 4 + 1) * 128]
                self.mm(pk_[:], lhsT=lt, rhs=wk[:, kc, :], start=(kc == 0), stop=(kc == 7), r=[mnTk, mnT2k, wkk], w=[pkk])
            for kc in range(8):
                lt = (mnT if kc < 4 else mnT2)[:, (kc % 4) * 128:(kc % 4 + 1) * 128]
                self.mm(pv_[:], lhsT=lt, rhs=wv[:, kc, :], start=(kc == 0), stop=(kc == 7), r=[mnTk, mnT2k, wvk], w=[pvk])
            if self.stage < 2.3:
                continue
            mo = junk
            kf, kfk = self.f32t()
            sq, sqk = self.f32t()
            self.cp('act', kf[:], pk_[:], r=[pkk], w=[kfk])
            self.cp('act', mo[:, 512:1024], pv_[:], r=[pvk], w=[jkk])
            self.cp('dve', self.mv[:, mb, :], pv_[:], r=[pvk], w=['mv'])
            self.tt('pool', sq[:], kf[:], kf[:], ALU.mult, r=[kfk], w=[sqk])
            if self.stage < 2.4:
                continue
            sm2, sm2k = self.smr.next()
            self.red(sm2[:, 0:4], sq[:].rearrange("p (h d) -> p h d", h=4), r=[sqk], w=[sm2k])
            self.actf(sm2[:, 4:8], sm2[:, 0:4], AF.Sqrt, bias=self.epsc[:, 0:1], scale=1.0 / 128, r=[sm2k, 'epsc'], w=[sm2k])
            self.recip(sm2[:, 8:12], sm2[:, 4:8], r=[sm2k], w=[sm2k])
            self.tt('dve', kf[:].rearrange("p (h d) -> p h d", h=4), kf[:].rearrange("p (h d) -> p h d", h=4),
                    sm2[:, 8:12].unsqueeze(2).to_broadcast([128, 4, 128]), ALU.mult, r=[kfk, sm2k], w=[kfk])
            self.tt('pool', mo[:, 0:512], kf[:], mkg_b, ALU.mult, r=[kfk, 'onsa'], w=[jkk])
            self.dma('sp', self.mem_o[l, mb * 128:(mb + 1) * 128, :], mo, r=[jkk], w=['mem_o'])
            if self.stage < 2.5:
                continue
            knb, knk = self.b16t()
            self.cp('pool', knb[:], mo[:, 0:512], r=[jkk], w=[knk])
            pt2, pt2k = self.psn()
            pt2b = pt2[:].bitcast(BF16)
            for hm in range(4):
                self.tr(pt2b[:, hm * 128:(hm + 1) * 128], knb[:, hm * 128:(hm + 1) * 128], self.ident_b[:], r=[knk, 'ident_b'], w=[pt2k])
            self.cp('act', self.mkT[:, :, mb * 128:(mb + 1) * 128], pt2b[:, 0:512].rearrange("p (h m) -> p h m", h=4), r=[pt2k], w=['mkT'])

    def load_x(self, l, t):
        if l == 0:
            for blk in range(4):
                xi, xik = self.tmior.next()
                self.dma('sp', xi[:], self.inp('x')[(t * 4 + blk) * 128:(t * 4 + blk + 1) * 128, :], w=[xik])
                for hf in range(2):
                    pb, pk = self.psn()
                    for c in range(4):
                        kc = hf * 4 + c
                        self.tr(pb[:, c * 128:(c + 1) * 128], xi[:, kc * 128:(kc + 1) * 128], self.ident_f[:], r=[xik, 'ident_f'], w=[pk])
                    self.cp('act' if hf == 0 else 'dve', self.xT[:, hf * 4:hf * 4 + 4, blk * 128:(blk + 1) * 128],
                            pb[:].rearrange("p (c f) -> p c f", c=4), r=[pk], w=[('xT', hf * 4 + c) for c in range(4)])
        else:
            for kc in range(8):
                self.dma('sp', self.xT[:, kc, :], self.hres[kc, :, t * TT:(t + 1) * TT], r=['hres'], w=[('xT', kc)])

    def rmsnorm(self, gi):
        ps, pk = self.psn()
        for kc in range(8):
            sq, sqk = self.b16t()
            self.actf(sq[:], self.xT[:, kc, :], AF.Square, r=[('xT', kc)], w=[sqk])
            self.mm(ps[:], lhsT=self.ones_b[:], rhs=sq[:], start=(kc == 0), stop=(kc == 7), r=['ones_b', sqk], w=[pk])
        rt, rtk = self.f32t()
        self.actf(rt[:], ps[:], AF.Sqrt, bias=self.epsc[:, 0:1], scale=1.0 / D, r=[pk, 'epsc'], w=[rtk])
        rs, rsk = self.f32t()
        self.recip(rs[:], rt[:], r=[rtk], w=[rsk])
        for kc in range(8):
            self.stt(self.xnT[:, kc, :], self.xT[:, kc, :], self.gcols[:, gi, kc:kc + 1], rs[:], ALU.mult, ALU.mult,
                     r=[('xT', kc), 'gcols', rsk], w=[('xnT', kc)])

    def headnorm_rope(self, hd, hk, nh, gain_b, cos_b, sin_b, inv_d):
        P = hd.shape[0]
        sq, sqk = self.tmr.next()
        sq = sq[0:P, 0:nh, :]
        sm, smk = self.smr.next()
        self.tt('pool', sq, hd, hd, ALU.mult, r=[hk], w=[sqk])
        self.red(sm[0:P, 0:nh], sq, r=[sqk], w=[smk])
        self.actf(sm[0:P, 12:12 + nh], sm[0:P, 0:nh], AF.Sqrt, bias=self.epsc[0:P, 0:1], scale=inv_d, r=[smk, 'epsc'], w=[smk])
        self.recip(sm[0:P, 0:nh], sm[0:P, 12:12 + nh], r=[smk], w=[smk])
        self.tt('dve', hd, hd, sm[0:P, 0:nh].unsqueeze(2).to_broadcast([P, nh, 64]), ALU.mult, r=[hk, smk], w=[hk])
        self.tt('pool', hd, hd, gain_b, ALU.mult, r=[hk, 'g12b'], w=[hk])
        hr, hrk = self.tmr.next()
        hr = hr[0:P, 0:nh, :]
        self.cp('pool', hr, hd, r=[hk], w=[hrk])
        tp, tpk = self.tmr.next()
        t1 = tp[0:P, 0:nh, 0:8]
        t2 = tp[0:P, 0:nh, 8:16]
        t3 = tp[0:P, 0:nh, 16:24]
        t4 = tp[0:P, 0:nh, 24:32]
        x1 = hd[:, :, 0:8]
        x2 = hd[:, :, 8:16]
        self.tt('dve', t1, x1, cos_b, ALU.mult, r=[hk, 'cosT'], w=[tpk])
        self.tt('dve', t2, x2, sin_b, ALU.mult, r=[hk, 'sinT'], w=[tpk])
        self.tt('dve', t3, x2, cos_b, ALU.mult, r=[hk, 'cosT'], w=[tpk])
        self.tt('dve', t4, x1, sin_b, ALU.mult, r=[hk, 'sinT'], w=[tpk])
        self.tt('dve', hr[:, :, 0:8], t1, t2, ALU.subtract, r=[tpk], w=[hrk])
        self.tt('dve', hr[:, :, 8:16], t3, t4, ALU.add, r=[tpk], w=[hrk])
        return hr, hrk

    def tokmajor(self, l, t, wA):
        (w0, w0k), (w1, w1k), (w2, w2k) = wA
        for blk in range(4):
            kb = t * 4 + blk
            cs = slice(blk * 128, (blk + 1) * 128)
            pq, pqk = self.psn()
            pa, pak = self.psn()
            pb, pbk = self.psn()
            for kc in range(8):
                lt = self.xnT[:, kc, cs]
                self.mm(pq[:], lhsT=lt, rhs=w0[:, kc, :], start=(kc == 0), stop=(kc == 7), r=[('xnT', kc), w0k], w=[pqk])
            for kc in range(8):
                lt = self.xnT[:, kc, cs]
                self.mm(pa[:], lhsT=lt, rhs=w1[:, kc, :], start=(kc == 0), stop=(kc == 7), r=[('xnT', kc), w1k], w=[pak])
            for kc in range(8):
                lt = self.xnT[:, kc, cs]
                self.mm(pb[:, 0:280], lhsT=lt, rhs=w2[:, kc, :], start=(kc == 0), stop=(kc == 7), r=[('xnT', kc), w2k], w=[pbk])
            hd, hk = self.tmr.next()
            self.cp('act', hd[:, 0:8, :].rearrange("p h d -> p (h d)"), pq[:], r=[pqk], w=[hk])
            self.cp('dve', hd[:, 8:10, :].rearrange("p h d -> p (h d)"), pa[:, 256:384], r=[pak], w=[hk])
            self.cp('dve', hd[:, 10:12, :].rearrange("p h d -> p (h d)"), pb[:, 0:128], r=[pbk], w=[hk])
            rows, rowsk = self.rowsr.next()
            self.cp('act', rows[:, 0:512], pa[:], r=[pak], w=[rowsk])
            self.cp('act', rows[:, 640:768], pb[:, 128:256], r=[pbk], w=[rowsk])
            self.actf(self.ngt[:, blk, :], pb[:, 256:280], AF.Sigmoid, r=[pbk], w=[('ngt', blk)])
            self.cp('dve', self.Vs[:, kb, :, 0:64], pa[:, 384:512].rearrange("p (g d) -> p g d", g=2), r=[pak], w=['Vs'])
            slot = (kb // 4) % 2 * 4 + kb % 4
            self.cp('dve', self.Vw[:, slot, :, 0:64], pb[:, 128:256].rearrange("p (g d) -> p g d", g=2), r=[pbk], w=['Vw'])
            cos_b = self.cosT[:, kb, :].unsqueeze(1).to_broadcast([128, 12, 8])
            sin_b = self.sinT[:, kb, :].unsqueeze(1).to_broadcast([128, 12, 8])
            hr, hrk = self.headnorm_rope(hd[:], hk, 12, self.g12b[:], cos_b, sin_b, 1.0 / 64)
            self.cp('pool', rows[:, 256:384], hr[:, 8:10, :].rearrange("p h d -> p (h d)"), r=[hrk], w=[rowsk])
            self.cp('pool', rows[:, 512:640], hr[:, 10:12, :].rearrange("p h d -> p (h d)"), r=[hrk], w=[rowsk])
            self.dma('sp', self.rows_o[l, kb * 128:(kb + 1) * 128, :], rows[:, 0:512], r=[rowsk], w=['rows_o'])
            if t == NTILE - 1:
                self.dma('sp', self.win_o[l, blk * 128:(blk + 1) * 128, :], rows[:, 512:768], r=[rowsk], w=['win_o'])
            ts_, tsk = self.tsrc.next()
            self.cp('pool', ts_[:, 0:4, :].rearrange("p c f -> p (c f)"), hd[:, 0:8, :].rearrange("p h d -> p (h d)"), r=[hk], w=[tsk])
            self.cp('pool', ts_[:, 4:8, :].rearrange("p c f -> p (c f)"), hr[:, 0:8, :].rearrange("p h d -> p (h d)"), r=[hrk], w=[tsk])
            self.cp('pool', ts_[:, 8:10, :].rearrange("p c f -> p (c f)"), hr[:, 8:12, :].rearrange("p h d -> p (h d)"), r=[hrk], w=[tsk])
            self.cp('act', ts_[:, 10:12, :].rearrange("p c f -> p (c f)"), pa[:, 0:256], r=[pak], w=[tsk])
            p0, p0k = self.psn()
            p1, p1k = self.psn()
            p0b = p0[:].bitcast(BF16)
            p1b = p1[:].bitcast(BF16)
            for c in range(8):
                self.tr(p0b[:, c * 128:(c + 1) * 128], ts_[:, c, :], self.ident_b[:], r=[tsk, 'ident_b'], w=[p0k])
            for c in range(4):
                self.tr(p1b[:, c * 128:(c + 1) * 128], ts_[:, 8 + c, :], self.ident_b[:], r=[tsk, 'ident_b'], w=[p1k])
            p0v = p0b.rearrange("p (c f) -> p c f", c=8)
            self.cp('act', self.QU[:, :, cs], p0v[:, 0:4, :], r=[p0k], w=['QU'])
            self.cp('dve', self.QB[0:64, 0::2, cs], p0v[0:64, 4:8, :], r=[p0k], w=['QBq'])
            self.cp('act', self.QB[0:64, 1::2, cs], p0v[64:128, 4:8, :], r=[p0k], w=['QBq'])
            ks = slice(kb * 128, (kb + 1) * 128)
            self.cp('dve', self.KE[0][0:64, ks], p1b[0:64, 0:128], r=[p1k], w=['KE0'])
            self.cp('act', self.KE[1][0:64, ks], p1b[64:128, 0:128], r=[p1k], w=['KE1'])
            wcs = slice((kb // 4) % 2 * 512 + (kb % 4) * 128, (kb // 4) % 2 * 512 + (kb % 4 + 1) * 128)
            self.cp('dve', self.kwT[0][0:64, wcs], p1b[0:64, 128:256], r=[p1k], w=['kwT0'])
            self.cp('act', self.kwT[1][0:64, wcs], p1b[64:128, 128:256], r=[p1k], w=['kwT1'])
            rs = slice(16 + blk * 128, 16 + (blk + 1) * 128)
            self.cp('dve', self.rawk[:, rs], p1b[:, 256:384], r=[p1k], w=['rawk'])
            self.cp('act', self.rawv[:, rs], p1b[:, 384:512], r=[p1k], w=['rawv'])

    def compress(self, l, t):
        w1 = self.load_w1(l)
        c0 = 32 * (t % 4)
        cc = t // 4
        for kind in range(2):
            v, key = w1[kind]
            raw, rawkey = (self.rawk, 'rawk') if kind == 0 else (self.rawv, 'rawv')
            for g in range(2):
                r0 = 64 * g
                ph, phk = self.psn()
                for s in range(32):
                    self.mm(ph[:, 0:32], lhsT=v[r0:r0 + 64, s, :], rhs=raw[r0:r0 + 64, s:s + 497:16],
                            start=(s == 0), stop=(s == 31), r=[key, rawkey], w=[phk])
                hs, hsk = self.b16t()
                self.actf(hs[:, 0:32], ph[:, 0:32], AF.Silu, bias=self.bpe[:, kind:kind + 1], r=[phk, 'bpe'], w=[hsk])
                pc, pck = self.psn()
                self.mm(pc[0:32, 0:64], lhsT=hs[:, 0:32], rhs=self.w2sb[:, kind, :], r=[hsk, 'w2sb'], w=[pck])
                if kind == 0:
                    kf, kfk = self.f32t()
                    sm, smk = self.smr.next()
                    self.actf(kf[0:32, 64:128], pc[0:32, 0:64], AF.Square, accum=sm[0:32, 0:1], r=[pck], w=[kfk, smk])
                    self.actf(sm[0:32, 1:2], sm[0:32, 0:1], AF.Sqrt, bias=self.epsc[0:32, 0:1], scale=1.0 / 64, r=[smk, 'epsc'], w=[smk])
                    self.recip(sm[0:32, 2:3], sm[0:32, 1:2], r=[smk], w=[smk])
                    kn, knk = self.b16t()
                    self.stt(kn[0:32, 0:64], pc[0:32, 0:64], sm[0:32, 2:3], self.k0gb[0:32, :], ALU.mult, ALU.mult,
                             r=[pck, smk, 'k0gb'], w=[knk])
                    self.cp('dve', kn[0:32, 64:128], kn[0:32, 0:64], r=[knk], w=[knk])
                    pt, ptk = self.psn()
                    ptb = pt[:].bitcast(BF16)
                    self.tr(ptb[:, 0:32], kn[0:32, 0:128], self.ident_b[0:32, 0:32], r=[knk, 'ident_b'], w=[ptk])
                    self.cp('act', self.kcT2[g][:, 32 * t:32 * t + 32], ptb[:, 0:32], r=[ptk], w=[f'kcT{g}'])
                else:
                    self.cp('act', self.Rc[g][c0:c0 + 32, cc, 0:64], pc[0:32, 0:64], r=[pck], w=[f'Rc{g}'])
                    if t == 0:
                        self.memset('pool', self.Rc[g][0:1, 0, 0:64], 0.0, w=[f'Rc{g}'])
        self.cp('pool', self.rawk[:, 0:16], self.rawk[:, 512:528], r=['rawk'], w=['rawk'])
        self.cp('pool', self.rawv[:, 0:16], self.rawv[:, 512:528], r=['rawv'], w=['rawv'])

    def fm_proj(self, l, t):
        w_in = self.inp('w_in')[l]
        wcx, wcxk = self.wload(w_in[:, C_CX:C_CX + 512], 8, 512)
        wcb, wcbk = self.wload(w_in[:, C_CB:C_CB + 512], 8, 512)
        wcc, wcck = self.wload(w_in[:, C_CC:C_CC + 512], 8, 512)
        xk = [('xnT', kc) for kc in range(8)]
        for ci in range(4):
            cs = slice(ci * 128, (ci + 1) * 128)
            px, pxk = self.psn()
            pc, pck = self.psn()
            pb, pbk = self.psn()
            for (pp, ppk, ww, wwk) in ((px, pxk, wcx, wcxk), (pc, pck, wcc, wcck), (pb, pbk, wcb, wcbk)):
                for kc in range(8):
                    self.mm(pp[:], lhsT=ww[:, kc, cs], rhs=self.xnT[:, kc, :], start=(kc == 0), stop=(kc == 7), r=[wwk, ('xnT', kc)], w=[ppk])
            cxs, cxk = self.f32t()
            self.cp('act', cxs[:], px[:], r=[pxk], w=[cxk])
            ue, uek = self.uextr.next()
            self.cp('pool', ue[:, 0:2], self.ucar[:, ci, :], r=['ucar'], w=[uek])
            self.tt('dve', ue[:, 2:514], pc[:], cxs[:], ALU.mult, r=[pck, cxk], w=[uek])
            a1, a1k = self.f32t()
            self.ts('pool', a1[:], ue[:, 0:512], self.cwc[:, ci, 0:1], ALU.mult, r=[uek, 'cwc'], w=[a1k])
            self.stt(a1[:], ue[:, 1:513], self.cwc[:, ci, 1:2], a1[:], ALU.mult, ALU.add, r=[uek, 'cwc', a1k], w=[a1k])
            self.stt(a1[:], ue[:, 2:514], self.cwc[:, ci, 2:3], a1[:], ALU.mult, ALU.add, r=[uek, 'cwc', a1k], w=[a1k])
            self.tt('dve', self.brT[:, 4 + ci, :], pb[:], a1[:], ALU.mult, r=[pbk, a1k], w=['brT1'])
            self.cp('pool', self.ucar[:, ci, :], ue[:, 512:514], r=[uek], w=['ucar'])
            if t == NTILE - 1:
                self.dma('sp', self.conv_o[l, :, cs].rearrange("j p -> p j"), ue[:, 512:514], r=[uek], w=['conv_o'], slow=True)

    def cmp_attend(self, t):
        nch = t // 4 + 1
        ngk = [('ngt', b) for b in range(4)]
        for h in range(8):
            g = h // 4
            r0 = 64 * (h % 2)
            pTs = []
            for cc in range(nch):
                sz = 128 if cc < nch - 1 else 32 * (t % 4 + 1)
                ps_, psk = self.psn()
                self.mm(ps_[0:sz, :], lhsT=self.kcT2[g][r0:r0 + 64, cc * 128:cc * 128 + sz], rhs=self.QU[r0:r0 + 64, h // 2, :],
                        r=[f'kcT{g}', 'QU'], w=[psk])
                pT, pTk = self.b16t()
                self.actf(pT[0:sz, :], ps_[0:sz, :], AF.Exp, scale=0.125, r=[psk], w=[pTk])
                if cc == nch - 1:
                    self.tt('pool', pT[sz - 32:sz, :], pT[sz - 32:sz, :], self.stair[sz - 32:sz, :], ALU.mult, r=[pTk, 'stair'], w=[pTk])
                pTs.append((pT, pTk, sz))
            banks = [self.psn(), self.psn()]
            for qb in range(4):
                bk, bkk = banks[qb // 2]
                off = (qb % 2) * 129
                for cc, (pT, pTk, sz) in enumerate(pTs):
                    self.mm(bk[:, off:off + 129], lhsT=pT[0:sz, qb * 128:(qb + 1) * 128], rhs=self.Rc[g][0:sz, cc, 0:129],
                            start=(cc == 0), stop=(cc == nch - 1), r=[pTk, f'Rc{g}'], w=[bkk])
            for qb in range(4):
                bk, bkk = banks[qb // 2]
                off = (qb % 2) * 129
                sm, smk = self.smr.next()
                self.ts('dve', sm[:, 0:1], bk[:, off + 64:off + 65], 1e-30, ALU.add, r=[bkk], w=[smk])
                self.recip(sm[:, 1:2], sm[:, 0:1], r=[smk], w=[smk])
                self.tt('dve', sm[:, 2:3], sm[:, 1:2], self.ngt[:, qb, h:h + 1], ALU.mult, r=[smk] + ngk, w=[smk])
                self.ts('dve', self.onsa[:, qb, h * 64:(h + 1) * 64], bk[:, off:off + 64], sm[:, 2:3], ALU.mult, r=[bkk, smk], w=['onsa'])
                if h % 4 == 0:
                    self.ts('dve', self.scr[:, qb, g, :], bk[:, off + 65:off + 129], sm[:, 1:2], ALU.mult, r=[bkk, smk], w=['scr'])
                else:
                    self.stt(self.scr[:, qb, g, :], bk[:, off + 65:off + 129], sm[:, 1:2], self.scr[:, qb, g, :], ALU.mult, ALU.add,
                             r=[bkk, smk, 'scr'], w=['scr'])

    def topk(self, t):
        sl, slk = self.selTr.next()
        self.dma('sp', sl[:, 0], self.inp('c_selA')[:, 4 * t:4 * t + 4, :], w=[slk])
        self.dma('sp', sl[:, 1], self.inp('c_selB')[:, 4 * t:4 * t + 4, :], w=[slk])
        for g in range(2):
            pt, ptk = self.psn()
            ptb = pt[:].bitcast(BF16)
            for qb in range(4):
                s2, s2k = self.selr.next()
                self.tt('dve', s2[:, 0, :], self.scr[:, qb, g, :], sl[:, 0, qb, :], ALU.mult, r=['scr', slk], w=[s2k])
                self.tt('dve', s2[:, 0, :], s2[:, 0, :], sl[:, 1, qb, :], ALU.add, r=[s2k, slk], w=[s2k])
                sm, smk = self.smr.next()
                self.S.add('dve', (lambda o, i: lambda e: e.max(out=o, in_=i))(sm[:, 0:8], s2[:, 0, :]), [s2k], [smk])
                self.S.add('dve', (lambda o, a, b: lambda e: e.match_replace(out=o, in_to_replace=a, in_values=b, imm_value=-1e30))(s2[:, 1, :], sm[:, 0:8], s2[:, 0, :]), [s2k, smk], [s2k])
                self.S.add('dve', (lambda o, i: lambda e: e.max(out=o, in_=i))(sm[:, 8:16], s2[:, 1, :]), [s2k], [smk])
                self.ts('dve', sm[:, 16:17], sm[:, 15:16], -1e8, ALU.max, r=[smk], w=[smk])
                self.ts('dve', s2[:, 1, :], s2[:, 0, :], sm[:, 16:17], ALU.is_ge, r=[s2k, smk], w=[s2k])
                bw, bwk = self.biasr.next()
                self.ts('dve', bw[:, 64:128], s2[:, 1, :], -1.0, ALU.add, -MASKV, ALU.mult, r=[s2k], w=[bwk])
                self.tr(ptb[:, qb * 128:(qb + 1) * 128], bw[:], self.ident_b[:], r=[bwk, 'ident_b'], w=[ptk])
            for hh in range(4):
                self.cp('act' if hh % 2 == 0 else 'dve', self.QB[64:128, g * 4 + hh, :], ptb[64:128, 0:512], r=[ptk], w=['QBb'])

    def norm_acc(self, acc, acck, h, ngoff):
        ngk = [('ngt', b) for b in range(4)]
        sm, smk = self.smr.next()
        self.recip(sm[:, 0:4], acc[:, 64:260:65], r=[acck], w=[smk])
        self.tt('dve', sm[:, 4:8], sm[:, 0:4], self.ngt[:, :, ngoff + h], ALU.mult, r=[smk] + ngk, w=[smk])
        for j in range(4):
            dst = self.onsa[:, j, h * 64:(h + 1) * 64]
            self.stt(dst, acc[:, j * 65:j * 65 + 64], sm[:, 4 + j:5 + j], dst, ALU.mult, ALU.add, r=[acck, smk, 'onsa'], w=['onsa'])

    def slc(self, t):
        for h in range(8):
            g = h // 4
            acc, acck = self.accr.next()

            def stage_a(kb):
                i0 = max(0, kb - 4 * t)
                c0 = 128 * i0
                ps_, psk = self.psn()
                self.mm(ps_[:, c0:512], lhsT=self.KE[g][:, kb * 128:(kb + 1) * 128], rhs=self.QB[:, h, c0:512],
                        r=[f'KE{g}', 'QBq', 'QBb'], w=[psk])
                pT, pTk = self.b16t()
                self.actf(pT[:, c0:512], ps_[:, c0:512], AF.Exp, scale=0.125, r=[psk], w=[pTk])
                if kb >= 4 * t:
                    self.tt('dve', pT[:, c0:c0 + 128], pT[:, c0:c0 + 128], self.tri[:], ALU.mult, r=[pTk, 'tri'], w=[pTk])
                return (kb, i0, pT, pTk)

            def stage_b(st, first):
                kb, i0, pT, pTk = st
                for j in range(i0, 4):
                    self.mm(acc[:, j * 65:(j + 1) * 65], lhsT=pT[:, j * 128:(j + 1) * 128], rhs=self.Vs[:, kb, g, 0:65],
                            start=(first and j == i0), stop=False, r=[pTk, 'Vs'], w=[acck])
            nkb = 4 * t + 4
            prev = stage_a(0)
            for kb in range(1, nkb):
                cur = stage_a(kb)
                stage_b(prev, prev[0] == 0)
                prev = cur
            stage_b(prev, prev[0] == 0)
            self.norm_acc(acc, acck, h, 8)

    def win(self, t):
        for h in range(8):
            g = h // 4
            acc, acck = self.accr.next()
            kbs = list(range(max(0, 4 * t - 4), 4 * t + 4))

            def stage_a(kb):
                jlo = max(0, kb - 4 * t)
                jhi = min(3, kb - 4 * t + 4)
                ring = (kb // 4) % 2
                wc0 = ring * 512 + (kb % 4) * 128
                slot = ring * 4 + kb % 4
                cl, ch = 128 * jlo, 128 * (jhi + 1)
                ps_, psk = self.psn()
                self.mm(ps_[:, cl:ch], lhsT=self.kwT[g][0:64, wc0:wc0 + 128], rhs=self.QB[0:64, h, cl:ch], r=[f'kwT{g}', 'QBq'], w=[psk])
                pT, pTk = self.b16t()
                self.actf(pT[:, cl:ch], ps_[:, cl:ch], AF.Exp, scale=0.125, r=[psk], w=[pTk])
                for j in range(jlo, jhi + 1):
                    d = 4 * t + j - kb
                    if d == 0:
                        self.tt('dve', pT[:, j * 128:(j + 1) * 128], pT[:, j * 128:(j + 1) * 128], self.tri[:], ALU.mult, r=[pTk, 'tri'], w=[pTk])
                    if d == 4:
                        self.tt('dve', pT[:, j * 128:(j + 1) * 128], pT[:, j * 128:(j + 1) * 128], self.strict[:], ALU.mult, r=[pTk, 'strict'], w=[pTk])
                return (kb, jlo, jhi, slot, pT, pTk)

            def stage_b(st, first):
                kb, jlo, jhi, slot, pT, pTk = st
                for j in range(jlo, jhi + 1):
                    self.mm(acc[:, j * 65:(j + 1) * 65], lhsT=pT[:, j * 128:(j + 1) * 128], rhs=self.Vw[:, slot, g, 0:65],
                            start=(first and j == jlo), stop=False, r=[pTk, 'Vw'], w=[acck])
            prev = stage_a(kbs[0])
            for kb in kbs[1:]:
                cur = stage_a(kb)
                stage_b(prev, prev[0] == kbs[0])
                prev = cur
            stage_b(prev, prev[0] == kbs[0])
            self.norm_acc(acc, acck, h, 16)

    def nsa_finalize(self):
        for j in range(4):
            ob, obk = self.b16t()
            self.cp('pool', ob[:], self.onsa[:, j, :], r=['onsa'], w=[obk])
            pt, ptk = self.psn()
            ptb = pt[:].bitcast(BF16)
            for c in range(4):
                self.tr(ptb[:, c * 128:(c + 1) * 128], ob[:, c * 128:(c + 1) * 128], self.ident_b[:], r=[obk, 'ident_b'], w=[ptk])
            self.cp('act', self.brT[:, 0:4, j * 128:(j + 1) * 128], ptb[:, 0:512].rearrange("p (c f) -> p c f", c=4), r=[ptk], w=['brT0'])

    def mem_attend(self):
        for hm in range(4):
            pTs = []
            for mb in range(2):
                ps_, psk = self.psn()
                self.mm(ps_[:], lhsT=self.mkT[:, hm, mb * 128:(mb + 1) * 128], rhs=self.mqT[:, hm, :], r=['mkT', 'mqT'], w=[psk])
                pT, pTk = self.b16t()
                self.actf(pT[:], ps_[:], AF.Exp, scale=128 ** -0.5, r=[psk], w=[pTk])
                pTs.append((pT, pTk))
            po, pok = self.psn()
            pd, pdk = self.psn()
            for mb, (pT, pTk) in enumerate(pTs):
                self.mm(po[:], lhsT=self.mv[:, mb, hm * 128:(hm + 1) * 128], rhs=pT[:], start=(mb == 0), stop=(mb == 1), r=['mv', pTk], w=[pok])
            for mb, (pT, pTk) in enumerate(pTs):
                self.mm(pd[:], lhsT=self.ones_b[:], rhs=pT[:], start=(mb == 0), stop=(mb == 1), r=['ones_b', pTk], w=[pdk])
            rc, rck = self.f32t()
            self.recip(rc[:], pd[:], r=[pdk], w=[rck])
            self.tt('dve', self.brT[:, 8 + hm, :], po[:], rc[:], ALU.mult, r=[pok, rck], w=['brT2'])

    def phase_b(self, l):
        w_in = self.inp('w_in')[l]
        QBK = ['QBq', 'QBb']
        for fcg in range(2):
            for n in range(3):
                wm, wmk = self.wload(w_in[:, C_MG + n * 1024 + fcg * 512:C_MG + n * 1024 + (fcg + 1) * 512], 8, 512)
                wb, wbk = self.wload(self.inp('w_branch')[l, n][:, fcg * 512:(fcg + 1) * 512], 4, 512)
                for fi in range(4):
                    fc = fcg * 4 + fi
                    cs = slice(fi * 128, (fi + 1) * 128)
                    pg, pgk = self.psn()
                    pp, ppk = self.psn()
                    for kc in range(8):
                        self.mm(pg[:], lhsT=wm[:, kc, cs], rhs=self.xnT[:, kc, :], start=(kc == 0), stop=(kc == 7), r=[wmk, ('xnT', kc)], w=[pgk])
                    for kc in range(4):
                        self.mm(pp[:], lhsT=wb[:, kc, cs], rhs=self.brT[:, n * 4 + kc, :], start=(kc == 0), stop=(kc == 3), r=[wbk, f'brT{n}'], w=[ppk])
                    sg, sgk = self.f32t()
                    self.actf(sg[:], pg[:], AF.Sigmoid, r=[pgk], w=[sgk])
                    if n == 0:
                        self.tt('dve', self.macc[:, fi, :], sg[:], pp[:], ALU.mult, r=[sgk, ppk], w=['onsa'])
                    else:
                        self.tt('dve', sg[:], sg[:], pp[:], ALU.mult, r=[sgk, ppk], w=[sgk])
                        if n == 1:
                            self.tt('pool', self.macc[:, fi, :], self.macc[:, fi, :], sg[:], ALU.add, r=[sgk, 'onsa'], w=['onsa'])
                        else:
                            self.tt('pool', self.mT[:, fc, :], self.macc[:, fi, :], sg[:], ALU.add, r=[sgk, 'onsa'], w=QBK)

    def phase_c(self, l):
        QBK = ['QBq', 'QBb']
        for half in range(2):
            wo, wok = self.wload(self.inp('w_out')[l][:, half * 512:(half + 1) * 512], 8, 512)
            for fi in range(4):
                fc = half * 4 + fi
                po, pok = self.psn()
                for kc in range(8):
                    self.mm(po[:], lhsT=wo[:, kc, fi * 128:(fi + 1) * 128], rhs=self.mT[:, kc, :], start=(kc == 0), stop=(kc == 7), r=[wok] + QBK, w=[pok])
                self.tt('dve', self.xT[:, fc, :], po[:], self.xT[:, fc, :], ALU.add, r=[pok, ('xT', fc)], w=[('xT', fc)])

    def phase_d(self, l):
        self.rmsnorm(1)
        w_gu = self.inp('w_gate_up')[l]
        w_dn = self.inp('w_down')[l]
        groups = [(0, 6), (6, 6), (12, 5), (17, 5)]
        for (j0, n) in groups:
            for p0 in range(0, n, 4):
                pn = min(4, n - p0)
                ja = j0 + p0
                wg, wgk = self.wload(w_gu[:, ja * 128:(ja + pn) * 128], 8, pn * 128)
                wu, wuk = self.wload(w_gu[:, DFF + ja * 128:DFF + (ja + pn) * 128], 8, pn * 128)
                for q in range(pn):
                    jj = p0 + q
                    cs = slice(q * 128, (q + 1) * 128)
                    pg, pgk = self.psn()
                    pu, puk = self.psn()
                    for kc in range(8):
                        self.mm(pg[:], lhsT=wg[:, kc, cs], rhs=self.xnT[:, kc, :], start=(kc == 0), stop=(kc == 7), r=[wgk, ('xnT', kc)], w=[pgk])
                    for kc in range(8):
                        self.mm(pu[:], lhsT=wu[:, kc, cs], rhs=self.xnT[:, kc, :], start=(kc == 0), stop=(kc == 7), r=[wuk, ('xnT', kc)], w=[puk])
                    sg, sgk = self.f32t()
                    self.actf(sg[:], pg[:], AF.Silu, r=[pgk], w=[sgk])
                    self.tt('dve', self.actT[:, jj, :], sg[:], pu[:], ALU.mult, r=[sgk, puk], w=[('actT', jj)])
            for cq in range(4):
                wd, wdk = self.wload(w_dn[j0 * 128:(j0 + n) * 128, cq * 256:(cq + 1) * 256], n, 256)
                for fi in range(2):
                    fc = cq * 2 + fi
                    pd, pdk = self.psn()
                    for kc in range(n):
                        self.mm(pd[:], lhsT=wd[:, kc, fi * 128:(fi + 1) * 128], rhs=self.actT[:, kc, :], start=(kc == 0), stop=(kc == n - 1),
                                r=[wdk, ('actT', kc)], w=[pdk])
                    self.tt('dve', self.xT[:, fc, :], pd[:], self.xT[:, fc, :], ALU.add, r=[pdk, ('xT', fc)], w=[('xT', fc)])

    def store_x(self, l, t):
        xk = [('xT', kc) for kc in range(8)]
        if l < self.nlayers - 1:
            self.dma('sp', self.hres[:, :, t * TT:(t + 1) * TT].rearrange("k p t -> p k t"), self.xT[:], r=xk, w=['hres'])
        else:
            for blk in range(4):
                xo, xok = self.tmior.next()
                for hf in range(2):
                    pb, pk = self.psn()
                    for c in range(4):
                        kc = hf * 4 + c
                        self.tr(pb[:, c * 128:(c + 1) * 128], self.xT[:, kc, blk * 128:(blk + 1) * 128], self.ident_f[:], r=[('xT', kc), 'ident_f'], w=[pk])
                    self.cp('act' if hf == 0 else 'dve', xo[:, hf * 512:(hf + 1) * 512], pb[:], r=[pk], w=[xok])
                self.dma('sp', self.y[(t * 4 + blk) * 128:(t * 4 + blk + 1) * 128, :], xo[:], r=[xok], w=['y'])

    def make_ctxs(self):
        class C:
            pass
        P = C()
        P.N, P.k = TT, ''
        P.xT, P.xnT, P.brT, P.mT, P.macc, P.actT, P.mqT = self.xT, self.xnT, self.brT, self.mT, self.macc, self.actT, self.mqT
        P.mTk, P.mack = ['QBq', 'QBb'], 'onsa'
        self.P = P
        X = C()
        X.N, X.k = 4, 's_'
        sb = self.S.sb
        X.xT = sb([128, 8, 4], F32, 's_xT')
        X.xnT = sb([128, 8, 4], BF16, 's_xnT')
        X.brT = sb([128, 12, 4], BF16, 's_brT')
        X.mT = sb([128, 8, 4], BF16, 's_mT')
        X.macc = sb([128, 4, 4], F32, 's_macc')
        X.actT = sb([128, 6, 4], BF16, 's_actT')
        X.mqT = sb([128, 4, 4], BF16, 's_mqT')
        X.mTk, X.mack = ['s_mT'], 's_macc'
        self.X = X

    def rmsnorm(self, gi, c=None):
        c = c or self.P
        N, k = c.N, c.k
        ps, pk = self.psn()
        for kc in range(8):
            sq, sqk = self.b16t()
            self.actf(sq[:, 0:N], c.xT[:, kc, :], AF.Square, r=[(k + 'xT', kc)], w=[sqk])
            self.mm(ps[:, 0:N], lhsT=self.ones_b[:], rhs=sq[:, 0:N], start=(kc == 0), stop=(kc == 7), r=['ones_b', sqk], w=[pk])
        rt, rtk = self.f32t()
        self.actf(rt[:, 0:N], ps[:, 0:N], AF.Sqrt, bias=self.epsc[:, 0:1], scale=1.0 / D, r=[pk, 'epsc'], w=[rtk])
        rs, rsk = self.f32t()
        self.recip(rs[:, 0:N], rt[:, 0:N], r=[rtk], w=[rsk])
        for kc in range(8):
            self.stt(c.xnT[:, kc, :], c.xT[:, kc, :], self.gcols[:, gi, kc:kc + 1], rs[:, 0:N], ALU.mult, ALU.mult,
                     r=[(k + 'xT', kc), 'gcols', rsk], w=[(k + 'xnT', kc)])

    def mq_proj(self, l, c=None):
        c = c or self.P
        N, k = c.N, c.k
        wmq, wmqk = self.wload(self.inp('w_in')[l][:, C_MQ:C_MQ + 512], 8, 512)
        for hm in range(4):
            cs = slice(hm * 128, (hm + 1) * 128)
            pm, pmk = self.psn()
            for kc in range(8):
                self.mm(pm[:, 0:N], lhsT=wmq[:, kc, cs], rhs=c.xnT[:, kc, :], start=(kc == 0), stop=(kc == 7), r=[wmqk, (k + 'xnT', kc)], w=[pmk])
            sq, sqk = self.b16t()
            self.actf(sq[:, 0:N], pm[:, 0:N], AF.Square, r=[pmk], w=[sqk])
            pss, pssk = self.psn()
            self.mm(pss[:, 0:N], lhsT=self.ones_b[:], rhs=sq[:, 0:N], r=['ones_b', sqk], w=[pssk])
            rt, rtk = self.f32t()
            self.actf(rt[:, 0:N], pss[:, 0:N], AF.Sqrt, bias=self.epsc[:, 0:1], scale=1.0 / 128, r=[pssk, 'epsc'], w=[rtk])
            self.recip(rt[:, 0:N], rt[:, 0:N], r=[rtk], w=[rtk])
            self.stt(c.mqT[:, hm, :], pm[:, 0:N], self.mqgc[:, 0:1], rt[:, 0:N], ALU.mult, ALU.mult, r=[pmk, 'mqgc', rtk], w=[k + 'mqT'])

    def phase_b(self, l, c=None):
        c = c or self.P
        N, k = c.N, c.k
        w_in = self.inp('w_in')[l]
        for fcg in range(2):
            for n in range(3):
                wm, wmk = self.wload(w_in[:, C_MG + n * 1024 + fcg * 512:C_MG + n * 1024 + (fcg + 1) * 512], 8, 512)
                wb, wbk = self.wload(self.inp('w_branch')[l, n][:, fcg * 512:(fcg + 1) * 512], 4, 512)
                for fi in range(4):
                    fc = fcg * 4 + fi
                    cs = slice(fi * 128, (fi + 1) * 128)
                    pg, pgk = self.psn()
                    pp, ppk = self.psn()
                    for kc in range(8):
                        self.mm(pg[:, 0:N], lhsT=wm[:, kc, cs], rhs=c.xnT[:, kc, :], start=(kc == 0), stop=(kc == 7), r=[wmk, (k + 'xnT', kc)], w=[pgk])
                    for kc in range(4):
                        self.mm(pp[:, 0:N], lhsT=wb[:, kc, cs], rhs=c.brT[:, n * 4 + kc, :], start=(kc == 0), stop=(kc == 3), r=[wbk, f'{k}brT{n}'], w=[ppk])
                    sg, sgk = self.f32t()
                    self.actf(sg[:, 0:N], pg[:, 0:N], AF.Sigmoid, r=[pgk], w=[sgk])
                    if n == 0:
                        self.tt('dve', c.macc[:, fi, :], sg[:, 0:N], pp[:, 0:N], ALU.mult, r=[sgk, ppk], w=[c.mack])
                    else:
                        self.tt('dve', sg[:, 0:N], sg[:, 0:N], pp[:, 0:N], ALU.mult, r=[sgk, ppk], w=[sgk])
                        if n == 1:
                            self.tt('pool', c.macc[:, fi, :], c.macc[:, fi, :], sg[:, 0:N], ALU.add, r=[sgk, c.mack], w=[c.mack])
                        else:
                            self.tt('pool', c.mT[:, fc, :], c.macc[:, fi, :], sg[:, 0:N], ALU.add, r=[sgk, c.mack], w=c.mTk)

    def phase_c(self, l, c=None):
        c = c or self.P
        N, k = c.N, c.k
        for half in range(2):
            wo, wok = self.wload(self.inp('w_out')[l][:, half * 512:(half + 1) * 512], 8, 512)
            for fi in range(4):
                fc = half * 4 + fi
                po, pok = self.psn()
                for kc in range(8):
                    self.mm(po[:, 0:N], lhsT=wo[:, kc, fi * 128:(fi + 1) * 128], rhs=c.mT[:, kc, :], start=(kc == 0), stop=(kc == 7), r=[wok] + c.mTk, w=[pok])
                self.tt('dve', c.xT[:, fc, :], po[:, 0:N], c.xT[:, fc, :], ALU.add, r=[pok, (k + 'xT', fc)], w=[(k + 'xT', fc)])

    def phase_d(self, l, c=None):
        c = c or self.P
        N, k = c.N, c.k
        self.rmsnorm(1, c)
        w_gu = self.inp('w_gate_up')[l]
        w_dn = self.inp('w_down')[l]
        groups = [(0, 6), (6, 6), (12, 5), (17, 5)]
        for (j0, n) in groups:
            for p0 in range(0, n, 4):
                pn = min(4, n - p0)
                ja = j0 + p0
                wg, wgk = self.wload(w_gu[:, ja * 128:(ja + pn) * 128], 8, pn * 128)
                wu, wuk = self.wload(w_gu[:, DFF + ja * 128:DFF + (ja + pn) * 128], 8, pn * 128)
                for q in range(pn):
                    jj = p0 + q
                    cs = slice(q * 128, (q + 1) * 128)
                    pg, pgk = self.psn()
                    pu, puk = self.psn()
                    for kc in range(8):
                        self.mm(pg[:, 0:N], lhsT=wg[:, kc, cs], rhs=c.xnT[:, kc, :], start=(kc == 0), stop=(kc == 7), r=[wgk, (k + 'xnT', kc)], w=[pgk])
                    for kc in range(8):
                        self.mm(pu[:, 0:N], lhsT=wu[:, kc, cs], rhs=c.xnT[:, kc, :], start=(kc == 0), stop=(kc == 7), r=[wuk, (k + 'xnT', kc)], w=[puk])
                    sg, sgk = self.f32t()
                    self.actf(sg[:, 0:N], pg[:, 0:N], AF.Silu, r=[pgk], w=[sgk])
                    self.tt('dve', c.actT[:, jj, :], sg[:, 0:N], pu[:, 0:N], ALU.mult, r=[sgk, puk], w=[(k + 'actT', jj)])
            for cq in range(4):
                wd, wdk = self.wload(w_dn[j0 * 128:(j0 + n) * 128, cq * 256:(cq + 1) * 256], n, 256)
                for fi in range(2):
                    fc = cq * 2 + fi
                    pd, pdk = self.psn()
                    for kc in range(n):
                        self.mm(pd[:, 0:N], lhsT=wd[:, kc, fi * 128:(fi + 1) * 128], rhs=c.actT[:, kc, :], start=(kc == 0), stop=(kc == n - 1),
                                r=[wdk, (k + 'actT', kc)], w=[pdk])
                    self.tt('dve', c.xT[:, fc, :], pd[:, 0:N], c.xT[:, fc, :], ALU.add, r=[pdk, (k + 'xT', fc)], w=[(k + 'xT', fc)])

    def declare_sample(self):
        i = self.din
        i("xs", [4, D]); i("pool", [DEPTH, 2560 * 128, 512]); i("cwin", [DEPTH, 4, 512, 256]); i("sconv", [DEPTH, 4, 2, 512])
        i("cmem", [DEPTH, 4, 256, 1024]); i("pt", [4, 64], I32)
        i("c_cos_s", [4, 8]); i("c_sin_s", [4, 8]); i("c_ovl_s", [128, 4, 129]); i("c_sA", [1, 129]); i("c_sB", [1, 129])
        i("c_e2", [2, 128]); i("c_lastmask", [128, 1]); i("c_winb", [128, 1]); i("c_pm64", [128, 1]); i("c_pidx", [128, 1])
        o = self.dout
        self.y_s = o("y_s", [4, D]); self.rows_s_o = o("rows_s", [DEPTH, 4, 512]); self.win_s_o = o("win_s", [DEPTH, 4, 512, 256])
        self.conv_s_o = o("conv_s", [DEPTH, 4, 2, 512])
        dr = lambda n, sh: self.nc.dram_tensor(n, sh, F32).ap()
        self.scr_rows = dr("scr_rows", [4, 768]); self.scr_q = dr("scr_q", [4, 512]); self.osc = dr("osc", [4, 3, 8, 64])
        sb = self.S.sb
        self.idxA = sb([128, 4, 32], I32, 'idxA'); self.idxB = sb([128, 4, 64], I32, 'idxB')
        self.qT_s = sb([128, 8, 4], BF16, 'qT_s'); self.ng_s = sb([4, 24], F32, 'ng_s')
        self.kcT_s = sb([128, 512], BF16, 'kcT_s'); self.Rs = [sb([128, 4, 194], BF16, f'Rs{g}') for g in range(2)]
        self.biask = sb([128, 2, 64], F32, 'biask'); self.cos_s = sb([4, 8], F32, 'cos_s'); self.sin_s = sb([4, 8], F32, 'sin_s')
        self.sA = sb([1, 129], F32, 'sA'); self.sB = sb([1, 129], F32, 'sB'); self.e2a = sb([1, 128], F32, 'e2a'); self.e2b = sb([1, 128], F32, 'e2b')
        self.lastm = sb([128, 1], F32, 'lastm'); self.winb = sb([128, 1], F32, 'winb'); self.ones_f = sb([4, 1], F32, 'ones_f')
        self.Vp = Rot([sb([128, 2, 66], BF16, f'Vp{k}') for k in range(2)], 'Vp')
        self.stT = sb([128, 4, 2, 4], F32, 'stT'); self.cso = sb([128, 4, 2, 4], F32, 'cso')
        of_ = self.onsa[:].rearrange("p a b -> p (a b)")
        self.nr = of_[0:1, 0:768]; self.nq = of_[0:1, 768:1280]; self.vne = sb([1, 2, 66], BF16, 'vne')

    def setup_sample(self):
        d = self.dma
        d('sp', self.cos_s[:], self.inp('c_cos_s'), w=['cos_s']); d('sp', self.sin_s[:], self.inp('c_sin_s'), w=['sin_s'])
        d('sp', self.sA[:], self.inp('c_sA'), w=['sA']); d('sp', self.sB[:], self.inp('c_sB'), w=['sB'])
        d('sp', self.e2a[:], self.inp('c_e2')[0:1, :], w=['e2a']); d('sp', self.e2b[:], self.inp('c_e2')[1:2, :], w=['e2b'])
        d('sp', self.lastm[:], self.inp('c_lastmask'), w=['lastm']); d('sp', self.winb[:], self.inp('c_winb'), w=['winb'])
        self.memset('pool', self.ones_f[:], 1.0, w=['ones_f'])
        for g in range(2):
            self.memset('pool', self.Rs[g][:, :, 64:65], 1.0, w=[f'Rs{g}'])
            d('pool', self.Rs[g][:, :, 65:194], self.inp('c_ovl_s'), w=[f'Rs{g}'])
        for k in range(2):
            self.memset('pool', self.Vp.bufs[k][:, :, 64:65], 1.0, w=[f'Vp{k}'])
        self.memset('pool', self.vne[:, :, 64:65], 1.0, w=['vne'])
        pt = self.inp('pt')
        pa, pak = self.selTr.next()
        ptbA = pa[:].rearrange("p a b c -> p (a b c)")[:, 0:128].bitcast(I32).rearrange("p (s q) -> p s q", s=4)
        ptbB = pa[:].rearrange("p a b c -> p (a b c)")[:, 128:384].bitcast(I32).rearrange("p (s q) -> p s q", s=4)
        pm, pmk = self.smr.next()
        d('sp', pm[:, 0:1], self.inp('c_pm64'), w=[pmk]); d('sp', pm[:, 1:2], self.inp('c_pidx'), w=[pmk])
        for h in range(2):
            d('sp', ptbA[64 * h:64 * h + 64], pt[:, h::2].partition_broadcast(64), w=[pak], slow=True)
        d('sp', ptbB, pt.partition_broadcast(128), w=[pak])
        self.ts('dve', self.idxA[:], ptbA, 64.0, ALU.mult, pm[:, 0:1], ALU.add, r=[pak, pmk], w=['idxA'])
        self.ts('dve', self.idxB[:], ptbB, 128.0, ALU.mult, pm[:, 1:2], ALU.add, r=[pak, pmk], w=['idxB'])

    def gather(self, dst, dkey, src, idx_ap, ikey, eoff):
        self.S.add('pool', lambda e: e.indirect_dma_start(out=dst, out_offset=None, in_=src, element_offset=eoff,
                                                           in_offset=bass.IndirectOffsetOnAxis(ap=idx_ap, axis=0)), [ikey], [dkey], dma=True)

    def sample_layer(self, l):
        X = self.X
        d = self.dma
        if l == 0:
            for s_ in range(4):
                d('sp', X.xT[:, :, s_], self.inp('xs')[s_].rearrange("(k p) -> p k", p=128), w=[('s_xT', kc) for kc in range(8)], slow=True)
        self.rmsnorm(0, X)
        w_in = self.inp('w_in')[l]
        wA = [self.wload(w_in[:, 0:512], 8, 512), self.wload(w_in[:, 512:1024], 8, 512), self.wload(w_in[:, 1024:1304], 8, 280)]
        pq, pqk = self.psn(); pa, pak = self.psn(); pb, pbk = self.psn()
        for (pp, ppk, (w, wk), nn) in ((pq, pqk, wA[0], 512), (pa, pak, wA[1], 512), (pb, pbk, wA[2], 280)):
            for kc in range(8):
                self.mm(pp[0:4, 0:nn], lhsT=X.xnT[:, kc, :], rhs=w[:, kc, :], start=(kc == 0), stop=(kc == 7), r=[('s_xnT', kc), wk], w=[ppk])
        hd_, hk = self.tmr.next()
        hd = hd_[0:4]
        self.cp('act', hd[:, 0:8, :].rearrange("p h d -> p (h d)"), pq[0:4, :], r=[pqk], w=[hk])
        self.cp('dve', hd[:, 8:10, :].rearrange("p h d -> p (h d)"), pa[0:4, 256:384], r=[pak], w=[hk])
        self.cp('dve', hd[:, 10:12, :].rearrange("p h d -> p (h d)"), pb[0:4, 0:128], r=[pbk], w=[hk])
        rows_, rowsk = self.rowsr.next()
        rows = rows_[0:4]
        self.cp('act', rows[:, 0:512], pa[0:4, :], r=[pak], w=[rowsk])
        self.cp('act', rows[:, 640:768], pb[0:4, 128:256], r=[pbk], w=[rowsk])
        self.actf(self.ng_s[:], pb[0:4, 256:280], AF.Sigmoid, r=[pbk], w=['ng_s'])
        cos_b = self.cos_s[:, :].unsqueeze(1).to_broadcast([4, 12, 8])
        sin_b = self.sin_s[:, :].unsqueeze(1).to_broadcast([4, 12, 8])
        hr, hrk = self.headnorm_rope(hd, hk, 12, self.g12b[0:4], cos_b, sin_b, 1.0 / 64)
        self.cp('pool', rows[:, 256:384], hr[:, 8:10, :].rearrange("p h d -> p (h d)"), r=[hrk], w=[rowsk])
        self.cp('pool', rows[:, 512:640], hr[:, 10:12, :].rearrange("p h d -> p (h d)"), r=[hrk], w=[rowsk])
        d('sp', self.rows_s_o[l], rows[:, 0:512], r=[rowsk], w=['rows_s_o'])
        d('sp', self.scr_rows, rows[:, 0:768], r=[rowsk], w=['scr_rows'])
        d('sp', self.scr_q, hr[:, 0:8, :].rearrange("p h d -> p (h d)"), r=[hrk], w=['scr_q'])
        d('sp', self.win_s_o[l, :, 511, :], rows[:, 512:768], r=[rowsk], w=['win_s_a'])
        d('sp', self.win_s_o[l, :, 0:511, :], self.inp('cwin')[l, :, 1:512, :], w=['win_s_b'])
        ts_, tsk = self.tsrc.next()
        tq = ts_[0:4]
        self.cp('pool', tq[:, 0:4, :].rearrange("p c (g d) -> p c g d", g=2), hd[:, 0:8, :].rearrange("p (g c) d -> p c g d", g=2), r=[hk], w=[tsk])
        self.cp('pool', tq[:, 4:8, :].rearrange("p c (g d) -> p c g d", g=2), hr[:, 0:8, :].rearrange("p (g c) d -> p c g d", g=2), r=[hrk], w=[tsk])
        p0, p0k = self.psn()
        p0b = p0[:].bitcast(BF16)
        for c in range(8):
            self.tr(p0b[:, c * 4:(c + 1) * 4], tq[:, c, :], self.ident_b[0:4, 0:4], r=[tsk, 'ident_b'], w=[p0k])
        self.cp('act', self.qT_s[:], p0b[:, 0:32].rearrange("p (c s) -> p c s", c=8), r=[p0k], w=['qT_s'])
        for s_ in range(4):
            for j_ in range(2):
                d('sp', self.stT[:, :, j_, s_], self.inp('sconv')[l, s_, j_].rearrange("(c p) -> p c", p=128), w=['stT'], slow=True)
        wcx, wcxk = self.wload(w_in[:, C_CX:C_CX + 512], 8, 512)
        wcb, wcbk = self.wload(w_in[:, C_CB:C_CB + 512], 8, 512)
        wcc, wcck = self.wload(w_in[:, C_CC:C_CC + 512], 8, 512)
        for ci in range(4):
            cs = slice(ci * 128, (ci + 1) * 128)
            px, pxk = self.psn(); pc, pck = self.psn(); pb2, pb2k = self.psn()
            for (pp, ppk, ww, wwk) in ((px, pxk, wcx, wcxk), (pc, pck, wcc, wcck), (pb2, pb2k, wcb, wcbk)):
                for kc in range(8):
                    self.mm(pp[:, 0:4], lhsT=ww[:, kc, cs], rhs=X.xnT[:, kc, :], start=(kc == 0), stop=(kc == 7), r=[wwk, ('s_xnT', kc)], w=[ppk])
            cxs, cxk = self.f32t()
            self.cp('act', cxs[:, 0:4], px[:, 0:4], r=[pxk], w=[cxk])
            self.tt('dve', self.cso[:, ci, 1, :], pc[:, 0:4], cxs[:, 0:4], ALU.mult, r=[pck, cxk], w=['cso'])
            self.cp('pool', self.cso[:, ci, 0, :], self.stT[:, ci, 1, :], r=['stT'], w=['cso'])
            a1, a1k = self.f32t()
            self.ts('pool', a1[:, 0:4], self.stT[:, ci, 0, :], self.cwc[:, ci, 0:1], ALU.mult, r=['stT', 'cwc'], w=[a1k])
            self.stt(a1[:, 0:4], self.stT[:, ci, 1, :], self.cwc[:, ci, 1:2], a1[:, 0:4], ALU.mult, ALU.add, r=['stT', 'cwc', a1k], w=[a1k])
            self.stt(a1[:, 0:4], self.cso[:, ci, 1, :], self.cwc[:, ci, 2:3], a1[:, 0:4], ALU.mult, ALU.add, r=['cso', 'cwc', a1k], w=[a1k])
            self.tt('dve', X.brT[:, 4 + ci, :], pb2[:, 0:4], a1[:, 0:4], ALU.mult, r=[pb2k, a1k], w=['s_brT1'])
        for s_ in range(4):
            for j_ in range(2):
                d('sp', self.conv_s_o[l, s_, j_].rearrange("(c p) -> p c", p=128), self.cso[:, :, j_, s_], r=['cso'], w=['conv_s_o'], slow=True)
        self.mq_proj(l, X)
        w1b, w1nk = self.wr.next()
        self.w1n = w1b[:, :].rearrange("p (k c h) -> p k c h", k=2, c=16)
        for kind in range(2):
            d('pool', self.w1n[:, kind], self.inp('cmp_w1')[l, kind].rearrange("(c p) h -> p c h", p=128), w=[w1nk])
        pool_l = self.inp('pool').rearrange("l r f -> (l r) f")
        pool_rp = self.inp('pool').rearrange("l (r two) f -> (l r) (two f)", two=2)
        eoff = l * 2560 * 128 * 512
        psr2 = Rot([self.psr.bufs[4], self.psr.bufs[5], self.accr.bufs[0]], 'x')
        keys2 = ['ps4', 'ps5', 'acc0']

        def ps2():
            k = psr2.i % 3
            b, _ = psr2.next()
            return b, keys2[k]
        Hps = [(self.psr.bufs[k], f'ps{k}') for k in range(4)]
        for s in range(4):
            for pp in range(32):
                g1, g1k = self.tmior.next()
                self.gather(g1[:, :], g1k, pool_rp, self.idxA[:, s, pp:pp + 1], 'idxA', eoff)
                pk, pkk = self.b16t()
                self.cp('dve' if pp % 2 == 0 else 'pool', pk[:].rearrange("p (kg s d) -> p kg s d", kg=4, s=2),
                        g1[:].rearrange("p (s kg d) -> p kg s d", s=2, kg=8)[:, 0:4], r=[g1k], w=[pkk])
                pt_, ptk = ps2()
                ptb = pt_[:].bitcast(BF16)
                for kg in range(4):
                    self.tr(ptb[:, kg * 128:(kg + 1) * 128], pk[:, kg * 128:(kg + 1) * 128], self.ident_b[:], r=[pkk, 'ident_b'], w=[ptk])
                xp, xpk = self.b16t()
                self.cp('act', xp[:], ptb[:, 0:512], r=[ptk], w=[xpk])
                for kg in range(4):
                    kind = kg // 2
                    H, Hk = Hps[kg]
                    for s8 in range(8):
                        self.mm(H[:, 16 * pp:16 * pp + 16], lhsT=self.w1n[:, kind, s8, :], rhs=xp[:, kg * 128 + s8:kg * 128 + 128:8],
                                start=(pp == 0 and s8 == 0), stop=False, r=[w1nk, xpk], w=[Hk])
                    for s8 in range(8):
                        if pp == 0:
                            self.mm(H[:, 0:15], lhsT=self.w1n[:, kind, 8 + s8, :], rhs=xp[:, kg * 128 + 8 + s8:kg * 128 + 128:8],
                                    start=False, stop=False, r=[w1nk, xpk], w=[Hk])
                        else:
                            self.mm(H[:, 16 * pp - 1:16 * pp + 15], lhsT=self.w1n[:, kind, 8 + s8, :], rhs=xp[:, kg * 128 + s8:kg * 128 + 128:8],
                                    start=False, stop=False, r=[w1nk, xpk], w=[Hk])
            kcn, kcnk = self.b16t()
            kcn4 = kcn[:].rearrange("p (c f) -> p c f", c=4)
            for kg in range(4):
                kind, g = kg // 2, kg % 2
                H, Hk = Hps[kg]
                hs, hsk = self.b16t()
                self.actf(hs[:], H[:], AF.Silu, bias=self.bpe[:, kind:kind + 1], r=[Hk, 'bpe'], w=[hsk])
                for ch in range(4):
                    pc, pck = ps2()
                    self.mm(pc[:, 0:64], lhsT=hs[:, ch * 128:(ch + 1) * 128], rhs=self.w2sb[:, kind, :], r=[hsk, 'w2sb'], w=[pck])
                    if kind == 0:
                        kf, kfk = self.f32t()
                        sm, smk = self.smr.next()
                        self.actf(kf[:, 0:64], pc[:, 0:64], AF.Square, accum=sm[:, 0:1], r=[pck], w=[kfk, smk])
                        self.actf(sm[:, 1:2], sm[:, 0:1], AF.Sqrt, bias=self.epsc[:, 0:1], scale=1.0 / 64, r=[smk, 'epsc'], w=[smk])
                        self.recip(sm[:, 2:3], sm[:, 1:2], r=[smk], w=[smk])
                        self.stt(kcn4[:, ch, g * 64:(g + 1) * 64], pc[:, 0:64], sm[:, 2:3], self.k0gb[:, :], ALU.mult, ALU.mult, r=[pck, smk, 'k0gb'], w=[kcnk])
                    else:
                        self.cp('act', self.Rs[g][:, ch, 0:64], pc[:, 0:64], r=[pck], w=[f'Rs{g}'])
            pt_, ptk = ps2()
            ptb = pt_[:].bitcast(BF16)
            for ch in range(4):
                self.tr(ptb[:, ch * 128:(ch + 1) * 128], kcn4[:, ch, :], self.ident_b[:], r=[kcnk, 'ident_b'], w=[ptk])
            self.cp('act', self.kcT_s[:], ptb[:, 0:512], r=[ptk], w=['kcT_s'])
            for g in range(2):
                pS, pSk = ps2()
                for ch in range(4):
                    for hh in range(4):
                        self.mm(pS[:, ch * 4 + hh:ch * 4 + hh + 1], lhsT=self.kcT_s[64 * g:64 * g + 64, ch * 128:(ch + 1) * 128],
                                rhs=self.qT_s[64 * g:64 * g + 64, hh, s:s + 1], r=['kcT_s', 'qT_s'], w=[pSk])
                pT, pTk = self.b16t()
                self.actf(pT[:, 0:16], pS[:, 0:16], AF.Exp, scale=0.125, r=[pSk], w=[pTk])
                self.ts('dve', pT[:, 12:16], pT[:, 12:16], self.lastm[:, 0:1], ALU.mult, r=[pTk, 'lastm'], w=[pTk])
                pO, pOk = ps2()
                for ch in range(4):
                    self.mm(pO[0:4, 0:194], lhsT=pT[:, ch * 4:(ch + 1) * 4], rhs=self.Rs[g][:, ch, 0:194], start=(ch == 0), stop=(ch == 3), r=[pTk, f'Rs{g}'], w=[pOk])
                sm, smk = self.smr.next()
                self.recip(sm[0:4, 0:1], pO[0:4, 64:65], r=[pOk], w=[smk])
                ob_, obk = self.f32t()
                self.ts('dve', ob_[0:4, 0:64], pO[0:4, 0:64], sm[0:4, 0:1], ALU.mult, r=[pOk, smk], w=[obk])
                d('sp', self.osc[s, 0, 4 * g:4 * g + 4, :], ob_[0:4, 0:64], r=[obk], w=[('osc', s, 0, g)])
                self.ts('dve', ob_[0:4, 128:257], pO[0:4, 65:194], sm[0:4, 0:1], ALU.mult, r=[pOk, smk], w=[obk])
                pR, pRk = ps2()
                self.mm(pR[0:1, 0:129], lhsT=self.ones_f[0:4, 0:1], rhs=ob_[0:4, 128:257], r=['ones_f', obk], w=[pRk])
                s2_, s2k = self.f32t()
                s2 = s2_[0:1]
                self.tt('dve', s2[:, 0:129], pR[0:1, 0:129], self.sA[:, :], ALU.mult, r=[pRk, 'sA'], w=[s2k])
                self.tt('dve', s2[:, 0:129], s2[:, 0:129], self.sB[:, :], ALU.add, r=[s2k, 'sB'], w=[s2k])
                sm2, sm2k = self.smr.next()
                self.S.add('dve', (lambda o, i: lambda e: e.max(out=o, in_=i))(sm2[0:1, 0:8], s2[:, 0:129]), [s2k], [sm2k])
                self.S.add('dve', (lambda o, a, b: lambda e: e.match_replace(out=o, in_to_replace=a, in_values=b, imm_value=-1e30))(s2[:, 256:385], sm2[0:1, 0:8], s2[:, 0:129]), [s2k, sm2k], [s2k])
                self.S.add('dve', (lambda o, i: lambda e: e.max(out=o, in_=i))(sm2[0:1, 8:16], s2[:, 256:385]), [s2k], [sm2k])
                self.ts('dve', s2[:, 256:385], s2[:, 0:129], sm2[0:1, 15:16], ALU.is_ge, r=[s2k, sm2k], w=[s2k])
                self.ts('dve', s2[:, 0:129], s2[:, 256:385], -1.0, ALU.add, -MASKV * 0.125, ALU.mult, r=[s2k], w=[s2k])
                pB, pBk = ps2()
                self.mm(pB[:, 0:64], lhsT=self.e2a[0:1, :], rhs=s2[:, 0:128:2], start=True, stop=False, r=['e2a', s2k], w=[pBk])
                self.mm(pB[:, 0:64], lhsT=self.e2b[0:1, :], rhs=s2[:, 1:128:2], start=False, stop=True, r=['e2b', s2k], w=[pBk])
                self.cp('act', self.biask[:, g, :], pB[:, 0:64], r=[pBk], w=['biask'])
            d('sp', self.nr, self.scr_rows[s:s + 1, :], r=['scr_rows'], w=['onsa'])
            d('sp', self.nq, self.scr_q[s:s + 1, :], r=['scr_q'], w=['onsa'])
            for br in (1, 2):
                koff, voff = (256, 384) if br == 1 else (512, 640)
                accS, accSk = self.accr.bufs[1], 'acc1'
                first = True
                nblk = 64 if br == 1 else 4
                if br == 2:
                    g3, g3k = self.tmior.next()
                    d('sp', g3[:].rearrange("p (b f) -> p b f", b=4), self.inp('cwin')[l, s].rearrange("(b p) f -> p b f", p=128), w=[g3k])
                for pg in range(nblk):
                    if br == 1:
                        g2, g2k = self.tmior.next()
                        self.gather(g2[:, 0:512], g2k, pool_l, self.idxB[:, s, pg:pg + 1], 'idxB', eoff)
                        ksrc, vsrc = g2[:, 256:384], g2[:, 384:512]
                    else:
                        g2k = g3k
                        ksrc, vsrc = g3[:, pg * 256:pg * 256 + 128], g3[:, pg * 256 + 128:pg * 256 + 256]
                    kb16, kbk = self.b16t()
                    self.cp('dve', kb16[:, 0:128], ksrc, r=[g2k], w=[kbk])
                    vp, vpk = self.Vp.next()
                    self.cp('pool', vp[:, :, 0:64], vsrc.rearrange("p (g d) -> p g d", g=2), r=[g2k], w=[vpk])
                    pt_, ptk = ps2()
                    ptb = pt_[:].bitcast(BF16)
                    self.tr(ptb[:, 0:128], kb16[:, 0:128], self.ident_b[:], r=[kbk, 'ident_b'], w=[ptk])
                    self.cp('act', kb16[:, 128:256], ptb[:, 0:128], r=[ptk], w=[kbk])
                    pS, pSk = ps2()
                    for h in range(8):
                        g, hh = h // 4, h % 4
                        self.mm(pS[:, h:h + 1], lhsT=kb16[64 * g:64 * g + 64, 128:256], rhs=self.qT_s[64 * g:64 * g + 64, 4 + hh, s:s + 1], r=[kbk, 'qT_s'], w=[pSk])
                    pT, pTk = self.b16t()
                    for g in range(2):
                        if br == 1:
                            bias = self.biask[:, g, pg:pg + 1]
                        else:
                            bias = self.winb[:, 0:1] if pg == 0 else None
                        self.actf(pT[:, 4 * g:4 * g + 4], pS[:, 4 * g:4 * g + 4], AF.Exp, bias=bias, scale=0.125, r=[pSk, 'biask', 'winb'], w=[pTk])
                    for g in range(2):
                        self.mm(accS[0:4, g * 65:(g + 1) * 65], lhsT=pT[:, 4 * g:4 * g + 4], rhs=vp[:, g, 0:65], start=first, stop=False, r=[pTk, vpk], w=[accSk])
                        first = False
                pr_, prk = self.f32t()
                pr = pr_[0:1]
                self.tt('dve', pr[:, 0:512].rearrange("p (g h d) -> p g h d", g=2, h=4), self.nq[:, :].rearrange("p (g h d) -> p g h d", g=2, h=4),
                        self.nr[:, koff:koff + 128].rearrange("p (g d) -> p g d", g=2).unsqueeze(2).to_broadcast([1, 2, 4, 64]), ALU.mult, r=['onsa'], w=[prk])
                sm3, sm3k = self.smr.next()
                self.red(sm3[0:1, 0:8], pr[:, 0:512].rearrange("p (h d) -> p h d", h=8), r=[prk], w=[sm3k])
                pn, pnk = self.b16t()
                self.actf(pn[0:1, 0:8], sm3[0:1, 0:8], AF.Exp, scale=0.125, r=[sm3k], w=[pnk])
                self.cp('dve', self.vne[:, :, 0:64], self.nr[:, voff:voff + 128].rearrange("p (g d) -> p g d", g=2), r=['onsa'], w=['vne'])
                for g in range(2):
                    self.mm(accS[0:4, g * 65:(g + 1) * 65], lhsT=pn[0:1, 4 * g:4 * g + 4], rhs=self.vne[0:1, g, 0:65], start=False, stop=True, r=[pnk, 'vne'], w=[accSk])
                sm4, sm4k = self.smr.next()
                self.recip(sm4[0:4, 0:2], accS[0:4, 64:130:65], r=[accSk], w=[sm4k])
                ob_, obk = self.f32t()
                for g in range(2):
                    self.ts('dve', ob_[0:4, g * 64:(g + 1) * 64], accS[0:4, g * 65:g * 65 + 64], sm4[0:4, g:g + 1], ALU.mult, r=[accSk, sm4k], w=[obk])
                    d('sp', self.osc[s, br, 4 * g:4 * g + 4, :], ob_[0:4, g * 64:(g + 1) * 64], r=[obk], w=[('osc', s, br, g)])
            cmb, cmk = self.wr.next()
            cm = cmb[:, 0:2048].rearrange("p (mb f) -> p mb f", mb=2)
            d('pool', cm, self.inp('cmem')[l, s].rearrange("(mb p) f -> p mb f", p=128), w=[cmk])
            pt_, ptk = ps2()
            ptb = pt_[:].bitcast(BF16)
            for mb in range(2):
                for hm in range(4):
                    self.tr(ptb[:, (mb * 4 + hm) * 128:(mb * 4 + hm + 1) * 128], cm[:, mb, hm * 128:(hm + 1) * 128], self.ident_b[:], r=[cmk, 'ident_b'], w=[ptk])
            self.cp('act', self.mkT[:].rearrange("p h (mb m) -> p mb h m", mb=2), ptb[:, 0:1024].rearrange("p (mb h m) -> p mb h m", mb=2, h=4), r=[ptk], w=['mkT'])
            pS, pSk = ps2()
            for mb in range(2):
                for hm in range(4):
                    self.mm(pS[:, mb * 4 + hm:mb * 4 + hm + 1], lhsT=self.mkT[:, hm, mb * 128:(mb + 1) * 128], rhs=X.mqT[:, hm, s:s + 1], r=['mkT', 's_mqT'], w=[pSk])
            pT, pTk = self.b16t()
            self.actf(pT[:, 0:8], pS[:, 0:8], AF.Exp, scale=128 ** -0.5, r=[pSk], w=[pTk])
            pO, pOk = ps2()
            pD, pDk = ps2()
            first = True
            for mb in range(2):
                for hm in range(4):
                    self.mm(pO[:, hm:hm + 1], lhsT=cm[:, mb, 512 + hm * 128:512 + (hm + 1) * 128], rhs=pT[:, mb * 4 + hm:mb * 4 + hm + 1], start=first, stop=False, r=[cmk, pTk], w=[pOk])
                    first = False
            for mb in range(2):
                self.mm(pD[:, 0:4], lhsT=self.ones_b[:], rhs=pT[:, mb * 4:(mb + 1) * 4], start=(mb == 0), stop=(mb == 1), r=['ones_b', pTk], w=[pDk])
            rc, rck = self.f32t()
            self.recip(rc[:, 0:4], pD[:, 0:4], r=[pDk], w=[rck])
            self.tt('dve', X.brT[:, 8:12, s], pO[:, 0:4], rc[:, 0:4], ALU.mult, r=[pOk, rck], w=['s_brT2'])
        on_, onk = self.f32t()
        on = on_[0:4]
        osk = [('osc', s, br, g) for s in range(4) for br in range(3) for g in range(2)]
        for br in range(3):
            ot_, otk = self.tmr.next()
            ot = ot_[0:4, 0:8, :]
            d('sp', ot, self.osc[:, br, :, :], r=osk, w=[otk])
            gb = self.ng_s[:, br * 8:(br + 1) * 8].unsqueeze(2).to_broadcast([4, 8, 64])
            if br == 0:
                self.tt('dve', on[:, 0:512].rearrange("p (h d) -> p h d", h=8), ot, gb, ALU.mult, r=[otk, 'ng_s'], w=[onk])
            else:
                self.tt('dve', ot, ot, gb, ALU.mult, r=[otk, 'ng_s'], w=[otk])
                self.tt('dve', on[:, 0:512], on[:, 0:512], ot.rearrange("p h d -> p (h d)"), ALU.add, r=[otk, onk], w=[onk])
        onb, onbk = self.b16t()
        self.cp('dve', onb[0:4, :], on[:, 0:512], r=[onk], w=[onbk])
        pt_, ptk = self.psn()
        ptb = pt_[:].bitcast(BF16)
        for c in range(4):
            self.tr(ptb[:, c * 4:(c + 1) * 4], onb[0:4, c * 128:(c + 1) * 128], self.ident_b[0:4, 0:4], r=[onbk, 'ident_b'], w=[ptk])
        self.cp('act', X.brT[:, 0:4, :], ptb[:, 0:16].rearrange("p (c s) -> p c s", c=4), r=[ptk], w=['s_brT0'])
        self.phase_b(l, X)
        self.phase_c(l, X)
        self.phase_d(l, X)
        if l == self.nlayers - 1:
            for s_ in range(4):
                d('sp', self.y_s[s_].rearrange("(k p) -> p k", p=128), X.xT[:, :, s_], r=[('s_xT', kc) for kc in range(8)], w=['y_s'], slow=True)

    def build(self, stage=99):
        self.declare()
        self.make_ctxs()
        if self.with_sample:
            self.declare_sample()
        self.stage = stage
        self.setup()
        if self.with_sample:
            self.setup_sample()
        for l in range(self.nlayers):
            if stage >= 1:
                self.layer_setup(l)
            for t in range(self.ntiles):
                if stage < 3:
                    continue
                self.load_x(l, t)
                self.rmsnorm(0)
                if stage < 4:
                    continue
                wA = [self.wload(self.inp('w_in')[l][:, 0:512], 8, 512), self.wload(self.inp('w_in')[l][:, 512:1024], 8, 512),
                      self.wload(self.inp('w_in')[l][:, 1024:1304], 8, 280)]
                self.tokmajor(l, t, wA)
                if stage < 5:
                    continue
                self.compress(l, t)
                self.fm_proj(l, t)
                self.mq_proj(l)
                self.cmp_attend(t)
                self.topk(t)
                self.slc(t)
                self.win(t)
                self.nsa_finalize()
                self.mem_attend()
                if stage < 6:
                    continue
                self.phase_b(l)
                self.phase_c(l)
                self.phase_d(l)
                self.store_x(l, t)
            if self.with_sample and stage >= 7:
                self.sample_layer(l)
        self.S.emit()
        return self.nc


def make_consts():
    c = {}
    pos = (np.arange(32)[None, :] * 128 + np.arange(128)[:, None]).astype(np.float32)
    inv = (500000.0 ** (-np.arange(8, dtype=np.float32) / 8)).astype(np.float32)
    ang = pos[:, :, None] * inv[None, None, :]
    c['c_cos'] = np.cos(ang).astype(np.float32)
    c['c_sin'] = np.sin(ang).astype(np.float32)
    q = pos.astype(np.int64)
    j = np.arange(64)[None, None, :]
    qb = (q // 64)[:, :, None]
    forced = (j == 0) | (j == qb) | (j == qb - 1)
    elig = (j * 64) <= q[:, :, None]
    A = (elig & ~forced).astype(np.float32)
    B = np.where(forced, 1e9, np.where(elig, 0.0, -1e9)).astype(np.float32)
    c['c_selA'] = A
    c['c_selB'] = B
    p = np.arange(128)[:, None]
    f = np.arange(128)[None, :]
    c['c_tri'] = (p <= f).astype(np.float32)
    c['c_strict'] = (p > f).astype(np.float32)
    f5 = np.arange(512)[None, :]
    c['c_stair'] = ((16 * (p % 32) + 15) <= f5).astype(np.float32)
    posn = np.arange(256)
    cc = posn - 1
    jj = np.arange(64)[None, :]
    ov = ((cc[:, None] * 16 < (jj + 1) * 64) & (cc[:, None] * 16 + 32 > jj * 64) & (cc[:, None] >= 0) & (cc[:, None] < 255))
    c['c_ovl'] = ov.astype(np.float32).reshape(2, 128, 64).transpose(1, 0, 2).copy()
    k = np.arange(T)[None, :]
    c['c_E'] = ((k // 64) == np.arange(64)[:, None]).astype(np.float32)
    c['c_ident'] = np.eye(128, dtype=np.float32)
    return c


def make_consts_sample():
    c = {}
    inv = (500000.0 ** (-np.arange(8, dtype=np.float32) / 8)).astype(np.float32)
    ang = np.float32(8192.0) * inv
    c['c_cos_s'] = np.tile(np.cos(ang).astype(np.float32)[None, :], (4, 1))
    c['c_sin_s'] = np.tile(np.sin(ang).astype(np.float32)[None, :], (4, 1))
    cc = np.arange(512)[:, None]
    jj = np.arange(129)[None, :]
    ov = ((cc * 16 < (jj + 1) * 64) & (cc * 16 + 32 > jj * 64) & (cc < 511))
    c['c_ovl_s'] = ov.astype(np.float32).reshape(4, 128, 129).transpose(1, 0, 2).copy()
    forced = np.zeros((1, 129), dtype=bool)
    forced[0, [0, 127, 128]] = True
    c['c_sA'] = (~forced).astype(np.float32)
    c['c_sB'] = np.where(forced, 1e9, 0.0).astype(np.float32)
    p = np.arange(128)
    c['c_e2'] = np.stack([(p < 64), (p >= 64)]).astype(np.float32)
    c['c_lastmask'] = (p < 127).astype(np.float32)[:, None]
    wb = np.zeros((128, 1), dtype=np.float32)
    wb[0, 0] = MASKV * 0.125
    c['c_winb'] = wb
    c['c_pm64'] = (p % 64).astype(np.float32)[:, None]
    c['c_pidx'] = p.astype(np.float32)[:, None]
    return c


_NC_CACHE = {}


def _get_program():
    if 'nc' not in _NC_CACHE:
        b = Builder(nlayers=DEPTH, ntiles=NTILE, with_sample=True)
        nc = b.build(99)
        _NC_CACHE['nc'] = nc
        _NC_CACHE['decl'] = set(b.decl)
    return _NC_CACHE['nc'], _NC_CACHE['decl']


def kernel(**inputs):
    nc, decl = _get_program()
    f32 = np.float32
    inp = {k: np.asarray(v) for k, v in inputs.items()}
    consts = make_consts()
    consts.update(make_consts_sample())
    qn, kn = inp['q_norm'], inp['k_norm']
    shared = dict(consts)
    for k in ['w_in', 'w_mem_kv', 'w_branch', 'w_out', 'w_gate_up', 'w_down', 'cmp_w1', 'cmp_w2', 'cmp_pe', 'conv_w',
              'norm_mix', 'norm_mem', 'norm_ffn', 'mem_q_norm']:
        shared[k] = inp[k]
    shared['g12'] = np.concatenate([np.tile(qn, (1, 8)), np.tile(kn[:, 1], (1, 2)), np.tile(kn[:, 2], (1, 2))], axis=1)
    shared['k0g'] = kn[:, 0]
    shared['mkg'] = np.tile(inp['mem_k_norm'], (1, 4))
    n_phys = inp['cache_nsa_kv'].shape[1]
    assert n_phys == 2560
    shared['pool'] = inp['cache_nsa_kv'].reshape(DEPTH, n_phys * 128, 512)
    in_maps = []
    for c in range(8):
        b = c % 4
        ss = slice(4 * c, 4 * c + 4)
        m = dict(shared)
        m['x'] = inp['x_prompt'][b]
        m['mem'] = inp['mem_prompt'][b]
        m['xs'] = inp['x_sample'][ss, 0]
        m['cwin'] = inp['cache_win_kv'][:, ss].reshape(DEPTH, 4, 512, 256)
        m['sconv'] = inp['state_conv'][:, ss]
        m['cmem'] = inp['cache_mem_kv'][:, ss].reshape(DEPTH, 4, 256, 1024)
        m['pt'] = inp['page_table'][ss].astype(np.int32)
        in_maps.append({k: np.ascontiguousarray(v) for k, v in m.items() if k in decl})
    res = run_bass_kernel_spmd(nc, in_maps, core_ids=list(range(8)))
    R = res.results
    y_p = np.stack([R[c]['y'] for c in range(4)], 0).astype(f32)
    y_s = np.concatenate([R[c]['y_s'] for c in range(8)], 0).reshape(32, 1, D).astype(f32)
    rows_p = np.stack([R[c]['rows_p'] for c in range(4)], 1).reshape(DEPTH, 4, T, 4, 2, 64).astype(f32)
    rows_s = np.concatenate([R[c]['rows_s'] for c in range(8)], 1).reshape(DEPTH, 32, 1, 4, 2, 64).astype(f32)
    win_p = np.stack([R[c]['win_p'] for c in range(4)], 1).reshape(DEPTH, 4, 512, 2, 2, 64).astype(f32)
    win_s = np.concatenate([R[c]['win_s'] for c in range(8)], 1).reshape(DEPTH, 32, 512, 2, 2, 64).astype(f32)
    conv_p = np.stack([R[c]['conv_p'] for c in range(4)], 1).astype(f32)
    conv_s = np.concatenate([R[c]['conv_s'] for c in range(8)], 1).astype(f32)
    mem_p = np.stack([R[c]['mem_p'] for c in range(4)], 1).reshape(DEPTH, 4, 256, 2, 4, 128).astype(f32)
    return (y_p, y_s, rows_p, rows_s, win_p, win_s, conv_p, conv_s, mem_p)
```

```python
import contextlib
import numpy as np
import concourse.bass as bass
import concourse.mybir as mybir
from concourse.bass_utils import run_bass_kernel_spmd

F32 = mybir.dt.float32
BF16 = mybir.dt.bfloat16
I32 = mybir.dt.int32
ALU = mybir.AluOpType
AF = mybir.ActivationFunctionType
AX = mybir.AxisListType
ENGS = ['pe', 'act', 'dve', 'pool', 'sp']

D = 1024
T = 4096
TT = 512
NTILE = 8
DEPTH = 2
N_IN = 6424
DFF = 2816
EPS = 1e-6
MASKV = -30000.0
C_CX, C_CB, C_CC, C_MQ, C_MG = 1304, 1816, 2328, 2840, 3352


class Sched:
    NDSEM = 8

    def __init__(self, nc):
        self.nc = nc
        self.ops = []
        self.stack = contextlib.ExitStack()
        self._n = 0

    def sb(self, shape, dt, name=None):
        self._n += 1
        return self.stack.enter_context(self.nc.sbuf_tensor(name or f"sb{self._n}", list(shape), dt))

    def ps(self, shape, dt=F32, name=None):
        self._n += 1
        return self.stack.enter_context(self.nc.psum_tensor(name or f"ps{self._n}", list(shape), dt))

    def add(self, eng, fn, r=(), w=(), dma=False):
        w = tuple(w) + tuple(k + '#rd' for k in r if isinstance(k, str) and (k.startswith('ps') or k.startswith('acc')) and eng != 'pe')
        self.ops.append(dict(eng=eng, fn=fn, r=tuple(r), w=tuple(w), dma=dma))

    def emit(self):
        nc = self.nc
        ops = self.ops
        n = len(ops)
        pos = [0] * n
        cnt = {e: 0 for e in ENGS}
        dcnt = {e: 0 for e in ENGS}
        dk = [0] * n
        for i, o in enumerate(ops):
            if o['dma']:
                dk[i] = dcnt[o['eng']]
                dcnt[o['eng']] += 1
            else:
                pos[i] = cnt[o['eng']]
                cnt[o['eng']] += 1
        last_w = {}
        readers = {}
        deps = [None] * n
        for i, o in enumerate(ops):
            d = set()
            for r in o['r']:
                if r in last_w:
                    d.add(last_w[r])
            for w in o['w']:
                if w in last_w:
                    d.add(last_w[w])
                d.update(readers.get(w, ()))
            d.discard(i)
            deps[i] = d
            for r in o['r']:
                readers.setdefault(r, []).append(i)
            for w in o['w']:
                last_w[w] = i
                readers[w] = []
        clock = {e: {p: -1 for p in ENGS} for e in ENGS}
        dma_seen = {e: set() for e in ENGS}
        opclock = [None] * n
        waits = [[] for _ in range(n)]
        signal = set()
        K = self.NDSEM
        dma_by_eng = {e: [] for e in ENGS}
        for i, o in enumerate(ops):
            E = o['eng']
            ck = clock[E]
            if o['dma']:
                k = dk[i]
                if k >= K:
                    prev = dma_by_eng[E][k - K]
                    if prev not in dma_seen[E]:
                        waits[i].append(('d', prev))
                        dma_seen[E].add(prev)
                dma_by_eng[E].append(i)
            for d in sorted(deps[i], reverse=True):
                od = ops[d]
                if od['dma']:
                    if d in dma_seen[E]:
                        continue
                    waits[i].append(('d', d))
                    dma_seen[E].add(d)
                else:
                    P = od['eng']
                    if P == E and E == 'pe':
                        continue
                    if ck[P] >= pos[d]:
                        continue
                    waits[i].append(('c', d))
                    signal.add(d)
                    oc = opclock[d]
                    for p in ENGS:
                        if oc[p] > ck[p]:
                            ck[p] = oc[p]
                    if pos[d] > ck[P]:
                        ck[P] = pos[d]
            if not o['dma']:
                opclock[i] = dict(ck)
        rank = {}
        rc = {e: 0 for e in ENGS}
        for i, o in enumerate(ops):
            if not o['dma'] and i in signal:
                rc[o['eng']] += 1
                rank[i] = rc[o['eng']]
        st = self.stack
        csem = {e: st.enter_context(nc.semaphore(f"c_{e}")) for e in ENGS}
        dsem = {e: [st.enter_context(nc.semaphore(f"d_{e}{j}")) for j in range(K)] for e in ENGS if dcnt[e] > 0}

        def dsv(d):
            k = dk[d]
            return dsem[ops[d]['eng']][k % K], 16 * (k // K + 1)

        by_eng = {e: [i for i, o in enumerate(ops) if o['eng'] == e] for e in ENGS}
        self.stats = dict(n=n, signals=len(signal), waits=sum(len(w) for w in waits),
                          per_eng={e: len(by_eng[e]) for e in ENGS})

        def mk(E):
            def body(e):
                for i in by_eng[E]:
                    o = ops[i]
                    for kind, d in waits[i]:
                        if kind == 'd':
                            s, v = dsv(d)
                            e.wait_ge(s, v)
                        else:
                            e.wait_ge(csem[ops[d]['eng']], rank[d])
                    ins = o['fn'](e)
                    if o['dma']:
                        s, v = dsv(i)
                        ins.then_inc(s, 16)
                    elif i in signal:
                        ins.then_inc(csem[E], 1)
                nd = dcnt[E]
                for j in range(min(K, nd)):
                    last_k = ((nd - 1 - j) // K) * K + j
                    e.wait_ge(dsem[E][j], 16 * (last_k // K + 1))
            return body

        with nc.Block() as block:
            block.tensor(mk('pe'))
            block.scalar(mk('act'))
            block.vector(mk('dve'))
            block.gpsimd(mk('pool'))
            block.sync(mk('sp'))
        st.close()


class Rot:
    def __init__(self, bufs, name):
        self.bufs = bufs
        self.name = name
        self.i = 0

    def next(self):
        k = self.i % len(self.bufs)
        self.i += 1
        return self.bufs[k], f"{self.name}{k}"


class Builder:
    def __init__(self, nlayers=DEPTH, ntiles=NTILE, with_sample=True):
        self.nlayers = nlayers
        self.ntiles = ntiles
        self.with_sample = with_sample
        self.nc = bass.Bass("TRN2", target_bir_lowering=False)
        self.S = Sched(self.nc)
        self.lazy = {}
        self.decl = {}

    def mm(self, out, lhsT, rhs, start=True, stop=True, r=(), w=()):
        self.S.add('pe', lambda e: e.matmul(out, lhsT=lhsT, rhs=rhs, start=start, stop=stop, skip_group_check=True), r, w)

    def tr(self, out, in_, ident, r=(), w=()):
        self.S.add('pe', lambda e: e.transpose(out, in_, ident), r, w)

    def actf(self, out, in_, func, bias=None, scale=1.0, accum=None, r=(), w=()):
        kw = {}
        if bias is not None:
            kw['bias'] = bias
        if accum is not None:
            kw['accum_out'] = accum
        self.S.add('act', lambda e: e.activation(out=out, in_=in_, func=func, scale=scale, **kw), r, w)

    def cp(self, eng, out, in_, r=(), w=()):
        if eng == 'pool':
            eng = 'act'
        if eng == 'act':
            self.S.add('act', lambda e: e.copy(out=out, in_=in_), r, w)
        else:
            self.S.add(eng, lambda e: e.tensor_copy(out=out, in_=in_), r, w)

    def tt(self, eng, out, in0, in1, op, r=(), w=()):
        if eng == 'pool':
            eng = 'dve'
        self.S.add(eng, lambda e: e.tensor_tensor(out=out, in0=in0, in1=in1, op=op), r, w)

    def ts(self, eng, out, in0, s1, op0, s2=None, op1=None, r=(), w=()):
        if eng == 'pool':
            eng = 'dve'
        if op1 is None:
            self.S.add(eng, lambda e: e.tensor_scalar(out=out, in0=in0, scalar1=s1, scalar2=None, op0=op0), r, w)
        else:
            self.S.add(eng, lambda e: e.tensor_scalar(out=out, in0=in0, scalar1=s1, scalar2=s2, op0=op0, op1=op1), r, w)

    def stt(self, out, in0, scalar, in1, op0, op1, r=(), w=()):
        self.S.add('dve', lambda e: e.scalar_tensor_tensor(out=out, in0=in0, scalar=scalar, in1=in1, op0=op0, op1=op1), r, w)

    def recip(self, out, in_, r=(), w=()):
        self.S.add('dve', lambda e: e.reciprocal(out=out, in_=in_), r, w)

    def red(self, out, in_, r=(), w=()):
        self.S.add('dve', lambda e: e.tensor_reduce(out=out, in_=in_, axis=AX.X, op=ALU.add), r, w)

    def memset(self, eng, ap, val, r=(), w=()):
        self.S.add(eng, lambda e: e.memset(ap, val), r, w)

    def dma(self, q, out, in_, r=(), w=(), slow=False):
        if slow:
            self.S.add(q, lambda e: e.dma_start(out=out, in_=in_, allow_slow_non_contiguous=True), r, w, dma=True)
        else:
            self.S.add(q, lambda e: e.dma_start(out=out, in_=in_), r, w, dma=True)

    def din(self, name, shape, dt=F32):
        self.lazy[name] = (list(shape), dt)
        return None

    def inp(self, name):
        if name not in self.decl:
            shape, dt = self.lazy[name]
            self.decl[name] = self.nc.dram_tensor(name, shape, dt, kind="ExternalInput").ap()
        return self.decl[name]

    def dout(self, name, shape, dt=F32):
        return self.nc.dram_tensor(name, list(shape), dt, kind="ExternalOutput").ap()

    def psn(self):
        return self.psr.next()

    def f32t(self):
        return self.f32r.next()

    def b16t(self):
        return self.b16r.next()

    def wload(self, src, nk, ncols, q='pool'):
        buf, key = self.wr.next()
        v = buf[:, 0:nk * ncols].rearrange("p (k c) -> p k c", k=nk)
        self.dma(q, v, src.rearrange("(k p) c -> p k c", p=128), w=[key])
        return v, key

    def declare(self):
        S = self.S
        i = self.din
        self.x = i("x", [T, D])
        self.mem = i("mem", [256, D])
        self.w_in = i("w_in", [DEPTH, D, N_IN])
        self.w_mem = i("w_mem_kv", [DEPTH, D, 1024])
        self.w_br = i("w_branch", [DEPTH, 3, 512, D])
        self.w_out = i("w_out", [DEPTH, D, D])
        self.w_gu = i("w_gate_up", [DEPTH, D, 2 * DFF])
        self.w_dn = i("w_down", [DEPTH, DFF, D])
        self.cw1 = i("cmp_w1", [DEPTH, 2, 2048, 128])
        self.cw2 = i("cmp_w2", [DEPTH, 2, 128, 64])
        self.cpe = i("cmp_pe", [DEPTH, 2, 32, 64])
        self.convw = i("conv_w", [DEPTH, 3, 512])
        self.g_mix = i("norm_mix", [DEPTH, D])
        self.g_mem = i("norm_mem", [DEPTH, D])
        self.g_ffn = i("norm_ffn", [DEPTH, D])
        self.g12 = i("g12", [DEPTH, 768])
        self.k0g = i("k0g", [DEPTH, 64])
        self.mqg = i("mem_q_norm", [DEPTH, 128])
        self.mkg = i("mkg", [DEPTH, 512])
        self.c_cos = i("c_cos", [128, 32, 8])
        self.c_sin = i("c_sin", [128, 32, 8])
        self.c_selA = i("c_selA", [128, 32, 64])
        self.c_selB = i("c_selB", [128, 32, 64])
        self.c_tri = i("c_tri", [128, 128])
        self.c_strict = i("c_strict", [128, 128])
        self.c_stair = i("c_stair", [128, 512])
        self.c_ovl = i("c_ovl", [128, 2, 64])
        self.c_E = i("c_E", [64, T])
        self.c_ident = i("c_ident", [128, 128])
        o = self.dout
        self.y = o("y", [T, D])
        self.rows_o = o("rows_p", [DEPTH, T, 512])
        self.win_o = o("win_p", [DEPTH, 512, 256])
        self.conv_o = o("conv_p", [DEPTH, 2, 512])
        self.mem_o = o("mem_p", [DEPTH, 256, 1024])
        self.hres = self.nc.dram_tensor("hres", [8, 128, T], F32).ap()

        sb = S.sb
        self.xT = sb([128, 8, TT], F32, 'xT')
        self.xnT = sb([128, 8, TT], BF16, 'xnT')
        self.QB = sb([128, 8, TT], BF16, 'QB')
        self.QU = sb([128, 4, TT], BF16, 'QU')
        self.mqT = sb([128, 4, TT], BF16, 'mqT')
        self.brT = sb([128, 12, TT], BF16, 'brT')
        self.mT = self.QB
        self.actT = sb([128, 6, TT], BF16, 'actT')
        self.tmior = Rot([sb([128, D], F32, f'tmio{k}') for k in range(2)], 'tmio')
        self.KE = [sb([128, T], BF16, f'KE{g}') for g in range(2)]
        self.Vs = sb([128, 32, 2, 66], BF16, 'Vs')
        self.kwT = [sb([64, 1024], BF16, f'kwT{g}') for g in range(2)]
        self.Vw = sb([128, 8, 2, 66], BF16, 'Vw')
        self.kcT2 = [sb([128, 256], BF16, f'kcT{g}') for g in range(2)]
        self.Rc = [sb([128, 2, 130], BF16, f'Rc{g}') for g in range(2)]
        self.rawk = sb([128, 528], BF16, 'rawk')
        self.rawv = sb([128, 528], BF16, 'rawv')
        self.mkT = sb([128, 4, 256], BF16, 'mkT')
        self.mv = sb([128, 2, 512], BF16, 'mv')
        self.ucar = sb([128, 4, 2], F32, 'ucar')
        self.uextr = Rot([sb([128, 514], F32, f'uext{k}') for k in range(2)], 'uext')
        self.onsa = sb([128, 4, 512], F32, 'onsa')
        self.macc = self.onsa
        self.scr = sb([128, 4, 2, 64], F32, 'scr')
        self.ngt = sb([128, 4, 24], F32, 'ngt')
        self.biasr = Rot([sb([128, 128], BF16, f'biasw{k}') for k in range(4)], 'biasw')
        self.selTr = Rot([sb([128, 2, 4, 64], F32, f'selT{k}') for k in range(2)], 'selT')
        self.ident_f = sb([128, 128], F32, 'ident_f')
        self.ident_b = sb([128, 128], BF16, 'ident_b')
        self.ones_b = sb([128, 128], BF16, 'ones_b')
        self.tri = sb([128, 128], BF16, 'tri')
        self.strict = sb([128, 128], BF16, 'strict')
        self.stair = sb([128, 512], BF16, 'stair')
        self.cosT = sb([128, 32, 8], F32, 'cosT')
        self.sinT = sb([128, 32, 8], F32, 'sinT')
        self.epsc = sb([128, 1], F32, 'epsc')
        self.gcols = sb([128, 3, 8], F32, 'gcols')
        self.g12b = sb([128, 12, 64], F32, 'g12b')
        self.k0gb = sb([128, 64], F32, 'k0gb')
        self.mqgc = sb([128, 1], F32, 'mqgc')
        self.cwc = sb([128, 4, 3], F32, 'cwc')
        self.w2sb = sb([128, 2, 64], BF16, 'w2sb')
        self.peT = sb([64, 2, 32], BF16, 'peT')
        self.bpe = sb([128, 2], F32, 'bpe')
        self.psr = Rot([S.ps([128, 512], F32, f'psb{k}') for k in range(6)], 'ps')
        self.accr = Rot([S.ps([128, 512], F32, f'acc{k}') for k in range(2)], 'acc')
        self.f32r = Rot([sb([128, 512], F32, f'f32t{k}') for k in range(4)], 'f32t')
        self.b16r = Rot([sb([128, 512], BF16, f'b16t{k}') for k in range(5)], 'b16t')
        self.wr = Rot([sb([128, 4096], BF16, f'wbuf{k}') for k in range(4)], 'wbuf')
        self.tmr = Rot([sb([128, 12, 64], F32, f'tm{k}') for k in range(4)], 'tm')
        self.smr = Rot([sb([128, 32], F32, f'sm{k}') for k in range(8)], 'sm')
        self.selr = Rot([sb([128, 2, 64], F32, f'sel{k}') for k in range(2)], 'sel')
        self.tsrc = Rot([sb([128, 12, 128], BF16, f'tsrc{k}') for k in range(1)], 'tsrc')
        self.rowsr = Rot([sb([128, 768], F32, f'rows{k}') for k in range(1)], 'rows')

    def setup(self):
        d = self.dma
        d('sp', self.ident_f[:], self.inp('c_ident'), w=['ident_f'])
        d('pool', self.ident_b[:], self.inp('c_ident'), w=['ident_b'])
        d('pool', self.tri[:], self.inp('c_tri'), w=['tri'])
        d('pool', self.strict[:], self.inp('c_strict'), w=['strict'])
        d('pool', self.stair[:], self.inp('c_stair'), w=['stair'])
        d('sp', self.cosT[:], self.inp('c_cos'), w=['cosT'])
        d('sp', self.sinT[:], self.inp('c_sin'), w=['sinT'])
        self.memset('pool', self.ones_b[:], 1.0, w=['ones_b'])
        self.memset('pool', self.epsc[:], EPS, w=['epsc'])
        for g in range(2):
            d('pool', self.KE[g][64:128, :], self.inp('c_E'), w=[f'KE{g}'])
            self.memset('pool', self.Rc[g][:, :, 64:65], 1.0, w=[f'Rc{g}'])
            self.memset('pool', self.Rc[g][0:1, 0, 64:65], 0.0, w=[f'Rc{g}'])
            d('pool', self.Rc[g][:, :, 65:129], self.inp('c_ovl'), w=[f'Rc{g}'])
        for k in range(4):
            self.memset('pool', self.biasr.bufs[k][:], 0.0, w=[f'biasw{k}'])
        self.memset('pool', self.Vs[:, :, :, 64:65], 1.0, w=['Vs'])
        self.memset('pool', self.Vw[:, :, :, 64:65], 1.0, w=['Vw'])

    def layer_setup(self, l):
        d = self.dma
        d('sp', self.gcols[:, 0, :], self.inp('norm_mix')[l].rearrange("(k p) -> p k", p=128), w=['gcols'], slow=True)
        d('sp', self.gcols[:, 1, :], self.inp('norm_ffn')[l].rearrange("(k p) -> p k", p=128), w=['gcols'], slow=True)
        d('sp', self.g12b[:].rearrange("p h d -> p (h d)"), self.inp('g12')[l].partition_broadcast(128), w=['g12b'])
        d('sp', self.k0gb[:], self.inp('k0g')[l].partition_broadcast(128), w=['k0gb'])
        d('sp', self.mqgc[:], self.inp('mem_q_norm')[l].rearrange("(p o) -> p o", o=1), w=['mqgc'], slow=True)
        for k in range(3):
            d('sp', self.cwc[:, :, k], self.inp('conv_w')[l, k].rearrange("(c p) -> p c", p=128), w=['cwc'], slow=True)
        d('pool', self.w2sb[:], self.inp('cmp_w2')[l].rearrange("k h e -> h k e"), w=['w2sb'])
        d('pool', self.peT[:], self.inp('cmp_pe')[l].rearrange("k s d -> d k s"), w=['peT'], slow=True)
        self.memset('pool', self.ucar[:], 0.0, w=['ucar'])
        self.memset('pool', self.rawk[:, 0:16], 0.0, w=['rawk'])
        self.memset('pool', self.rawv[:, 0:16], 0.0, w=['rawv'])
        w1 = self.load_w1(l)
        pb, pk = self.psn()
        for kind in range(2):
            v, key = w1[kind]
            for s in range(32):
                self.mm(pb[:, kind:kind + 1], lhsT=v[0:64, s, :], rhs=self.peT[0:64, kind, s:s + 1],
                        start=(s == 0), stop=(s == 31), r=[key, 'peT'], w=[pk])
        self.cp('act', self.bpe[:], pb[:, 0:2], r=[pk], w=['bpe'])
        if self.stage >= 2:
            self.mem_kv(l)

    def load_w1(self, l):
        res = []
        for kind in range(2):
            buf, key = self.wr.next()
            v = buf[:, :].rearrange("p (s h) -> p s h", s=32)
            src = self.inp('cmp_w1')[l, kind].rearrange("(s d) h -> d s h", d=64)
            self.dma('pool', v[0:64], src, w=[key])
            self.dma('pool', v[64:128], src, w=[key])
            res.append((v, key))
        return res

    def mem_kv(self, l):
        scr_ = self.onsa[:].rearrange("p a b -> p (a b)")
        gmem_b = scr_[:, 0:1024]
        mkg_b = scr_[:, 1024:1536]
        self.dma('sp', gmem_b, self.inp('norm_mem')[l].partition_broadcast(128), w=['onsa'])
        self.dma('sp', mkg_b, self.inp('mkg')[l].partition_broadcast(128), w=['onsa'])
        wk, wkk = self.wload(self.inp('w_mem_kv')[l][:, 0:512], 8, 512)
        wv, wvk = self.wload(self.inp('w_mem_kv')[l][:, 512:1024], 8, 512)
        for mb in range(2):
            mt_, mtk = self.tmior.next()
            mt = mt_[:]
            self.dma('sp', mt, self.inp('mem')[mb * 128:(mb + 1) * 128, :], w=[mtk])
            sm, smk = self.smr.next()
            jk_, jkk = self.tmior.next()
            junk = jk_[:]
            self.actf(junk, mt, AF.Square, accum=sm[:, 0:1], r=[mtk], w=[jkk, smk])
            self.actf(sm[:, 1:2], sm[:, 0:1], AF.Sqrt, bias=self.epsc[:, 0:1], scale=1.0 / D, r=[smk, 'epsc'], w=[smk])
            self.recip(sm[:, 2:3], sm[:, 1:2], r=[smk], w=[smk])
            self.stt(junk, mt, sm[:, 2:3], gmem_b, ALU.mult, ALU.mult, r=[mtk, smk, 'onsa'], w=[jkk])
            if self.stage < 2.1:
                continue
            mnb, mnk = self.b16t()
            mnb2, mnk2 = self.b16t()
            self.cp('dve', mnb[:], junk[:, 0:512], r=[jkk], w=[mnk])
            self.cp('dve', mnb2[:], junk[:, 512:1024], r=[jkk], w=[mnk2])
            if self.stage < 2.12:
                continue
            pt, ptk = self.psn()
            ptb = pt[:].bitcast(BF16)
            for kc in range(8):
                src = (mnb if kc < 4 else mnb2)[:, (kc % 4) * 128:(kc % 4 + 1) * 128]
                self.tr(ptb[:, kc * 128:(kc + 1) * 128], src, self.ident_b[:], r=[mnk, mnk2, 'ident_b'], w=[ptk])
            if self.stage < 2.14:
                continue
            mnT, mnTk = self.b16t()
            mnT2, mnT2k = self.b16t()
            if self.stage != 2.15:
                self.cp('act', mnT[:], ptb[:, 0:512], r=[ptk], w=[mnTk])
            if self.stage != 2.16:
                self.cp('dve', mnT2[:], ptb[:, 512:1024], r=[ptk], w=[mnT2k])
            if self.stage < 2.2:
                continue
            pk_, pkk = self.psn()
            pv_, pvk = self.psn()
            for kc in range(8):
                lt = (mnT if kc < 4 else mnT2)[:, (kc % 4) * 128:(kc % 4 + 1) * 128]
                self.mm(pk_[:], lhsT=lt, rhs=wk[:, kc, :], start=(kc == 0), stop=(kc == 7), r=[mnTk, mnT2k, wkk], w=[pkk])
            for kc in range(8):
                lt = (mnT if kc < 4 else mnT2)[:, (kc % 4) * 128:(kc % 4 + 1) * 128]
                self.mm(pv_[:], lhsT=lt, rhs=wv[:, kc, :], start=(kc == 0), stop=(kc == 7), r=[mnTk, mnT2k, wvk], w=[pvk])
            if self.stage < 2.3:
                continue
            mo = junk
            kf, kfk = self.f32t()
            sq, sqk = self.f32t()
            self.cp('act', kf[:], pk_[:], r=[pkk], w=[kfk])
            self.cp('act', mo[:, 512:1024], pv_[:], r=[pvk], w=[jkk])
            self.cp('dve', self.mv[:, mb, :], pv_[:], r=[pvk], w=['mv'])
            self.tt('pool', sq[:], kf[:], kf[:], ALU.mult, r=[kfk], w=[sqk])
            if self.stage < 2.4:
                continue
            sm2, sm2k = self.smr.next()
            self.red(sm2[:, 0:4], sq[:].rearrange("p (h d) -> p h d", h=4), r=[sqk], w=[sm2k])
            self.actf(sm2[:, 4:8], sm2[:, 0:4], AF.Sqrt, bias=self.epsc[:, 0:1], scale=1.0 / 128, r=[sm2k, 'epsc'], w=[sm2k])
            self.recip(sm2[:, 8:12], sm2[:, 4:8], r=[sm2k], w=[sm2k])
            self.tt('dve', kf[:].rearrange("p (h d) -> p h d", h=4), kf[:].rearrange("p (h d) -> p h d", h=4),
                    sm2[:, 8:12].unsqueeze(2).to_broadcast([128, 4, 128]), ALU.mult, r=[kfk, sm2k], w=[kfk])
            self.tt('pool', mo[:, 0:512], kf[:], mkg_b, ALU.mult, r=[kfk, 'onsa'], w=[jkk])
            self.dma('sp', self.mem_o[l, mb * 128:(mb + 1) * 128, :], mo, r=[jkk], w=['mem_o'])
            if self.stage < 2.5:
                continue
            knb, knk = self.b16t()
            self.cp('pool', knb[:], mo[:, 0:512], r=[jkk], w=[knk])
            pt2, pt2k = self.psn()
            pt2b = pt2[:].bitcast(BF16)
            for hm in range(4):
                self.tr(pt2b[:, hm * 128:(hm + 1) * 128], knb[:, hm * 128:(hm + 1) * 128], self.ident_b[:], r=[knk, 'ident_b'], w=[pt2k])
            self.cp('act', self.mkT[:, :, mb * 128:(mb + 1) * 128], pt2b[:, 0:512].rearrange("p (h m) -> p h m", h=4), r=[pt2k], w=['mkT'])

    def load_x(self, l, t):
        if l == 0:
            for blk in range(4):
                xi, xik = self.tmior.next()
                self.dma('sp', xi[:], self.inp('x')[(t * 4 + blk) * 128:(t * 4 + blk + 1) * 128, :], w=[xik])
                for hf in range(2):
                    pb, pk = self.psn()
                    for c in range(4):
                        kc = hf * 4 + c
                        self.tr(pb[:, c * 128:(c + 1) * 128], xi[:, kc * 128:(kc + 1) * 128], self.ident_f[:], r=[xik, 'ident_f'], w=[pk])
                    self.cp('act' if hf == 0 else 'dve', self.xT[:, hf * 4:hf * 4 + 4, blk * 128:(blk + 1) * 128],
                            pb[:].rearrange("p (c f) -> p c f", c=4), r=[pk], w=[('xT', hf * 4 + c) for c in range(4)])
        else:
            for kc in range(8):
                self.dma('sp', self.xT[:, kc, :], self.hres[kc, :, t * TT:(t + 1) * TT], r=['hres'], w=[('xT', kc)])

    def rmsnorm(self, gi):
        ps, pk = self.psn()
        for kc in range(8):
            sq, sqk = self.b16t()
            self.actf(sq[:], self.xT[:, kc, :], AF.Square, r=[('xT', kc)], w=[sqk])
            self.mm(ps[:], lhsT=self.ones_b[:], rhs=sq[:], start=(kc == 0), stop=(kc == 7), r=['ones_b', sqk], w=[pk])
        rt, rtk = self.f32t()
        self.actf(rt[:], ps[:], AF.Sqrt, bias=self.epsc[:, 0:1], scale=1.0 / D, r=[pk, 'epsc'], w=[rtk])
        rs, rsk = self.f32t()
        self.recip(rs[:], rt[:], r=[rtk], w=[rsk])
        for kc in range(8):
            self.stt(self.xnT[:, kc, :], self.xT[:, kc, :], self.gcols[:, gi, kc:kc + 1], rs[:], ALU.mult, ALU.mult,
                     r=[('xT', kc), 'gcols', rsk], w=[('xnT', kc)])

    def headnorm_rope(self, hd, hk, nh, gain_b, cos_b, sin_b, inv_d):
        P = hd.shape[0]
        sq, sqk = self.tmr.next()
        sq = sq[0:P, 0:nh, :]
        sm, smk = self.smr.next()
        self.tt('pool', sq, hd, hd, ALU.mult, r=[hk], w=[sqk])
        self.red(sm[0:P, 0:nh], sq, r=[sqk], w=[smk])
        self.actf(sm[0:P, 12:12 + nh], sm[0:P, 0:nh], AF.Sqrt, bias=self.epsc[0:P, 0:1], scale=inv_d, r=[smk, 'epsc'], w=[smk])
        self.recip(sm[0:P, 0:nh], sm[0:P, 12:12 + nh], r=[smk], w=[smk])
        self.tt('dve', hd, hd, sm[0:P, 0:nh].unsqueeze(2).to_broadcast([P, nh, 64]), ALU.mult, r=[hk, smk], w=[hk])
        self.tt('pool', hd, hd, gain_b, ALU.mult, r=[hk, 'g12b'], w=[hk])
        hr, hrk = self.tmr.next()
        hr = hr[0:P, 0:nh, :]
        self.cp('pool', hr, hd, r=[hk], w=[hrk])
        tp, tpk = self.tmr.next()
        t1 = tp[0:P, 0:nh, 0:8]
        t2 = tp[0:P, 0:nh, 8:16]
        t3 = tp[0:P, 0:nh, 16:24]
        t4 = tp[0:P, 0:nh, 24:32]
        x1 = hd[:, :, 0:8]
        x2 = hd[:, :, 8:16]
        self.tt('dve', t1, x1, cos_b, ALU.mult, r=[hk, 'cosT'], w=[tpk])
        self.tt('dve', t2, x2, sin_b, ALU.mult, r=[hk, 'sinT'], w=[tpk])
        self.tt('dve', t3, x2, cos_b, ALU.mult, r=[hk, 'cosT'], w=[tpk])
        self.tt('dve', t4, x1, sin_b, ALU.mult, r=[hk, 'sinT'], w=[tpk])
        self.tt('dve', hr[:, :, 0:8], t1, t2, ALU.subtract, r=[tpk], w=[hrk])
        self.tt('dve', hr[:, :, 8:16], t3, t4, ALU.add, r=[tpk], w=[hrk])
        return hr, hrk

    def tokmajor(self, l, t, wA):
        (w0, w0k), (w1, w1k), (w2, w2k) = wA
        for blk in range(4):
            kb = t * 4 + blk
            cs = slice(blk * 128, (blk + 1) * 128)
            pq, pqk = self.psn()
            pa, pak = self.psn()
            pb, pbk = self.psn()
            for kc in range(8):
                lt = self.xnT[:, kc, cs]
                self.mm(pq[:], lhsT=lt, rhs=w0[:, kc, :], start=(kc == 0), stop=(kc == 7), r=[('xnT', kc), w0k], w=[pqk])
            for kc in range(8):
                lt = self.xnT[:, kc, cs]
                self.mm(pa[:], lhsT=lt, rhs=w1[:, kc, :], start=(kc == 0), stop=(kc == 7), r=[('xnT', kc), w1k], w=[pak])
            for kc in range(8):
                lt = self.xnT[:, kc, cs]
                self.mm(pb[:, 0:280], lhsT=lt, rhs=w2[:, kc, :], start=(kc == 0), stop=(kc == 7), r=[('xnT', kc), w2k], w=[pbk])
            hd, hk = self.tmr.next()
            self.cp('act', hd[:, 0:8, :].rearrange("p h d -> p (h d)"), pq[:], r=[pqk], w=[hk])
            self.cp('dve', hd[:, 8:10, :].rearrange("p h d -> p (h d)"), pa[:, 256:384], r=[pak], w=[hk])
            self.cp('dve', hd[:, 10:12, :].rearrange("p h d -> p (h d)"), pb[:, 0:128], r=[pbk], w=[hk])
            rows, rowsk = self.rowsr.next()
            self.cp('act', rows[:, 0:512], pa[:], r=[pak], w=[rowsk])
            self.cp('act', rows[:, 640:768], pb[:, 128:256], r=[pbk], w=[rowsk])
            self.actf(self.ngt[:, blk, :], pb[:, 256:280], AF.Sigmoid, r=[pbk], w=[('ngt', blk)])
            self.cp('dve', self.Vs[:, kb, :, 0:64], pa[:, 384:512].rearrange("p (g d) -> p g d", g=2), r=[pak], w=['Vs'])
            slot = (kb // 4) % 2 * 4 + kb % 4
            self.cp('dve', self.Vw[:, slot, :, 0:64], pb[:, 128:256].rearrange("p (g d) -> p g d", g=2), r=[pbk], w=['Vw'])
            cos_b = self.cosT[:, kb, :].unsqueeze(1).to_broadcast([128, 12, 8])
            sin_b = self.sinT[:, kb, :].unsqueeze(1).to_broadcast([128, 12, 8])
            hr, hrk = self.headnorm_rope(hd[:], hk, 12, self.g12b[:], cos_b, sin_b, 1.0 / 64)
            self.cp('pool', rows[:, 256:384], hr[:, 8:10, :].rearrange("p h d -> p (h d)"), r=[hrk], w=[rowsk])
            self.cp('pool', rows[:, 512:640], hr[:, 10:12, :].rearrange("p h d -> p (h d)"), r=[hrk], w=[rowsk])
            self.dma('sp', self.rows_o[l, kb * 128:(kb + 1) * 128, :], rows[:, 0:512], r=[rowsk], w=['rows_o'])
            if t == NTILE - 1:
                self.dma('sp', self.win_o[l, blk * 128:(blk + 1) * 128, :], rows[:, 512:768], r=[rowsk], w=['win_o'])
            ts_, tsk = self.tsrc.next()
            self.cp('pool', ts_[:, 0:4, :].rearrange("p c f -> p (c f)"), hd[:, 0:8, :].rearrange("p h d -> p (h d)"), r=[hk], w=[tsk])
            self.cp('pool', ts_[:, 4:8, :].rearrange("p c f -> p (c f)"), hr[:, 0:8, :].rearrange("p h d -> p (h d)"), r=[hrk], w=[tsk])
            self.cp('pool', ts_[:, 8:10, :].rearrange("p c f -> p (c f)"), hr[:, 8:12, :].rearrange("p h d -> p (h d)"), r=[hrk], w=[tsk])
            self.cp('act', ts_[:, 10:12, :].rearrange("p c f -> p (c f)"), pa[:, 0:256], r=[pak], w=[tsk])
            p0, p0k = self.psn()
            p1, p1k = self.psn()
            p0b = p0[:].bitcast(BF16)
            p1b = p1[:].bitcast(BF16)
            for c in range(8):
                self.tr(p0b[:, c * 128:(c + 1) * 128], ts_[:, c, :], self.ident_b[:], r=[tsk, 'ident_b'], w=[p0k])
            for c in range(4):
                self.tr(p1b[:, c * 128:(c + 1) * 128], ts_[:, 8 + c, :], self.ident_b[:], r=[tsk, 'ident_b'], w=[p1k])
            p0v = p0b.rearrange("p (c f) -> p c f", c=8)
            self.cp('act', self.QU[:, :, cs], p0v[:, 0:4, :], r=[p0k], w=['QU'])
            self.cp('dve', self.QB[0:64, 0::2, cs], p0v[0:64, 4:8, :], r=[p0k], w=['QBq'])
            self.cp('act', self.QB[0:64, 1::2, cs], p0v[64:128, 4:8, :], r=[p0k], w=['QBq'])
            ks = slice(kb * 128, (kb + 1) * 128)
            self.cp('dve', self.KE[0][0:64, ks], p1b[0:64, 0:128], r=[p1k], w=['KE0'])
            self.cp('act', self.KE[1][0:64, ks], p1b[64:128, 0:128], r=[p1k], w=['KE1'])
            wcs = slice((kb // 4) % 2 * 512 + (kb % 4) * 128, (kb // 4) % 2 * 512 + (kb % 4 + 1) * 128)
            self.cp('dve', self.kwT[0][0:64, wcs], p1b[0:64, 128:256], r=[p1k], w=['kwT0'])
            self.cp('act', self.kwT[1][0:64, wcs], p1b[64:128, 128:256], r=[p1k], w=['kwT1'])
            rs = slice(16 + blk * 128, 16 + (blk + 1) * 128)
            self.cp('dve', self.rawk[:, rs], p1b[:, 256:384], r=[p1k], w=['rawk'])
            self.cp('act', self.rawv[:, rs], p1b[:, 384:512], r=[p1k], w=['rawv'])

    def compress(self, l, t):
        w1 = self.load_w1(l)
        c0 = 32 * (t % 4)
        cc = t // 4
        for kind in range(2):
            v, key = w1[kind]
            raw, rawkey = (self.rawk, 'rawk') if kind == 0 else (self.rawv, 'rawv')
            for g in range(2):
                r0 = 64 * g
                ph, phk = self.psn()
                for s in range(32):
                    self.mm(ph[:, 0:32], lhsT=v[r0:r0 + 64, s, :], rhs=raw[r0:r0 + 64, s:s + 497:16],
                            start=(s == 0), stop=(s == 31), r=[key, rawkey], w=[phk])
                hs, hsk = self.b16t()
                self.actf(hs[:, 0:32], ph[:, 0:32], AF.Silu, bias=self.bpe[:, kind:kind + 1], r=[phk, 'bpe'], w=[hsk])
                pc, pck = self.psn()
                self.mm(pc[0:32, 0:64], lhsT=hs[:, 0:32], rhs=self.w2sb[:, kind, :], r=[hsk, 'w2sb'], w=[pck])
                if kind == 0:
                    kf, kfk = self.f32t()
                    sm, smk = self.smr.next()
                    self.actf(kf[0:32, 64:128], pc[0:32, 0:64], AF.Square, accum=sm[0:32, 0:1], r=[pck], w=[kfk, smk])
                    self.actf(sm[0:32, 1:2], sm[0:32, 0:1], AF.Sqrt, bias=self.epsc[0:32, 0:1], scale=1.0 / 64, r=[smk, 'epsc'], w=[smk])
                    self.recip(sm[0:32, 2:3], sm[0:32, 1:2], r=[smk], w=[smk])
                    kn, knk = self.b16t()
                    self.stt(kn[0:32, 0:64], pc[0:32, 0:64], sm[0:32, 2:3], self.k0gb[0:32, :], ALU.mult, ALU.mult,
                             r=[pck, smk, 'k0gb'], w=[knk])
                    self.cp('dve', kn[0:32, 64:128], kn[0:32, 0:64], r=[knk], w=[knk])
                    pt, ptk = self.psn()
                    ptb = pt[:].bitcast(BF16)
                    self.tr(ptb[:, 0:32], kn[0:32, 0:128], self.ident_b[0:32, 0:32], r=[knk, 'ident_b'], w=[ptk])
                    self.cp('act', self.kcT2[g][:, 32 * t:32 * t + 32], ptb[:, 0:32], r=[ptk], w=[f'kcT{g}'])
                else:
                    self.cp('act', self.Rc[g][c0:c0 + 32, cc, 0:64], pc[0:32, 0:64], r=[pck], w=[f'Rc{g}'])
                    if t == 0:
                        self.memset('pool', self.Rc[g][0:1, 0, 0:64], 0.0, w=[f'Rc{g}'])
        self.cp('pool', self.rawk[:, 0:16], self.rawk[:, 512:528], r=['rawk'], w=['rawk'])
        self.cp('pool', self.rawv[:, 0:16], self.rawv[:, 512:528], r=['rawv'], w=['rawv'])

    def fm_proj(self, l, t):
        w_in = self.inp('w_in')[l]
        wcx, wcxk = self.wload(w_in[:, C_CX:C_CX + 512], 8, 512)
        wcb, wcbk = self.wload(w_in[:, C_CB:C_CB + 512], 8, 512)
        wcc, wcck = self.wload(w_in[:, C_CC:C_CC + 512], 8, 512)
        xk = [('xnT', kc) for kc in range(8)]
        for ci in range(4):
            cs = slice(ci * 128, (ci + 1) * 128)
            px, pxk = self.psn()
            pc, pck = self.psn()
            pb, pbk = self.psn()
            for (pp, ppk, ww, wwk) in ((px, pxk, wcx, wcxk), (pc, pck, wcc, wcck), (pb, pbk, wcb, wcbk)):
                for kc in range(8):
                    self.mm(pp[:], lhsT=ww[:, kc, cs], rhs=self.xnT[:, kc, :], start=(kc == 0), stop=(kc == 7), r=[wwk, ('xnT', kc)], w=[ppk])
            cxs, cxk = self.f32t()
            self.cp('act', cxs[:], px[:], r=[pxk], w=[cxk])
            ue, uek = self.uextr.next()
            self.cp('pool', ue[:, 0:2], self.ucar[:, ci, :], r=['ucar'], w=[uek])
            self.tt('dve', ue[:, 2:514], pc[:], cxs[:], ALU.mult, r=[pck, cxk], w=[uek])
            a1, a1k = self.f32t()
            self.ts('pool', a1[:], ue[:, 0:512], self.cwc[:, ci, 0:1], ALU.mult, r=[uek, 'cwc'], w=[a1k])
            self.stt(a1[:], ue[:, 1:513], self.cwc[:, ci, 1:2], a1[:], ALU.mult, ALU.add, r=[uek, 'cwc', a1k], w=[a1k])
            self.stt(a1[:], ue[:, 2:514], self.cwc[:, ci, 2:3], a1[:], ALU.mult, ALU.add, r=[uek, 'cwc', a1k], w=[a1k])
            self.tt('dve', self.brT[:, 4 + ci, :], pb[:], a1[:], ALU.mult, r=[pbk, a1k], w=['brT1'])
            self.cp('pool', self.ucar[:, ci, :], ue[:, 512:514], r=[uek], w=['ucar'])
            if t == NTILE - 1:
                self.dma('sp', self.conv_o[l, :, cs].rearrange("j p -> p j"), ue[:, 512:514], r=[uek], w=['conv_o'], slow=True)

    def cmp_attend(self, t):
        nch = t // 4 + 1
        ngk = [('ngt', b) for b in range(4)]
        for h in range(8):
            g = h // 4
            r0 = 64 * (h % 2)
            pTs = []
            for cc in range(nch):
                sz = 128 if cc < nch - 1 else 32 * (t % 4 + 1)
                ps_, psk = self.psn()
                self.mm(ps_[0:sz, :], lhsT=self.kcT2[g][r0:r0 + 64, cc * 128:cc * 128 + sz], rhs=self.QU[r0:r0 + 64, h // 2, :],
                        r=[f'kcT{g}', 'QU'], w=[psk])
                pT, pTk = self.b16t()
                self.actf(pT[0:sz, :], ps_[0:sz, :], AF.Exp, scale=0.125, r=[psk], w=[pTk])
                if cc == nch - 1:
                    self.tt('pool', pT[sz - 32:sz, :], pT[sz - 32:sz, :], self.stair[sz - 32:sz, :], ALU.mult, r=[pTk, 'stair'], w=[pTk])
                pTs.append((pT, pTk, sz))
            banks = [self.psn(), self.psn()]
            for qb in range(4):
                bk, bkk = banks[qb // 2]
                off = (qb % 2) * 129
                for cc, (pT, pTk, sz) in enumerate(pTs):
                    self.mm(bk[:, off:off + 129], lhsT=pT[0:sz, qb * 128:(qb + 1) * 128], rhs=self.Rc[g][0:sz, cc, 0:129],
                            start=(cc == 0), stop=(cc == nch - 1), r=[pTk, f'Rc{g}'], w=[bkk])
            for qb in range(4):
                bk, bkk = banks[qb // 2]
                off = (qb % 2) * 129
                sm, smk = self.smr.next()
                self.ts('dve', sm[:, 0:1], bk[:, off + 64:off + 65], 1e-30, ALU.add, r=[bkk], w=[smk])
                self.recip(sm[:, 1:2], sm[:, 0:1], r=[smk], w=[smk])
                self.tt('dve', sm[:, 2:3], sm[:, 1:2], self.ngt[:, qb, h:h + 1], ALU.mult, r=[smk] + ngk, w=[smk])
                self.ts('dve', self.onsa[:, qb, h * 64:(h + 1) * 64], bk[:, off:off + 64], sm[:, 2:3], ALU.mult, r=[bkk, smk], w=['onsa'])
                if h % 4 == 0:
                    self.ts('dve', self.scr[:, qb, g, :], bk[:, off + 65:off + 129], sm[:, 1:2], ALU.mult, r=[bkk, smk], w=['scr'])
                else:
                    self.stt(self.scr[:, qb, g, :], bk[:, off + 65:off + 129], sm[:, 1:2], self.scr[:, qb, g, :], ALU.mult, ALU.add,
                             r=[bkk, smk, 'scr'], w=['scr'])

    def topk(self, t):
        sl, slk = self.selTr.next()
        self.dma('sp', sl[:, 0], self.inp('c_selA')[:, 4 * t:4 * t + 4, :], w=[slk])
        self.dma('sp', sl[:, 1], self.inp('c_selB')[:, 4 * t:4 * t + 4, :], w=[slk])
        for g in range(2):
            pt, ptk = self.psn()
            ptb = pt[:].bitcast(BF16)
            for qb in range(4):
                s2, s2k = self.selr.next()
                self.tt('dve', s2[:, 0, :], self.scr[:, qb, g, :], sl[:, 0, qb, :], ALU.mult, r=['scr', slk], w=[s2k])
                self.tt('dve', s2[:, 0, :], s2[:, 0, :], sl[:, 1, qb, :], ALU.add, r=[s2k, slk], w=[s2k])
                sm, smk = self.smr.next()
                self.S.add('dve', (lambda o, i: lambda e: e.max(out=o, in_=i))(sm[:, 0:8], s2[:, 0, :]), [s2k], [smk])
                self.S.add('dve', (lambda o, a, b: lambda e: e.match_replace(out=o, in_to_replace=a, in_values=b, imm_value=-1e30))(s2[:, 1, :], sm[:, 0:8], s2[:, 0, :]), [s2k, smk], [s2k])
                self.S.add('dve', (lambda o, i: lambda e: e.max(out=o, in_=i))(sm[:, 8:16], s2[:, 1, :]), [s2k], [smk])
                self.ts('dve', sm[:, 16:17], sm[:, 15:16], -1e8, ALU.max, r=[smk], w=[smk])
                self.ts('dve', s2[:, 1, :], s2[:, 0, :], sm[:, 16:17], ALU.is_ge, r=[s2k, smk], w=[s2k])
                bw, bwk = self.biasr.next()
                self.ts('dve', bw[:, 64:128], s2[:, 1, :], -1.0, ALU.add, -MASKV, ALU.mult, r=[s2k], w=[bwk])
                self.tr(ptb[:, qb * 128:(qb + 1) * 128], bw[:], self.ident_b[:], r=[bwk, 'ident_b'], w=[ptk])
            for hh in range(4):
                self.cp('act' if hh % 2 == 0 else 'dve', self.QB[64:128, g * 4 + hh, :], ptb[64:128, 0:512], r=[ptk], w=['QBb'])

    def norm_acc(self, acc, acck, h, ngoff):
        ngk = [('ngt', b) for b in range(4)]
        sm, smk = self.smr.next()
        self.recip(sm[:, 0:4], acc[:, 64:260:65], r=[acck], w=[smk])
        self.tt('dve', sm[:, 4:8], sm[:, 0:4], self.ngt[:, :, ngoff + h], ALU.mult, r=[smk] + ngk, w=[smk])
        for j in range(4):
            dst = self.onsa[:, j, h * 64:(h + 1) * 64]
            self.stt(dst, acc[:, j * 65:j * 65 + 64], sm[:, 4 + j:5 + j], dst, ALU.mult, ALU.add, r=[acck, smk, 'onsa'], w=['onsa'])

    def slc(self, t):
        for h in range(8):
            g = h // 4
            acc, acck = self.accr.next()

            def stage_a(kb):
                i0 = max(0, kb - 4 * t)
                c0 = 128 * i0
                ps_, psk = self.psn()
                self.mm(ps_[:, c0:512], lhsT=self.KE[g][:, kb * 128:(kb + 1) * 128], rhs=self.QB[:, h, c0:512],
                        r=[f'KE{g}', 'QBq', 'QBb'], w=[psk])
                pT, pTk = self.b16t()
                self.actf(pT[:, c0:512], ps_[:, c0:512], AF.Exp, scale=0.125, r=[psk], w=[pTk])
                if kb >= 4 * t:
                    self.tt('dve', pT[:, c0:c0 + 128], pT[:, c0:c0 + 128], self.tri[:], ALU.mult, r=[pTk, 'tri'], w=[pTk])
                return (kb, i0, pT, pTk)

            def stage_b(st, first):
                kb, i0, pT, pTk = st
                for j in range(i0, 4):
                    self.mm(acc[:, j * 65:(j + 1) * 65], lhsT=pT[:, j * 128:(j + 1) * 128], rhs=self.Vs[:, kb, g, 0:65],
                            start=(first and j == i0), stop=False, r=[pTk, 'Vs'], w=[acck])
            nkb = 4 * t + 4
            prev = stage_a(0)
            for kb in range(1, nkb):
                cur = stage_a(kb)
                stage_b(prev, prev[0] == 0)
                prev = cur
            stage_b(prev, prev[0] == 0)
            self.norm_acc(acc, acck, h, 8)

    def win(self, t):
        for h in range(8):
            g = h // 4
            acc, acck = self.accr.next()
            kbs = list(range(max(0, 4 * t - 4), 4 * t + 4))

            def stage_a(kb):
                jlo = max(0, kb - 4 * t)
                jhi = min(3, kb - 4 * t + 4)
                ring = (kb // 4) % 2
                wc0 = ring * 512 + (kb % 4) * 128
                slot = ring * 4 + kb % 4
                cl, ch = 128 * jlo, 128 * (jhi + 1)
                ps_, psk = self.psn()
                self.mm(ps_[:, cl:ch], lhsT=self.kwT[g][0:64, wc0:wc0 + 128], rhs=self.QB[0:64, h, cl:ch], r=[f'kwT{g}', 'QBq'], w=[psk])
                pT, pTk = self.b16t()
                self.actf(pT[:, cl:ch], ps_[:, cl:ch], AF.Exp, scale=0.125, r=[psk], w=[pTk])
                for j in range(jlo, jhi + 1):
                    d = 4 * t + j - kb
                    if d == 0:
                        self.tt('dve', pT[:, j * 128:(j + 1) * 128], pT[:, j * 128:(j + 1) * 128], self.tri[:], ALU.mult, r=[pTk, 'tri'], w=[pTk])
                    if d == 4:
                        self.tt('dve', pT[:, j * 128:(j + 1) * 128], pT[:, j * 128:(j + 1) * 128], self.strict[:], ALU.mult, r=[pTk, 'strict'], w=[pTk])
                return (kb, jlo, jhi, slot, pT, pTk)

            def stage_b(st, first):
                kb, jlo, jhi, slot, pT, pTk = st
                for j in range(jlo, jhi + 1):
                    self.mm(acc[:, j * 65:(j + 1) * 65], lhsT=pT[:, j * 128:(j + 1) * 128], rhs=self.Vw[:, slot, g, 0:65],
                            start=(first and j == jlo), stop=False, r=[pTk, 'Vw'], w=[acck])
            prev = stage_a(kbs[0])
            for kb in kbs[1:]:
                cur = stage_a(kb)
                stage_b(prev, prev[0] == kbs[0])
                prev = cur
            stage_b(prev, prev[0] == kbs[0])
            self.norm_acc(acc, acck, h, 16)

    def nsa_finalize(self):
        for j in range(4):
            ob, obk = self.b16t()
            self.cp('pool', ob[:], self.onsa[:, j, :], r=['onsa'], w=[obk])
            pt, ptk = self.psn()
            ptb = pt[:].bitcast(BF16)
            for c in range(4):
                self.tr(ptb[:, c * 128:(c + 1) * 128], ob[:, c * 128:(c + 1) * 128], self.ident_b[:], r=[obk, 'ident_b'], w=[ptk])
            self.cp('act', self.brT[:, 0:4, j * 128:(j + 1) * 128], ptb[:, 0:512].rearrange("p (c f) -> p c f", c=4), r=[ptk], w=['brT0'])

    def mem_attend(self):
        for hm in range(4):
            pTs = []
            for mb in range(2):
                ps_, psk = self.psn()
                self.mm(ps_[:], lhsT=self.mkT[:, hm, mb * 128:(mb + 1) * 128], rhs=self.mqT[:, hm, :], r=['mkT', 'mqT'], w=[psk])
                pT, pTk = self.b16t()
                self.actf(pT[:], ps_[:], AF.Exp, scale=128 ** -0.5, r=[psk], w=[pTk])
                pTs.append((pT, pTk))
            po, pok = self.psn()
            pd, pdk = self.psn()
            for mb, (pT, pTk) in enumerate(pTs):
                self.mm(po[:], lhsT=self.mv[:, mb, hm * 128:(hm + 1) * 128], rhs=pT[:], start=(mb == 0), stop=(mb == 1), r=['mv', pTk], w=[pok])
            for mb, (pT, pTk) in enumerate(pTs):
                self.mm(pd[:], lhsT=self.ones_b[:], rhs=pT[:], start=(mb == 0), stop=(mb == 1), r=['ones_b', pTk], w=[pdk])
            rc, rck = self.f32t()
            self.recip(rc[:], pd[:], r=[pdk], w=[rck])
            self.tt('dve', self.brT[:, 8 + hm, :], po[:], rc[:], ALU.mult, r=[pok, rck], w=['brT2'])

    def phase_b(self, l):
        w_in = self.inp('w_in')[l]
        QBK = ['QBq', 'QBb']
        for fcg in range(2):
            for n in range(3):
                wm, wmk = self.wload(w_in[:, C_MG + n * 1024 + fcg * 512:C_MG + n * 1024 + (fcg + 1) * 512], 8, 512)
                wb, wbk = self.wload(self.inp('w_branch')[l, n][:, fcg * 512:(fcg + 1) * 512], 4, 512)
                for fi in range(4):
                    fc = fcg * 4 + fi
                    cs = slice(fi * 128, (fi + 1) * 128)
                    pg, pgk = self.psn()
                    pp, ppk = self.psn()
                    for kc in range(8):
                        self.mm(pg[:], lhsT=wm[:, kc, cs], rhs=self.xnT[:, kc, :], start=(kc == 0), stop=(kc == 7), r=[wmk, ('xnT', kc)], w=[pgk])
                    for kc in range(4):
                        self.mm(pp[:], lhsT=wb[:, kc, cs], rhs=self.brT[:, n * 4 + kc, :], start=(kc == 0), stop=(kc == 3), r=[wbk, f'brT{n}'], w=[ppk])
                    sg, sgk = self.f32t()
                    self.actf(sg[:], pg[:], AF.Sigmoid, r=[pgk], w=[sgk])
                    if n == 0:
                        self.tt('dve', self.macc[:, fi, :], sg[:], pp[:], ALU.mult, r=[sgk, ppk], w=['onsa'])
                    else:
                        self.tt('dve', sg[:], sg[:], pp[:], ALU.mult, r=[sgk, ppk], w=[sgk])
                        if n == 1:
                            self.tt('pool', self.macc[:, fi, :], self.macc[:, fi, :], sg[:], ALU.add, r=[sgk, 'onsa'], w=['onsa'])
                        else:
                            self.tt('pool', self.mT[:, fc, :], self.macc[:, fi, :], sg[:], ALU.add, r=[sgk, 'onsa'], w=QBK)

    def phase_c(self, l):
        QBK = ['QBq', 'QBb']
        for half in range(2):
            wo, wok = self.wload(self.inp('w_out')[l][:, half * 512:(half + 1) * 512], 8, 512)
            for fi in range(4):
                fc = half * 4 + fi
                po, pok = self.psn()
                for kc in range(8):
                    self.mm(po[:], lhsT=wo[:, kc, fi * 128:(fi + 1) * 128], rhs=self.mT[:, kc, :], start=(kc == 0), stop=(kc == 7), r=[wok] + QBK, w=[pok])
                self.tt('dve', self.xT[:, fc, :], po[:], self.xT[:, fc, :], ALU.add, r=[pok, ('xT', fc)], w=[('xT', fc)])

    def phase_d(self, l):
        self.rmsnorm(1)
        w_gu = self.inp('w_gate_up')[l]
        w_dn = self.inp('w_down')[l]
        groups = [(0, 6), (6, 6), (12, 5), (17, 5)]
        for (j0, n) in groups:
            for p0 in range(0, n, 4):
                pn = min(4, n - p0)
                ja = j0 + p0
                wg, wgk = self.wload(w_gu[:, ja * 128:(ja + pn) * 128], 8, pn * 128)
                wu, wuk = self.wload(w_gu[:, DFF + ja * 128:DFF + (ja + pn) * 128], 8, pn * 128)
                for q in range(pn):
                    jj = p0 + q
                    cs = slice(q * 128, (q + 1) * 128)
                    pg, pgk = self.psn()
                    pu, puk = self.psn()
                    for kc in range(8):
                        self.mm(pg[:], lhsT=wg[:, kc, cs], rhs=self.xnT[:, kc, :], start=(kc == 0), stop=(kc == 7), r=[wgk, ('xnT', kc)], w=[pgk])
                    for kc in range(8):
                        self.mm(pu[:], lhsT=wu[:, kc, cs], rhs=self.xnT[:, kc, :], start=(kc == 0), stop=(kc == 7), r=[wuk, ('xnT', kc)], w=[puk])
                    sg, sgk = self.f32t()
                    self.actf(sg[:], pg[:], AF.Silu, r=[pgk], w=[sgk])
                    self.tt('dve', self.actT[:, jj, :], sg[:], pu[:], ALU.mult, r=[sgk, puk], w=[('actT', jj)])
            for cq in range(4):
                wd, wdk = self.wload(w_dn[j0 * 128:(j0 + n) * 128, cq * 256:(cq + 1) * 256], n, 256)
                for fi in range(2):
                    fc = cq * 2 + fi
                    pd, pdk = self.psn()
                    for kc in range(n):
                        self.mm(pd[:], lhsT=wd[:, kc, fi * 128:(fi + 1) * 128], rhs=self.actT[:, kc, :], start=(kc == 0), stop=(kc == n - 1),
                                r=[wdk, ('actT', kc)], w=[pdk])
                    self.tt('dve', self.xT[:, fc, :], pd[:], self.xT[:, fc, :], ALU.add, r=[pdk, ('xT', fc)], w=[('xT', fc)])

    def store_x(self, l, t):
        xk = [('xT', kc) for kc in range(8)]
        if l < self.nlayers - 1:
            self.dma('sp', self.hres[:, :, t * TT:(t + 1) * TT].rearrange("k p t -> p k t"), self.xT[:], r=xk, w=['hres'])
        else:
            for blk in range(4):
                xo, xok = self.tmior.next()
                for hf in range(2):
                    pb, pk = self.psn()
                    for c in range(4):
                        kc = hf * 4 + c
                        self.tr(pb[:, c * 128:(c + 1) * 128], self.xT[:, kc, blk * 128:(blk + 1) * 128], self.ident_f[:], r=[('xT', kc), 'ident_f'], w=[pk])
                    self.cp('act' if hf == 0 else 'dve', xo[:, hf * 512:(hf + 1) * 512], pb[:], r=[pk], w=[xok])
                self.dma('sp', self.y[(t * 4 + blk) * 128:(t * 4 + blk + 1) * 128, :], xo[:], r=[xok], w=['y'])

    def make_ctxs(self):
        class C:
            pass
        P = C()
        P.N, P.k = TT, ''
        P.xT, P.xnT, P.brT, P.mT, P.macc, P.actT, P.mqT = self.xT, self.xnT, self.brT, self.mT, self.macc, self.actT, self.mqT
        P.mTk, P.mack = ['QBq', 'QBb'], 'onsa'
        self.P = P
        X = C()
        X.N, X.k = 4, 's_'
        sb = self.S.sb
        X.xT = sb([128, 8, 4], F32, 's_xT')
        X.xnT = sb([128, 8, 4], BF16, 's_xnT')
        X.brT = sb([128, 12, 4], BF16, 's_brT')
        X.mT = sb([128, 8, 4], BF16, 's_mT')
        X.macc = sb([128, 4, 4], F32, 's_macc')
        X.actT = sb([128, 6, 4], BF16, 's_actT')
        X.mqT = sb([128, 4, 4], BF16, 's_mqT')
        X.mTk, X.mack = ['s_mT'], 's_macc'
        self.X = X

    def rmsnorm(self, gi, c=None):
        c = c or self.P
        N, k = c.N, c.k
        ps, pk = self.psn()
        for kc in range(8):
            sq, sqk = self.b16t()
            self.actf(sq[:, 0:N], c.xT[:, kc, :], AF.Square, r=[(k + 'xT', kc)], w=[sqk])
            self.mm(ps[:, 0:N], lhsT=self.ones_b[:], rhs=sq[:, 0:N], start=(kc == 0), stop=(kc == 7), r=['ones_b', sqk], w=[pk])
        rt, rtk = self.f32t()
        self.actf(rt[:, 0:N], ps[:, 0:N], AF.Sqrt, bias=self.epsc[:, 0:1], scale=1.0 / D, r=[pk, 'epsc'], w=[rtk])
        rs, rsk = self.f32t()
        self.recip(rs[:, 0:N], rt[:, 0:N], r=[rtk], w=[rsk])
        for kc in range(8):
            self.stt(c.xnT[:, kc, :], c.xT[:, kc, :], self.gcols[:, gi, kc:kc + 1], rs[:, 0:N], ALU.mult, ALU.mult,
                     r=[(k + 'xT', kc), 'gcols', rsk], w=[(k + 'xnT', kc)])

    def mq_proj(self, l, c=None):
        c = c or self.P
        N, k = c.N, c.k
        wmq, wmqk = self.wload(self.inp('w_in')[l][:, C_MQ:C_MQ + 512], 8, 512)
        for hm in range(4):
            cs = slice(hm * 128, (hm + 1) * 128)
            pm, pmk = self.psn()
            for kc in range(8):
                self.mm(pm[:, 0:N], lhsT=wmq[:, kc, cs], rhs=c.xnT[:, kc, :], start=(kc == 0), stop=(kc == 7), r=[wmqk, (k + 'xnT', kc)], w=[pmk])
            sq, sqk = self.b16t()
            self.actf(sq[:, 0:N], pm[:, 0:N], AF.Square, r=[pmk], w=[sqk])
            pss, pssk = self.psn()
            self.mm(pss[:, 0:N], lhsT=self.ones_b[:], rhs=sq[:, 0:N], r=['ones_b', sqk], w=[pssk])
            rt, rtk = self.f32t()
            self.actf(rt[:, 0:N], pss[:, 0:N], AF.Sqrt, bias=self.epsc[:, 0:1], scale=1.0 / 128, r=[pssk, 'epsc'], w=[rtk])
            self.recip(rt[:, 0:N], rt[:, 0:N], r=[rtk], w=[rtk])
            self.stt(c.mqT[:, hm, :], pm[:, 0:N], self.mqgc[:, 0:1], rt[:, 0:N], ALU.mult, ALU.mult, r=[pmk, 'mqgc', rtk], w=[k + 'mqT'])

    def phase_b(self, l, ctxs=None):
        ctxs = ctxs or [self.P]
        w_in = self.inp('w_in')[l]
        for fcg in range(2):
            for n in range(3):
                wm, wmk = self.wload(w_in[:, C_MG + n * 1024 + fcg * 512:C_MG + n * 1024 + (fcg + 1) * 512], 8, 512)
                wb, wbk = self.wload(self.inp('w_branch')[l, n][:, fcg * 512:(fcg + 1) * 512], 4, 512)
                for fi in range(4):
                    fc = fcg * 4 + fi
                    cs = slice(fi * 128, (fi + 1) * 128)
                    for c in ctxs:
                        N, k = c.N, c.k
                        pg, pgk = self.psn()
                        pp, ppk = self.psn()
                        for kc in range(8):
                            self.mm(pg[:, 0:N], lhsT=wm[:, kc, cs], rhs=c.xnT[:, kc, :], start=(kc == 0), stop=(kc == 7), r=[wmk, (k + 'xnT', kc)], w=[pgk])
                        for kc in range(4):
                            self.mm(pp[:, 0:N], lhsT=wb[:, kc, cs], rhs=c.brT[:, n * 4 + kc, :], start=(kc == 0), stop=(kc == 3), r=[wbk, f'{k}brT{n}'], w=[ppk])
                        sg, sgk = self.f32t()
                        self.actf(sg[:, 0:N], pg[:, 0:N], AF.Sigmoid, r=[pgk], w=[sgk])
                        if n == 0:
                            self.tt('dve', c.macc[:, fi, :], sg[:, 0:N], pp[:, 0:N], ALU.mult, r=[sgk, ppk], w=[c.mack])
                        else:
                            self.tt('dve', sg[:, 0:N], sg[:, 0:N], pp[:, 0:N], ALU.mult, r=[sgk, ppk], w=[sgk])
                            if n == 1:
                                self.tt('pool', c.macc[:, fi, :], c.macc[:, fi, :], sg[:, 0:N], ALU.add, r=[sgk, c.mack], w=[c.mack])
                            else:
                                self.tt('pool', c.mT[:, fc, :], c.macc[:, fi, :], sg[:, 0:N], ALU.add, r=[sgk, c.mack], w=c.mTk)

    def phase_c(self, l, ctxs=None):
        ctxs = ctxs or [self.P]
        for half in range(2):
            wo, wok = self.wload(self.inp('w_out')[l][:, half * 512:(half + 1) * 512], 8, 512)
            for fi in range(4):
                fc = half * 4 + fi
                for c in ctxs:
                    N, k = c.N, c.k
                    po, pok = self.psn()
                    for kc in range(8):
                        self.mm(po[:, 0:N], lhsT=wo[:, kc, fi * 128:(fi + 1) * 128], rhs=c.mT[:, kc, :], start=(kc == 0), stop=(kc == 7), r=[wok] + c.mTk, w=[pok])
                    self.tt('dve', c.xT[:, fc, :], po[:, 0:N], c.xT[:, fc, :], ALU.add, r=[pok, (k + 'xT', fc)], w=[(k + 'xT', fc)])

    def phase_d(self, l, ctxs=None):
        ctxs = ctxs or [self.P]
        for c in ctxs:
            self.rmsnorm(1, c)
        w_gu = self.inp('w_gate_up')[l]
        w_dn = self.inp('w_down')[l]
        groups = [(0, 6), (6, 6), (12, 5), (17, 5)]
        for (j0, n) in groups:
            for p0 in range(0, n, 4):
                pn = min(4, n - p0)
                ja = j0 + p0
                wg, wgk = self.wload(w_gu[:, ja * 128:(ja + pn) * 128], 8, pn * 128)
                wu, wuk = self.wload(w_gu[:, DFF + ja * 128:DFF + (ja + pn) * 128], 8, pn * 128)
                for q in range(pn):
                    jj = p0 + q
                    cs = slice(q * 128, (q + 1) * 128)
                    for c in ctxs:
                        N, k = c.N, c.k
                        pg, pgk = self.psn()
                        pu, puk = self.psn()
                        for kc in range(8):
                            self.mm(pg[:, 0:N], lhsT=wg[:, kc, cs], rhs=c.xnT[:, kc, :], start=(kc == 0), stop=(kc == 7), r=[wgk, (k + 'xnT', kc)], w=[pgk])
                        for kc in range(8):
                            self.mm(pu[:, 0:N], lhsT=wu[:, kc, cs], rhs=c.xnT[:, kc, :], start=(kc == 0), stop=(kc == 7), r=[wuk, (k + 'xnT', kc)], w=[puk])
                        sg, sgk = self.f32t()
                        self.actf(sg[:, 0:N], pg[:, 0:N], AF.Silu, r=[pgk], w=[sgk])
                        self.tt('dve', c.actT[:, jj, :], sg[:, 0:N], pu[:, 0:N], ALU.mult, r=[sgk, puk], w=[(k + 'actT', jj)])
            for cq in range(4):
                wd, wdk = self.wload(w_dn[j0 * 128:(j0 + n) * 128, cq * 256:(cq + 1) * 256], n, 256)
                for fi in range(2):
                    fc = cq * 2 + fi
                    for c in ctxs:
                        N, k = c.N, c.k
                        pd, pdk = self.psn()
                        for kc in range(n):
                            self.mm(pd[:, 0:N], lhsT=wd[:, kc, fi * 128:(fi + 1) * 128], rhs=c.actT[:, kc, :], start=(kc == 0), stop=(kc == n - 1),
                                    r=[wdk, (k + 'actT', kc)], w=[pdk])
                        self.tt('dve', c.xT[:, fc, :], pd[:, 0:N], c.xT[:, fc, :], ALU.add, r=[pdk, (k + 'xT', fc)], w=[(k + 'xT', fc)])

    def declare_sample(self):
        i = self.din
        i("xs", [4, D]); i("pool", [DEPTH, 2560 * 128, 512]); i("cwin", [DEPTH, 4, 512, 256]); i("sconv", [DEPTH, 4, 2, 512])
        i("cmem", [DEPTH, 4, 256, 1024]); i("pt", [4, 64], I32)
        i("c_cos_s", [4, 8]); i("c_sin_s", [4, 8]); i("c_ovl_s", [128, 4, 129]); i("c_sA", [1, 129]); i("c_sB", [1, 129])
        i("c_e2", [2, 128]); i("c_lastmask", [128, 1]); i("c_winb", [128, 1]); i("c_pm64", [128, 1]); i("c_pidx", [128, 1])
        o = self.dout
        self.y_s = o("y_s", [4, D]); self.rows_s_o = o("rows_s", [DEPTH, 4, 512]); self.win_s_o = o("win_s", [DEPTH, 4, 512, 256])
        self.conv_s_o = o("conv_s", [DEPTH, 4, 2, 512])
        dr = lambda n, sh: self.nc.dram_tensor(n, sh, F32).ap()
        self.scr_rows = dr("scr_rows", [4, 768]); self.scr_q = dr("scr_q", [4, 512]); self.osc = dr("osc", [4, 3, 8, 64])
        sb = self.S.sb
        self.idxA = sb([128, 4, 32], I32, 'idxA'); self.idxB = sb([128, 4, 64], I32, 'idxB')
        self.qT_s = sb([128, 8, 4], BF16, 'qT_s'); self.ng_s = sb([4, 24], F32, 'ng_s')
        self.kcT_s = sb([128, 512], BF16, 'kcT_s'); self.Rs = [sb([128, 4, 194], BF16, f'Rs{g}') for g in range(2)]
        self.biask = sb([128, 2, 64], F32, 'biask'); self.cos_s = sb([4, 8], F32, 'cos_s'); self.sin_s = sb([4, 8], F32, 'sin_s')
        self.sA = sb([1, 129], F32, 'sA'); self.sB = sb([1, 129], F32, 'sB'); self.e2a = sb([1, 128], F32, 'e2a'); self.e2b = sb([1, 128], F32, 'e2b')
        self.lastm = sb([128, 1], F32, 'lastm'); self.winb = sb([128, 1], F32, 'winb'); self.ones_f = sb([4, 1], F32, 'ones_f')
        self.Vp = Rot([sb([128, 2, 66], BF16, f'Vp{k}') for k in range(2)], 'Vp')
        self.stT = sb([128, 4, 2, 4], F32, 'stT'); self.cso = sb([128, 4, 2, 4], F32, 'cso')
        of_ = self.onsa[:].rearrange("p a b -> p (a b)")
        self.nr = of_[0:1, 0:768]; self.nq = of_[0:1, 768:1280]; self.vne = sb([1, 2, 66], BF16, 'vne')

    def setup_sample(self):
        d = self.dma
        d('sp', self.cos_s[:], self.inp('c_cos_s'), w=['cos_s']); d('sp', self.sin_s[:], self.inp('c_sin_s'), w=['sin_s'])
        d('sp', self.sA[:], self.inp('c_sA'), w=['sA']); d('sp', self.sB[:], self.inp('c_sB'), w=['sB'])
        d('sp', self.e2a[:], self.inp('c_e2')[0:1, :], w=['e2a']); d('sp', self.e2b[:], self.inp('c_e2')[1:2, :], w=['e2b'])
        d('sp', self.lastm[:], self.inp('c_lastmask'), w=['lastm']); d('sp', self.winb[:], self.inp('c_winb'), w=['winb'])
        self.memset('pool', self.ones_f[:], 1.0, w=['ones_f'])
        for g in range(2):
            self.memset('pool', self.Rs[g][:, :, 64:65], 1.0, w=[f'Rs{g}'])
            d('pool', self.Rs[g][:, :, 65:194], self.inp('c_ovl_s'), w=[f'Rs{g}'])
        for k in range(2):
            self.memset('pool', self.Vp.bufs[k][:, :, 64:65], 1.0, w=[f'Vp{k}'])
        self.memset('pool', self.vne[:, :, 64:65], 1.0, w=['vne'])
        pt = self.inp('pt')
        pa, pak = self.selTr.next()
        ptbA = pa[:].rearrange("p a b c -> p (a b c)")[:, 0:128].bitcast(I32).rearrange("p (s q) -> p s q", s=4)
        ptbB = pa[:].rearrange("p a b c -> p (a b c)")[:, 128:384].bitcast(I32).rearrange("p (s q) -> p s q", s=4)
        pm, pmk = self.smr.next()
        d('sp', pm[:, 0:1], self.inp('c_pm64'), w=[pmk]); d('sp', pm[:, 1:2], self.inp('c_pidx'), w=[pmk])
        for h in range(2):
            d('sp', ptbA[64 * h:64 * h + 64], pt[:, h::2].partition_broadcast(64), w=[pak], slow=True)
        d('sp', ptbB, pt.partition_broadcast(128), w=[pak])
        self.ts('dve', self.idxA[:], ptbA, 64.0, ALU.mult, pm[:, 0:1], ALU.add, r=[pak, pmk], w=['idxA'])
        self.ts('dve', self.idxB[:], ptbB, 128.0, ALU.mult, pm[:, 1:2], ALU.add, r=[pak, pmk], w=['idxB'])

    def gather(self, dst, dkey, src, idx_ap, ikey, eoff):
        self.S.add('pool', lambda e: e.indirect_dma_start(out=dst, out_offset=None, in_=src, element_offset=eoff,
                                                           in_offset=bass.IndirectOffsetOnAxis(ap=idx_ap, axis=0)), [ikey], [dkey], dma=True)

    def sample_layer(self, l):
        X = self.X
        d = self.dma
        if l == 0:
            for s_ in range(4):
                d('sp', X.xT[:, :, s_], self.inp('xs')[s_].rearrange("(k p) -> p k", p=128), w=[('s_xT', kc) for kc in range(8)], slow=True)
        self.rmsnorm(0, X)
        w_in = self.inp('w_in')[l]
        wA = [self.wload(w_in[:, 0:512], 8, 512), self.wload(w_in[:, 512:1024], 8, 512), self.wload(w_in[:, 1024:1304], 8, 280)]
        pq, pqk = self.psn(); pa, pak = self.psn(); pb, pbk = self.psn()
        for (pp, ppk, (w, wk), nn) in ((pq, pqk, wA[0], 512), (pa, pak, wA[1], 512), (pb, pbk, wA[2], 280)):
            for kc in range(8):
                self.mm(pp[0:4, 0:nn], lhsT=X.xnT[:, kc, :], rhs=w[:, kc, :], start=(kc == 0), stop=(kc == 7), r=[('s_xnT', kc), wk], w=[ppk])
        hd_, hk = self.tmr.next()
        hd = hd_[0:4]
        self.cp('act', hd[:, 0:8, :].rearrange("p h d -> p (h d)"), pq[0:4, :], r=[pqk], w=[hk])
        self.cp('dve', hd[:, 8:10, :].rearrange("p h d -> p (h d)"), pa[0:4, 256:384], r=[pak], w=[hk])
        self.cp('dve', hd[:, 10:12, :].rearrange("p h d -> p (h d)"), pb[0:4, 0:128], r=[pbk], w=[hk])
        rows_, rowsk = self.rowsr.next()
        rows = rows_[0:4]
        self.cp('act', rows[:, 0:512], pa[0:4, :], r=[pak], w=[rowsk])
        self.cp('act', rows[:, 640:768], pb[0:4, 128:256], r=[pbk], w=[rowsk])
        self.actf(self.ng_s[:], pb[0:4, 256:280], AF.Sigmoid, r=[pbk], w=['ng_s'])
        cos_b = self.cos_s[:, :].unsqueeze(1).to_broadcast([4, 12, 8])
        sin_b = self.sin_s[:, :].unsqueeze(1).to_broadcast([4, 12, 8])
        hr, hrk = self.headnorm_rope(hd, hk, 12, self.g12b[0:4], cos_b, sin_b, 1.0 / 64)
        self.cp('pool', rows[:, 256:384], hr[:, 8:10, :].rearrange("p h d -> p (h d)"), r=[hrk], w=[rowsk])
        self.cp('pool', rows[:, 512:640], hr[:, 10:12, :].rearrange("p h d -> p (h d)"), r=[hrk], w=[rowsk])
        d('sp', self.rows_s_o[l], rows[:, 0:512], r=[rowsk], w=['rows_s_o'])
        d('sp', self.scr_rows, rows[:, 0:768], r=[rowsk], w=['scr_rows'])
        d('sp', self.scr_q, hr[:, 0:8, :].rearrange("p h d -> p (h d)"), r=[hrk], w=['scr_q'])
        d('sp', self.win_s_o[l, :, 511, :], rows[:, 512:768], r=[rowsk], w=['win_s_a'])
        d('sp', self.win_s_o[l, :, 0:511, :], self.inp('cwin')[l, :, 1:512, :], w=['win_s_b'])
        ts_, tsk = self.tsrc.next()
        tq = ts_[0:4]
        self.cp('pool', tq[:, 0:4, :].rearrange("p c (g d) -> p c g d", g=2), hd[:, 0:8, :].rearrange("p (g c) d -> p c g d", g=2), r=[hk], w=[tsk])
        self.cp('pool', tq[:, 4:8, :].rearrange("p c (g d) -> p c g d", g=2), hr[:, 0:8, :].rearrange("p (g c) d -> p c g d", g=2), r=[hrk], w=[tsk])
        p0, p0k = self.psn()
        p0b = p0[:].bitcast(BF16)
        for c in range(8):
            self.tr(p0b[:, c * 4:(c + 1) * 4], tq[:, c, :], self.ident_b[0:4, 0:4], r=[tsk, 'ident_b'], w=[p0k])
        self.cp('act', self.qT_s[:], p0b[:, 0:32].rearrange("p (c s) -> p c s", c=8), r=[p0k], w=['qT_s'])
        for s_ in range(4):
            for j_ in range(2):
                d('sp', self.stT[:, :, j_, s_], self.inp('sconv')[l, s_, j_].rearrange("(c p) -> p c", p=128), w=['stT'], slow=True)
        wcx, wcxk = self.wload(w_in[:, C_CX:C_CX + 512], 8, 512)
        wcb, wcbk = self.wload(w_in[:, C_CB:C_CB + 512], 8, 512)
        wcc, wcck = self.wload(w_in[:, C_CC:C_CC + 512], 8, 512)
        for ci in range(4):
            cs = slice(ci * 128, (ci + 1) * 128)
            px, pxk = self.psn(); pc, pck = self.psn(); pb2, pb2k = self.psn()
            for (pp, ppk, ww, wwk) in ((px, pxk, wcx, wcxk), (pc, pck, wcc, wcck), (pb2, pb2k, wcb, wcbk)):
                for kc in range(8):
                    self.mm(pp[:, 0:4], lhsT=ww[:, kc, cs], rhs=X.xnT[:, kc, :], start=(kc == 0), stop=(kc == 7), r=[wwk, ('s_xnT', kc)], w=[ppk])
            cxs, cxk = self.f32t()
            self.cp('act', cxs[:, 0:4], px[:, 0:4], r=[pxk], w=[cxk])
            self.tt('dve', self.cso[:, ci, 1, :], pc[:, 0:4], cxs[:, 0:4], ALU.mult, r=[pck, cxk], w=['cso'])
            self.cp('pool', self.cso[:, ci, 0, :], self.stT[:, ci, 1, :], r=['stT'], w=['cso'])
            a1, a1k = self.f32t()
            self.ts('pool', a1[:, 0:4], self.stT[:, ci, 0, :], self.cwc[:, ci, 0:1], ALU.mult, r=['stT', 'cwc'], w=[a1k])
            self.stt(a1[:, 0:4], self.stT[:, ci, 1, :], self.cwc[:, ci, 1:2], a1[:, 0:4], ALU.mult, ALU.add, r=['stT', 'cwc', a1k], w=[a1k])
            self.stt(a1[:, 0:4], self.cso[:, ci, 1, :], self.cwc[:, ci, 2:3], a1[:, 0:4], ALU.mult, ALU.add, r=['cso', 'cwc', a1k], w=[a1k])
            self.tt('dve', X.brT[:, 4 + ci, :], pb2[:, 0:4], a1[:, 0:4], ALU.mult, r=[pb2k, a1k], w=['s_brT1'])
        for s_ in range(4):
            for j_ in range(2):
                d('sp', self.conv_s_o[l, s_, j_].rearrange("(c p) -> p c", p=128), self.cso[:, :, j_, s_], r=['cso'], w=['conv_s_o'], slow=True)
        self.mq_proj(l, X)
        w1b, w1nk = self.wr.next()
        self.w1n = w1b[:, :].rearrange("p (k c h) -> p k c h", k=2, c=16)
        for kind in range(2):
            d('pool', self.w1n[:, kind], self.inp('cmp_w1')[l, kind].rearrange("(c p) h -> p c h", p=128), w=[w1nk])
        pool_l = self.inp('pool').rearrange("l r f -> (l r) f")
        pool_rp = self.inp('pool').rearrange("l (r two) f -> (l r) (two f)", two=2)
        eoff = l * 2560 * 128 * 512
        psr2 = Rot([self.psr.bufs[4], self.psr.bufs[5], self.accr.bufs[0]], 'x')
        keys2 = ['ps4', 'ps5', 'acc0']

        def ps2():
            k = psr2.i % 3
            b, _ = psr2.next()
            return b, keys2[k]
        Hps = [(self.psr.bufs[k], f'ps{k}') for k in range(4)]
        for s in range(4):
            for pp in range(32):
                g1, g1k = self.tmior.next()
                self.gather(g1[:, :], g1k, pool_rp, self.idxA[:, s, pp:pp + 1], 'idxA', eoff)
                pk, pkk = self.b16t()
                self.cp('dve' if pp % 2 == 0 else 'pool', pk[:].rearrange("p (kg s d) -> p kg s d", kg=4, s=2),
                        g1[:].rearrange("p (s kg d) -> p kg s d", s=2, kg=8)[:, 0:4], r=[g1k], w=[pkk])
                pt_, ptk = ps2()
                ptb = pt_[:].bitcast(BF16)
                for kg in range(4):
                    self.tr(ptb[:, kg * 128:(kg + 1) * 128], pk[:, kg * 128:(kg + 1) * 128], self.ident_b[:], r=[pkk, 'ident_b'], w=[ptk])
                xp, xpk = self.b16t()
                self.cp('act', xp[:], ptb[:, 0:512], r=[ptk], w=[xpk])
                for kg in range(4):
                    kind = kg // 2
                    H, Hk = Hps[kg]
                    for s8 in range(8):
                        self.mm(H[:, 16 * pp:16 * pp + 16], lhsT=self.w1n[:, kind, s8, :], rhs=xp[:, kg * 128 + s8:kg * 128 + 128:8],
                                start=(pp == 0 and s8 == 0), stop=False, r=[w1nk, xpk], w=[Hk])
                    for s8 in range(8):
                        if pp == 0:
                            self.mm(H[:, 0:15], lhsT=self.w1n[:, kind, 8 + s8, :], rhs=xp[:, kg * 128 + 8 + s8:kg * 128 + 128:8],
                                    start=False, stop=False, r=[w1nk, xpk], w=[Hk])
                        else:
                            self.mm(H[:, 16 * pp - 1:16 * pp + 15], lhsT=self.w1n[:, kind, 8 + s8, :], rhs=xp[:, kg * 128 + s8:kg * 128 + 128:8],
                                    start=False, stop=False, r=[w1nk, xpk], w=[Hk])
            kcn, kcnk = self.b16t()
            kcn4 = kcn[:].rearrange("p (c f) -> p c f", c=4)
            for kg in range(4):
                kind, g = kg // 2, kg % 2
                H, Hk = Hps[kg]
                hs, hsk = self.b16t()
                self.actf(hs[:], H[:], AF.Silu, bias=self.bpe[:, kind:kind + 1], r=[Hk, 'bpe'], w=[hsk])
                for ch in range(4):
                    pc, pck = ps2()
                    self.mm(pc[:, 0:64], lhsT=hs[:, ch * 128:(ch + 1) * 128], rhs=self.w2sb[:, kind, :], r=[hsk, 'w2sb'], w=[pck])
                    if kind == 0:
                        kf, kfk = self.f32t()
                        sm, smk = self.smr.next()
                        self.actf(kf[:, 0:64], pc[:, 0:64], AF.Square, accum=sm[:, 0:1], r=[pck], w=[kfk, smk])
                        self.actf(sm[:, 1:2], sm[:, 0:1], AF.Sqrt, bias=self.epsc[:, 0:1], scale=1.0 / 64, r=[smk, 'epsc'], w=[smk])
                        self.recip(sm[:, 2:3], sm[:, 1:2], r=[smk], w=[smk])
                        self.stt(kcn4[:, ch, g * 64:(g + 1) * 64], pc[:, 0:64], sm[:, 2:3], self.k0gb[:, :], ALU.mult, ALU.mult, r=[pck, smk, 'k0gb'], w=[kcnk])
                    else:
                        self.cp('act', self.Rs[g][:, ch, 0:64], pc[:, 0:64], r=[pck], w=[f'Rs{g}'])
            pt_, ptk = ps2()
            ptb = pt_[:].bitcast(BF16)
            for ch in range(4):
                self.tr(ptb[:, ch * 128:(ch + 1) * 128], kcn4[:, ch, :], self.ident_b[:], r=[kcnk, 'ident_b'], w=[ptk])
            self.cp('act', self.kcT_s[:], ptb[:, 0:512], r=[ptk], w=['kcT_s'])
            for g in range(2):
                pS, pSk = ps2()
                for ch in range(4):
                    for hh in range(4):
                        self.mm(pS[:, ch * 4 + hh:ch * 4 + hh + 1], lhsT=self.kcT_s[64 * g:64 * g + 64, ch * 128:(ch + 1) * 128],
                                rhs=self.qT_s[64 * g:64 * g + 64, hh, s:s + 1], r=['kcT_s', 'qT_s'], w=[pSk])
                pT, pTk = self.b16t()
                self.actf(pT[:, 0:16], pS[:, 0:16], AF.Exp, scale=0.125, r=[pSk], w=[pTk])
                self.ts('dve', pT[:, 12:16], pT[:, 12:16], self.lastm[:, 0:1], ALU.mult, r=[pTk, 'lastm'], w=[pTk])
                pO, pOk = ps2()
                for ch in range(4):
                    self.mm(pO[0:4, 0:194], lhsT=pT[:, ch * 4:(ch + 1) * 4], rhs=self.Rs[g][:, ch, 0:194], start=(ch == 0), stop=(ch == 3), r=[pTk, f'Rs{g}'], w=[pOk])
                sm, smk = self.smr.next()
                self.recip(sm[0:4, 0:1], pO[0:4, 64:65], r=[pOk], w=[smk])
                ob_, obk = self.f32t()
                self.ts('dve', ob_[0:4, 0:64], pO[0:4, 0:64], sm[0:4, 0:1], ALU.mult, r=[pOk, smk], w=[obk])
                d('sp', self.osc[s, 0, 4 * g:4 * g + 4, :], ob_[0:4, 0:64], r=[obk], w=[('osc', s, 0, g)])
                self.ts('dve', ob_[0:4, 128:257], pO[0:4, 65:194], sm[0:4, 0:1], ALU.mult, r=[pOk, smk], w=[obk])
                pR, pRk = ps2()
                self.mm(pR[0:1, 0:129], lhsT=self.ones_f[0:4, 0:1], rhs=ob_[0:4, 128:257], r=['ones_f', obk], w=[pRk])
                s2_, s2k = self.f32t()
                s2 = s2_[0:1]
                self.tt('dve', s2[:, 0:129], pR[0:1, 0:129], self.sA[:, :], ALU.mult, r=[pRk, 'sA'], w=[s2k])
                self.tt('dve', s2[:, 0:129], s2[:, 0:129], self.sB[:, :], ALU.add, r=[s2k, 'sB'], w=[s2k])
                sm2, sm2k = self.smr.next()
                self.S.add('dve', (lambda o, i: lambda e: e.max(out=o, in_=i))(sm2[0:1, 0:8], s2[:, 0:129]), [s2k], [sm2k])
                self.S.add('dve', (lambda o, a, b: lambda e: e.match_replace(out=o, in_to_replace=a, in_values=b, imm_value=-1e30))(s2[:, 256:385], sm2[0:1, 0:8], s2[:, 0:129]), [s2k, sm2k], [s2k])
                self.S.add('dve', (lambda o, i: lambda e: e.max(out=o, in_=i))(sm2[0:1, 8:16], s2[:, 256:385]), [s2k], [sm2k])
                self.ts('dve', s2[:, 256:385], s2[:, 0:129], sm2[0:1, 15:16], ALU.is_ge, r=[s2k, sm2k], w=[s2k])
                self.ts('dve', s2[:, 0:129], s2[:, 256:385], -1.0, ALU.add, -MASKV * 0.125, ALU.mult, r=[s2k], w=[s2k])
                pB, pBk = ps2()
                self.mm(pB[:, 0:64], lhsT=self.e2a[0:1, :], rhs=s2[:, 0:128:2], start=True, stop=False, r=['e2a', s2k], w=[pBk])
                self.mm(pB[:, 0:64], lhsT=self.e2b[0:1, :], rhs=s2[:, 1:128:2], start=False, stop=True, r=['e2b', s2k], w=[pBk])
                self.cp('act', self.biask[:, g, :], pB[:, 0:64], r=[pBk], w=['biask'])
            d('sp', self.nr, self.scr_rows[s:s + 1, :], r=['scr_rows'], w=['onsa'])
            d('sp', self.nq, self.scr_q[s:s + 1, :], r=['scr_q'], w=['onsa'])
            for br in (1, 2):
                koff, voff = (256, 384) if br == 1 else (512, 640)
                accS, accSk = self.accr.bufs[1], 'acc1'
                first = True
                nblk = 64 if br == 1 else 4
                if br == 2:
                    g3, g3k = self.tmior.next()
                    d('sp', g3[:].rearrange("p (b f) -> p b f", b=4), self.inp('cwin')[l, s].rearrange("(b p) f -> p b f", p=128), w=[g3k])
                for pg in range(nblk):
                    if br == 1:
                        g2, g2k = self.tmior.next()
                        self.gather(g2[:, 0:512], g2k, pool_l, self.idxB[:, s, pg:pg + 1], 'idxB', eoff)
                        ksrc, vsrc = g2[:, 256:384], g2[:, 384:512]
                    else:
                        g2k = g3k
                        ksrc, vsrc = g3[:, pg * 256:pg * 256 + 128], g3[:, pg * 256 + 128:pg * 256 + 256]
                    kb16, kbk = self.b16t()
                    self.cp('dve', kb16[:, 0:128], ksrc, r=[g2k], w=[kbk])
                    vp, vpk = self.Vp.next()
                    self.cp('pool', vp[:, :, 0:64], vsrc.rearrange("p (g d) -> p g d", g=2), r=[g2k], w=[vpk])
                    pt_, ptk = ps2()
                    ptb = pt_[:].bitcast(BF16)
                    self.tr(ptb[:, 0:128], kb16[:, 0:128], self.ident_b[:], r=[kbk, 'ident_b'], w=[ptk])
                    self.cp('act', kb16[:, 128:256], ptb[:, 0:128], r=[ptk], w=[kbk])
                    pS, pSk = ps2()
                    for h in range(8):
                        g, hh = h // 4, h % 4
                        self.mm(pS[:, h:h + 1], lhsT=kb16[64 * g:64 * g + 64, 128:256], rhs=self.qT_s[64 * g:64 * g + 64, 4 + hh, s:s + 1], r=[kbk, 'qT_s'], w=[pSk])
                    pT, pTk = self.b16t()
                    for g in range(2):
                        if br == 1:
                            bias = self.biask[:, g, pg:pg + 1]
                        else:
                            bias = self.winb[:, 0:1] if pg == 0 else None
                        self.actf(pT[:, 4 * g:4 * g + 4], pS[:, 4 * g:4 * g + 4], AF.Exp, bias=bias, scale=0.125, r=[pSk, 'biask', 'winb'], w=[pTk])
                    for g in range(2):
                        self.mm(accS[0:4, g * 65:(g + 1) * 65], lhsT=pT[:, 4 * g:4 * g + 4], rhs=vp[:, g, 0:65], start=first, stop=False, r=[pTk, vpk], w=[accSk])
                        first = False
                pr_, prk = self.f32t()
                pr = pr_[0:1]
                self.tt('dve', pr[:, 0:512].rearrange("p (g h d) -> p g h d", g=2, h=4), self.nq[:, :].rearrange("p (g h d) -> p g h d", g=2, h=4),
                        self.nr[:, koff:koff + 128].rearrange("p (g d) -> p g d", g=2).unsqueeze(2).to_broadcast([1, 2, 4, 64]), ALU.mult, r=['onsa'], w=[prk])
                sm3, sm3k = self.smr.next()
                self.red(sm3[0:1, 0:8], pr[:, 0:512].rearrange("p (h d) -> p h d", h=8), r=[prk], w=[sm3k])
                pn, pnk = self.b16t()
                self.actf(pn[0:1, 0:8], sm3[0:1, 0:8], AF.Exp, scale=0.125, r=[sm3k], w=[pnk])
                self.cp('dve', self.vne[:, :, 0:64], self.nr[:, voff:voff + 128].rearrange("p (g d) -> p g d", g=2), r=['onsa'], w=['vne'])
                for g in range(2):
                    self.mm(accS[0:4, g * 65:(g + 1) * 65], lhsT=pn[0:1, 4 * g:4 * g + 4], rhs=self.vne[0:1, g, 0:65], start=False, stop=True, r=[pnk, 'vne'], w=[accSk])
                sm4, sm4k = self.smr.next()
                self.recip(sm4[0:4, 0:2], accS[0:4, 64:130:65], r=[accSk], w=[sm4k])
                ob_, obk = self.f32t()
                for g in range(2):
                    self.ts('dve', ob_[0:4, g * 64:(g + 1) * 64], accS[0:4, g * 65:g * 65 + 64], sm4[0:4, g:g + 1], ALU.mult, r=[accSk, sm4k], w=[obk])
                    d('sp', self.osc[s, br, 4 * g:4 * g + 4, :], ob_[0:4, g * 64:(g + 1) * 64], r=[obk], w=[('osc', s, br, g)])
            cmb, cmk = self.wr.next()
            cm = cmb[:, 0:2048].rearrange("p (mb f) -> p mb f", mb=2)
            d('pool', cm, self.inp('cmem')[l, s].rearrange("(mb p) f -> p mb f", p=128), w=[cmk])
            pt_, ptk = ps2()
            ptb = pt_[:].bitcast(BF16)
            for mb in range(2):
                for hm in range(4):
                    self.tr(ptb[:, (mb * 4 + hm) * 128:(mb * 4 + hm + 1) * 128], cm[:, mb, hm * 128:(hm + 1) * 128], self.ident_b[:], r=[cmk, 'ident_b'], w=[ptk])
            self.cp('act', self.mkT[:].rearrange("p h (mb m) -> p mb h m", mb=2), ptb[:, 0:1024].rearrange("p (mb h m) -> p mb h m", mb=2, h=4), r=[ptk], w=['mkT'])
            pS, pSk = ps2()
            for mb in range(2):
                for hm in range(4):
                    self.mm(pS[:, mb * 4 + hm:mb * 4 + hm + 1], lhsT=self.mkT[:, hm, mb * 128:(mb + 1) * 128], rhs=X.mqT[:, hm, s:s + 1], r=['mkT', 's_mqT'], w=[pSk])
            pT, pTk = self.b16t()
            self.actf(pT[:, 0:8], pS[:, 0:8], AF.Exp, scale=128 ** -0.5, r=[pSk], w=[pTk])
            pO, pOk = ps2()
            pD, pDk = ps2()
            first = True
            for mb in range(2):
                for hm in range(4):
                    self.mm(pO[:, hm:hm + 1], lhsT=cm[:, mb, 512 + hm * 128:512 + (hm + 1) * 128], rhs=pT[:, mb * 4 + hm:mb * 4 + hm + 1], start=first, stop=False, r=[cmk, pTk], w=[pOk])
                    first = False
            for mb in range(2):
                self.mm(pD[:, 0:4], lhsT=self.ones_b[:], rhs=pT[:, mb * 4:(mb + 1) * 4], start=(mb == 0), stop=(mb == 1), r=['ones_b', pTk], w=[pDk])
            rc, rck = self.f32t()
            self.recip(rc[:, 0:4], pD[:, 0:4], r=[pDk], w=[rck])
            self.tt('dve', X.brT[:, 8:12, s], pO[:, 0:4], rc[:, 0:4], ALU.mult, r=[pOk, rck], w=['s_brT2'])
        on_, onk = self.f32t()
        on = on_[0:4]
        osk = [('osc', s, br, g) for s in range(4) for br in range(3) for g in range(2)]
        for br in range(3):
            ot_, otk = self.tmr.next()
            ot = ot_[0:4, 0:8, :]
            d('sp', ot, self.osc[:, br, :, :], r=osk, w=[otk])
            gb = self.ng_s[:, br * 8:(br + 1) * 8].unsqueeze(2).to_broadcast([4, 8, 64])
            if br == 0:
                self.tt('dve', on[:, 0:512].rearrange("p (h d) -> p h d", h=8), ot, gb, ALU.mult, r=[otk, 'ng_s'], w=[onk])
            else:
                self.tt('dve', ot, ot, gb, ALU.mult, r=[otk, 'ng_s'], w=[otk])
                self.tt('dve', on[:, 0:512], on[:, 0:512], ot.rearrange("p h d -> p (h d)"), ALU.add, r=[otk, onk], w=[onk])
        onb, onbk = self.b16t()
        self.cp('dve', onb[0:4, :], on[:, 0:512], r=[onk], w=[onbk])
        pt_, ptk = self.psn()
        ptb = pt_[:].bitcast(BF16)
        for c in range(4):
            self.tr(ptb[:, c * 4:(c + 1) * 4], onb[0:4, c * 128:(c + 1) * 128], self.ident_b[0:4, 0:4], r=[onbk, 'ident_b'], w=[ptk])
        self.cp('act', X.brT[:, 0:4, :], ptb[:, 0:16].rearrange("p (c s) -> p c s", c=4), r=[ptk], w=['s_brT0'])

    def sample_tail(self, l):
        X = self.X
        d = self.dma
        if l == self.nlayers - 1:
            for s_ in range(4):
                d('sp', self.y_s[s_].rearrange("(k p) -> p k", p=128), X.xT[:, :, s_], r=[('s_xT', kc) for kc in range(8)], w=['y_s'], slow=True)

    def build(self, stage=99):
        self.declare()
        self.make_ctxs()
        if self.with_sample:
            self.declare_sample()
        self.stage = stage
        self.setup()
        if self.with_sample:
            self.setup_sample()
        for l in range(self.nlayers):
            if stage >= 1:
                self.layer_setup(l)
            for t in range(self.ntiles):
                if stage < 3:
                    continue
                self.load_x(l, t)
                self.rmsnorm(0)
                if stage < 4:
                    continue
                wA = [self.wload(self.inp('w_in')[l][:, 0:512], 8, 512), self.wload(self.inp('w_in')[l][:, 512:1024], 8, 512),
                      self.wload(self.inp('w_in')[l][:, 1024:1304], 8, 280)]
                self.tokmajor(l, t, wA)
                if stage < 5:
                    continue
                self.compress(l, t)
                self.fm_proj(l, t)
                self.mq_proj(l)
                self.cmp_attend(t)
                self.topk(t)
                self.slc(t)
                self.win(t)
                self.nsa_finalize()
                self.mem_attend()
                if stage < 6:
                    continue
                ctxs = [self.P]
                if self.with_sample and t == self.ntiles - 1:
                    self.sample_layer(l)
                    ctxs = [self.P, self.X]
                self.phase_b(l, ctxs)
                self.phase_c(l, ctxs)
                self.phase_d(l, ctxs)
                self.store_x(l, t)
                if self.with_sample and t == self.ntiles - 1:
                    self.sample_tail(l)
        self.S.emit()
        return self.nc


def make_consts():
    c = {}
    pos = (np.arange(32)[None, :] * 128 + np.arange(128)[:, None]).astype(np.float32)
    inv = (500000.0 ** (-np.arange(8, dtype=np.float32) / 8)).astype(np.float32)
    ang = pos[:, :, None] * inv[None, None, :]
    c['c_cos'] = np.cos(ang).astype(np.float32)
    c['c_sin'] = np.sin(ang).astype(np.float32)
    q = pos.astype(np.int64)
    j = np.arange(64)[None, None, :]
    qb = (q // 64)[:, :, None]
    forced = (j == 0) | (j == qb) | (j == qb - 1)
    elig = (j * 64) <= q[:, :, None]
    A = (elig & ~forced).astype(np.float32)
    B = np.where(forced, 1e9, np.where(elig, 0.0, -1e9)).astype(np.float32)
    c['c_selA'] = A
    c['c_selB'] = B
    p = np.arange(128)[:, None]
    f = np.arange(128)[None, :]
    c['c_tri'] = (p <= f).astype(np.float32)
    c['c_strict'] = (p > f).astype(np.float32)
    f5 = np.arange(512)[None, :]
    c['c_stair'] = ((16 * (p % 32) + 15) <= f5).astype(np.float32)
    posn = np.arange(256)
    cc = posn - 1
    jj = np.arange(64)[None, :]
    ov = ((cc[:, None] * 16 < (jj + 1) * 64) & (cc[:, None] * 16 + 32 > jj * 64) & (cc[:, None] >= 0) & (cc[:, None] < 255))
    c['c_ovl'] = ov.astype(np.float32).reshape(2, 128, 64).transpose(1, 0, 2).copy()
    k = np.arange(T)[None, :]
    c['c_E'] = ((k // 64) == np.arange(64)[:, None]).astype(np.float32)
    c['c_ident'] = np.eye(128, dtype=np.float32)
    return c


def make_consts_sample():
    c = {}
    inv = (500000.0 ** (-np.arange(8, dtype=np.float32) / 8)).astype(np.float32)
    ang = np.float32(8192.0) * inv
    c['c_cos_s'] = np.tile(np.cos(ang).astype(np.float32)[None, :], (4, 1))
    c['c_sin_s'] = np.tile(np.sin(ang).astype(np.float32)[None, :], (4, 1))
    cc = np.arange(512)[:, None]
    jj = np.arange(129)[None, :]
    ov = ((cc * 16 < (jj + 1) * 64) & (cc * 16 + 32 > jj * 64) & (cc < 511))
    c['c_ovl_s'] = ov.astype(np.float32).reshape(4, 128, 129).transpose(1, 0, 2).copy()
    forced = np.zeros((1, 129), dtype=bool)
    forced[0, [0, 127, 128]] = True
    c['c_sA'] = (~forced).astype(np.float32)
    c['c_sB'] = np.where(forced, 1e9, 0.0).astype(np.float32)
    p = np.arange(128)
    c['c_e2'] = np.stack([(p < 64), (p >= 64)]).astype(np.float32)
    c['c_lastmask'] = (p < 127).astype(np.float32)[:, None]
    wb = np.zeros((128, 1), dtype=np.float32)
    wb[0, 0] = MASKV * 0.125
    c['c_winb'] = wb
    c['c_pm64'] = (p % 64).astype(np.float32)[:, None]
    c['c_pidx'] = p.astype(np.float32)[:, None]
    return c


_NC_CACHE = {}


def _get_program():
    if 'nc' not in _NC_CACHE:
        b = Builder(nlayers=DEPTH, ntiles=NTILE, with_sample=True)
        nc = b.build(99)
        _NC_CACHE['nc'] = nc
        _NC_CACHE['decl'] = set(b.decl)
    return _NC_CACHE['nc'], _NC_CACHE['decl']


def kernel(**inputs):
    nc, decl = _get_program()
    f32 = np.float32
    inp = {k: np.asarray(v) for k, v in inputs.items()}
    consts = make_consts()
    consts.update(make_consts_sample())
    qn, kn = inp['q_norm'], inp['k_norm']
    shared = dict(consts)
    for k in ['w_in', 'w_mem_kv', 'w_branch', 'w_out', 'w_gate_up', 'w_down', 'cmp_w1', 'cmp_w2', 'cmp_pe', 'conv_w',
              'norm_mix', 'norm_mem', 'norm_ffn', 'mem_q_norm']:
        shared[k] = inp[k]
    shared['g12'] = np.concatenate([np.tile(qn, (1, 8)), np.tile(kn[:, 1], (1, 2)), np.tile(kn[:, 2], (1, 2))], axis=1)
    shared['k0g'] = kn[:, 0]
    shared['mkg'] = np.tile(inp['mem_k_norm'], (1, 4))
    n_phys = inp['cache_nsa_kv'].shape[1]
    assert n_phys == 2560
    shared['pool'] = inp['cache_nsa_kv'].reshape(DEPTH, n_phys * 128, 512)
    in_maps = []
    for c in range(8):
        b = c % 4
        ss = slice(4 * c, 4 * c + 4)
        m = dict(shared)
        m['x'] = inp['x_prompt'][b]
        m['mem'] = inp['mem_prompt'][b]
        m['xs'] = inp['x_sample'][ss, 0]
        m['cwin'] = inp['cache_win_kv'][:, ss].reshape(DEPTH, 4, 512, 256)
        m['sconv'] = inp['state_conv'][:, ss]
        m['cmem'] = inp['cache_mem_kv'][:, ss].reshape(DEPTH, 4, 256, 1024)
        m['pt'] = inp['page_table'][ss].astype(np.int32)
        in_maps.append({k: np.ascontiguousarray(v) for k, v in m.items() if k in decl})
    res = run_bass_kernel_spmd(nc, in_maps, core_ids=list(range(8)))
    R = res.results
    y_p = np.stack([R[c]['y'] for c in range(4)], 0).astype(f32)
    y_s = np.concatenate([R[c]['y_s'] for c in range(8)], 0).reshape(32, 1, D).astype(f32)
    rows_p = np.stack([R[c]['rows_p'] for c in range(4)], 1).reshape(DEPTH, 4, T, 4, 2, 64).astype(f32)
    rows_s = np.concatenate([R[c]['rows_s'] for c in range(8)], 1).reshape(DEPTH, 32, 1, 4, 2, 64).astype(f32)
    win_p = np.stack([R[c]['win_p'] for c in range(4)], 1).reshape(DEPTH, 4, 512, 2, 2, 64).astype(f32)
    win_s = np.concatenate([R[c]['win_s'] for c in range(8)], 1).reshape(DEPTH, 32, 512, 2, 2, 64).astype(f32)
    conv_p = np.stack([R[c]['conv_p'] for c in range(4)], 1).astype(f32)
    conv_s = np.concatenate([R[c]['conv_s'] for c in range(8)], 1).astype(f32)
    mem_p = np.stack([R[c]['mem_p'] for c in range(4)], 1).reshape(DEPTH, 4, 256, 2, 4, 128).astype(f32)
    return (y_p, y_s, rows_p, rows_s, win_p, win_s, conv_p, conv_s, mem_p)
```

```python
import contextlib
import numpy as np
import concourse.bass as bass
import concourse.mybir as mybir
from concourse.bass_utils import run_bass_kernel_spmd

F32 = mybir.dt.float32
BF16 = mybir.dt.bfloat16
I32 = mybir.dt.int32
ALU = mybir.AluOpType
AF = mybir.ActivationFunctionType
AX = mybir.AxisListType
ENGS = ['pe', 'act', 'dve', 'pool', 'sp']

D = 1024
T = 4096
TT = 512
NTILE = 8
DEPTH = 2
N_IN = 6424
DFF = 2816
EPS = 1e-6
MASKV = -30000.0
C_CX, C_CB, C_CC, C_MQ, C_MG = 1304, 1816, 2328, 2840, 3352


class Sched:
    NDSEM = 8

    def __init__(self, nc):
        self.nc = nc
        self.ops = []
        self.stack = contextlib.ExitStack()
        self._n = 0

    def sb(self, shape, dt, name=None):
        self._n += 1
        return self.stack.enter_context(self.nc.sbuf_tensor(name or f"sb{self._n}", list(shape), dt))

    def ps(self, shape, dt=F32, name=None):
        self._n += 1
        return self.stack.enter_context(self.nc.psum_tensor(name or f"ps{self._n}", list(shape), dt))

    def add(self, eng, fn, r=(), w=(), dma=False):
        w = tuple(w) + tuple(k + '#rd' for k in r if isinstance(k, str) and (k.startswith('ps') or k.startswith('acc')) and eng != 'pe')
        self.ops.append(dict(eng=eng, fn=fn, r=tuple(r), w=tuple(w), dma=dma))

    def emit(self):
        nc = self.nc
        ops = self.ops
        n = len(ops)
        pos = [0] * n
        cnt = {e: 0 for e in ENGS}
        dcnt = {e: 0 for e in ENGS}
        dk = [0] * n
        for i, o in enumerate(ops):
            if o['dma']:
                dk[i] = dcnt[o['eng']]
                dcnt[o['eng']] += 1
            else:
                pos[i] = cnt[o['eng']]
                cnt[o['eng']] += 1
        last_w = {}
        readers = {}
        deps = [None] * n
        for i, o in enumerate(ops):
            d = set()
            for r in o['r']:
                if r in last_w:
                    d.add(last_w[r])
            for w in o['w']:
                if w in last_w:
                    d.add(last_w[w])
                d.update(readers.get(w, ()))
            d.discard(i)
            deps[i] = d
            for r in o['r']:
                readers.setdefault(r, []).append(i)
            for w in o['w']:
                last_w[w] = i
                readers[w] = []
        clock = {e: {p: -1 for p in ENGS} for e in ENGS}
        dma_seen = {e: set() for e in ENGS}
        opclock = [None] * n
        waits = [[] for _ in range(n)]
        signal = set()
        K = self.NDSEM
        dma_by_eng = {e: [] for e in ENGS}
        for i, o in enumerate(ops):
            E = o['eng']
            ck = clock[E]
            if o['dma']:
                k = dk[i]
                if k >= K:
                    prev = dma_by_eng[E][k - K]
                    if prev not in dma_seen[E]:
                        waits[i].append(('d', prev))
                        dma_seen[E].add(prev)
                dma_by_eng[E].append(i)
            for d in sorted(deps[i], reverse=True):
                od = ops[d]
                if od['dma']:
                    if d in dma_seen[E]:
                        continue
                    waits[i].append(('d', d))
                    dma_seen[E].add(d)
                else:
                    P = od['eng']
                    if P == E and E == 'pe':
                        continue
                    if ck[P] >= pos[d]:
                        continue
                    waits[i].append(('c', d))
                    signal.add(d)
                    oc = opclock[d]
                    for p in ENGS:
                        if oc[p] > ck[p]:
                            ck[p] = oc[p]
                    if pos[d] > ck[P]:
                        ck[P] = pos[d]
            if not o['dma']:
                opclock[i] = dict(ck)
        rank = {}
        rc = {e: 0 for e in ENGS}
        for i, o in enumerate(ops):
            if not o['dma'] and i in signal:
                rc[o['eng']] += 1
                rank[i] = rc[o['eng']]
        st = self.stack
        csem = {e: st.enter_context(nc.semaphore(f"c_{e}")) for e in ENGS}
        dsem = {e: [st.enter_context(nc.semaphore(f"d_{e}{j}")) for j in range(K)] for e in ENGS if dcnt[e] > 0}

        def dsv(d):
            k = dk[d]
            return dsem[ops[d]['eng']][k % K], 16 * (k // K + 1)

        by_eng = {e: [i for i, o in enumerate(ops) if o['eng'] == e] for e in ENGS}
        self.stats = dict(n=n, signals=len(signal), waits=sum(len(w) for w in waits),
                          per_eng={e: len(by_eng[e]) for e in ENGS})

        def mk(E):
            def body(e):
                for i in by_eng[E]:
                    o = ops[i]
                    for kind, d in waits[i]:
                        if kind == 'd':
                            s, v = dsv(d)
                            e.wait_ge(s, v)
                        else:
                            e.wait_ge(csem[ops[d]['eng']], rank[d])
                    ins = o['fn'](e)
                    if o['dma']:
                        s, v = dsv(i)
                        ins.then_inc(s, 16)
                    elif i in signal:
                        ins.then_inc(csem[E], 1)
                nd = dcnt[E]
                for j in range(min(K, nd)):
                    last_k = ((nd - 1 - j) // K) * K + j
                    e.wait_ge(dsem[E][j], 16 * (last_k // K + 1))
            return body

        with nc.Block() as block:
            block.tensor(mk('pe'))
            block.scalar(mk('act'))
            block.vector(mk('dve'))
            block.gpsimd(mk('pool'))
            block.sync(mk('sp'))
        st.close()


class Rot:
    def __init__(self, bufs, name):
        self.bufs = bufs
        self.name = name
        self.i = 0

    def next(self):
        k = self.i % len(self.bufs)
        self.i += 1
        return self.bufs[k], f"{self.name}{k}"


class Builder:
    def __init__(self, nlayers=DEPTH, ntiles=NTILE, with_sample=True):
        self.nlayers = nlayers
        self.ntiles = ntiles
        self.with_sample = with_sample
        self.nc = bass.Bass("TRN2", target_bir_lowering=False)
        self.S = Sched(self.nc)
        self.lazy = {}
        self.decl = {}

    def mm(self, out, lhsT, rhs, start=True, stop=True, r=(), w=()):
        self.S.add('pe', lambda e: e.matmul(out, lhsT=lhsT, rhs=rhs, start=start, stop=stop, skip_group_check=True), r, w)

    def tr(self, out, in_, ident, r=(), w=()):
        self.S.add('pe', lambda e: e.transpose(out, in_, ident), r, w)

    def actf(self, out, in_, func, bias=None, scale=1.0, accum=None, r=(), w=()):
        kw = {}
        if bias is not None:
            kw['bias'] = bias
        if accum is not None:
            kw['accum_out'] = accum
        self.S.add('act', lambda e: e.activation(out=out, in_=in_, func=func, scale=scale, **kw), r, w)

    def cp(self, eng, out, in_, r=(), w=()):
        if eng == 'pool':
            eng = 'act'
        if eng == 'act':
            self.S.add('act', lambda e: e.copy(out=out, in_=in_), r, w)
        else:
            self.S.add(eng, lambda e: e.tensor_copy(out=out, in_=in_), r, w)

    def tt(self, eng, out, in0, in1, op, r=(), w=()):
        if eng == 'pool':
            eng = 'dve'
        self.S.add(eng, lambda e: e.tensor_tensor(out=out, in0=in0, in1=in1, op=op), r, w)

    def ts(self, eng, out, in0, s1, op0, s2=None, op1=None, r=(), w=()):
        if eng == 'pool':
            eng = 'dve'
        if op1 is None:
            self.S.add(eng, lambda e: e.tensor_scalar(out=out, in0=in0, scalar1=s1, scalar2=None, op0=op0), r, w)
        else:
            self.S.add(eng, lambda e: e.tensor_scalar(out=out, in0=in0, scalar1=s1, scalar2=s2, op0=op0, op1=op1), r, w)

    def stt(self, out, in0, scalar, in1, op0, op1, r=(), w=()):
        self.S.add('dve', lambda e: e.scalar_tensor_tensor(out=out, in0=in0, scalar=scalar, in1=in1, op0=op0, op1=op1), r, w)

    def recip(self, out, in_, r=(), w=()):
        self.S.add('dve', lambda e: e.reciprocal(out=out, in_=in_), r, w)

    def red(self, out, in_, r=(), w=()):
        self.S.add('dve', lambda e: e.tensor_reduce(out=out, in_=in_, axis=AX.X, op=ALU.add), r, w)

    def memset(self, eng, ap, val, r=(), w=()):
        self.S.add(eng, lambda e: e.memset(ap, val), r, w)

    def dma(self, q, out, in_, r=(), w=(), slow=False):
        if slow:
            self.S.add(q, lambda e: e.dma_start(out=out, in_=in_, allow_slow_non_contiguous=True), r, w, dma=True)
        else:
            self.S.add(q, lambda e: e.dma_start(out=out, in_=in_), r, w, dma=True)

    def din(self, name, shape, dt=F32):
        self.lazy[name] = (list(shape), dt)
        return None

    def inp(self, name):
        if name not in self.decl:
            shape, dt = self.lazy[name]
            self.decl[name] = self.nc.dram_tensor(name, shape, dt, kind="ExternalInput").ap()
        return self.decl[name]

    def dout(self, name, shape, dt=F32):
        return self.nc.dram_tensor(name, list(shape), dt, kind="ExternalOutput").ap()

    def psn(self):
        return self.psr.next()

    def f32t(self):
        return self.f32r.next()

    def b16t(self):
        return self.b16r.next()

    def wload(self, src, nk, ncols, q='pool'):
        buf, key = self.wr.next()
        v = buf[:, 0:nk * ncols].rearrange("p (k c) -> p k c", k=nk)
        self.dma(q, v, src.rearrange("(k p) c -> p k c", p=128), w=[key])
        return v, key

    def declare(self):
        S = self.S
        i = self.din
        self.x = i("x", [T, D])
        self.mem = i("mem", [256, D])
        self.w_in = i("w_in", [DEPTH, D, N_IN])
        self.w_mem = i("w_mem_kv", [DEPTH, D, 1024])
        self.w_br = i("w_branch", [DEPTH, 3, 512, D])
        self.w_out = i("w_out", [DEPTH, D, D])
        self.w_gu = i("w_gate_up", [DEPTH, D, 2 * DFF])
        self.w_dn = i("w_down", [DEPTH, DFF, D])
        self.cw1 = i("cmp_w1", [DEPTH, 2, 2048, 128])
        self.cw2 = i("cmp_w2", [DEPTH, 2, 128, 64])
        self.cpe = i("cmp_pe", [DEPTH, 2, 32, 64])
        self.convw = i("conv_w", [DEPTH, 3, 512])
        self.g_mix = i("norm_mix", [DEPTH, D])
        self.g_mem = i("norm_mem", [DEPTH, D])
        self.g_ffn = i("norm_ffn", [DEPTH, D])
        self.g12 = i("g12", [DEPTH, 768])
        self.k0g = i("k0g", [DEPTH, 64])
        self.mqg = i("mem_q_norm", [DEPTH, 128])
        self.mkg = i("mkg", [DEPTH, 512])
        self.c_cos = i("c_cos", [128, 32, 8])
        self.c_sin = i("c_sin", [128, 32, 8])
        self.c_selA = i("c_selA", [128, 32, 64])
        self.c_selB = i("c_selB", [128, 32, 64])
        self.c_tri = i("c_tri", [128, 128])
        self.c_strict = i("c_strict", [128, 128])
        self.c_stair = i("c_stair", [128, 512])
        self.c_ovl = i("c_ovl", [128, 2, 64])
        self.c_E = i("c_E", [64, T])
        self.c_ident = i("c_ident", [128, 128])
        o = self.dout
        self.y = o("y", [T, D])
        self.rows_o = o("rows_p", [DEPTH, T, 512])
        self.win_o = o("win_p", [DEPTH, 512, 256])
        self.conv_o = o("conv_p", [DEPTH, 2, 512])
        self.mem_o = o("mem_p", [DEPTH, 256, 1024])
        self.hres = self.nc.dram_tensor("hres", [8, 128, T], F32).ap()

        sb = S.sb
        self.xT = sb([128, 8, TT], F32, 'xT')
        self.xnT = sb([128, 8, TT], BF16, 'xnT')
        self.QB = sb([128, 8, TT], BF16, 'QB')
        self.QU = sb([128, 4, TT], BF16, 'QU')
        self.mqT = sb([128, 4, TT], BF16, 'mqT')
        self.brT = sb([128, 12, TT], BF16, 'brT')
        self.mT = self.QB
        self.actT = sb([128, 6, TT], BF16, 'actT')
        self.tmior = Rot([sb([128, D], F32, f'tmio{k}') for k in range(2)], 'tmio')
        self.KE = [sb([128, T], BF16, f'KE{g}') for g in range(2)]
        self.Vs = sb([128, 32, 2, 66], BF16, 'Vs')
        self.kwT = [sb([64, 1024], BF16, f'kwT{g}') for g in range(2)]
        self.Vw = sb([128, 8, 2, 66], BF16, 'Vw')
        self.kcT2 = [sb([128, 256], BF16, f'kcT{g}') for g in range(2)]
        self.Rc = [sb([128, 2, 130], BF16, f'Rc{g}') for g in range(2)]
        self.rawk = sb([128, 528], BF16, 'rawk')
        self.rawv = sb([128, 528], BF16, 'rawv')
        self.mkT = sb([128, 4, 256], BF16, 'mkT')
        self.mv = sb([128, 2, 512], BF16, 'mv')
        self.ucar = sb([128, 4, 2], F32, 'ucar')
        self.uextr = Rot([sb([128, 514], F32, f'uext{k}') for k in range(2)], 'uext')
        self.onsa = sb([128, 4, 512], F32, 'onsa')
        self.macc = self.onsa
        self.scr = sb([128, 4, 2, 64], F32, 'scr')
        self.ngt = sb([128, 4, 24], F32, 'ngt')
        self.biasr = Rot([sb([128, 128], BF16, f'biasw{k}') for k in range(2)], 'biasw')
        self.selTr = Rot([sb([128, 2, 4, 64], F32, f'selT{k}') for k in range(1)], 'selT')
        self.ident_f = sb([128, 128], F32, 'ident_f')
        self.ident_b = sb([128, 128], BF16, 'ident_b')
        self.ones_b = sb([128, 128], BF16, 'ones_b')
        self.tri = sb([128, 128], BF16, 'tri')
        self.strict = sb([128, 128], BF16, 'strict')
        self.stair = sb([128, 512], BF16, 'stair')
        self.cosT = sb([128, 32, 8], F32, 'cosT')
        self.sinT = sb([128, 32, 8], F32, 'sinT')
        self.epsc = sb([128, 1], F32, 'epsc')
        self.gcols = sb([128, 3, 8], F32, 'gcols')
        self.g12b = sb([128, 12, 64], F32, 'g12b')
        self.k0gb = sb([128, 64], F32, 'k0gb')
        self.mqgc = sb([128, 1], F32, 'mqgc')
        self.cwc = sb([128, 4, 3], F32, 'cwc')
        self.w2sb = sb([128, 2, 64], BF16, 'w2sb')
        self.peT = sb([64, 2, 32], BF16, 'peT')
        self.bpe = sb([128, 2], F32, 'bpe')
        self.psr = Rot([S.ps([128, 512], F32, f'psb{k}') for k in range(6)], 'ps')
        self.accr = Rot([S.ps([128, 512], F32, f'acc{k}') for k in range(2)], 'acc')
        self.f32r = Rot([sb([128, 512], F32, f'f32t{k}') for k in range(4)], 'f32t')
        self.b16r = Rot([sb([128, 512], BF16, f'b16t{k}') for k in range(5)], 'b16t')
        self.wr = Rot([sb([128, 4096], BF16, f'wbuf{k}') for k in range(4)], 'wbuf')
        self.tmr = Rot([sb([128, 12, 64], F32, f'tm{k}') for k in range(4)], 'tm')
        self.smr = Rot([sb([128, 32], F32, f'sm{k}') for k in range(8)], 'sm')
        self.selr = Rot([sb([128, 2, 64], F32, f'sel{k}') for k in range(2)], 'sel')
        self.tsrc = Rot([sb([128, 12, 128], BF16, f'tsrc{k}') for k in range(2)], 'tsrc')
        self.rowsr = Rot([sb([128, 768], F32, f'rows{k}') for k in range(1)], 'rows')

    def setup(self):
        d = self.dma
        d('sp', self.ident_f[:], self.inp('c_ident'), w=['ident_f'])
        d('pool', self.ident_b[:], self.inp('c_ident'), w=['ident_b'])
        d('pool', self.tri[:], self.inp('c_tri'), w=['tri'])
        d('pool', self.strict[:], self.inp('c_strict'), w=['strict'])
        d('pool', self.stair[:], self.inp('c_stair'), w=['stair'])
        d('sp', self.cosT[:], self.inp('c_cos'), w=['cosT'])
        d('sp', self.sinT[:], self.inp('c_sin'), w=['sinT'])
        self.memset('pool', self.ones_b[:], 1.0, w=['ones_b'])
        self.memset('pool', self.epsc[:], EPS, w=['epsc'])
        for g in range(2):
            d('pool', self.KE[g][64:128, :], self.inp('c_E'), w=[f'KE{g}'])
            self.memset('pool', self.Rc[g][:, :, 64:65], 1.0, w=[f'Rc{g}'])
            self.memset('pool', self.Rc[g][0:1, 0, 64:65], 0.0, w=[f'Rc{g}'])
            d('pool', self.Rc[g][:, :, 65:129], self.inp('c_ovl'), w=[f'Rc{g}'])
        for k in range(2):
            self.memset('pool', self.biasr.bufs[k][:], 0.0, w=[f'biasw{k}'])
        self.memset('pool', self.Vs[:, :, :, 64:65], 1.0, w=['Vs'])
        self.memset('pool', self.Vw[:, :, :, 64:65], 1.0, w=['Vw'])

    def layer_setup(self, l):
        d = self.dma
        d('sp', self.gcols[:, 0, :], self.inp('norm_mix')[l].rearrange("(k p) -> p k", p=128), w=['gcols'], slow=True)
        d('sp', self.gcols[:, 1, :], self.inp('norm_ffn')[l].rearrange("(k p) -> p k", p=128), w=['gcols'], slow=True)
        d('sp', self.g12b[:].rearrange("p h d -> p (h d)"), self.inp('g12')[l].partition_broadcast(128), w=['g12b'])
        d('sp', self.k0gb[:], self.inp('k0g')[l].partition_broadcast(128), w=['k0gb'])
        d('sp', self.mqgc[:], self.inp('mem_q_norm')[l].rearrange("(p o) -> p o", o=1), w=['mqgc'], slow=True)
        for k in range(3):
            d('sp', self.cwc[:, :, k], self.inp('conv_w')[l, k].rearrange("(c p) -> p c", p=128), w=['cwc'], slow=True)
        d('pool', self.w2sb[:], self.inp('cmp_w2')[l].rearrange("k h e -> h k e"), w=['w2sb'])
        d('pool', self.peT[:], self.inp('cmp_pe')[l].rearrange("k s d -> d k s"), w=['peT'], slow=True)
        self.memset('pool', self.ucar[:], 0.0, w=['ucar'])
        self.memset('pool', self.rawk[:, 0:16], 0.0, w=['rawk'])
        self.memset('pool', self.rawv[:, 0:16], 0.0, w=['rawv'])
        w1 = self.load_w1(l)
        pb, pk = self.psn()
        for kind in range(2):
            v, key = w1[kind]
            for s in range(32):
                self.mm(pb[:, kind:kind + 1], lhsT=v[0:64, s, :], rhs=self.peT[0:64, kind, s:s + 1],
                        start=(s == 0), stop=(s == 31), r=[key, 'peT'], w=[pk])
        self.cp('act', self.bpe[:], pb[:, 0:2], r=[pk], w=['bpe'])
        if self.stage >= 2:
            self.mem_kv(l)

    def load_w1(self, l):
        res = []
        for kind in range(2):
            buf, key = self.wr.next()
            v = buf[:, :].rearrange("p (s h) -> p s h", s=32)
            src = self.inp('cmp_w1')[l, kind].rearrange("(s d) h -> d s h", d=64)
            self.dma('pool', v[0:64], src, w=[key])
            self.dma('pool', v[64:128], src, w=[key])
            res.append((v, key))
        return res

    def mem_kv(self, l):
        scr_ = self.onsa[:].rearrange("p a b -> p (a b)")
        gmem_b = scr_[:, 0:1024]
        mkg_b = scr_[:, 1024:1536]
        self.dma('sp', gmem_b, self.inp('norm_mem')[l].partition_broadcast(128), w=['onsa'])
        self.dma('sp', mkg_b, self.inp('mkg')[l].partition_broadcast(128), w=['onsa'])
        wk, wkk = self.wload(self.inp('w_mem_kv')[l][:, 0:512], 8, 512)
        wv, wvk = self.wload(self.inp('w_mem_kv')[l][:, 512:1024], 8, 512)
        for mb in range(2):
            mt_, mtk = self.tmior.next()
            mt = mt_[:]
            self.dma('sp', mt, self.inp('mem')[mb * 128:(mb + 1) * 128, :], w=[mtk])
            sm, smk = self.smr.next()
            jk_, jkk = self.tmior.next()
            junk = jk_[:]
            self.actf(junk, mt, AF.Square, accum=sm[:, 0:1], r=[mtk], w=[jkk, smk])
            self.actf(sm[:, 1:2], sm[:, 0:1], AF.Sqrt, bias=self.epsc[:, 0:1], scale=1.0 / D, r=[smk, 'epsc'], w=[smk])
            self.recip(sm[:, 2:3], sm[:, 1:2], r=[smk], w=[smk])
            self.stt(junk, mt, sm[:, 2:3], gmem_b, ALU.mult, ALU.mult, r=[mtk, smk, 'onsa'], w=[jkk])
            if self.stage < 2.1:
                continue
            mnb, mnk = self.b16t()
            mnb2, mnk2 = self.b16t()
            self.cp('dve', mnb[:], junk[:, 0:512], r=[jkk], w=[mnk])
            self.cp('dve', mnb2[:], junk[:, 512:1024], r=[jkk], w=[mnk2])
            if self.stage < 2.12:
                continue
            pt, ptk = self.psn()
            ptb = pt[:].bitcast(BF16)
            for kc in range(8):
                src = (mnb if kc < 4 else mnb2)[:, (kc % 4) * 128:(kc % 4 + 1) * 128]
                self.tr(ptb[:, kc * 128:(kc + 1) * 128], src, self.ident_b[:], r=[mnk, mnk2, 'ident_b'], w=[ptk])
            if self.stage < 2.14:
                continue
            mnT, mnTk = self.b16t()
            mnT2, mnT2k = self.b16t()
            if self.stage != 2.15:
                self.cp('act', mnT[:], ptb[:, 0:512], r=[ptk], w=[mnTk])
            if self.stage != 2.16:
                self.cp('dve', mnT2[:], ptb[:, 512:1024], r=[ptk], w=[mnT2k])
            if self.stage < 2.2:
                continue
            pk_, pkk = self.psn()
            pv_, pvk = self.psn()
            for kc in range(8):
                lt = (mnT if kc < 4 else mnT2)[:, (kc % 4) * 128:(kc % 4 + 1) * 128]
                self.mm(pk_[:], lhsT=lt, rhs=wk[:, kc, :], start=(kc == 0), stop=(kc == 7), r=[mnTk, mnT2k, wkk], w=[pkk])
            for kc in range(8):
                lt = (mnT if kc < 4 else mnT2)[:, (kc % 4) * 128:(kc % 4 + 1) * 128]
                self.mm(pv_[:], lhsT=lt, rhs=wv[:, kc, :], start=(kc == 0), stop=(kc == 7), r=[mnTk, mnT2k, wvk], w=[pvk])
            if self.stage < 2.3:
                continue
            mo = junk
            kf, kfk = self.f32t()
            sq, sqk = self.f32t()
            self.cp('act', kf[:], pk_[:], r=[pkk], w=[kfk])
            self.cp('act', mo[:, 512:1024], pv_[:], r=[pvk], w=[jkk])
            self.cp('dve', self.mv[:, mb, :], pv_[:], r=[pvk], w=['mv'])
            self.tt('pool', sq[:], kf[:], kf[:], ALU.mult, r=[kfk], w=[sqk])
            if self.stage < 2.4:
                continue
            sm2, sm2k = self.smr.next()
            self.red(sm2[:, 0:4], sq[:].rearrange("p (h d) -> p h d", h=4), r=[sqk], w=[sm2k])
            self.actf(sm2[:, 4:8], sm2[:, 0:4], AF.Sqrt, bias=self.epsc[:, 0:1], scale=1.0 / 128, r=[sm2k, 'epsc'], w=[sm2k])
            self.recip(sm2[:, 8:12], sm2[:, 4:8], r=[sm2k], w=[sm2k])
            self.tt('dve', kf[:].rearrange("p (h d) -> p h d", h=4), kf[:].rearrange("p (h d) -> p h d", h=4),
                    sm2[:, 8:12].unsqueeze(2).to_broadcast([128, 4, 128]), ALU.mult, r=[kfk, sm2k], w=[kfk])
            self.tt('pool', mo[:, 0:512], kf[:], mkg_b, ALU.mult, r=[kfk, 'onsa'], w=[jkk])
            self.dma('sp', self.mem_o[l, mb * 128:(mb + 1) * 128, :], mo, r=[jkk], w=['mem_o'])
            if self.stage < 2.5:
                continue
            knb, knk = self.b16t()
            self.cp('pool', knb[:], mo[:, 0:512], r=[jkk], w=[knk])
            pt2, pt2k = self.psn()
            pt2b = pt2[:].bitcast(BF16)
            for hm in range(4):
                self.tr(pt2b[:, hm * 128:(hm + 1) * 128], knb[:, hm * 128:(hm + 1) * 128], self.ident_b[:], r=[knk, 'ident_b'], w=[pt2k])
            self.cp('act', self.mkT[:, :, mb * 128:(mb + 1) * 128], pt2b[:, 0:512].rearrange("p (h m) -> p h m", h=4), r=[pt2k], w=['mkT'])

    def load_x(self, l, t):
        if l == 0:
            for blk in range(4):
                xi, xik = self.tmior.next()
                self.dma('sp', xi[:], self.inp('x')[(t * 4 + blk) * 128:(t * 4 + blk + 1) * 128, :], w=[xik])
                for hf in range(2):
                    pb, pk = self.psn()
                    for c in range(4):
                        kc = hf * 4 + c
                        self.tr(pb[:, c * 128:(c + 1) * 128], xi[:, kc * 128:(kc + 1) * 128], self.ident_f[:], r=[xik, 'ident_f'], w=[pk])
                    self.cp('act' if hf == 0 else 'dve', self.xT[:, hf * 4:hf * 4 + 4, blk * 128:(blk + 1) * 128],
                            pb[:].rearrange("p (c f) -> p c f", c=4), r=[pk], w=[('xT', hf * 4 + c) for c in range(4)])
        else:
            for kc in range(8):
                self.dma('sp', self.xT[:, kc, :], self.hres[kc, :, t * TT:(t + 1) * TT], r=['hres'], w=[('xT', kc)])

    def rmsnorm(self, gi):
        ps, pk = self.psn()
        for kc in range(8):
            sq, sqk = self.b16t()
            self.actf(sq[:], self.xT[:, kc, :], AF.Square, r=[('xT', kc)], w=[sqk])
            self.mm(ps[:], lhsT=self.ones_b[:], rhs=sq[:], start=(kc == 0), stop=(kc == 7), r=['ones_b', sqk], w=[pk])
        rt, rtk = self.f32t()
        self.actf(rt[:], ps[:], AF.Sqrt, bias=self.epsc[:, 0:1], scale=1.0 / D, r=[pk, 'epsc'], w=[rtk])
        rs, rsk = self.f32t()
        self.recip(rs[:], rt[:], r=[rtk], w=[rsk])
        for kc in range(8):
            self.stt(self.xnT[:, kc, :], self.xT[:, kc, :], self.gcols[:, gi, kc:kc + 1], rs[:], ALU.mult, ALU.mult,
                     r=[('xT', kc), 'gcols', rsk], w=[('xnT', kc)])

    def headnorm_rope(self, hd, hk, nh, gain_b, cos_b, sin_b, inv_d):
        P = hd.shape[0]
        sq, sqk = self.tmr.next()
        sq = sq[0:P, 0:nh, :]
        sm, smk = self.smr.next()
        self.tt('pool', sq, hd, hd, ALU.mult, r=[hk], w=[sqk])
        self.red(sm[0:P, 0:nh], sq, r=[sqk], w=[smk])
        self.actf(sm[0:P, 12:12 + nh], sm[0:P, 0:nh], AF.Sqrt, bias=self.epsc[0:P, 0:1], scale=inv_d, r=[smk, 'epsc'], w=[smk])
        self.recip(sm[0:P, 0:nh], sm[0:P, 12:12 + nh], r=[smk], w=[smk])
        self.tt('dve', hd, hd, sm[0:P, 0:nh].unsqueeze(2).to_broadcast([P, nh, 64]), ALU.mult, r=[hk, smk], w=[hk])
        self.tt('pool', hd, hd, gain_b, ALU.mult, r=[hk, 'g12b'], w=[hk])
        hr, hrk = self.tmr.next()
        hr = hr[0:P, 0:nh, :]
        self.cp('pool', hr, hd, r=[hk], w=[hrk])
        tp, tpk = self.tmr.next()
        t1 = tp[0:P, 0:nh, 0:8]
        t2 = tp[0:P, 0:nh, 8:16]
        t3 = tp[0:P, 0:nh, 16:24]
        t4 = tp[0:P, 0:nh, 24:32]
        x1 = hd[:, :, 0:8]
        x2 = hd[:, :, 8:16]
        self.tt('dve', t1, x1, cos_b, ALU.mult, r=[hk, 'cosT'], w=[tpk])
        self.tt('dve', t2, x2, sin_b, ALU.mult, r=[hk, 'sinT'], w=[tpk])
        self.tt('dve', t3, x2, cos_b, ALU.mult, r=[hk, 'cosT'], w=[tpk])
        self.tt('dve', t4, x1, sin_b, ALU.mult, r=[hk, 'sinT'], w=[tpk])
        self.tt('dve', hr[:, :, 0:8], t1, t2, ALU.subtract, r=[tpk], w=[hrk])
        self.tt('dve', hr[:, :, 8:16], t3, t4, ALU.add, r=[tpk], w=[hrk])
        return hr, hrk

    def tokmajor(self, l, t, wA):
        pend = None
        for blk in range(4):
            st = self.tok_part1(l, t, wA, blk)
            if pend is not None:
                self.tok_part2(t, pend)
            pend = st
        self.tok_part2(t, pend)

    def tok_part1(self, l, t, wA, blk):
        (w0, w0k), (w1, w1k), (w2, w2k) = wA
        if True:
            kb = t * 4 + blk
            cs = slice(blk * 128, (blk + 1) * 128)
            pq, pqk = self.psn()
            pa, pak = self.psn()
            pb, pbk = self.psn()
            for kc in range(8):
                lt = self.xnT[:, kc, cs]
                self.mm(pq[:], lhsT=lt, rhs=w0[:, kc, :], start=(kc == 0), stop=(kc == 7), r=[('xnT', kc), w0k], w=[pqk])
            for kc in range(8):
                lt = self.xnT[:, kc, cs]
                self.mm(pa[:], lhsT=lt, rhs=w1[:, kc, :], start=(kc == 0), stop=(kc == 7), r=[('xnT', kc), w1k], w=[pak])
            for kc in range(8):
                lt = self.xnT[:, kc, cs]
                self.mm(pb[:, 0:280], lhsT=lt, rhs=w2[:, kc, :], start=(kc == 0), stop=(kc == 7), r=[('xnT', kc), w2k], w=[pbk])
            hd, hk = self.tmr.next()
            self.cp('act', hd[:, 0:8, :].rearrange("p h d -> p (h d)"), pq[:], r=[pqk], w=[hk])
            self.cp('dve', hd[:, 8:10, :].rearrange("p h d -> p (h d)"), pa[:, 256:384], r=[pak], w=[hk])
            self.cp('dve', hd[:, 10:12, :].rearrange("p h d -> p (h d)"), pb[:, 0:128], r=[pbk], w=[hk])
            rows, rowsk = self.rowsr.next()
            self.cp('act', rows[:, 0:512], pa[:], r=[pak], w=[rowsk])
            self.cp('act', rows[:, 640:768], pb[:, 128:256], r=[pbk], w=[rowsk])
            self.actf(self.ngt[:, blk, :], pb[:, 256:280], AF.Sigmoid, r=[pbk], w=[('ngt', blk)])
            self.cp('dve', self.Vs[:, kb, :, 0:64], pa[:, 384:512].rearrange("p (g d) -> p g d", g=2), r=[pak], w=['Vs'])
            slot = (kb // 4) % 2 * 4 + kb % 4
            self.cp('dve', self.Vw[:, slot, :, 0:64], pb[:, 128:256].rearrange("p (g d) -> p g d", g=2), r=[pbk], w=['Vw'])
            cos_b = self.cosT[:, kb, :].unsqueeze(1).to_broadcast([128, 12, 8])
            sin_b = self.sinT[:, kb, :].unsqueeze(1).to_broadcast([128, 12, 8])
            hr, hrk = self.headnorm_rope(hd[:], hk, 12, self.g12b[:], cos_b, sin_b, 1.0 / 64)
            self.cp('pool', rows[:, 256:384], hr[:, 8:10, :].rearrange("p h d -> p (h d)"), r=[hrk], w=[rowsk])
            self.cp('pool', rows[:, 512:640], hr[:, 10:12, :].rearrange("p h d -> p (h d)"), r=[hrk], w=[rowsk])
            self.dma('sp', self.rows_o[l, kb * 128:(kb + 1) * 128, :], rows[:, 0:512], r=[rowsk], w=['rows_o'])
            if t == NTILE - 1:
                self.dma('sp', self.win_o[l, blk * 128:(blk + 1) * 128, :], rows[:, 512:768], r=[rowsk], w=['win_o'])
            ts_, tsk = self.tsrc.next()
            self.cp('pool', ts_[:, 0:4, :].rearrange("p c f -> p (c f)"), hd[:, 0:8, :].rearrange("p h d -> p (h d)"), r=[hk], w=[tsk])
            self.cp('pool', ts_[:, 4:8, :].rearrange("p c f -> p (c f)"), hr[:, 0:8, :].rearrange("p h d -> p (h d)"), r=[hrk], w=[tsk])
            self.cp('pool', ts_[:, 8:10, :].rearrange("p c f -> p (c f)"), hr[:, 8:12, :].rearrange("p h d -> p (h d)"), r=[hrk], w=[tsk])
            self.cp('act', ts_[:, 10:12, :].rearrange("p c f -> p (c f)"), pa[:, 0:256], r=[pak], w=[tsk])
            return (blk, ts_, tsk)

    def tok_part2(self, t, st):
        blk, ts_, tsk = st
        kb = t * 4 + blk
        cs = slice(blk * 128, (blk + 1) * 128)
        if True:
            p0, p0k = self.psn()
            p1, p1k = self.psn()
            p0b = p0[:].bitcast(BF16)
            p1b = p1[:].bitcast(BF16)
            for c in range(8):
                self.tr(p0b[:, c * 128:(c + 1) * 128], ts_[:, c, :], self.ident_b[:], r=[tsk, 'ident_b'], w=[p0k])
            for c in range(4):
                self.tr(p1b[:, c * 128:(c + 1) * 128], ts_[:, 8 + c, :], self.ident_b[:], r=[tsk, 'ident_b'], w=[p1k])
            p0v = p0b.rearrange("p (c f) -> p c f", c=8)
            self.cp('act', self.QU[:, :, cs], p0v[:, 0:4, :], r=[p0k], w=['QU'])
            self.cp('dve', self.QB[0:64, 0::2, cs], p0v[0:64, 4:8, :], r=[p0k], w=['QBq'])
            self.cp('act', self.QB[0:64, 1::2, cs], p0v[64:128, 4:8, :], r=[p0k], w=['QBq'])
            ks = slice(kb * 128, (kb + 1) * 128)
            self.cp('dve', self.KE[0][0:64, ks], p1b[0:64, 0:128], r=[p1k], w=['KE0'])
            self.cp('act', self.KE[1][0:64, ks], p1b[64:128, 0:128], r=[p1k], w=['KE1'])
            wcs = slice((kb // 4) % 2 * 512 + (kb % 4) * 128, (kb // 4) % 2 * 512 + (kb % 4 + 1) * 128)
            self.cp('dve', self.kwT[0][0:64, wcs], p1b[0:64, 128:256], r=[p1k], w=['kwT0'])
            self.cp('act', self.kwT[1][0:64, wcs], p1b[64:128, 128:256], r=[p1k], w=['kwT1'])
            rs = slice(16 + blk * 128, 16 + (blk + 1) * 128)
            self.cp('dve', self.rawk[:, rs], p1b[:, 256:384], r=[p1k], w=['rawk'])
            self.cp('act', self.rawv[:, rs], p1b[:, 384:512], r=[p1k], w=['rawv'])

    def compress(self, l, t):
        w1 = self.load_w1(l)
        c0 = 32 * (t % 4)
        cc = t // 4
        for kind in range(2):
            v, key = w1[kind]
            raw, rawkey = (self.rawk, 'rawk') if kind == 0 else (self.rawv, 'rawv')
            for g in range(2):
                r0 = 64 * g
                ph, phk = self.psn()
                for s in range(32):
                    self.mm(ph[:, 0:32], lhsT=v[r0:r0 + 64, s, :], rhs=raw[r0:r0 + 64, s:s + 497:16],
                            start=(s == 0), stop=(s == 31), r=[key, rawkey], w=[phk])
                hs, hsk = self.b16t()
                self.actf(hs[:, 0:32], ph[:, 0:32], AF.Silu, bias=self.bpe[:, kind:kind + 1], r=[phk, 'bpe'], w=[hsk])
                pc, pck = self.psn()
                self.mm(pc[0:32, 0:64], lhsT=hs[:, 0:32], rhs=self.w2sb[:, kind, :], r=[hsk, 'w2sb'], w=[pck])
                if kind == 0:
                    kf, kfk = self.f32t()
                    sm, smk = self.smr.next()
                    self.actf(kf[0:32, 64:128], pc[0:32, 0:64], AF.Square, accum=sm[0:32, 0:1], r=[pck], w=[kfk, smk])
                    self.actf(sm[0:32, 1:2], sm[0:32, 0:1], AF.Sqrt, bias=self.epsc[0:32, 0:1], scale=1.0 / 64, r=[smk, 'epsc'], w=[smk])
                    self.recip(sm[0:32, 2:3], sm[0:32, 1:2], r=[smk], w=[smk])
                    kn, knk = self.b16t()
                    self.stt(kn[0:32, 0:64], pc[0:32, 0:64], sm[0:32, 2:3], self.k0gb[0:32, :], ALU.mult, ALU.mult,
                             r=[pck, smk, 'k0gb'], w=[knk])
                    self.cp('dve', kn[0:32, 64:128], kn[0:32, 0:64], r=[knk], w=[knk])
                    pt, ptk = self.psn()
                    ptb = pt[:].bitcast(BF16)
                    self.tr(ptb[:, 0:32], kn[0:32, 0:128], self.ident_b[0:32, 0:32], r=[knk, 'ident_b'], w=[ptk])
                    self.cp('act', self.kcT2[g][:, 32 * t:32 * t + 32], ptb[:, 0:32], r=[ptk], w=[f'kcT{g}'])
                else:
                    self.cp('act', self.Rc[g][c0:c0 + 32, cc, 0:64], pc[0:32, 0:64], r=[pck], w=[f'Rc{g}'])
                    if t == 0:
                        self.memset('pool', self.Rc[g][0:1, 0, 0:64], 0.0, w=[f'Rc{g}'])
        self.cp('pool', self.rawk[:, 0:16], self.rawk[:, 512:528], r=['rawk'], w=['rawk'])
        self.cp('pool', self.rawv[:, 0:16], self.rawv[:, 512:528], r=['rawv'], w=['rawv'])

    def fm_proj(self, l, t):
        w_in = self.inp('w_in')[l]
        wcx, wcxk = self.wload(w_in[:, C_CX:C_CX + 512], 8, 512)
        wcb, wcbk = self.wload(w_in[:, C_CB:C_CB + 512], 8, 512)
        wcc, wcck = self.wload(w_in[:, C_CC:C_CC + 512], 8, 512)
        xk = [('xnT', kc) for kc in range(8)]
        for ci in range(4):
            cs = slice(ci * 128, (ci + 1) * 128)
            px, pxk = self.psn()
            pc, pck = self.psn()
            pb, pbk = self.psn()
            for (pp, ppk, ww, wwk) in ((px, pxk, wcx, wcxk), (pc, pck, wcc, wcck), (pb, pbk, wcb, wcbk)):
                for kc in range(8):
                    self.mm(pp[:], lhsT=ww[:, kc, cs], rhs=self.xnT[:, kc, :], start=(kc == 0), stop=(kc == 7), r=[wwk, ('xnT', kc)], w=[ppk])
            cxs, cxk = self.f32t()
            self.cp('act', cxs[:], px[:], r=[pxk], w=[cxk])
            ue, uek = self.uextr.next()
            self.cp('pool', ue[:, 0:2], self.ucar[:, ci, :], r=['ucar'], w=[uek])
            self.tt('dve', ue[:, 2:514], pc[:], cxs[:], ALU.mult, r=[pck, cxk], w=[uek])
            a1, a1k = self.f32t()
            self.ts('pool', a1[:], ue[:, 0:512], self.cwc[:, ci, 0:1], ALU.mult, r=[uek, 'cwc'], w=[a1k])
            self.stt(a1[:], ue[:, 1:513], self.cwc[:, ci, 1:2], a1[:], ALU.mult, ALU.add, r=[uek, 'cwc', a1k], w=[a1k])
            self.stt(a1[:], ue[:, 2:514], self.cwc[:, ci, 2:3], a1[:], ALU.mult, ALU.add, r=[uek, 'cwc', a1k], w=[a1k])
            self.tt('dve', self.brT[:, 4 + ci, :], pb[:], a1[:], ALU.mult, r=[pbk, a1k], w=['brT1'])
            self.cp('pool', self.ucar[:, ci, :], ue[:, 512:514], r=[uek], w=['ucar'])
            if t == NTILE - 1:
                self.dma('sp', self.conv_o[l, :, cs].rearrange("j p -> p j"), ue[:, 512:514], r=[uek], w=['conv_o'], slow=True)

    def cmp_attend(self, t, between=None):
        nch = t // 4 + 1
        ngk = [('ngt', b) for b in range(4)]
        for h in range(8):
            g = h // 4
            r0 = 64 * (h % 2)
            pTs = []
            for cc in range(nch):
                sz = 128 if cc < nch - 1 else 32 * (t % 4 + 1)
                ps_, psk = self.psn()
                self.mm(ps_[0:sz, :], lhsT=self.kcT2[g][r0:r0 + 64, cc * 128:cc * 128 + sz], rhs=self.QU[r0:r0 + 64, h // 2, :],
                        r=[f'kcT{g}', 'QU'], w=[psk])
                pT, pTk = self.b16t()
                self.actf(pT[0:sz, :], ps_[0:sz, :], AF.Exp, scale=0.125, r=[psk], w=[pTk])
                if cc == nch - 1:
                    self.tt('pool', pT[sz - 32:sz, :], pT[sz - 32:sz, :], self.stair[sz - 32:sz, :], ALU.mult, r=[pTk, 'stair'], w=[pTk])
                pTs.append((pT, pTk, sz))
            banks = [self.psn(), self.psn()]
            for qb in range(4):
                bk, bkk = banks[qb // 2]
                off = (qb % 2) * 129
                for cc, (pT, pTk, sz) in enumerate(pTs):
                    self.mm(bk[:, off:off + 129], lhsT=pT[0:sz, qb * 128:(qb + 1) * 128], rhs=self.Rc[g][0:sz, cc, 0:129],
                            start=(cc == 0), stop=(cc == nch - 1), r=[pTk, f'Rc{g}'], w=[bkk])
            for qb in range(4):
                bk, bkk = banks[qb // 2]
                off = (qb % 2) * 129
                sm, smk = self.smr.next()
                self.ts('dve', sm[:, 0:1], bk[:, off + 64:off + 65], 1e-30, ALU.add, r=[bkk], w=[smk])
                self.recip(sm[:, 1:2], sm[:, 0:1], r=[smk], w=[smk])
                self.tt('dve', sm[:, 2:3], sm[:, 1:2], self.ngt[:, qb, h:h + 1], ALU.mult, r=[smk] + ngk, w=[smk])
                self.ts('dve', self.onsa[:, qb, h * 64:(h + 1) * 64], bk[:, off:off + 64], sm[:, 2:3], ALU.mult, r=[bkk, smk], w=['onsa'])
                if h % 4 == 0:
                    self.ts('dve', self.scr[:, qb, g, :], bk[:, off + 65:off + 129], sm[:, 1:2], ALU.mult, r=[bkk, smk], w=['scr'])
                else:
                    self.stt(self.scr[:, qb, g, :], bk[:, off + 65:off + 129], sm[:, 1:2], self.scr[:, qb, g, :], ALU.mult, ALU.add,
                             r=[bkk, smk, 'scr'], w=['scr'])
            if between is not None:
                between(h)

    def topk(self, t):
        sl, slk = self.selTr.next()
        self.dma('sp', sl[:, 0], self.inp('c_selA')[:, 4 * t:4 * t + 4, :], w=[slk])
        self.dma('sp', sl[:, 1], self.inp('c_selB')[:, 4 * t:4 * t + 4, :], w=[slk])
        for g in range(2):
            pt, ptk = self.psn()
            ptb = pt[:].bitcast(BF16)
            for qb in range(4):
                s2, s2k = self.selr.next()
                self.tt('dve', s2[:, 0, :], self.scr[:, qb, g, :], sl[:, 0, qb, :], ALU.mult, r=['scr', slk], w=[s2k])
                self.tt('dve', s2[:, 0, :], s2[:, 0, :], sl[:, 1, qb, :], ALU.add, r=[s2k, slk], w=[s2k])
                sm, smk = self.smr.next()
                self.S.add('dve', (lambda o, i: lambda e: e.max(out=o, in_=i))(sm[:, 0:8], s2[:, 0, :]), [s2k], [smk])
                self.S.add('dve', (lambda o, a, b: lambda e: e.match_replace(out=o, in_to_replace=a, in_values=b, imm_value=-1e30))(s2[:, 1, :], sm[:, 0:8], s2[:, 0, :]), [s2k, smk], [s2k])
                self.S.add('dve', (lambda o, i: lambda e: e.max(out=o, in_=i))(sm[:, 8:16], s2[:, 1, :]), [s2k], [smk])
                self.ts('dve', sm[:, 16:17], sm[:, 15:16], -1e8, ALU.max, r=[smk], w=[smk])
                self.ts('dve', s2[:, 1, :], s2[:, 0, :], sm[:, 16:17], ALU.is_ge, r=[s2k, smk], w=[s2k])
                bw, bwk = self.biasr.next()
                self.ts('dve', bw[:, 64:128], s2[:, 1, :], -1.0, ALU.add, -MASKV, ALU.mult, r=[s2k], w=[bwk])
                self.tr(ptb[:, qb * 128:(qb + 1) * 128], bw[:], self.ident_b[:], r=[bwk, 'ident_b'], w=[ptk])
            for hh in range(4):
                self.cp('act' if hh % 2 == 0 else 'dve', self.QB[64:128, g * 4 + hh, :], ptb[64:128, 0:512], r=[ptk], w=['QBb'])

    def norm_acc(self, acc, acck, h, ngoff):
        ngk = [('ngt', b) for b in range(4)]
        sm, smk = self.smr.next()
        self.recip(sm[:, 0:4], acc[:, 64:260:65], r=[acck], w=[smk])
        self.tt('dve', sm[:, 4:8], sm[:, 0:4], self.ngt[:, :, ngoff + h], ALU.mult, r=[smk] + ngk, w=[smk])
        for j in range(4):
            dst = self.onsa[:, j, h * 64:(h + 1) * 64]
            self.stt(dst, acc[:, j * 65:j * 65 + 64], sm[:, 4 + j:5 + j], dst, ALU.mult, ALU.add, r=[acck, smk, 'onsa'], w=['onsa'])

    def slc(self, t):
        for h in range(8):
            g = h // 4
            acc, acck = self.accr.next()

            def stage_a(kb):
                i0 = max(0, kb - 4 * t)
                c0 = 128 * i0
                ps_, psk = self.psn()
                self.mm(ps_[:, c0:512], lhsT=self.KE[g][:, kb * 128:(kb + 1) * 128], rhs=self.QB[:, h, c0:512],
                        r=[f'KE{g}', 'QBq', 'QBb'], w=[psk])
                pT, pTk = self.b16t()
                self.actf(pT[:, c0:512], ps_[:, c0:512], AF.Exp, scale=0.125, r=[psk], w=[pTk])
                if kb >= 4 * t:
                    self.tt('dve', pT[:, c0:c0 + 128], pT[:, c0:c0 + 128], self.tri[:], ALU.mult, r=[pTk, 'tri'], w=[pTk])
                return (kb, i0, pT, pTk)

            def stage_b(st, first):
                kb, i0, pT, pTk = st
                for j in range(i0, 4):
                    self.mm(acc[:, j * 65:(j + 1) * 65], lhsT=pT[:, j * 128:(j + 1) * 128], rhs=self.Vs[:, kb, g, 0:65],
                            start=(first and j == i0), stop=False, r=[pTk, 'Vs'], w=[acck])
            nkb = 4 * t + 4
            prev = stage_a(0)
            for kb in range(1, nkb):
                cur = stage_a(kb)
                stage_b(prev, prev[0] == 0)
                prev = cur
            stage_b(prev, prev[0] == 0)
            self.norm_acc(acc, acck, h, 8)

    def win(self, t):
        for h in range(8):
            g = h // 4
            acc, acck = self.accr.next()
            kbs = list(range(max(0, 4 * t - 4), 4 * t + 4))

            def stage_a(kb):
                jlo = max(0, kb - 4 * t)
                jhi = min(3, kb - 4 * t + 4)
                ring = (kb // 4) % 2
                wc0 = ring * 512 + (kb % 4) * 128
                slot = ring * 4 + kb % 4
                cl, ch = 128 * jlo, 128 * (jhi + 1)
                ps_, psk = self.psn()
                self.mm(ps_[:, cl:ch], lhsT=self.kwT[g][0:64, wc0:wc0 + 128], rhs=self.QB[0:64, h, cl:ch], r=[f'kwT{g}', 'QBq'], w=[psk])
                pT, pTk = self.b16t()
                self.actf(pT[:, cl:ch], ps_[:, cl:ch], AF.Exp, scale=0.125, r=[psk], w=[pTk])
                for j in range(jlo, jhi + 1):
                    d = 4 * t + j - kb
                    if d == 0:
                        self.tt('dve', pT[:, j * 128:(j + 1) * 128], pT[:, j * 128:(j + 1) * 128], self.tri[:], ALU.mult, r=[pTk, 'tri'], w=[pTk])
                    if d == 4:
                        self.tt('dve', pT[:, j * 128:(j + 1) * 128], pT[:, j * 128:(j + 1) * 128], self.strict[:], ALU.mult, r=[pTk, 'strict'], w=[pTk])
                return (kb, jlo, jhi, slot, pT, pTk)

            def stage_b(st, first):
                kb, jlo, jhi, slot, pT, pTk = st
                for j in range(jlo, jhi + 1):
                    self.mm(acc[:, j * 65:(j + 1) * 65], lhsT=pT[:, j * 128:(j + 1) * 128], rhs=self.Vw[:, slot, g, 0:65],
                            start=(first and j == jlo), stop=False, r=[pTk, 'Vw'], w=[acck])
            prev = stage_a(kbs[0])
            for kb in kbs[1:]:
                cur = stage_a(kb)
                stage_b(prev, prev[0] == kbs[0])
                prev = cur
            stage_b(prev, prev[0] == kbs[0])
            self.norm_acc(acc, acck, h, 16)

    def nsa_finalize(self):
        for j in range(4):
            ob, obk = self.b16t()
            self.cp('pool', ob[:], self.onsa[:, j, :], r=['onsa'], w=[obk])
            pt, ptk = self.psn()
            ptb = pt[:].bitcast(BF16)
            for c in range(4):
                self.tr(ptb[:, c * 128:(c + 1) * 128], ob[:, c * 128:(c + 1) * 128], self.ident_b[:], r=[obk, 'ident_b'], w=[ptk])
            self.cp('act', self.brT[:, 0:4, j * 128:(j + 1) * 128], ptb[:, 0:512].rearrange("p (c f) -> p c f", c=4), r=[ptk], w=['brT0'])

    def mem_attend(self, heads=None):
        for hm in (heads if heads is not None else range(4)):
            pTs = []
            for mb in range(2):
                ps_, psk = self.psn()
                self.mm(ps_[:], lhsT=self.mkT[:, hm, mb * 128:(mb + 1) * 128], rhs=self.mqT[:, hm, :], r=['mkT', 'mqT'], w=[psk])
                pT, pTk = self.b16t()
                self.actf(pT[:], ps_[:], AF.Exp, scale=128 ** -0.5, r=[psk], w=[pTk])
                pTs.append((pT, pTk))
            po, pok = self.psn()
            pd, pdk = self.psn()
            for mb, (pT, pTk) in enumerate(pTs):
                self.mm(po[:], lhsT=self.mv[:, mb, hm * 128:(hm + 1) * 128], rhs=pT[:], start=(mb == 0), stop=(mb == 1), r=['mv', pTk], w=[pok])
            for mb, (pT, pTk) in enumerate(pTs):
                self.mm(pd[:], lhsT=self.ones_b[:], rhs=pT[:], start=(mb == 0), stop=(mb == 1), r=['ones_b', pTk], w=[pdk])
            rc, rck = self.f32t()
            self.recip(rc[:], pd[:], r=[pdk], w=[rck])
            self.tt('dve', self.brT[:, 8 + hm, :], po[:], rc[:], ALU.mult, r=[pok, rck], w=['brT2'])

    def phase_b(self, l):
        w_in = self.inp('w_in')[l]
        QBK = ['QBq', 'QBb']
        for fcg in range(2):
            for n in range(3):
                wm, wmk = self.wload(w_in[:, C_MG + n * 1024 + fcg * 512:C_MG + n * 1024 + (fcg + 1) * 512], 8, 512)
                wb, wbk = self.wload(self.inp('w_branch')[l, n][:, fcg * 512:(fcg + 1) * 512], 4, 512)
                for fi in range(4):
                    fc = fcg * 4 + fi
                    cs = slice(fi * 128, (fi + 1) * 128)
                    pg, pgk = self.psn()
                    pp, ppk = self.psn()
                    for kc in range(8):
                        self.mm(pg[:], lhsT=wm[:, kc, cs], rhs=self.xnT[:, kc, :], start=(kc == 0), stop=(kc == 7), r=[wmk, ('xnT', kc)], w=[pgk])
                    for kc in range(4):
                        self.mm(pp[:], lhsT=wb[:, kc, cs], rhs=self.brT[:, n * 4 + kc, :], start=(kc == 0), stop=(kc == 3), r=[wbk, f'brT{n}'], w=[ppk])
                    sg, sgk = self.f32t()
                    self.actf(sg[:], pg[:], AF.Sigmoid, r=[pgk], w=[sgk])
                    if n == 0:
                        self.tt('dve', self.macc[:, fi, :], sg[:], pp[:], ALU.mult, r=[sgk, ppk], w=['onsa'])
                    else:
                        self.tt('dve', sg[:], sg[:], pp[:], ALU.mult, r=[sgk, ppk], w=[sgk])
                        if n == 1:
                            self.tt('pool', self.macc[:, fi, :], self.macc[:, fi, :], sg[:], ALU.add, r=[sgk, 'onsa'], w=['onsa'])
                        else:
                            self.tt('pool', self.mT[:, fc, :], self.macc[:, fi, :], sg[:], ALU.add, r=[sgk, 'onsa'], w=QBK)

    def phase_c(self, l):
        QBK = ['QBq', 'QBb']
        for half in range(2):
            wo, wok = self.wload(self.inp('w_out')[l][:, half * 512:(half + 1) * 512], 8, 512)
            for fi in range(4):
                fc = half * 4 + fi
                po, pok = self.psn()
                for kc in range(8):
                    self.mm(po[:], lhsT=wo[:, kc, fi * 128:(fi + 1) * 128], rhs=self.mT[:, kc, :], start=(kc == 0), stop=(kc == 7), r=[wok] + QBK, w=[pok])
                self.tt('dve', self.xT[:, fc, :], po[:], self.xT[:, fc, :], ALU.add, r=[pok, ('xT', fc)], w=[('xT', fc)])

    def phase_d(self, l):
        self.rmsnorm(1)
        w_gu = self.inp('w_gate_up')[l]
        w_dn = self.inp('w_down')[l]
        groups = [(0, 6), (6, 6), (12, 5), (17, 5)]
        for (j0, n) in groups:
            for p0 in range(0, n, 4):
                pn = min(4, n - p0)
                ja = j0 + p0
                wg, wgk = self.wload(w_gu[:, ja * 128:(ja + pn) * 128], 8, pn * 128)
                wu, wuk = self.wload(w_gu[:, DFF + ja * 128:DFF + (ja + pn) * 128], 8, pn * 128)
                for q in range(pn):
                    jj = p0 + q
                    cs = slice(q * 128, (q + 1) * 128)
                    pg, pgk = self.psn()
                    pu, puk = self.psn()
                    for kc in range(8):
                        self.mm(pg[:], lhsT=wg[:, kc, cs], rhs=self.xnT[:, kc, :], start=(kc == 0), stop=(kc == 7), r=[wgk, ('xnT', kc)], w=[pgk])
                    for kc in range(8):
                        self.mm(pu[:], lhsT=wu[:, kc, cs], rhs=self.xnT[:, kc, :], start=(kc == 0), stop=(kc == 7), r=[wuk, ('xnT', kc)], w=[puk])
                    sg, sgk = self.f32t()
                    self.actf(sg[:], pg[:], AF.Silu, r=[pgk], w=[sgk])
                    self.tt('dve', self.actT[:, jj, :], sg[:], pu[:], ALU.mult, r=[sgk, puk], w=[('actT', jj)])
            for cq in range(4):
                wd, wdk = self.wload(w_dn[j0 * 128:(j0 + n) * 128, cq * 256:(cq + 1) * 256], n, 256)
                for fi in range(2):
                    fc = cq * 2 + fi
                    pd, pdk = self.psn()
                    for kc in range(n):
                        self.mm(pd[:], lhsT=wd[:, kc, fi * 128:(fi + 1) * 128], rhs=self.actT[:, kc, :], start=(kc == 0), stop=(kc == n - 1),
                                r=[wdk, ('actT', kc)], w=[pdk])
                    self.tt('dve', self.xT[:, fc, :], pd[:], self.xT[:, fc, :], ALU.add, r=[pdk, ('xT', fc)], w=[('xT', fc)])

    def store_x(self, l, t):
        xk = [('xT', kc) for kc in range(8)]
        if l < self.nlayers - 1:
            self.dma('sp', self.hres[:, :, t * TT:(t + 1) * TT].rearrange("k p t -> p k t"), self.xT[:], r=xk, w=['hres'])
        else:
            for blk in range(4):
                xo, xok = self.tmior.next()
                for hf in range(2):
                    pb, pk = self.psn()
                    for c in range(4):
                        kc = hf * 4 + c
                        self.tr(pb[:, c * 128:(c + 1) * 128], self.xT[:, kc, blk * 128:(blk + 1) * 128], self.ident_f[:], r=[('xT', kc), 'ident_f'], w=[pk])
                    self.cp('act' if hf == 0 else 'dve', xo[:, hf * 512:(hf + 1) * 512], pb[:], r=[pk], w=[xok])
                self.dma('sp', self.y[(t * 4 + blk) * 128:(t * 4 + blk + 1) * 128, :], xo[:], r=[xok], w=['y'])

    def make_ctxs(self):
        class C:
            pass
        P = C()
        P.N, P.k = TT, ''
        P.xT, P.xnT, P.brT, P.mT, P.macc, P.actT, P.mqT = self.xT, self.xnT, self.brT, self.mT, self.macc, self.actT, self.mqT
        P.mTk, P.mack = ['QBq', 'QBb'], 'onsa'
        self.P = P
        X = C()
        X.N, X.k = 4, 's_'
        sb = self.S.sb
        X.xT = sb([128, 8, 4], F32, 's_xT')
        X.xnT = sb([128, 8, 4], BF16, 's_xnT')
        X.brT = sb([128, 12, 4], BF16, 's_brT')
        X.mT = sb([128, 8, 4], BF16, 's_mT')
        X.macc = sb([128, 4, 4], F32, 's_macc')
        X.actT = sb([128, 6, 4], BF16, 's_actT')
        X.mqT = sb([128, 4, 4], BF16, 's_mqT')
        X.mTk, X.mack = ['s_mT'], 's_macc'
        self.X = X

    def rmsnorm(self, gi, c=None):
        c = c or self.P
        N, k = c.N, c.k
        ps, pk = self.psn()
        for kc in range(8):
            sq, sqk = self.b16t()
            self.actf(sq[:, 0:N], c.xT[:, kc, :], AF.Square, r=[(k + 'xT', kc)], w=[sqk])
            self.mm(ps[:, 0:N], lhsT=self.ones_b[:], rhs=sq[:, 0:N], start=(kc == 0), stop=(kc == 7), r=['ones_b', sqk], w=[pk])
        rt, rtk = self.f32t()
        self.actf(rt[:, 0:N], ps[:, 0:N], AF.Sqrt, bias=self.epsc[:, 0:1], scale=1.0 / D, r=[pk, 'epsc'], w=[rtk])
        rs, rsk = self.f32t()
        self.recip(rs[:, 0:N], rt[:, 0:N], r=[rtk], w=[rsk])
        for kc in range(8):
            self.stt(c.xnT[:, kc, :], c.xT[:, kc, :], self.gcols[:, gi, kc:kc + 1], rs[:, 0:N], ALU.mult, ALU.mult,
                     r=[(k + 'xT', kc), 'gcols', rsk], w=[(k + 'xnT', kc)])

    def mq_proj(self, l, c=None, heads=None, wq=None):
        c = c or self.P
        N, k = c.N, c.k
        wmq, wmqk = wq if wq is not None else self.wload(self.inp('w_in')[l][:, C_MQ:C_MQ + 512], 8, 512)
        for hm in (heads if heads is not None else range(4)):
            cs = slice(hm * 128, (hm + 1) * 128)
            pm, pmk = self.psn()
            for kc in range(8):
                self.mm(pm[:, 0:N], lhsT=wmq[:, kc, cs], rhs=c.xnT[:, kc, :], start=(kc == 0), stop=(kc == 7), r=[wmqk, (k + 'xnT', kc)], w=[pmk])
            sq, sqk = self.b16t()
            self.actf(sq[:, 0:N], pm[:, 0:N], AF.Square, r=[pmk], w=[sqk])
            pss, pssk = self.psn()
            self.mm(pss[:, 0:N], lhsT=self.ones_b[:], rhs=sq[:, 0:N], r=['ones_b', sqk], w=[pssk])
            rt, rtk = self.f32t()
            self.actf(rt[:, 0:N], pss[:, 0:N], AF.Sqrt, bias=self.epsc[:, 0:1], scale=1.0 / 128, r=[pssk, 'epsc'], w=[rtk])
            self.recip(rt[:, 0:N], rt[:, 0:N], r=[rtk], w=[rtk])
            self.stt(c.mqT[:, hm, :], pm[:, 0:N], self.mqgc[:, 0:1], rt[:, 0:N], ALU.mult, ALU.mult, r=[pmk, 'mqgc', rtk], w=[k + 'mqT'])

    def phase_b(self, l, ctxs=None):
        ctxs = ctxs or [self.P]
        w_in = self.inp('w_in')[l]
        for fcg in range(2):
            for n in range(3):
                wm, wmk = self.wload(w_in[:, C_MG + n * 1024 + fcg * 512:C_MG + n * 1024 + (fcg + 1) * 512], 8, 512)
                wb, wbk = self.wload(self.inp('w_branch')[l, n][:, fcg * 512:(fcg + 1) * 512], 4, 512)
                for fi in range(4):
                    fc = fcg * 4 + fi
                    cs = slice(fi * 128, (fi + 1) * 128)
                    for c in ctxs:
                        N, k = c.N, c.k
                        pg, pgk = self.psn()
                        pp, ppk = self.psn()
                        for kc in range(8):
                            self.mm(pg[:, 0:N], lhsT=wm[:, kc, cs], rhs=c.xnT[:, kc, :], start=(kc == 0), stop=(kc == 7), r=[wmk, (k + 'xnT', kc)], w=[pgk])
                        for kc in range(4):
                            self.mm(pp[:, 0:N], lhsT=wb[:, kc, cs], rhs=c.brT[:, n * 4 + kc, :], start=(kc == 0), stop=(kc == 3), r=[wbk, f'{k}brT{n}'], w=[ppk])
                        sg, sgk = self.f32t()
                        self.actf(sg[:, 0:N], pg[:, 0:N], AF.Sigmoid, r=[pgk], w=[sgk])
                        if n == 0:
                            self.tt('dve', c.macc[:, fi, :], sg[:, 0:N], pp[:, 0:N], ALU.mult, r=[sgk, ppk], w=[c.mack])
                        else:
                            self.tt('dve', sg[:, 0:N], sg[:, 0:N], pp[:, 0:N], ALU.mult, r=[sgk, ppk], w=[sgk])
                            if n == 1:
                                self.tt('pool', c.macc[:, fi, :], c.macc[:, fi, :], sg[:, 0:N], ALU.add, r=[sgk, c.mack], w=[c.mack])
                            else:
                                self.tt('pool', c.mT[:, fc, :], c.macc[:, fi, :], sg[:, 0:N], ALU.add, r=[sgk, c.mack], w=c.mTk)

    def phase_c(self, l, ctxs=None):
        ctxs = ctxs or [self.P]
        for half in range(2):
            wo, wok = self.wload(self.inp('w_out')[l][:, half * 512:(half + 1) * 512], 8, 512)
            for fi in range(4):
                fc = half * 4 + fi
                for c in ctxs:
                    N, k = c.N, c.k
                    po, pok = self.psn()
                    for kc in range(8):
                        self.mm(po[:, 0:N], lhsT=wo[:, kc, fi * 128:(fi + 1) * 128], rhs=c.mT[:, kc, :], start=(kc == 0), stop=(kc == 7), r=[wok] + c.mTk, w=[pok])
                    self.tt('dve', c.xT[:, fc, :], po[:, 0:N], c.xT[:, fc, :], ALU.add, r=[pok, (k + 'xT', fc)], w=[(k + 'xT', fc)])

    def phase_d(self, l, ctxs=None):
        ctxs = ctxs or [self.P]
        for c in ctxs:
            self.rmsnorm(1, c)
        w_gu = self.inp('w_gate_up')[l]
        w_dn = self.inp('w_down')[l]
        groups = [(0, 6), (6, 6), (12, 5), (17, 5)]
        for (j0, n) in groups:
            for p0 in range(0, n, 4):
                pn = min(4, n - p0)
                ja = j0 + p0
                wg, wgk = self.wload(w_gu[:, ja * 128:(ja + pn) * 128], 8, pn * 128)
                wu, wuk = self.wload(w_gu[:, DFF + ja * 128:DFF + (ja + pn) * 128], 8, pn * 128)
                for q in range(pn):
                    jj = p0 + q
                    cs = slice(q * 128, (q + 1) * 128)
                    for c in ctxs:
                        N, k = c.N, c.k
                        pg, pgk = self.psn()
                        pu, puk = self.psn()
                        for kc in range(8):
                            self.mm(pg[:, 0:N], lhsT=wg[:, kc, cs], rhs=c.xnT[:, kc, :], start=(kc == 0), stop=(kc == 7), r=[wgk, (k + 'xnT', kc)], w=[pgk])
                        for kc in range(8):
                            self.mm(pu[:, 0:N], lhsT=wu[:, kc, cs], rhs=c.xnT[:, kc, :], start=(kc == 0), stop=(kc == 7), r=[wuk, (k + 'xnT', kc)], w=[puk])
                        sg, sgk = self.f32t()
                        self.actf(sg[:, 0:N], pg[:, 0:N], AF.Silu, r=[pgk], w=[sgk])
                        self.tt('dve', c.actT[:, jj, :], sg[:, 0:N], pu[:, 0:N], ALU.mult, r=[sgk, puk], w=[(k + 'actT', jj)])
            for cq in range(4):
                wd, wdk = self.wload(w_dn[j0 * 128:(j0 + n) * 128, cq * 256:(cq + 1) * 256], n, 256)
                for fi in range(2):
                    fc = cq * 2 + fi
                    for c in ctxs:
                        N, k = c.N, c.k
                        pd, pdk = self.psn()
                        for kc in range(n):
                            self.mm(pd[:, 0:N], lhsT=wd[:, kc, fi * 128:(fi + 1) * 128], rhs=c.actT[:, kc, :], start=(kc == 0), stop=(kc == n - 1),
                                    r=[wdk, (k + 'actT', kc)], w=[pdk])
                        self.tt('dve', c.xT[:, fc, :], pd[:, 0:N], c.xT[:, fc, :], ALU.add, r=[pdk, (k + 'xT', fc)], w=[(k + 'xT', fc)])

    def declare_sample(self):
        i = self.din
        i("xs", [4, D]); i("pool", [DEPTH, 2560 * 128, 512]); i("cwin", [DEPTH, 4, 512, 256]); i("sconv", [DEPTH, 4, 2, 512])
        i("cmem", [DEPTH, 4, 256, 1024]); i("pt", [4, 64], I32)
        i("c_cos_s", [4, 8]); i("c_sin_s", [4, 8]); i("c_ovl_s", [128, 4, 129]); i("c_sA", [1, 129]); i("c_sB", [1, 129])
        i("c_e2", [2, 128]); i("c_lastmask", [128, 1]); i("c_winb", [128, 1]); i("c_pm64", [128, 1]); i("c_pidx", [128, 1])
        o = self.dout
        self.y_s = o("y_s", [4, D]); self.rows_s_o = o("rows_s", [DEPTH, 4, 512]); self.win_s_o = o("win_s", [DEPTH, 4, 512, 256])
        self.conv_s_o = o("conv_s", [DEPTH, 4, 2, 512])
        dr = lambda n, sh: self.nc.dram_tensor(n, sh, F32).ap()
        self.scr_rows = dr("scr_rows", [4, 768]); self.scr_q = dr("scr_q", [4, 512]); self.osc = dr("osc", [4, 3, 8, 64])
        sb = self.S.sb
        self.idxA = sb([128, 4, 32], I32, 'idxA'); self.idxB = sb([128, 4, 64], I32, 'idxB')
        self.qT_s = sb([128, 8, 4], BF16, 'qT_s'); self.ng_s = sb([4, 24], F32, 'ng_s')
        self.kcT_s = sb([128, 512], BF16, 'kcT_s'); self.Rs = [sb([128, 4, 194], BF16, f'Rs{g}') for g in range(2)]
        self.biask = sb([128, 2, 64], F32, 'biask'); self.cos_s = sb([4, 8], F32, 'cos_s'); self.sin_s = sb([4, 8], F32, 'sin_s')
        self.sA = sb([1, 129], F32, 'sA'); self.sB = sb([1, 129], F32, 'sB'); self.e2a = sb([1, 128], F32, 'e2a'); self.e2b = sb([1, 128], F32, 'e2b')
        self.lastm = sb([128, 1], F32, 'lastm'); self.winb = sb([128, 1], F32, 'winb'); self.ones_f = sb([4, 1], F32, 'ones_f')
        self.Vp = Rot([sb([128, 2, 66], BF16, f'Vp{k}') for k in range(2)], 'Vp')
        self.stT = sb([128, 4, 2, 4], F32, 'stT'); self.cso = sb([128, 4, 2, 4], F32, 'cso')
        of_ = self.onsa[:].rearrange("p a b -> p (a b)")
        self.nr = of_[0:1, 0:768]; self.nq = of_[0:1, 768:1280]; self.vne = sb([1, 2, 66], BF16, 'vne')

    def setup_sample(self):
        d = self.dma
        d('sp', self.cos_s[:], self.inp('c_cos_s'), w=['cos_s']); d('sp', self.sin_s[:], self.inp('c_sin_s'), w=['sin_s'])
        d('sp', self.sA[:], self.inp('c_sA'), w=['sA']); d('sp', self.sB[:], self.inp('c_sB'), w=['sB'])
        d('sp', self.e2a[:], self.inp('c_e2')[0:1, :], w=['e2a']); d('sp', self.e2b[:], self.inp('c_e2')[1:2, :], w=['e2b'])
        d('sp', self.lastm[:], self.inp('c_lastmask'), w=['lastm']); d('sp', self.winb[:], self.inp('c_winb'), w=['winb'])
        self.memset('pool', self.ones_f[:], 1.0, w=['ones_f'])
        for g in range(2):
            self.memset('pool', self.Rs[g][:, :, 64:65], 1.0, w=[f'Rs{g}'])
            d('pool', self.Rs[g][:, :, 65:194], self.inp('c_ovl_s'), w=[f'Rs{g}'])
        for k in range(2):
            self.memset('pool', self.Vp.bufs[k][:, :, 64:65], 1.0, w=[f'Vp{k}'])
        self.memset('pool', self.vne[:, :, 64:65], 1.0, w=['vne'])
        pt = self.inp('pt')
        pa, pak = self.selTr.next()
        ptbA = pa[:].rearrange("p a b c -> p (a b c)")[:, 0:128].bitcast(I32).rearrange("p (s q) -> p s q", s=4)
        ptbB = pa[:].rearrange("p a b c -> p (a b c)")[:, 128:384].bitcast(I32).rearrange("p (s q) -> p s q", s=4)
        pm, pmk = self.smr.next()
        d('sp', pm[:, 0:1], self.inp('c_pm64'), w=[pmk]); d('sp', pm[:, 1:2], self.inp('c_pidx'), w=[pmk])
        for h in range(2):
            d('sp', ptbA[64 * h:64 * h + 64], pt[:, h::2].partition_broadcast(64), w=[pak], slow=True)
        d('sp', ptbB, pt.partition_broadcast(128), w=[pak])
        self.ts('dve', self.idxA[:], ptbA, 64.0, ALU.mult, pm[:, 0:1], ALU.add, r=[pak, pmk], w=['idxA'])
        self.ts('dve', self.idxB[:], ptbB, 128.0, ALU.mult, pm[:, 1:2], ALU.add, r=[pak, pmk], w=['idxB'])

    def gather(self, dst, dkey, src, idx_ap, ikey, eoff):
        self.S.add('pool', lambda e: e.indirect_dma_start(out=dst, out_offset=None, in_=src, element_offset=eoff,
                                                           in_offset=bass.IndirectOffsetOnAxis(ap=idx_ap, axis=0)), [ikey], [dkey], dma=True)

    def sample_layer(self, l):
        X = self.X
        d = self.dma
        if l == 0:
            for s_ in range(4):
                d('sp', X.xT[:, :, s_], self.inp('xs')[s_].rearrange("(k p) -> p k", p=128), w=[('s_xT', kc) for kc in range(8)], slow=True)
        self.rmsnorm(0, X)
        w_in = self.inp('w_in')[l]
        wA = [self.wload(w_in[:, 0:512], 8, 512), self.wload(w_in[:, 512:1024], 8, 512), self.wload(w_in[:, 1024:1304], 8, 280)]
        pq, pqk = self.psn(); pa, pak = self.psn(); pb, pbk = self.psn()
        for (pp, ppk, (w, wk), nn) in ((pq, pqk, wA[0], 512), (pa, pak, wA[1], 512), (pb, pbk, wA[2], 280)):
            for kc in range(8):
                self.mm(pp[0:4, 0:nn], lhsT=X.xnT[:, kc, :], rhs=w[:, kc, :], start=(kc == 0), stop=(kc == 7), r=[('s_xnT', kc), wk], w=[ppk])
        hd_, hk = self.tmr.next()
        hd = hd_[0:4]
        self.cp('act', hd[:, 0:8, :].rearrange("p h d -> p (h d)"), pq[0:4, :], r=[pqk], w=[hk])
        self.cp('dve', hd[:, 8:10, :].rearrange("p h d -> p (h d)"), pa[0:4, 256:384], r=[pak], w=[hk])
        self.cp('dve', hd[:, 10:12, :].rearrange("p h d -> p (h d)"), pb[0:4, 0:128], r=[pbk], w=[hk])
        rows_, rowsk = self.rowsr.next()
        rows = rows_[0:4]
        self.cp('act', rows[:, 0:512], pa[0:4, :], r=[pak], w=[rowsk])
        self.cp('act', rows[:, 640:768], pb[0:4, 128:256], r=[pbk], w=[rowsk])
        self.actf(self.ng_s[:], pb[0:4, 256:280], AF.Sigmoid, r=[pbk], w=['ng_s'])
        cos_b = self.cos_s[:, :].unsqueeze(1).to_broadcast([4, 12, 8])
        sin_b = self.sin_s[:, :].unsqueeze(1).to_broadcast([4, 12, 8])
        hr, hrk = self.headnorm_rope(hd, hk, 12, self.g12b[0:4], cos_b, sin_b, 1.0 / 64)
        self.cp('pool', rows[:, 256:384], hr[:, 8:10, :].rearrange("p h d -> p (h d)"), r=[hrk], w=[rowsk])
        self.cp('pool', rows[:, 512:640], hr[:, 10:12, :].rearrange("p h d -> p (h d)"), r=[hrk], w=[rowsk])
        d('sp', self.rows_s_o[l], rows[:, 0:512], r=[rowsk], w=['rows_s_o'])
        d('sp', self.scr_rows, rows[:, 0:768], r=[rowsk], w=['scr_rows'])
        d('sp', self.scr_q, hr[:, 0:8, :].rearrange("p h d -> p (h d)"), r=[hrk], w=['scr_q'])
        d('sp', self.win_s_o[l, :, 511, :], rows[:, 512:768], r=[rowsk], w=['win_s_a'])
        d('sp', self.win_s_o[l, :, 0:511, :], self.inp('cwin')[l, :, 1:512, :], w=['win_s_b'])
        ts_, tsk = self.tsrc.next()
        tq = ts_[0:4]
        self.cp('pool', tq[:, 0:4, :].rearrange("p c (g d) -> p c g d", g=2), hd[:, 0:8, :].rearrange("p (g c) d -> p c g d", g=2), r=[hk], w=[tsk])
        self.cp('pool', tq[:, 4:8, :].rearrange("p c (g d) -> p c g d", g=2), hr[:, 0:8, :].rearrange("p (g c) d -> p c g d", g=2), r=[hrk], w=[tsk])
        p0, p0k = self.psn()
        p0b = p0[:].bitcast(BF16)
        for c in range(8):
            self.tr(p0b[:, c * 4:(c + 1) * 4], tq[:, c, :], self.ident_b[0:4, 0:4], r=[tsk, 'ident_b'], w=[p0k])
        self.cp('act', self.qT_s[:], p0b[:, 0:32].rearrange("p (c s) -> p c s", c=8), r=[p0k], w=['qT_s'])
        for s_ in range(4):
            for j_ in range(2):
                d('sp', self.stT[:, :, j_, s_], self.inp('sconv')[l, s_, j_].rearrange("(c p) -> p c", p=128), w=['stT'], slow=True)
        wcx, wcxk = self.wload(w_in[:, C_CX:C_CX + 512], 8, 512)
        wcb, wcbk = self.wload(w_in[:, C_CB:C_CB + 512], 8, 512)
        wcc, wcck = self.wload(w_in[:, C_CC:C_CC + 512], 8, 512)
        for ci in range(4):
            cs = slice(ci * 128, (ci + 1) * 128)
            px, pxk = self.psn(); pc, pck = self.psn(); pb2, pb2k = self.psn()
            for (pp, ppk, ww, wwk) in ((px, pxk, wcx, wcxk), (pc, pck, wcc, wcck), (pb2, pb2k, wcb, wcbk)):
                for kc in range(8):
                    self.mm(pp[:, 0:4], lhsT=ww[:, kc, cs], rhs=X.xnT[:, kc, :], start=(kc == 0), stop=(kc == 7), r=[wwk, ('s_xnT', kc)], w=[ppk])
            cxs, cxk = self.f32t()
            self.cp('act', cxs[:, 0:4], px[:, 0:4], r=[pxk], w=[cxk])
            self.tt('dve', self.cso[:, ci, 1, :], pc[:, 0:4], cxs[:, 0:4], ALU.mult, r=[pck, cxk], w=['cso'])
            self.cp('pool', self.cso[:, ci, 0, :], self.stT[:, ci, 1, :], r=['stT'], w=['cso'])
            a1, a1k = self.f32t()
            self.ts('pool', a1[:, 0:4], self.stT[:, ci, 0, :], self.cwc[:, ci, 0:1], ALU.mult, r=['stT', 'cwc'], w=[a1k])
            self.stt(a1[:, 0:4], self.stT[:, ci, 1, :], self.cwc[:, ci, 1:2], a1[:, 0:4], ALU.mult, ALU.add, r=['stT', 'cwc', a1k], w=[a1k])
            self.stt(a1[:, 0:4], self.cso[:, ci, 1, :], self.cwc[:, ci, 2:3], a1[:, 0:4], ALU.mult, ALU.add, r=['cso', 'cwc', a1k], w=[a1k])
            self.tt('dve', X.brT[:, 4 + ci, :], pb2[:, 0:4], a1[:, 0:4], ALU.mult, r=[pb2k, a1k], w=['s_brT1'])
        for s_ in range(4):
            for j_ in range(2):
                d('sp', self.conv_s_o[l, s_, j_].rearrange("(c p) -> p c", p=128), self.cso[:, :, j_, s_], r=['cso'], w=['conv_s_o'], slow=True)
        self.mq_proj(l, X)
        w1b, w1nk = self.wr.next()
        self.w1n = w1b[:, :].rearrange("p (k c h) -> p k c h", k=2, c=16)
        for kind in range(2):
            d('pool', self.w1n[:, kind], self.inp('cmp_w1')[l, kind].rearrange("(c p) h -> p c h", p=128), w=[w1nk])
        pool_l = self.inp('pool').rearrange("l r f -> (l r) f")
        pool_rp = self.inp('pool').rearrange("l (r two) f -> (l r) (two f)", two=2)
        eoff = l * 2560 * 128 * 512
        psr2 = Rot([self.psr.bufs[4], self.psr.bufs[5], self.accr.bufs[0]], 'x')
        keys2 = ['ps4', 'ps5', 'acc0']

        def ps2():
            k = psr2.i % 3
            b, _ = psr2.next()
            return b, keys2[k]
        Hps = [(self.psr.bufs[k], f'ps{k}') for k in range(4)]
        for s in range(4):
            for pp in range(32):
                g1, g1k = self.tmior.next()
                self.gather(g1[:, :], g1k, pool_rp, self.idxA[:, s, pp:pp + 1], 'idxA', eoff)
                pk, pkk = self.b16t()
                self.cp('dve' if pp % 2 == 0 else 'pool', pk[:].rearrange("p (kg s d) -> p kg s d", kg=4, s=2),
                        g1[:].rearrange("p (s kg d) -> p kg s d", s=2, kg=8)[:, 0:4], r=[g1k], w=[pkk])
                pt_, ptk = ps2()
                ptb = pt_[:].bitcast(BF16)
                for kg in range(4):
                    self.tr(ptb[:, kg * 128:(kg + 1) * 128], pk[:, kg * 128:(kg + 1) * 128], self.ident_b[:], r=[pkk, 'ident_b'], w=[ptk])
                xp, xpk = self.b16t()
                self.cp('act', xp[:], ptb[:, 0:512], r=[ptk], w=[xpk])
                for kg in range(4):
                    kind = kg // 2
                    H, Hk = Hps[kg]
                    for s8 in range(8):
                        self.mm(H[:, 16 * pp:16 * pp + 16], lhsT=self.w1n[:, kind, s8, :], rhs=xp[:, kg * 128 + s8:kg * 128 + 128:8],
                                start=(pp == 0 and s8 == 0), stop=False, r=[w1nk, xpk], w=[Hk])
                    for s8 in range(8):
                        if pp == 0:
                            self.mm(H[:, 0:15], lhsT=self.w1n[:, kind, 8 + s8, :], rhs=xp[:, kg * 128 + 8 + s8:kg * 128 + 128:8],
                                    start=False, stop=False, r=[w1nk, xpk], w=[Hk])
                        else:
                            self.mm(H[:, 16 * pp - 1:16 * pp + 15], lhsT=self.w1n[:, kind, 8 + s8, :], rhs=xp[:, kg * 128 + s8:kg * 128 + 128:8],
                                    start=False, stop=False, r=[w1nk, xpk], w=[Hk])
            kcn, kcnk = self.b16t()
            kcn4 = kcn[:].rearrange("p (c f) -> p c f", c=4)
            for kg in range(4):
                kind, g = kg // 2, kg % 2
                H, Hk = Hps[kg]
                hs, hsk = self.b16t()
                self.actf(hs[:], H[:], AF.Silu, bias=self.bpe[:, kind:kind + 1], r=[Hk, 'bpe'], w=[hsk])
                for ch in range(4):
                    pc, pck = ps2()
                    self.mm(pc[:, 0:64], lhsT=hs[:, ch * 128:(ch + 1) * 128], rhs=self.w2sb[:, kind, :], r=[hsk, 'w2sb'], w=[pck])
                    if kind == 0:
                        kf, kfk = self.f32t()
                        sm, smk = self.smr.next()
                        self.actf(kf[:, 0:64], pc[:, 0:64], AF.Square, accum=sm[:, 0:1], r=[pck], w=[kfk, smk])
                        self.actf(sm[:, 1:2], sm[:, 0:1], AF.Sqrt, bias=self.epsc[:, 0:1], scale=1.0 / 64, r=[smk, 'epsc'], w=[smk])
                        self.recip(sm[:, 2:3], sm[:, 1:2], r=[smk], w=[smk])
                        self.stt(kcn4[:, ch, g * 64:(g + 1) * 64], pc[:, 0:64], sm[:, 2:3], self.k0gb[:, :], ALU.mult, ALU.mult, r=[pck, smk, 'k0gb'], w=[kcnk])
                    else:
                        self.cp('act', self.Rs[g][:, ch, 0:64], pc[:, 0:64], r=[pck], w=[f'Rs{g}'])
            pt_, ptk = ps2()
            ptb = pt_[:].bitcast(BF16)
            for ch in range(4):
                self.tr(ptb[:, ch * 128:(ch + 1) * 128], kcn4[:, ch, :], self.ident_b[:], r=[kcnk, 'ident_b'], w=[ptk])
            self.cp('act', self.kcT_s[:], ptb[:, 0:512], r=[ptk], w=['kcT_s'])
            for g in range(2):
                pS, pSk = ps2()
                for ch in range(4):
                    for hh in range(4):
                        self.mm(pS[:, ch * 4 + hh:ch * 4 + hh + 1], lhsT=self.kcT_s[64 * g:64 * g + 64, ch * 128:(ch + 1) * 128],
                                rhs=self.qT_s[64 * g:64 * g + 64, hh, s:s + 1], r=['kcT_s', 'qT_s'], w=[pSk])
                pT, pTk = self.b16t()
                self.actf(pT[:, 0:16], pS[:, 0:16], AF.Exp, scale=0.125, r=[pSk], w=[pTk])
                self.ts('dve', pT[:, 12:16], pT[:, 12:16], self.lastm[:, 0:1], ALU.mult, r=[pTk, 'lastm'], w=[pTk])
                pO, pOk = ps2()
                for ch in range(4):
                    self.mm(pO[0:4, 0:194], lhsT=pT[:, ch * 4:(ch + 1) * 4], rhs=self.Rs[g][:, ch, 0:194], start=(ch == 0), stop=(ch == 3), r=[pTk, f'Rs{g}'], w=[pOk])
                sm, smk = self.smr.next()
                self.recip(sm[0:4, 0:1], pO[0:4, 64:65], r=[pOk], w=[smk])
                ob_, obk = self.f32t()
                self.ts('dve', ob_[0:4, 0:64], pO[0:4, 0:64], sm[0:4, 0:1], ALU.mult, r=[pOk, smk], w=[obk])
                d('sp', self.osc[s, 0, 4 * g:4 * g + 4, :], ob_[0:4, 0:64], r=[obk], w=[('osc', s, 0, g)])
                self.ts('dve', ob_[0:4, 128:257], pO[0:4, 65:194], sm[0:4, 0:1], ALU.mult, r=[pOk, smk], w=[obk])
                pR, pRk = ps2()
                self.mm(pR[0:1, 0:129], lhsT=self.ones_f[0:4, 0:1], rhs=ob_[0:4, 128:257], r=['ones_f', obk], w=[pRk])
                s2_, s2k = self.f32t()
                s2 = s2_[0:1]
                self.tt('dve', s2[:, 0:129], pR[0:1, 0:129], self.sA[:, :], ALU.mult, r=[pRk, 'sA'], w=[s2k])
                self.tt('dve', s2[:, 0:129], s2[:, 0:129], self.sB[:, :], ALU.add, r=[s2k, 'sB'], w=[s2k])
                sm2, sm2k = self.smr.next()
                self.S.add('dve', (lambda o, i: lambda e: e.max(out=o, in_=i))(sm2[0:1, 0:8], s2[:, 0:129]), [s2k], [sm2k])
                self.S.add('dve', (lambda o, a, b: lambda e: e.match_replace(out=o, in_to_replace=a, in_values=b, imm_value=-1e30))(s2[:, 256:385], sm2[0:1, 0:8], s2[:, 0:129]), [s2k, sm2k], [s2k])
                self.S.add('dve', (lambda o, i: lambda e: e.max(out=o, in_=i))(sm2[0:1, 8:16], s2[:, 256:385]), [s2k], [sm2k])
                self.ts('dve', s2[:, 256:385], s2[:, 0:129], sm2[0:1, 15:16], ALU.is_ge, r=[s2k, sm2k], w=[s2k])
                self.ts('dve', s2[:, 0:129], s2[:, 256:385], -1.0, ALU.add, -MASKV * 0.125, ALU.mult, r=[s2k], w=[s2k])
                pB, pBk = ps2()
                self.mm(pB[:, 0:64], lhsT=self.e2a[0:1, :], rhs=s2[:, 0:128:2], start=True, stop=False, r=['e2a', s2k], w=[pBk])
                self.mm(pB[:, 0:64], lhsT=self.e2b[0:1, :], rhs=s2[:, 1:128:2], start=False, stop=True, r=['e2b', s2k], w=[pBk])
                self.cp('act', self.biask[:, g, :], pB[:, 0:64], r=[pBk], w=['biask'])
            d('sp', self.nr, self.scr_rows[s:s + 1, :], r=['scr_rows'], w=['onsa'])
            d('sp', self.nq, self.scr_q[s:s + 1, :], r=['scr_q'], w=['onsa'])
            for br in (1, 2):
                koff, voff = (256, 384) if br == 1 else (512, 640)
                accS, accSk = self.accr.bufs[1], 'acc1'
                first = True
                nblk = 64 if br == 1 else 4
                if br == 2:
                    g3, g3k = self.tmior.next()
                    d('sp', g3[:].rearrange("p (b f) -> p b f", b=4), self.inp('cwin')[l, s].rearrange("(b p) f -> p b f", p=128), w=[g3k])
                for pg in range(nblk):
                    if br == 1:
                        g2, g2k = self.tmior.next()
                        self.gather(g2[:, 0:512], g2k, pool_l, self.idxB[:, s, pg:pg + 1], 'idxB', eoff)
                        ksrc, vsrc = g2[:, 256:384], g2[:, 384:512]
                    else:
                        g2k = g3k
                        ksrc, vsrc = g3[:, pg * 256:pg * 256 + 128], g3[:, pg * 256 + 128:pg * 256 + 256]
                    kb16, kbk = self.b16t()
                    self.cp('dve', kb16[:, 0:128], ksrc, r=[g2k], w=[kbk])
                    vp, vpk = self.Vp.next()
                    self.cp('pool', vp[:, :, 0:64], vsrc.rearrange("p (g d) -> p g d", g=2), r=[g2k], w=[vpk])
                    pt_, ptk = ps2()
                    ptb = pt_[:].bitcast(BF16)
                    self.tr(ptb[:, 0:128], kb16[:, 0:128], self.ident_b[:], r=[kbk, 'ident_b'], w=[ptk])
                    self.cp('act', kb16[:, 128:256], ptb[:, 0:128], r=[ptk], w=[kbk])
                    pS, pSk = ps2()
                    for h in range(8):
                        g, hh = h // 4, h % 4
                        self.mm(pS[:, h:h + 1], lhsT=kb16[64 * g:64 * g + 64, 128:256], rhs=self.qT_s[64 * g:64 * g + 64, 4 + hh, s:s + 1], r=[kbk, 'qT_s'], w=[pSk])
                    pT, pTk = self.b16t()
                    for g in range(2):
                        if br == 1:
                            bias = self.biask[:, g, pg:pg + 1]
                        else:
                            bias = self.winb[:, 0:1] if pg == 0 else None
                        self.actf(pT[:, 4 * g:4 * g + 4], pS[:, 4 * g:4 * g + 4], AF.Exp, bias=bias, scale=0.125, r=[pSk, 'biask', 'winb'], w=[pTk])
                    for g in range(2):
                        self.mm(accS[0:4, g * 65:(g + 1) * 65], lhsT=pT[:, 4 * g:4 * g + 4], rhs=vp[:, g, 0:65], start=first, stop=False, r=[pTk, vpk], w=[accSk])
                        first = False
                pr_, prk = self.f32t()
                pr = pr_[0:1]
                self.tt('dve', pr[:, 0:512].rearrange("p (g h d) -> p g h d", g=2, h=4), self.nq[:, :].rearrange("p (g h d) -> p g h d", g=2, h=4),
                        self.nr[:, koff:koff + 128].rearrange("p (g d) -> p g d", g=2).unsqueeze(2).to_broadcast([1, 2, 4, 64]), ALU.mult, r=['onsa'], w=[prk])
                sm3, sm3k = self.smr.next()
                self.red(sm3[0:1, 0:8], pr[:, 0:512].rearrange("p (h d) -> p h d", h=8), r=[prk], w=[sm3k])
                pn, pnk = self.b16t()
                self.actf(pn[0:1, 0:8], sm3[0:1, 0:8], AF.Exp, scale=0.125, r=[sm3k], w=[pnk])
                self.cp('dve', self.vne[:, :, 0:64], self.nr[:, voff:voff + 128].rearrange("p (g d) -> p g d", g=2), r=['onsa'], w=['vne'])
                for g in range(2):
                    self.mm(accS[0:4, g * 65:(g + 1) * 65], lhsT=pn[0:1, 4 * g:4 * g + 4], rhs=self.vne[0:1, g, 0:65], start=False, stop=True, r=[pnk, 'vne'], w=[accSk])
                sm4, sm4k = self.smr.next()
                self.recip(sm4[0:4, 0:2], accS[0:4, 64:130:65], r=[accSk], w=[sm4k])
                ob_, obk = self.f32t()
                for g in range(2):
                    self.ts('dve', ob_[0:4, g * 64:(g + 1) * 64], accS[0:4, g * 65:g * 65 + 64], sm4[0:4, g:g + 1], ALU.mult, r=[accSk, sm4k], w=[obk])
                    d('sp', self.osc[s, br, 4 * g:4 * g + 4, :], ob_[0:4, g * 64:(g + 1) * 64], r=[obk], w=[('osc', s, br, g)])
            cmb, cmk = self.wr.next()
            cm = cmb[:, 0:2048].rearrange("p (mb f) -> p mb f", mb=2)
            d('pool', cm, self.inp('cmem')[l, s].rearrange("(mb p) f -> p mb f", p=128), w=[cmk])
            pt_, ptk = ps2()
            ptb = pt_[:].bitcast(BF16)
            for mb in range(2):
                for hm in range(4):
                    self.tr(ptb[:, (mb * 4 + hm) * 128:(mb * 4 + hm + 1) * 128], cm[:, mb, hm * 128:(hm + 1) * 128], self.ident_b[:], r=[cmk, 'ident_b'], w=[ptk])
            self.cp('act', self.mkT[:].rearrange("p h (mb m) -> p mb h m", mb=2), ptb[:, 0:1024].rearrange("p (mb h m) -> p mb h m", mb=2, h=4), r=[ptk], w=['mkT'])
            pS, pSk = ps2()
            for mb in range(2):
                for hm in range(4):
                    self.mm(pS[:, mb * 4 + hm:mb * 4 + hm + 1], lhsT=self.mkT[:, hm, mb * 128:(mb + 1) * 128], rhs=X.mqT[:, hm, s:s + 1], r=['mkT', 's_mqT'], w=[pSk])
            pT, pTk = self.b16t()
            self.actf(pT[:, 0:8], pS[:, 0:8], AF.Exp, scale=128 ** -0.5, r=[pSk], w=[pTk])
            pO, pOk = ps2()
            pD, pDk = ps2()
            first = True
            for mb in range(2):
                for hm in range(4):
                    self.mm(pO[:, hm:hm + 1], lhsT=cm[:, mb, 512 + hm * 128:512 + (hm + 1) * 128], rhs=pT[:, mb * 4 + hm:mb * 4 + hm + 1], start=first, stop=False, r=[cmk, pTk], w=[pOk])
                    first = False
            for mb in range(2):
                self.mm(pD[:, 0:4], lhsT=self.ones_b[:], rhs=pT[:, mb * 4:(mb + 1) * 4], start=(mb == 0), stop=(mb == 1), r=['ones_b', pTk], w=[pDk])
            rc, rck = self.f32t()
            self.recip(rc[:, 0:4], pD[:, 0:4], r=[pDk], w=[rck])
            self.tt('dve', X.brT[:, 8:12, s], pO[:, 0:4], rc[:, 0:4], ALU.mult, r=[pOk, rck], w=['s_brT2'])
        on_, onk = self.f32t()
        on = on_[0:4]
        osk = [('osc', s, br, g) for s in range(4) for br in range(3) for g in range(2)]
        for br in range(3):
            ot_, otk = self.tmr.next()
            ot = ot_[0:4, 0:8, :]
            d('sp', ot, self.osc[:, br, :, :], r=osk, w=[otk])
            gb = self.ng_s[:, br * 8:(br + 1) * 8].unsqueeze(2).to_broadcast([4, 8, 64])
            if br == 0:
                self.tt('dve', on[:, 0:512].rearrange("p (h d) -> p h d", h=8), ot, gb, ALU.mult, r=[otk, 'ng_s'], w=[onk])
            else:
                self.tt('dve', ot, ot, gb, ALU.mult, r=[otk, 'ng_s'], w=[otk])
                self.tt('dve', on[:, 0:512], on[:, 0:512], ot.rearrange("p h d -> p (h d)"), ALU.add, r=[otk, onk], w=[onk])
        onb, onbk = self.b16t()
        self.cp('dve', onb[0:4, :], on[:, 0:512], r=[onk], w=[onbk])
        pt_, ptk = self.psn()
        ptb = pt_[:].bitcast(BF16)
        for c in range(4):
            self.tr(ptb[:, c * 4:(c + 1) * 4], onb[0:4, c * 128:(c + 1) * 128], self.ident_b[0:4, 0:4], r=[onbk, 'ident_b'], w=[ptk])
        self.cp('act', X.brT[:, 0:4, :], ptb[:, 0:16].rearrange("p (c s) -> p c s", c=4), r=[ptk], w=['s_brT0'])

    def sample_tail(self, l):
        X = self.X
        d = self.dma
        if l == self.nlayers - 1:
            for s_ in range(4):
                d('sp', self.y_s[s_].rearrange("(k p) -> p k", p=128), X.xT[:, :, s_], r=[('s_xT', kc) for kc in range(8)], w=['y_s'], slow=True)

    def build(self, stage=99):
        self.declare()
        self.make_ctxs()
        if self.with_sample:
            self.declare_sample()
        self.stage = stage
        self.setup()
        if self.with_sample:
            self.setup_sample()
        for l in range(self.nlayers):
            if stage >= 1:
                self.layer_setup(l)
            for t in range(self.ntiles):
                if stage < 3:
                    continue
                self.load_x(l, t)
                self.rmsnorm(0)
                if stage < 4:
                    continue
                wA = [self.wload(self.inp('w_in')[l][:, 0:512], 8, 512), self.wload(self.inp('w_in')[l][:, 512:1024], 8, 512),
                      self.wload(self.inp('w_in')[l][:, 1024:1304], 8, 280)]
                self.tokmajor(l, t, wA)
                if stage < 5:
                    continue
                self.compress(l, t)
                self.fm_proj(l, t)
                wq = self.wload(self.inp('w_in')[l][:, C_MQ:C_MQ + 512], 8, 512)

                def between(h, l=l, wq=wq):
                    if h < 4:
                        self.mq_proj(l, None, [h], wq)
                    else:
                        self.mem_attend([h - 4])
                self.cmp_attend(t, between)
                self.topk(t)
                self.slc(t)
                self.win(t)
                self.nsa_finalize()
                if stage < 6:
                    continue
                ctxs = [self.P]
                if self.with_sample and t == self.ntiles - 1:
                    self.sample_layer(l)
                    ctxs = [self.P, self.X]
                self.phase_b(l, ctxs)
                self.phase_c(l, ctxs)
                self.phase_d(l, ctxs)
                self.store_x(l, t)
                if self.with_sample and t == self.ntiles - 1:
                    self.sample_tail(l)
        self.S.emit()
        return self.nc


def make_consts():
    c = {}
    pos = (np.arange(32)[None, :] * 128 + np.arange(128)[:, None]).astype(np.float32)
    inv = (500000.0 ** (-np.arange(8, dtype=np.float32) / 8)).astype(np.float32)
    ang = pos[:, :, None] * inv[None, None, :]
    c['c_cos'] = np.cos(ang).astype(np.float32)
    c['c_sin'] = np.sin(ang).astype(np.float32)
    q = pos.astype(np.int64)
    j = np.arange(64)[None, None, :]
    qb = (q // 64)[:, :, None]
    forced = (j == 0) | (j == qb) | (j == qb - 1)
    elig = (j * 64) <= q[:, :, None]
    A = (elig & ~forced).astype(np.float32)
    B = np.where(forced, 1e9, np.where(elig, 0.0, -1e9)).astype(np.float32)
    c['c_selA'] = A
    c['c_selB'] = B
    p = np.arange(128)[:, None]
    f = np.arange(128)[None, :]
    c['c_tri'] = (p <= f).astype(np.float32)
    c['c_strict'] = (p > f).astype(np.float32)
    f5 = np.arange(512)[None, :]
    c['c_stair'] = ((16 * (p % 32) + 15) <= f5).astype(np.float32)
    posn = np.arange(256)
    cc = posn - 1
    jj = np.arange(64)[None, :]
    ov = ((cc[:, None] * 16 < (jj + 1) * 64) & (cc[:, None] * 16 + 32 > jj * 64) & (cc[:, None] >= 0) & (cc[:, None] < 255))
    c['c_ovl'] = ov.astype(np.float32).reshape(2, 128, 64).transpose(1, 0, 2).copy()
    k = np.arange(T)[None, :]
    c['c_E'] = ((k // 64) == np.arange(64)[:, None]).astype(np.float32)
    c['c_ident'] = np.eye(128, dtype=np.float32)
    return c


def make_consts_sample():
    c = {}
    inv = (500000.0 ** (-np.arange(8, dtype=np.float32) / 8)).astype(np.float32)
    ang = np.float32(8192.0) * inv
    c['c_cos_s'] = np.tile(np.cos(ang).astype(np.float32)[None, :], (4, 1))
    c['c_sin_s'] = np.tile(np.sin(ang).astype(np.float32)[None, :], (4, 1))
    cc = np.arange(512)[:, None]
    jj = np.arange(129)[None, :]
    ov = ((cc * 16 < (jj + 1) * 64) & (cc * 16 + 32 > jj * 64) & (cc < 511))
    c['c_ovl_s'] = ov.astype(np.float32).reshape(4, 128, 129).transpose(1, 0, 2).copy()
    forced = np.zeros((1, 129), dtype=bool)
    forced[0, [0, 127, 128]] = True
    c['c_sA'] = (~forced).astype(np.float32)
    c['c_sB'] = np.where(forced, 1e9, 0.0).astype(np.float32)
    p = np.arange(128)
    c['c_e2'] = np.stack([(p < 64), (p >= 64)]).astype(np.float32)
    c['c_lastmask'] = (p < 127).astype(np.float32)[:, None]
    wb = np.zeros((128, 1), dtype=np.float32)
    wb[0, 0] = MASKV * 0.125
    c['c_winb'] = wb
    c['c_pm64'] = (p % 64).astype(np.float32)[:, None]
    c['c_pidx'] = p.astype(np.float32)[:, None]
    return c


_NC_CACHE = {}


def _get_program():
    if 'nc' not in _NC_CACHE:
        b = Builder(nlayers=DEPTH, ntiles=NTILE, with_sample=True)
        nc = b.build(99)
        _NC_CACHE['nc'] = nc
        _NC_CACHE['decl'] = set(b.decl)
    return _NC_CACHE['nc'], _NC_CACHE['decl']


def kernel(**inputs):
    nc, decl = _get_program()
    f32 = np.float32
    inp = {k: np.asarray(v) for k, v in inputs.items()}
    consts = make_consts()
    consts.update(make_consts_sample())
    qn, kn = inp['q_norm'], inp['k_norm']
    shared = dict(consts)
    for k in ['w_in', 'w_mem_kv', 'w_branch', 'w_out', 'w_gate_up', 'w_down', 'cmp_w1', 'cmp_w2', 'cmp_pe', 'conv_w',
              'norm_mix', 'norm_mem', 'norm_ffn', 'mem_q_norm']:
        shared[k] = inp[k]
    shared['g12'] = np.concatenate([np.tile(qn, (1, 8)), np.tile(kn[:, 1], (1, 2)), np.tile(kn[:, 2], (1, 2))], axis=1)
    shared['k0g'] = kn[:, 0]
    shared['mkg'] = np.tile(inp['mem_k_norm'], (1, 4))
    n_phys = inp['cache_nsa_kv'].shape[1]
    assert n_phys == 2560
    shared['pool'] = inp['cache_nsa_kv'].reshape(DEPTH, n_phys * 128, 512)
    in_maps = []
    for c in range(8):
        b = c % 4
        ss = slice(4 * c, 4 * c + 4)
        m = dict(shared)
        m['x'] = inp['x_prompt'][b]
        m['mem'] = inp['mem_prompt'][b]
        m['xs'] = inp['x_sample'][ss, 0]
        m['cwin'] = inp['cache_win_kv'][:, ss].reshape(DEPTH, 4, 512, 256)
        m['sconv'] = inp['state_conv'][:, ss]
        m['cmem'] = inp['cache_mem_kv'][:, ss].reshape(DEPTH, 4, 256, 1024)
        m['pt'] = inp['page_table'][ss].astype(np.int32)
        in_maps.append({k: np.ascontiguousarray(v) for k, v in m.items() if k in decl})
    res = run_bass_kernel_spmd(nc, in_maps, core_ids=list(range(8)))
    R = res.results
    y_p = np.stack([R[c]['y'] for c in range(4)], 0).astype(f32)
    y_s = np.concatenate([R[c]['y_s'] for c in range(8)], 0).reshape(32, 1, D).astype(f32)
    rows_p = np.stack([R[c]['rows_p'] for c in range(4)], 1).reshape(DEPTH, 4, T, 4, 2, 64).astype(f32)
    rows_s = np.concatenate([R[c]['rows_s'] for c in range(8)], 1).reshape(DEPTH, 32, 1, 4, 2, 64).astype(f32)
    win_p = np.stack([R[c]['win_p'] for c in range(4)], 1).reshape(DEPTH, 4, 512, 2, 2, 64).astype(f32)
    win_s = np.concatenate([R[c]['win_s'] for c in range(8)], 1).reshape(DEPTH, 32, 512, 2, 2, 64).astype(f32)
    conv_p = np.stack([R[c]['conv_p'] for c in range(4)], 1).astype(f32)
    conv_s = np.concatenate([R[c]['conv_s'] for c in range(8)], 1).astype(f32)
    mem_p = np.stack([R[c]['mem_p'] for c in range(4)], 1).reshape(DEPTH, 4, 256, 2, 4, 128).astype(f32)
    return (y_p, y_s, rows_p, rows_s, win_p, win_s, conv_p, conv_s, mem_p)
```
